# Optimizing a Trainium2 kernel written in Bass

```python
import math
import jax
import jax.numpy as jnp
from jax import lax
import numpy as np

D_MODEL = 1024
BATCH = 4
SEQ = 4096
DEPTH = 2

N_MEM = 256
EPS = 1e-6
ROPE_THETA = 10000.0
Q_BLOCK = 128
NEG_BIG = -1e30
TINY = 1e-20

NSA_HEADS = 8
NSA_KV_GROUPS = 2
NSA_HPG = NSA_HEADS // NSA_KV_GROUPS
HEAD_DIM = 64
CMP_BLOCK = 32
CMP_STRIDE = 16
CMP_HIDDEN = 128
SEL_BLOCK = 64
SEL_TOPK = 16
WINDOW = 512
FORCE_SCORE = 1e6
NSA_Q_WIDTH = NSA_HEADS * HEAD_DIM
NSA_KV_WIDTH = NSA_KV_GROUPS * HEAD_DIM

HGRN_HEADS = 4
HGRN_EXPAND = 64
HGRN_VDIM = 64
HGRN_CHUNK = 64
HGRN_K_WIDTH = HGRN_HEADS * HGRN_EXPAND
HGRN_V_WIDTH = HGRN_HEADS * HGRN_VDIM

GDN_HEADS = 4
GDN_DK = 64
GDN_DV = 64
GDN_CONV = 4
GDN_CHUNK = 64
GDN_K_WIDTH = GDN_HEADS * GDN_DK
GDN_V_WIDTH = GDN_HEADS * GDN_DV

XATTN_HEADS = 4
XATTN_HEAD_DIM = 128
XATTN_WIDTH = XATTN_HEADS * XATTN_HEAD_DIM

D_FF = 2816
FFN_CONV = 3

IN_SIZES = (
    NSA_Q_WIDTH, NSA_KV_WIDTH, NSA_KV_WIDTH, NSA_KV_WIDTH, NSA_KV_WIDTH, NSA_KV_WIDTH, NSA_KV_WIDTH, 3 * NSA_HEADS,
    HGRN_K_WIDTH, HGRN_K_WIDTH, HGRN_V_WIDTH, HGRN_V_WIDTH,
    GDN_K_WIDTH, GDN_K_WIDTH, GDN_V_WIDTH, GDN_V_WIDTH, GDN_HEADS, GDN_HEADS,
    D_MODEL, D_MODEL, D_MODEL,
)
IN_WIDTH = sum(IN_SIZES)

kernel_name = 'hybrid_nsa_hgrn2_gdn_block'

F32 = jnp.float32


def _split(z, sizes):
    offs = [int(o) for o in np.cumsum(sizes)[:-1]]
    return jnp.split(z, offs, axis=-1)


def rmsnorm(x, g):
    xf = x.astype(F32)
    y = xf * lax.rsqrt(jnp.mean(xf * xf, axis=-1, keepdims=True) + EPS)
    return (y * g.astype(F32)).astype(x.dtype)


def l2norm(x):
    xf = x.astype(F32)
    return xf * lax.rsqrt(jnp.sum(xf * xf, axis=-1, keepdims=True) + EPS)


def masked_softmax(s, mask):
    s = jnp.where(mask, s, NEG_BIG)
    m = jnp.max(s, axis=-1, keepdims=True)
    e = jnp.where(mask, jnp.exp(s - m), 0.0)
    return e / jnp.maximum(jnp.sum(e, axis=-1, keepdims=True), 1e-30)


def masked_exp(d, mask):
    return jnp.where(mask, jnp.exp(jnp.where(mask, d, 0.0)), 0.0)


def rope_tables(seq_len):
    inv_freq = 1.0 / (ROPE_THETA ** (jnp.arange(0, HEAD_DIM, 2, dtype=F32) / HEAD_DIM))
    ang = jnp.arange(seq_len, dtype=F32)[:, None] * inv_freq[None, :]
    return jnp.cos(ang), jnp.sin(ang)


def apply_rope(x, cos, sin):
    half = x.shape[-1] // 2
    xf = x.astype(F32)
    x1, x2 = xf[..., :half], xf[..., half:]
    c = cos[None, :, None, :]
    s = sin[None, :, None, :]
    return jnp.concatenate([x1 * c - x2 * s, x2 * c + x1 * s], axis=-1).astype(x.dtype)


def causal_depthwise_conv(x, w):
    k = w.shape[0]
    return lax.conv_general_dilated(
        x, w[:, None, :].astype(x.dtype), window_strides=(1,), padding=((k - 1, 0),),
        dimension_numbers=('NWC', 'WIO', 'NWC'), feature_group_count=x.shape[-1])


def _to_chunks(t, c):
    b, s, h = t.shape[:3]
    t = t.reshape((b, s // c, c, h) + t.shape[3:])
    return t.transpose((1, 0, 3, 2) + tuple(range(4, t.ndim)))


def _from_chunks(t):
    n, b, h, c, d = t.shape
    return t.transpose(1, 0, 3, 2, 4).reshape(b, n * c, h, d)


def nsa_attention(q, k_cmp, v_cmp, k_sel, v_sel, k_win, v_win, gate_logits,
                  q_norm, k_norm, pos_k, pos_v, ck_w1, ck_w2, cv_w1, cv_w2, cos, sin):
    B, S, _ = q.shape
    G, HG, DH = NSA_KV_GROUPS, NSA_HPG, HEAD_DIM
    scale = DH ** -0.5
    q = apply_rope(rmsnorm(q.reshape(B, S, NSA_HEADS, DH), q_norm), cos, sin).reshape(B, S, G, HG, DH)

    def keys(k, g):
        return apply_rope(rmsnorm(k.reshape(B, S, G, DH), g), cos, sin)

    k_cmp, k_sel, k_win = keys(k_cmp, k_norm[0]), keys(k_sel, k_norm[1]), keys(k_win, k_norm[2])
    v_cmp, v_sel, v_win = (v.reshape(B, S, G, DH) for v in (v_cmp, v_sel, v_win))
    t_pos = jnp.arange(S)

    n_cmp = (S - CMP_BLOCK) // CMP_STRIDE + 1
    tok = jnp.arange(n_cmp)[:, None] * CMP_STRIDE + jnp.arange(CMP_BLOCK)[None, :]

    def compress(t, pos, w1, w2):
        blk = t[:, tok] + pos[None, None, :, None, :].astype(t.dtype)
        blk = blk.transpose(0, 1, 3, 2, 4).reshape(B, n_cmp, G, CMP_BLOCK * DH)
        return jax.nn.gelu(blk @ w1) @ w2

    kc = compress(k_cmp, pos_k, ck_w1, ck_w2)
    vc = compress(v_cmp, pos_v, cv_w1, cv_w2)
    cmp_mask = tok[:, -1][None, :] <= t_pos[:, None]
    s_cmp = jnp.einsum('bsghd,bngd->bghsn', q, kc).astype(F32) * scale
    p_cmp = masked_softmax(s_cmp, cmp_mask)
    o_cmp = jnp.einsum('bghsn,bngd->bsghd', p_cmp.astype(vc.dtype), vc)

    n_sel = S // SEL_BLOCK
    top = min(SEL_TOPK, n_sel)
    overlap = jax.nn.one_hot(tok // SEL_BLOCK, n_sel, dtype=F32).sum(axis=1)
    imp = jnp.einsum('bghsn,nj->bgsj', p_cmp, overlap)
    blk = jnp.arange(n_sel)[None, :]
    cur = (t_pos // SEL_BLOCK)[:, None]
    forced = (blk == 0) | (blk == cur) | (blk == cur - 1)
    valid = blk <= cur
    imp = jnp.where(valid, jnp.where(forced, FORCE_SCORE, imp), -1.0)
    _, sel_idx = lax.top_k(imp, top)

    n_qb = S // Q_BLOCK
    ks_tbl = k_sel.reshape(B, n_sel, SEL_BLOCK, G, DH).transpose(0, 3, 1, 2, 4)
    vs_tbl = v_sel.reshape(B, n_sel, SEL_BLOCK, G, DH).transpose(0, 3, 1, 2, 4)
    q_blocks = q.reshape(B, n_qb, Q_BLOCK, G, HG, DH).transpose(1, 0, 3, 2, 4, 5)
    idx_blocks = sel_idx.reshape(B, G, n_qb, Q_BLOCK, top).transpose(2, 0, 1, 3, 4)
    starts = jnp.arange(n_qb) * Q_BLOCK
    gather = jax.vmap(jax.vmap(lambda tbl, ix: tbl[ix]))

    def sel_block(args):
        qb, ib, start = args
        kg = gather(ks_tbl, ib).reshape(B, G, Q_BLOCK, top * SEL_BLOCK, DH)
        vg = gather(vs_tbl, ib).reshape(B, G, Q_BLOCK, top * SEL_BLOCK, DH)
        kpos = (ib[..., None] * SEL_BLOCK + jnp.arange(SEL_BLOCK)).reshape(B, G, Q_BLOCK, top * SEL_BLOCK)
        qpos = start + jnp.arange(Q_BLOCK)
        mask = (kpos <= qpos[None, None, :, None])[:, :, :, None, :]
        s = jnp.einsum('bgqhd,bgqnd->bgqhn', qb, kg).astype(F32) * scale
        p = masked_softmax(s, mask)
        return jnp.einsum('bgqhn,bgqnd->bgqhd', p.astype(vg.dtype), vg)

    o_sel = lax.map(sel_block, (q_blocks, idx_blocks, starts))
    o_sel = o_sel.transpose(1, 0, 3, 2, 4, 5).reshape(B, S, G, HG, DH)

    n_pre = WINDOW // Q_BLOCK
    band = (n_pre + 1) * Q_BLOCK

    def banded(t):
        tp = jnp.pad(t, ((0, 0), (WINDOW, 0), (0, 0), (0, 0))).reshape(B, n_qb + n_pre, Q_BLOCK, G, DH)
        return jnp.concatenate([tp[:, j:j + n_qb] for j in range(n_pre + 1)], axis=2)

    kw, vw = banded(k_win), banded(v_win)
    qw = q.reshape(B, n_qb, Q_BLOCK, G, HG, DH)
    qpos = starts[:, None] + jnp.arange(Q_BLOCK)[None, :]
    kpos = starts[:, None] - WINDOW + jnp.arange(band)[None, :]
    dist = qpos[:, :, None] - kpos[:, None, :]
    wmask = (dist >= 0) & (dist < WINDOW) & (kpos[:, None, :] >= 0)
    s_win = jnp.einsum('bnqghd,bnkgd->bnghqk', qw, kw).astype(F32) * scale
    p_win = masked_softmax(s_win, wmask[None, :, None, None])
    o_win = jnp.einsum('bnghqk,bnkgd->bnqghd', p_win.astype(vw.dtype), vw).reshape(B, S, G, HG, DH)

    g = jax.nn.sigmoid(gate_logits.astype(F32)).reshape(B, S, G, HG, 3)
    o = g[..., 0:1] * o_cmp + g[..., 1:2] * o_sel + g[..., 2:3] * o_win
    return o.reshape(B, S, NSA_Q_WIDTH).astype(q.dtype)


def gla_chunk_scan(q, k, v, log_f):
    B, S, H, DK = q.shape
    DV = v.shape[-1]
    C = HGRN_CHUNK
    xs = tuple(_to_chunks(t.astype(F32), C) for t in (q, k, v, log_f))
    causal = jnp.tril(jnp.ones((C, C), dtype=bool))[:, :, None]

    def step(state, inp):
        qc, kc, vc, gc = inp
        b = jnp.cumsum(gc, axis=2)
        o_inter = jnp.einsum('bhtd,bhde->bhte', qc * jnp.exp(b), state)
        decay = masked_exp(b[:, :, :, None, :] - b[:, :, None, :, :], causal)
        attn = jnp.einsum('bhtd,bhsd,bhtsd->bhts', qc, kc, decay)
        o_intra = jnp.einsum('bhts,bhse->bhte', attn, vc)
        b_last = b[:, :, -1:, :]
        state = jnp.exp(b_last[:, :, 0, :, None]) * state + jnp.einsum(
            'bhsd,bhse->bhde', kc * jnp.exp(b_last - b), vc)
        return state, o_inter + o_intra

    _, o = lax.scan(step, jnp.zeros((B, H, DK, DV), F32), xs)
    return _from_chunks(o)


def hgrn2_recurrence(q, f_logit, i, out_gate, lower_bound, out_norm):
    B, S, _ = q.shape
    H, DK, DV = HGRN_HEADS, HGRN_EXPAND, HGRN_VDIM
    qh = jax.nn.silu(q).reshape(B, S, H, DK)
    fz = f_logit.astype(F32).reshape(B, S, H, DK)
    lb = lower_bound.astype(F32).reshape(H, DK)
    f = lb + (1.0 - lb) * jax.nn.sigmoid(fz)
    log_f = jnp.log(jnp.maximum(f, TINY))
    k = (1.0 - lb) * jax.nn.sigmoid(-fz)
    v = i.reshape(B, S, H, DV)
    o = gla_chunk_scan(qh, k, v, log_f)
    o = rmsnorm(o, out_norm) * jax.nn.sigmoid(out_gate.astype(F32)).reshape(B, S, H, DV)
    return o.reshape(B, S, H * DV).astype(q.dtype)


def gated_delta_chunk_scan(q, k, v, beta, log_alpha):
    B, S, H, DK = q.shape
    DV = v.shape[-1]
    C = GDN_CHUNK
    xs = tuple(_to_chunks(t.astype(F32), C) for t in (q, k, v, beta, log_alpha))
    strict = jnp.tril(jnp.ones((C, C), dtype=bool), -1)
    incl = jnp.tril(jnp.ones((C, C), dtype=bool))
    eye = jnp.eye(C, dtype=F32)

    def step(state, inp):
        qc, kc, vc, bc, gc = inp
        b = jnp.cumsum(gc, axis=-1)
        diff = b[..., :, None] - b[..., None, :]
        dec_strict = masked_exp(diff, strict)
        dec_incl = masked_exp(diff, incl)
        kb = kc * bc[..., None]
        lower = eye + jnp.einsum('bhtd,bhsd->bhts', kb, kc) * dec_strict
        rhs = vc * bc[..., None] - jnp.einsum('bhtd,bhde->bhte', kb * jnp.exp(b)[..., None], state)
        v_new = lax.linalg.triangular_solve(lower, rhs, left_side=True, lower=True)
        o = jnp.einsum('bhtd,bhde->bhte', qc * jnp.exp(b)[..., None], state) + jnp.einsum(
            'bhts,bhse->bhte', jnp.einsum('bhtd,bhsd->bhts', qc, kc) * dec_incl, v_new)
        b_last = b[..., -1:]
        state = jnp.exp(b_last)[..., None] * state + jnp.einsum(
            'bhsd,bhse->bhde', kc * jnp.exp(b_last - b)[..., None], v_new)
        return state, o

    _, o = lax.scan(step, jnp.zeros((B, H, DK, DV), F32), xs)
    return _from_chunks(o)


def gated_deltanet(q, k, v, z, beta_logit, a_logit, conv_w, a_log, dt_bias, out_norm):
    B, S, _ = q.shape
    H, DK, DV = GDN_HEADS, GDN_DK, GDN_DV
    qkv = jax.nn.silu(causal_depthwise_conv(jnp.concatenate([q, k, v], axis=-1), conv_w))
    qc, kc, vc = jnp.split(qkv, [GDN_K_WIDTH, 2 * GDN_K_WIDTH], axis=-1)
    qc = l2norm(qc.reshape(B, S, H, DK)) * DK ** -0.5
    kc = l2norm(kc.reshape(B, S, H, DK))
    vc = vc.reshape(B, S, H, DV)
    beta = jax.nn.sigmoid(beta_logit.astype(F32))
    log_alpha = -jnp.exp(a_log.astype(F32)) * jax.nn.softplus(a_logit.astype(F32) + dt_bias.astype(F32))
    o = gated_delta_chunk_scan(qc, kc, vc, beta, log_alpha)
    o = rmsnorm(o, out_norm) * jax.nn.silu(z.astype(F32)).reshape(B, S, H, DV)
    return o.reshape(B, S, H * DV).astype(z.dtype)


def hybrid_mixer(h, w_in, nsa_q_norm, nsa_k_norm, cmp_pos_k, cmp_pos_v, cmp_k_w1, cmp_k_w2, cmp_v_w1, cmp_v_w2,
                 hgrn_lb, hgrn_out_norm, gdn_conv, gdn_a_log, gdn_dt_bias, gdn_out_norm,
                 w_branch_a, w_branch_b, w_branch_c, w_mix_out, cos, sin):
    (nq, nkc, nvc, nks, nvs, nkw, nvw, ngate,
     hq, hf, hi, hg,
     cq, ck, cv, cz, cb, ca,
     ma, mb, mc) = _split(h @ w_in, IN_SIZES)
    ya = nsa_attention(nq, nkc, nvc, nks, nvs, nkw, nvw, ngate, nsa_q_norm, nsa_k_norm,
                       cmp_pos_k, cmp_pos_v, cmp_k_w1, cmp_k_w2, cmp_v_w1, cmp_v_w2, cos, sin)
    yb = hgrn2_recurrence(hq, hf, hi, hg, hgrn_lb, hgrn_out_norm)
    yc = gated_deltanet(cq, ck, cv, cz, cb, ca, gdn_conv, gdn_a_log, gdn_dt_bias, gdn_out_norm)
    merged = (jax.nn.sigmoid(ma) * (ya @ w_branch_a)
              + jax.nn.sigmoid(mb) * (yb @ w_branch_b)
              + jax.nn.sigmoid(mc) * (yc @ w_branch_c))
    return merged @ w_mix_out


def memory_cross_attention(h, mem_k, mem_v, w_q, q_norm, k_norm, w_o):
    B, S, _ = h.shape
    q = rmsnorm((h @ w_q).reshape(B, S, XATTN_HEADS, XATTN_HEAD_DIM), q_norm)
    k = rmsnorm(mem_k, k_norm)
    s = jnp.einsum('bshd,bmhd->bhsm', q, k).astype(F32) * XATTN_HEAD_DIM ** -0.5
    p = jax.nn.softmax(s, axis=-1).astype(mem_v.dtype)
    o = jnp.einsum('bhsm,bmhd->bshd', p, mem_v).reshape(B, S, XATTN_WIDTH)
    return o @ w_o


def conv_glu_ffn(h, w_up, conv_w, w_down):
    u = causal_depthwise_conv(h @ w_up, conv_w)
    a, b = jnp.split(u, 2, axis=-1)
    return (jax.nn.silu(a) * b) @ w_down


def setup_inputs(seed: int = 0) -> dict:
    key = jax.random.key(seed)
    keys = jax.random.split(key, 33)
    L = DEPTH

    def nrm(i, shape, scale):
        return jax.random.normal(keys[i], shape, F32) * scale

    def gain(i, shape):
        return 1.0 + 0.02 * jax.random.normal(keys[i], shape, F32)

    u = jax.random.uniform(keys[18], (L, GDN_HEADS), F32)
    dt = jnp.exp(u * (math.log(0.1) - math.log(0.001)) + math.log(0.001))
    return {
        'x': nrm(0, (BATCH, SEQ, D_MODEL), 1.0),
        'mem': nrm(1, (BATCH, N_MEM, D_MODEL), 1.0),
        'mem_norm': gain(2, (D_MODEL,)),
        'mem_w_kv': nrm(3, (D_MODEL, 2 * XATTN_WIDTH), D_MODEL ** -0.5),
        'hgrn_lb_logits': nrm(4, (L, HGRN_K_WIDTH), 0.5),
        'norm_mix': gain(5, (L, D_MODEL)),
        'w_in': nrm(6, (L, D_MODEL, IN_WIDTH), D_MODEL ** -0.5),
        'nsa_q_norm': gain(7, (L, HEAD_DIM)),
        'nsa_k_norm': gain(8, (L, 3, HEAD_DIM)),
        'cmp_pos_k': nrm(9, (L, CMP_BLOCK, HEAD_DIM), 0.1),
        'cmp_pos_v': nrm(10, (L, CMP_BLOCK, HEAD_DIM), 0.1),
        'cmp_k_w1': nrm(11, (L, CMP_BLOCK * HEAD_DIM, CMP_HIDDEN), (CMP_BLOCK * HEAD_DIM) ** -0.5),
        'cmp_k_w2': nrm(12, (L, CMP_HIDDEN, HEAD_DIM), CMP_HIDDEN ** -0.5),
        'cmp_v_w1': nrm(13, (L, CMP_BLOCK * HEAD_DIM, CMP_HIDDEN), (CMP_BLOCK * HEAD_DIM) ** -0.5),
        'cmp_v_w2': nrm(14, (L, CMP_HIDDEN, HEAD_DIM), CMP_HIDDEN ** -0.5),
        'hgrn_out_norm': gain(15, (L, HGRN_VDIM)),
        'gdn_conv': nrm(16, (L, GDN_CONV, 2 * GDN_K_WIDTH + GDN_V_WIDTH), GDN_CONV ** -0.5),
        'gdn_a_log': jnp.log(jax.random.uniform(keys[17], (L, GDN_HEADS), F32, minval=1.0, maxval=16.0)),
        'gdn_dt_bias': dt + jnp.log(-jnp.expm1(-dt)),
        'gdn_out_norm': gain(19, (L, GDN_DV)),
        'w_branch_a': nrm(20, (L, NSA_Q_WIDTH, D_MODEL), NSA_Q_WIDTH ** -0.5),
        'w_branch_b': nrm(21, (L, HGRN_V_WIDTH, D_MODEL), HGRN_V_WIDTH ** -0.5),
        'w_branch_c': nrm(22, (L, GDN_V_WIDTH, D_MODEL), GDN_V_WIDTH ** -0.5),
        'w_mix_out': nrm(23, (L, D_MODEL, D_MODEL), D_MODEL ** -0.5),
        'norm_cross': gain(24, (L, D_MODEL)),
        'xattn_wq': nrm(25, (L, D_MODEL, XATTN_WIDTH), D_MODEL ** -0.5),
        'xattn_q_norm': gain(26, (L, XATTN_HEAD_DIM)),
        'xattn_k_norm': gain(27, (L, XATTN_HEAD_DIM)),
        'xattn_wo': nrm(28, (L, XATTN_WIDTH, D_MODEL), XATTN_WIDTH ** -0.5),
        'norm_ffn': gain(29, (L, D_MODEL)),
        'ffn_w_up': nrm(30, (L, D_MODEL, 2 * D_FF), D_MODEL ** -0.5),
        'ffn_conv': nrm(31, (L, FFN_CONV, 2 * D_FF), FFN_CONV ** -0.5),
        'ffn_w_down': nrm(32, (L, D_FF, D_MODEL), D_FF ** -0.5),
    }


def reference(x, mem, mem_norm, mem_w_kv, hgrn_lb_logits, norm_mix, w_in, nsa_q_norm, nsa_k_norm,
              cmp_pos_k, cmp_pos_v, cmp_k_w1, cmp_k_w2, cmp_v_w1, cmp_v_w2, hgrn_out_norm,
              gdn_conv, gdn_a_log, gdn_dt_bias, gdn_out_norm, w_branch_a, w_branch_b, w_branch_c,
              w_mix_out, norm_cross, xattn_wq, xattn_q_norm, xattn_k_norm, xattn_wo,
              norm_ffn, ffn_w_up, ffn_conv, ffn_w_down):
    B, S, _ = x.shape
    cos, sin = rope_tables(S)
    mkv = rmsnorm(mem, mem_norm) @ mem_w_kv
    mem_k, mem_v = jnp.split(mkv, 2, axis=-1)
    mem_k = mem_k.reshape(B, mem.shape[1], XATTN_HEADS, XATTN_HEAD_DIM)
    mem_v = mem_v.reshape(B, mem.shape[1], XATTN_HEADS, XATTN_HEAD_DIM)
    probs = jax.nn.softmax(hgrn_lb_logits.astype(F32), axis=0)
    lower_bounds = jnp.cumsum(probs, axis=0) - probs[0:1]
    for l in range(DEPTH):
        x = x + hybrid_mixer(rmsnorm(x, norm_mix[l]), w_in[l], nsa_q_norm[l], nsa_k_norm[l],
                             cmp_pos_k[l], cmp_pos_v[l], cmp_k_w1[l], cmp_k_w2[l], cmp_v_w1[l], cmp_v_w2[l],
                             lower_bounds[l], hgrn_out_norm[l], gdn_conv[l], gdn_a_log[l], gdn_dt_bias[l],
                             gdn_out_norm[l], w_branch_a[l], w_branch_b[l], w_branch_c[l], w_mix_out[l],
                             cos, sin)
        x = x + memory_cross_attention(rmsnorm(x, norm_cross[l]), mem_k, mem_v, xattn_wq[l],
                                       xattn_q_norm[l], xattn_k_norm[l], xattn_wo[l])
        x = x + conv_glu_ffn(rmsnorm(x, norm_ffn[l]), ffn_w_up[l], ffn_conv[l], ffn_w_down[l])
    return x
```

```python
import numpy as np
import concourse.bass as bass
import concourse.mybir as mybir
from concourse.bass_utils import run_bass_kernel_spmd

F32 = mybir.dt.float32
BF16 = mybir.dt.bfloat16
AF = mybir.ActivationFunctionType
ALU = mybir.AluOpType
AX = mybir.AxisListType
F32R = mybir.dt.float32r
FP32R = False


class Buf:
    __slots__ = ("ap", "w", "r", "name")

    def __init__(self, ap, name):
        self.ap = ap
        self.name = name
        self.w = None
        self.r = {}

    def __getitem__(self, k):
        return self.ap[k]


class MK:
    NDMA = 6

    def __init__(self):
        self.nc = bass.Bass("TRN2", target_bir_lowering=False)
        nc = self.nc
        self.eng = {"pe": nc.tensor, "act": nc.scalar, "dve": nc.vector, "pool": nc.gpsimd, "sp": nc.sync}
        self.sem = {}
        self.cnt = {}
        for e in self.eng:
            self.sem[e] = nc.alloc_semaphore("s_" + e)
            self.cnt[e] = 0
        self.dq = {}
        for q in ("sp", "pool", "act"):
            ks = []
            for i in range(self.NDMA):
                k = "d_%s%d" % (q, i)
                self.sem[k] = nc.alloc_semaphore(k)
                self.cnt[k] = 0
                ks.append(k)
            self.dq[q] = [ks, 0]
        self.obs = {e: {} for e in self.eng}
        self.nbuf = 0
        self.live = []
        self.n_ins = 0
        self.marks = []
        self._pm = None

    def dram(self, name, shape, dt=F32, kind="Internal"):
        t = self.nc.dram_tensor(name, list(shape), dt, kind=kind)
        return Buf(t.ap(), name)

    def sb(self, shape, dt=F32, name=None):
        self.nbuf += 1
        name = "%s_%d" % (name or "sb", self.nbuf)
        g = self.nc.sbuf_tensor(name, list(shape), dt)
        h = g.__enter__()
        self.live[-1].append(g)
        return Buf(h.ap(), name)

    def ps(self, shape, dt=F32, name=None):
        self.nbuf += 1
        name = "%s_%d" % (name or "ps", self.nbuf)
        g = self.nc.psum_tensor(name, list(shape), dt)
        h = g.__enter__()
        self.live[-1].append(g)
        return Buf(h.ap(), name)

    def phase_begin(self):
        self.live.append([])

    def phase_end(self):
        self.barrier()
        for g in reversed(self.live.pop()):
            g.__exit__(None, None, None)

    def _wait(self, e, evs):
        eng = self.eng[e]
        ob = self.obs[e]
        for k, v in evs:
            if k == e and e == "pe":
                continue
            if ob.get(k, 0) < v:
                eng.wait_ge(self.sem[k], v)
                ob[k] = v

    def _deps(self, e, reads, writes):
        evs = []
        for t in reads:
            if t.w is not None:
                evs.append(t.w)
        for t in writes:
            if t.w is not None:
                evs.append(t.w)
            for k, v in t.r.items():
                if k != e:
                    evs.append((k, v))
        return evs

    def _mark(self, ev, reads, writes):
        k, v = ev
        for t in reads:
            if t.r.get(k, 0) < v:
                t.r[k] = v
        for t in writes:
            t.w = ev
            t.r = {}

    def op(self, e, fn, reads=(), writes=(), inc=True):
        self._wait(e, self._deps(e, reads, writes))
        ins = fn(self.eng[e])
        self.n_ins += 1
        self._domark(ins)
        if inc:
            self.cnt[e] += 1
            ins.then_inc(self.sem[e], 1)
            ev = (e, self.cnt[e])
        else:
            ev = (e, self.cnt[e] + 1)
        self._mark(ev, reads, writes)
        return ins

    def dma(self, q, out, in_, reads=(), writes=(), **kw):
        ks, i = self.dq[q]
        k = ks[i % len(ks)]
        self.dq[q][1] = i + 1
        evs = self._deps(q, reads, writes)
        if self.cnt[k] > 0:
            evs.append((k, self.cnt[k]))
        self._wait(q, evs)
        ins = self.eng[q].dma_start(out=out, in_=in_, **kw)
        self.n_ins += 1
        self._domark(ins)
        self.cnt[k] += 16
        ins.then_inc(self.sem[k], 16)
        self._mark((k, self.cnt[k]), reads, writes)
        return ins

    def mark(self, name):
        self._pm = name

    def _domark(self, ins):
        if self._pm is not None:
            try:
                self.marks.append((self._pm, ins.ins.name))
            except Exception as ex:
                self.marks.append((self._pm, repr(ex)))
            self._pm = None

    def barrier(self):
        allev = [(k, v) for k, v in self.cnt.items() if v > 0]
        for e in self.eng:
            self._wait(e, [(k, v) for k, v in allev if k != e])

    def finish(self):
        self.barrier()

    def mm(self, out, pairs, reads, tr=None):
        n = len(pairs)
        if FP32R:
            pairs = [((l.bitcast(F32R) if l.dtype == F32 else l), (r.bitcast(F32R) if r.dtype == F32 else r)) for l, r in pairs]
        for i, (l, r) in enumerate(pairs):
            self.op("pe", lambda pe, l=l, r=r, i=i: pe.matmul(out, l, r, start=(i == 0), stop=(i == n - 1)),
                    reads=reads, writes=[tr], inc=(i == n - 1))


def run_pipelined(gens, width, stagger=0):
    it = iter(gens)
    active = []
    done = set()
    exhausted = False
    head = stagger
    while True:
        while not exhausted and len(active) < (1 if head > 0 else width):
            g = next(it, None)
            if g is None:
                exhausted = True
                break
            active.append([g, None])
        if not active:
            break
        progressed = False
        for slot in list(active):
            g, blk = slot
            if blk is not None and blk not in done:
                continue
            slot[1] = None
            progressed = True
            try:
                r = next(g)
            except StopIteration:
                active.remove(slot)
                continue
            if isinstance(r, tuple):
                if r[0] == "wait":
                    slot[1] = r[1]
                elif r[0] == "done":
                    done.add(r[1])
        if head > 0:
            head -= 1
        assert progressed, "pipeline deadlock"


def run_pipelined_multi(streams):
    st = [{"it": iter(x["gens"]), "w": x["width"], "p": x.get("period", 1), "active": [], "ex": False} for x in streams]
    done = set()
    rnd = 0
    while True:
        alive = False
        for S_ in st:
            while not S_["ex"] and len(S_["active"]) < S_["w"]:
                g = next(S_["it"], None)
                if g is None:
                    S_["ex"] = True
                    break
                S_["active"].append([g, None])
            if S_["active"]:
                alive = True
        if not alive:
            break
        progressed = False
        only_bg = all((not S_["active"]) for S_ in st if S_["p"] == 1)
        for S_ in st:
            if S_["p"] > 1 and (rnd % S_["p"]) != 0 and not only_bg:
                continue
            for slot in list(S_["active"]):
                g, blk = slot
                if blk is not None and blk not in done:
                    continue
                slot[1] = None
                progressed = True
                try:
                    r = next(g)
                except StopIteration:
                    S_["active"].remove(slot)
                    continue
                if isinstance(r, tuple):
                    if r[0] == "wait":
                        slot[1] = r[1]
                    elif r[0] == "done":
                        done.add(r[1])
        rnd += 1
        assert progressed or any(S_["p"] > 1 for S_ in st), "pipeline deadlock"


D = 1024
EPS = 1e-6
TINY = 1e-20


def bcast_rows(ap1d, n):
    return ap1d.partition_broadcast(n)


def load_weight_bf16(mk, w_d, K, N, name, q="sp", chunk=2048):
    kc = K // 128
    wb = mk.sb([128, kc, N], BF16, name)
    mk.phase_begin()
    stg = [mk.sb([128, min(N, chunk)], F32, "wstg") for _ in range(4)]
    i = 0
    for k in range(kc):
        for c0 in range(0, N, chunk):
            c1 = min(N, c0 + chunk)
            s = stg[i % 4]
            mk.dma(("sp", "act")[i % 2], s.ap[:, 0:c1 - c0], w_d.ap[k * 128:(k + 1) * 128, c0:c1], reads=[w_d], writes=[s])
            e = "pool" if i % 2 == 0 else "dve"
            mk.op(e, lambda en, s=s, k=k, c0=c0, c1=c1: en.tensor_copy(wb.ap[:, k, c0:c1], s.ap[:, 0:c1 - c0]),
                  reads=[s], writes=[wb])
            i += 1
    mk.phase_end()
    return wb


def rmsnorm_to_fm(mk, xt, gB, ident, hT, col0, scr, ssb, hb, psT):
    mk.op("act", lambda en: en.activation(out=scr.ap, in_=xt.ap, func=AF.Square, accum_out=ssb.ap[:, 0:1]),
          reads=[xt], writes=[scr, ssb])
    mk.op("act", lambda en: en.activation(out=ssb.ap[:, 1:2], in_=ssb.ap[:, 0:1], func=AF.Sqrt, scale=1.0 / D, bias=EPS),
          reads=[ssb], writes=[ssb])
    mk.op("dve", lambda en: en.reciprocal(ssb.ap[:, 2:3], ssb.ap[:, 1:2]), reads=[ssb], writes=[ssb])
    mk.op("dve", lambda en: en.scalar_tensor_tensor(hb.ap, xt.ap, ssb.ap[:, 2:3], gB.ap, ALU.mult, ALU.mult),
          reads=[xt, ssb, gB], writes=[hb])
    tm_to_fm(mk, hb, ident, hT, col0, psT)


def tm_to_fm(mk, hb, ident, hT, col0, psT, nk=8):
    for k in range(nk):
        mk.op("pe", lambda pe, k=k: pe.transpose(psT.ap[:, k, :], hb.ap[:, k * 128:(k + 1) * 128], ident.ap),
              reads=[hb, ident], writes=[psT], inc=(k == nk - 1))
    mk.op("act", lambda en: en.copy(hT.ap[:, 0:nk, col0:col0 + 128], psT.ap[:, 0:nk, :]), reads=[psT], writes=[hT])


def norm_block(mk, jobs, gB, ident, slots):
    def gen(idx, xt, src, hT, col0, kind):
        sl = slots[idx % len(slots)]
        junk, ssb, hb, psT = sl["junk"], sl["ssb"], sl["hb"], sl["psT"]
        if src is not None:
            mk.dma("sp", xt.ap, src[1], reads=[src[0]], writes=[xt])
        if kind == "norm":
            mk.op("act", lambda en: en.activation(out=junk.ap, in_=xt.ap, func=AF.Square, accum_out=ssb.ap[:, 0:1]),
                  reads=[xt], writes=[junk, ssb])
            yield
            mk.op("act", lambda en: en.activation(out=ssb.ap[:, 1:2], in_=ssb.ap[:, 0:1], func=AF.Ln, scale=1.0 / D, bias=EPS),
                  reads=[ssb], writes=[ssb])
            mk.op("act", lambda en: en.activation(out=ssb.ap[:, 2:3], in_=ssb.ap[:, 1:2], func=AF.Exp, scale=-0.5), reads=[ssb], writes=[ssb])
            mk.op("dve", lambda en: en.scalar_tensor_tensor(hb.ap, xt.ap, ssb.ap[:, 2:3], gB.ap, ALU.mult, ALU.mult),
                  reads=[xt, ssb, gB], writes=[hb])
        else:
            mk.op("pool", lambda en: en.tensor_copy(hb.ap, xt.ap), reads=[xt], writes=[hb])
        yield
        for k in range(8):
            mk.op("pe", lambda pe, k=k: pe.transpose(psT.ap[:, k, :], hb.ap[:, k * 128:(k + 1) * 128], ident.ap),
                  reads=[hb, ident], writes=[psT], inc=(k == 7))
        yield
        mk.op("act", lambda en: en.copy(hT.ap[:, 0:8, col0:col0 + 128], psT.ap[:, 0:8, :]), reads=[psT], writes=[hT])
    run_pipelined((gen(i, *j) for i, j in enumerate(jobs)), len(slots), stagger=1)


def norm_slots(mk, n, psTs):
    return [{"junk": mk.sb([128, D], BF16, "junk"), "ssb": mk.sb([128, 4], F32, "ssb"), "hb": mk.sb([128, D], BF16, "hb"), "psT": psTs[i]}
            for i in range(n)]


def stage_proj(mk, x_d, g_d, w_d, ztm_d, zfm_d, T, ntm, nfm, ident_d):
    mk.phase_begin()
    N = ntm + nfm
    wb = load_weight_bf16(mk, w_d, D, N, "w_in")
    gB = mk.sb([128, D], F32, "gB")
    mk.dma("sp", gB.ap, bcast_rows(g_d.ap, 128), reads=[g_d], writes=[gB])
    identf = mk.sb([128, 128], F32, "identf")
    ident = mk.sb([128, 128], BF16, "ident")
    mk.dma("sp", identf.ap, ident_d.ap, reads=[ident_d], writes=[identf])
    mk.op("dve", lambda en: en.tensor_copy(ident.ap, identf.ap), reads=[identf], writes=[ident])
    xts = [mk.sb([128, D], F32, "xt") for _ in range(2)]
    TB = min(512, T)
    NTB = TB // 128
    hTs = [mk.sb([128, 8, TB], BF16, "hT") for _ in range(2)]
    psTs = [mk.ps([128, 8, 128], BF16, "psT") for _ in range(2)]
    pss = [mk.ps([128, 512], F32, "psm") for _ in range(4)]
    nsl = norm_slots(mk, 2, psTs)
    ofm = [mk.sb([128, 512], F32, "ofm") for _ in range(3)]
    otm = [mk.sb([128, max(ntm, 1)], F32, "otm") for _ in range(2)]
    it = 0
    ip = 0
    io = 0
    for tb in range(T // TB):
        hT = hTs[tb % 2]
        jobs = []
        for j in range(NTB):
            t0 = tb * TB + j * 128
            jobs.append((xts[it % 2], (x_d, x_d.ap[t0:t0 + 128, :]), hT, j * 128, "norm"))
            it += 1
        norm_block(mk, jobs, gB, ident, nsl)
        for c in range(nfm // 128):
            ps = pss[ip % 4]
            ip += 1
            c0 = ntm + c * 128
            mk.mm(ps.ap[:, 0:TB], [(wb.ap[:, k, c0:c0 + 128], hT.ap[:, k, :]) for k in range(8)], reads=[wb, hT], tr=ps)
            o = ofm[io % 3]
            e = "act" if io % 2 == 0 else "dve"
            io += 1
            if e == "act":
                mk.op("act", lambda en, o=o, ps=ps: en.copy(o.ap[:, 0:TB], ps.ap[:, 0:TB]), reads=[ps], writes=[o])
            else:
                mk.op("dve", lambda en, o=o, ps=ps: en.tensor_copy(o.ap[:, 0:TB], ps.ap[:, 0:TB]), reads=[ps], writes=[o])
            mk.dma("pool", zfm_d.ap[c * 128:(c + 1) * 128, tb * TB:(tb + 1) * TB], o.ap[:, 0:TB], reads=[o], writes=[zfm_d])
        for j in range(NTB):
            o = otm[(tb * NTB + j) % 2]
            t0 = tb * TB + j * 128
            for c0 in range(0, ntm, 512):
                c1 = min(ntm, c0 + 512)
                ps = pss[ip % 4]
                ip += 1
                mk.mm(ps.ap[:, 0:c1 - c0], [(hT.ap[:, k, j * 128:(j + 1) * 128], wb.ap[:, k, c0:c1]) for k in range(8)],
                      reads=[wb, hT], tr=ps)
                e = "act" if io % 2 == 0 else "dve"
                io += 1
                if e == "act":
                    mk.op("act", lambda en, o=o, ps=ps, c0=c0, c1=c1: en.copy(o.ap[:, c0:c1], ps.ap[:, 0:c1 - c0]),
                          reads=[ps], writes=[o])
                else:
                    mk.op("dve", lambda en, o=o, ps=ps, c0=c0, c1=c1: en.tensor_copy(o.ap[:, c0:c1], ps.ap[:, 0:c1 - c0]),
                          reads=[ps], writes=[o])
            if ntm:
                mk.dma("pool", ztm_d.ap[t0:t0 + 128, :], o.ap, reads=[o], writes=[ztm_d])
    mk.phase_end()


def evac(mk, i, out_ap, in_ap, reads, writes):
    if i % 2 == 0:
        mk.op("act", lambda en: en.copy(out_ap, in_ap), reads=reads, writes=writes)
    else:
        mk.op("dve", lambda en: en.tensor_copy(out_ap, in_ap), reads=reads, writes=writes)


def load_consts_ident(mk, ident_d):
    identf = mk.sb([128, 128], F32, "identf")
    ident = mk.sb([128, 128], BF16, "ident")
    mk.dma("sp", identf.ap, ident_d.ap, reads=[ident_d], writes=[identf])
    mk.op("dve", lambda en: en.tensor_copy(ident.ap, identf.ap), reads=[identf], writes=[ident])
    return ident, identf


def tok_blocks(T, full=512):
    out = []
    rem = T % full
    pos = 0
    if rem:
        out.append((0, rem // 128))
        pos = rem
    while pos < T:
        out.append((pos, full // 128))
        pos += full
    return out


def stage_merge(mk, x_d, y_d, xo_d, T, gn_d, wg_d, wa_d, wb_d, wc_d, wmix_d, ident_d):
    mk.phase_begin()
    wg = load_weight_bf16(mk, wg_d, D, 3072, "wg")
    wa = load_weight_bf16(mk, wa_d, 512, D, "wa")
    wbb = load_weight_bf16(mk, wb_d, 256, D, "wb")
    wc = load_weight_bf16(mk, wc_d, 256, D, "wc")
    wmix = load_weight_bf16(mk, wmix_d, D, D, "wmix")
    gB = mk.sb([128, D], F32, "gB")
    mk.dma("sp", gB.ap, bcast_rows(gn_d.ap, 128), reads=[gn_d], writes=[gB])
    ident, _ = load_consts_ident(mk, ident_d)
    xts = [mk.sb([128, D], F32, "xt") for _ in range(4)]
    yts = [mk.sb([128, D], F32, "yt") for _ in range(2)]
    hT = mk.sb([128, 8, 512], BF16, "hT")
    yT = mk.sb([128, 8, 512], BF16, "yT")
    mT = mk.sb([128, 8, 512], BF16, "mT")
    sg = [mk.sb([128, 512], F32, "sg") for _ in range(3)]
    acc = [mk.sb([128, 512], F32, "acc") for _ in range(2)]
    tmp = [mk.sb([128, 512], F32, "tmp") for _ in range(2)]
    psTs = [mk.ps([128, 8, 128], BF16, "psT") for _ in range(2)]
    pss = [mk.ps([128, 512], F32, "psm") for _ in range(5)]
    nsl = norm_slots(mk, 2, psTs)
    ip = 0
    it = 0
    for (tbase, nt) in tok_blocks(T):
        Wd = nt * 128
        jobs = []
        for j in range(nt):
            t0 = tbase + j * 128
            jobs.append((xts[j], (x_d, x_d.ap[t0:t0 + 128, :]), hT, j * 128, "norm"))
            jobs.append((yts[j % 2], (y_d, y_d.ap[t0:t0 + 128, :]), yT, j * 128, "cast"))
        norm_block(mk, jobs, gB, ident, nsl)
        for c in range(8):
            a = acc[c % 2]
            for b, (wbr, koff, nk) in enumerate(((wa, 0, 4), (wbb, 4, 2), (wc, 6, 2))):
                psg = pss[ip % 5]
                ip += 1
                g0 = b * 1024 + c * 128
                mk.mm(psg.ap[:, 0:Wd], [(wg.ap[:, k, g0:g0 + 128], hT.ap[:, k, 0:Wd]) for k in range(8)], reads=[wg, hT], tr=psg)
                s = sg[b]
                mk.op("act", lambda en, s=s, psg=psg, Wd=Wd: en.activation(out=s.ap[:, 0:Wd], in_=psg.ap[:, 0:Wd], func=AF.Sigmoid),
                      reads=[psg], writes=[s])
                psb = pss[ip % 5]
                ip += 1
                mk.mm(psb.ap[:, 0:Wd], [(wbr.ap[:, k, c * 128:(c + 1) * 128], yT.ap[:, koff + k, 0:Wd]) for k in range(nk)],
                      reads=[wbr, yT], tr=psb)
                if b == 0:
                    mk.op("dve", lambda en, a=a, s=s, psb=psb, Wd=Wd: en.tensor_tensor(a.ap[:, 0:Wd], s.ap[:, 0:Wd], psb.ap[:, 0:Wd], ALU.mult),
                          reads=[s, psb], writes=[a])
                else:
                    t = tmp[b % 2]
                    mk.op("dve", lambda en, t=t, s=s, psb=psb, Wd=Wd: en.tensor_tensor(t.ap[:, 0:Wd], s.ap[:, 0:Wd], psb.ap[:, 0:Wd], ALU.mult),
                          reads=[s, psb], writes=[t])
                    if b == 1:
                        mk.op("pool", lambda en, a=a, t=t, Wd=Wd: en.tensor_tensor(a.ap[:, 0:Wd], a.ap[:, 0:Wd], t.ap[:, 0:Wd], ALU.add),
                              reads=[a, t], writes=[a])
                    else:
                        mk.op("pool", lambda en, a=a, t=t, c=c, Wd=Wd: en.tensor_tensor(mT.ap[:, c, 0:Wd], a.ap[:, 0:Wd], t.ap[:, 0:Wd], ALU.add),
                              reads=[a, t], writes=[mT])
        for j in range(nt):
            xt = xts[j]
            t0 = tbase + j * 128
            for c0 in (0, 512):
                ps = pss[ip % 5]
                ip += 1
                mk.mm(ps.ap, [(mT.ap[:, k, j * 128:(j + 1) * 128], wmix.ap[:, k, c0:c0 + 512]) for k in range(8)],
                      reads=[mT, wmix], tr=ps)
                mk.op("dve", lambda en, xt=xt, ps=ps, c0=c0: en.tensor_tensor(xt.ap[:, c0:c0 + 512], xt.ap[:, c0:c0 + 512], ps.ap, ALU.add),
                      reads=[xt, ps], writes=[xt])
            mk.dma("pool", xo_d.ap[t0:t0 + 128, :], xt.ap, reads=[xt], writes=[xo_d])
    mk.phase_end()


def stage_select(mk, src_d, dst_d, Tt, off, hsel_d):
    mk.phase_begin()
    hB = mk.sb([128, 1], F32, "hB")
    mk.dma("sp", hB.ap, bcast_rows(hsel_d.ap, 128), reads=[hsel_d], writes=[hB])
    W = 3
    lo = [mk.sb([128, D], F32, "lo") for _ in range(W)]
    hi = [mk.sb([128, D], F32, "hi") for _ in range(W)]
    Th = off
    for tt in range(Tt // 128):
        k = tt % W
        t0 = tt * 128
        mk.dma("sp", lo[k].ap, src_d.ap[t0:t0 + 128, :], reads=[src_d], writes=[lo[k]])
        mk.dma("act", hi[k].ap, src_d.ap[Th + t0:Th + t0 + 128, :], reads=[src_d], writes=[hi[k]])
        eng = "dve" if tt % 2 == 0 else "pool"
        mk.op(eng, lambda en, k=k: en.tensor_tensor(hi[k].ap, hi[k].ap, lo[k].ap, ALU.subtract), reads=[hi[k], lo[k]], writes=[hi[k]])
        mk.op("dve", lambda en, k=k: en.scalar_tensor_tensor(lo[k].ap, hi[k].ap, hB.ap[:, 0:1], lo[k].ap, ALU.mult, ALU.add),
              reads=[hi[k], lo[k], hB], writes=[lo[k]])
        mk.dma("pool", dst_d.ap[t0:t0 + 128, :], lo[k].ap, reads=[lo[k]], writes=[dst_d])
    mk.phase_end()


def stage_ffn(mk, x_d, xo_d, T, gn_d, wup_d, cw_d, wd_d, ident_d):
    TB = 512
    NT = TB // 128
    mk.phase_begin()
    wup = load_weight_bf16(mk, wup_d, D, 5632, "wup")
    wd = load_weight_bf16(mk, wd_d, 2816, D, "wd")
    gB = mk.sb([128, D], F32, "gB")
    mk.dma("sp", gB.ap, bcast_rows(gn_d.ap, 128), reads=[gn_d], writes=[gB])
    cw = mk.sb([128, 44, 3], F32, "cw")
    mk.dma("sp", cw.ap, cw_d.ap.rearrange("(c p) k -> p c k", p=128), reads=[cw_d], writes=[cw])
    ident, _ = load_consts_ident(mk, ident_d)
    halos = [mk.sb([128, 2], F32, "halo") for _ in range(44)]
    for hb_ in halos:
        mk.op("pool", lambda en, hb_=hb_: en.memset(hb_.ap, 0.0), writes=[hb_])
    xts = [mk.sb([128, D], F32, "xt") for _ in range(2)]
    hT = mk.sb([128, 8, TB], BF16, "hT")
    W = 2
    us = [[mk.sb([128, TB + 2], F32, "u") for _ in range(2)] for _ in range(W)]
    cas = [mk.sb([128, TB], F32, "ca") for _ in range(W)]
    cbs = [mk.sb([128, TB], F32, "cb") for _ in range(W)]
    gTs = [mk.sb([128, TB], BF16, "gT") for _ in range(22)]
    psTs = [mk.ps([128, 8, 128], BF16, "psT") for _ in range(1)]
    nsl = norm_slots(mk, 1, psTs)
    psu = [[mk.ps([128, 512], F32, "psu") for _ in range(2)] for _ in range(W)]
    psd = mk.ps([128, 512], F32, "psd")
    it = 0
    nit = [0]
    for (tbase, nt) in tok_blocks(T, TB):
        Wd = nt * 128
        jobs = []
        for j in range(nt):
            t0 = tbase + j * 128
            jobs.append((xts[j % 2], (x_d, x_d.ap[t0:t0 + 128, :]), hT, j * 128, "norm"))
        norm_block(mk, jobs, gB, ident, nsl)

        def cbody(c, k):
            cv = []
            for half, ct in enumerate((c, c + 22)):
                ps = psu[k][half]
                mk.mm(ps.ap[:, 0:Wd], [(wup.ap[:, kk, ct * 128:(ct + 1) * 128], hT.ap[:, kk, 0:Wd]) for kk in range(8)],
                      reads=[wup, hT], tr=ps)
                u = us[k][half]
                hl = halos[ct]
                mk.op("act", lambda en, u=u, hl=hl: en.copy(u.ap[:, 0:2], hl.ap), reads=[hl], writes=[u])
                mk.op("act", lambda en, u=u, ps=ps: en.copy(u.ap[:, 2:Wd + 2], ps.ap[:, 0:Wd]), reads=[ps], writes=[u])
                o = (cas if half == 0 else cbs)[k]
                mk.op("act", lambda en, o=o, ps=ps, ct=ct: en.activation(out=o.ap[:, 0:Wd], in_=ps.ap[:, 0:Wd], func=AF.Copy, scale=cw.ap[:, ct, 2:3]),
                      reads=[ps, cw], writes=[o])
                yield
                mk.op("act", lambda en, u=u, hl=hl: en.copy(hl.ap, u.ap[:, Wd:Wd + 2]), reads=[u], writes=[hl])
                mk.op("dve", lambda en, o=o, u=u, ct=ct: en.scalar_tensor_tensor(o.ap[:, 0:Wd], u.ap[:, 1:Wd + 1], cw.ap[:, ct, 1:2], o.ap[:, 0:Wd], ALU.mult, ALU.add),
                      reads=[u, cw, o], writes=[o])
                mk.op("dve", lambda en, o=o, u=u, ct=ct: en.scalar_tensor_tensor(o.ap[:, 0:Wd], u.ap[:, 0:Wd], cw.ap[:, ct, 0:1], o.ap[:, 0:Wd], ALU.mult, ALU.add),
                      reads=[u, cw, o], writes=[o])
                cv.append(o)
                yield
            a_, b_ = cv
            mk.op("act", lambda en, a_=a_: en.activation(out=a_.ap[:, 0:Wd], in_=a_.ap[:, 0:Wd], func=AF.Silu), reads=[a_], writes=[a_])
            yield
            mk.op("dve", lambda en, a_=a_, b_=b_, c=c: en.tensor_tensor(gTs[c].ap[:, 0:Wd], a_.ap[:, 0:Wd], b_.ap[:, 0:Wd], ALU.mult),
                  reads=[a_, b_], writes=[gTs[c]])

        def gens():
            for c in range(22):
                k = nit[0] % W
                nit[0] += 1
                yield cbody(c, k)
        run_pipelined(gens(), W, stagger=2)
        for j in range(nt):
            t0 = tbase + j * 128
            xt = xts[j % 2]
            mk.dma("sp", xt.ap, x_d.ap[t0:t0 + 128, :], reads=[x_d], writes=[xt])
            for c0 in (0, 512):
                ps = psd
                mk.mm(ps.ap, [(gTs[k].ap[:, j * 128:(j + 1) * 128], wd.ap[:, k, c0:c0 + 512]) for k in range(22)],
                      reads=gTs + [wd], tr=ps)
                mk.op("dve", lambda en, xt=xt, ps=ps, c0=c0: en.tensor_tensor(xt.ap[:, c0:c0 + 512], xt.ap[:, c0:c0 + 512], ps.ap, ALU.add),
                      reads=[xt, ps], writes=[xt])
            mk.dma("pool", xo_d.ap[t0:t0 + 128, :], xt.ap, reads=[xt], writes=[xo_d])
    mk.phase_end()

def bc_last(ap2, n):
    return ap2.unsqueeze(2).to_broadcast([ap2.shape[0], ap2.shape[1], n])


def bc_mid(ap2, h):
    return ap2.unsqueeze(1).to_broadcast([ap2.shape[0], h, ap2.shape[1]])


def exp_sigmoid(mk, out_b, out_ap, in_b, in_ap, neg=False, eng2="dve"):
    mk.op("act", lambda en: en.activation(out=out_ap, in_=in_ap, func=AF.Exp, scale=(1.0 if neg else -1.0)), reads=[in_b], writes=[out_b])
    mk.op("act", lambda en: en.activation(out=out_ap, in_=out_ap, func=AF.Ln, bias=1.0), reads=[out_b], writes=[out_b])
    mk.op("act", lambda en: en.activation(out=out_ap, in_=out_ap, func=AF.Exp, scale=-1.0), reads=[out_b], writes=[out_b])


def exp_silu(mk, out_b, out_ap, in_b, in_ap, tmp_b, tmp_ap):
    exp_sigmoid(mk, tmp_b, tmp_ap, in_b, in_ap)
    mk.op("dve", lambda en: en.tensor_tensor(out_ap, in_ap, tmp_ap, ALU.mult), reads=[in_b, tmp_b], writes=[out_b])


def head_rmsnorm(mk, src, H, dh, gB, out, scr, st, scale=1.0, np_=128):
    n = H * dh
    s3 = lambda b: b.ap[0:np_, 0:n].rearrange("p (h d) -> p h d", h=H)
    mk.op("pool", lambda en: en.tensor_tensor(scr.ap[0:np_, 0:n], src.ap[0:np_, 0:n], src.ap[0:np_, 0:n], ALU.mult),
          reads=[src], writes=[scr])
    mk.op("dve", lambda en: en.tensor_reduce(st.ap[0:np_, 0:H], s3(scr), AX.X, ALU.add), reads=[scr], writes=[st])
    mk.op("act", lambda en: en.activation(out=st.ap[0:np_, H:2 * H], in_=st.ap[0:np_, 0:H], func=AF.Ln, scale=1.0 / dh, bias=EPS),
          reads=[st], writes=[st])
    mk.op("act", lambda en: en.activation(out=st.ap[0:np_, 2 * H:3 * H], in_=st.ap[0:np_, H:2 * H], func=AF.Exp, scale=-0.5),
          reads=[st], writes=[st])
    mk.op("dve", lambda en: en.tensor_tensor(s3(scr), s3(src), bc_last(st.ap[0:np_, 2 * H:3 * H], dh), ALU.mult),
          reads=[src, st], writes=[scr])
    mk.op("dve", lambda en: en.scalar_tensor_tensor(s3(out), s3(scr), float(scale), bc_mid(gB.ap[0:np_, 0:dh], H), ALU.mult, ALU.mult),
          reads=[scr, gB], writes=[out])


def stage_xattn(mk, x_d, xo_d, T, gn_d, mkv_d, wq_d, qn_d, kn_d, wo_d, ident_d):
    mk.phase_begin()
    H, DH, M = 4, 128, 256
    wq = load_weight_bf16(mk, wq_d, D, 512, "wq")
    wo = load_weight_bf16(mk, wo_d, 512, D, "wo")
    gB = mk.sb([128, D], F32, "gB")
    mk.dma("sp", gB.ap, bcast_rows(gn_d.ap, 128), reads=[gn_d], writes=[gB])
    gq = mk.sb([128, DH], F32, "gq")
    mk.dma("sp", gq.ap, bcast_rows(qn_d.ap, 128), reads=[qn_d], writes=[gq])
    gk = mk.sb([128, DH], F32, "gk")
    mk.dma("sp", gk.ap, bcast_rows(kn_d.ap, 128), reads=[kn_d], writes=[gk])
    ident, _ = load_consts_ident(mk, ident_d)
    kT = mk.sb([128, H, M], BF16, "kT")
    vaug = mk.sb([128, 2, H, DH + 1], BF16, "vaug")
    scr = mk.sb([128, D], F32, "scr")
    st = mk.sb([128, 12], F32, "st")
    psTs = [mk.ps([128, 8, 128], BF16, "psT") for _ in range(2)]
    pss = [mk.ps([128, 512], F32, "psm") for _ in range(5)]
    hbs = [mk.sb([128, D], BF16, "hb") for _ in range(2)]
    mt_t = mk.sb([128, D], F32, "memt")
    mk.op("pool", lambda en: en.memset(vaug.ap, 1.0), writes=[vaug])
    for mt in range(2):
        mk.dma("sp", mt_t.ap, mkv_d.ap[mt * 128:(mt + 1) * 128, :], reads=[mkv_d], writes=[mt_t])
        hb = hbs[mt]
        head_rmsnorm(mk, mt_t, H, DH, gk, hb, scr, st)
        for h in range(H):
            mk.op("pe", lambda pe, h=h, hb=hb: pe.transpose(psTs[0].ap[:, h, :], hb.ap[:, h * DH:(h + 1) * DH], ident.ap),
                  reads=[hb, ident], writes=[psTs[0]], inc=(h == H - 1))
        mk.op("act", lambda en, mt=mt: en.copy(kT.ap[:, :, mt * 128:(mt + 1) * 128], psTs[0].ap[:, 0:H, :]),
              reads=[psTs[0]], writes=[kT])
        mk.op("dve", lambda en, mt=mt: en.tensor_copy(vaug.ap[:, mt, :, 0:DH], mt_t.ap[:, 512:1024].rearrange("p (h d) -> p h d", h=H)),
              reads=[mt_t], writes=[vaug])
    xts = [mk.sb([128, D], F32, "xt") for _ in range(4)]
    nsl = [{"junk": mk.sb([128, D], BF16, "junk"), "ssb": mk.sb([128, 4], F32, "ssb"), "hb": hbs[i_], "psT": psTs[i_]} for i_ in range(2)]
    hT = mk.sb([128, 8, 512], BF16, "hT")
    W = 2
    qf = [mk.sb([128, 512], F32, "qf") for _ in range(W)]
    qb = [mk.sb([128, 512], BF16, "qb") for _ in range(W)]
    scrs = [mk.sb([128, 512], F32, "scrq") for _ in range(W)]
    sts = [mk.sb([128, 12], F32, "stq") for _ in range(W)]
    qT = mk.sb([128, H, 512], BF16, "qT")
    PT = mk.sb([128, H * 2, 512], BF16, "PT")
    of = [mk.sb([128, 512], F32, "of") for _ in range(W)]
    ob = [mk.sb([128, 512], BF16, "ob") for _ in range(W)]
    oT = [mk.sb([128, H, 128], BF16, "oT") for _ in range(W)]
    rd = [mk.sb([128, 4], F32, "rd") for _ in range(W)]
    pq = [pss[0], pss[1]]
    po = [[pss[2], pss[3]], [pss[4], pss[0]]]
    it = 0
    for (tbase, nt) in tok_blocks(T):
        Wd = nt * 128
        jobs = []
        for j in range(nt):
            t0 = tbase + j * 128
            jobs.append((xts[j], (x_d, x_d.ap[t0:t0 + 128, :]), hT, j * 128, "norm"))
        norm_block(mk, jobs, gB, ident, nsl)

        def qbody(j):
            k = j % W
            ps = pq[k]
            mk.mm(ps.ap, [(hT.ap[:, kk, j * 128:(j + 1) * 128], wq.ap[:, kk, :]) for kk in range(8)], reads=[hT, wq], tr=ps)
            q = qf[k]
            mk.op("act", lambda en: en.copy(q.ap, ps.ap), reads=[ps], writes=[q])
            yield
            qq = qb[k]
            head_rmsnorm(mk, q, H, DH, gq, qq, scrs[k], sts[k], scale=DH ** -0.5)
            yield
            psT = psTs[k]
            for h in range(H):
                mk.op("pe", lambda pe, h=h: pe.transpose(psT.ap[:, h, :], qq.ap[:, h * DH:(h + 1) * DH], ident.ap),
                      reads=[qq, ident], writes=[psT], inc=(h == H - 1))
            mk.op("act", lambda en: en.copy(qT.ap[:, :, j * 128:(j + 1) * 128], psT.ap[:, 0:H, :]),
                  reads=[psT], writes=[qT])
        run_pipelined((qbody(j) for j in range(nt)), W, stagger=1)
        ip = 0
        for h in range(H):
            for mt in range(2):
                ps = pss[ip % 2]
                ip += 1
                mk.mm(ps.ap[:, 0:Wd], [(kT.ap[:, h, mt * 128:(mt + 1) * 128], qT.ap[:, h, 0:Wd])], reads=[kT, qT], tr=ps)
                mk.op("act", lambda en, ps=ps, h=h, mt=mt, Wd=Wd: en.activation(out=PT.ap[:, h * 2 + mt, 0:Wd], in_=ps.ap[:, 0:Wd], func=AF.Exp),
                      reads=[ps], writes=[PT])

        def obody(j):
            k = j % W
            xt = xts[j]
            t0 = tbase + j * 128
            o_f = of[k]
            r = rd[k]
            for hp in range(2):
                ps = po[k][hp]
                for hh in range(2):
                    h = hp * 2 + hh
                    mk.mm(ps.ap[:, hh * 129:(hh + 1) * 129],
                          [(PT.ap[:, h * 2 + mt, j * 128:(j + 1) * 128], vaug.ap[:, mt, h, :]) for mt in range(2)],
                          reads=[PT, vaug], tr=ps)
                for hh in range(2):
                    h = hp * 2 + hh
                    mk.op("dve", lambda en, ps=ps, hh=hh, h=h: en.reciprocal(r.ap[:, h:h + 1], ps.ap[:, hh * 129 + 128:hh * 129 + 129]),
                          reads=[ps], writes=[r])
                    mk.op("dve", lambda en, ps=ps, hh=hh, h=h: en.tensor_scalar(
                        o_f.ap[:, h * 128:(h + 1) * 128], ps.ap[:, hh * 129:hh * 129 + 128], r.ap[:, h:h + 1], None, ALU.mult),
                        reads=[ps, r], writes=[o_f])
                yield
            o_b = ob[k]
            mk.op("pool", lambda en: en.tensor_copy(o_b.ap, o_f.ap), reads=[o_f], writes=[o_b])
            psT = psTs[k]
            for h in range(H):
                mk.op("pe", lambda pe, h=h: pe.transpose(psT.ap[:, h, :], o_b.ap[:, h * DH:(h + 1) * DH], ident.ap),
                      reads=[o_b, ident], writes=[psT], inc=(h == H - 1))
            o_T = oT[k]
            mk.op("act", lambda en: en.copy(o_T.ap, psT.ap[:, 0:H, :]), reads=[psT], writes=[o_T])
            yield
            for ci, c0 in enumerate((0, 512)):
                ps = po[k][ci]
                mk.mm(ps.ap, [(o_T.ap[:, h, :], wo.ap[:, h, c0:c0 + 512]) for h in range(H)], reads=[o_T, wo], tr=ps)
                mk.op("dve", lambda en, ps=ps, c0=c0: en.tensor_tensor(xt.ap[:, c0:c0 + 512], xt.ap[:, c0:c0 + 512], ps.ap, ALU.add),
                      reads=[xt, ps], writes=[xt])
                yield
            mk.dma("pool", xo_d.ap[t0:t0 + 128, :], xt.ap, reads=[xt], writes=[xo_d])
        run_pipelined((obody(j) for j in range(nt)), W, stagger=1)
    mk.phase_end()


def stage_hgrn(mk, ztm_d, c_hf, c_hi, c_hg, zfm_d, r_hq, r_hf, y_d, ycol, T, lbl_d, layer, onorm_d, tri_d, ident_d, defer=False):
    H, DK, C = 4, 64, 64
    mk.phase_begin()
    tri = mk.sb([64, 64], F32, "tri")
    mk.dma("sp", tri.ap, tri_d.ap, reads=[tri_d], writes=[tri])
    identf = mk.sb([128, 128], F32, "identf")
    mk.dma("sp", identf.ap, ident_d.ap, reads=[ident_d], writes=[identf])
    gO = mk.sb([64, 64], F32, "gO")
    mk.dma("sp", gO.ap, bcast_rows(onorm_d.ap, 64), reads=[onorm_d], writes=[gO])
    lbB = mk.sb([64, 256], F32, "lbB")
    omlbB = mk.sb([64, 256], F32, "omlbB")
    lbT = mk.sb([64, 4], F32, "lbT")
    omlbT = mk.sb([64, 4], F32, "omlbT")
    if layer == 0:
        mk.op("dve", lambda en: en.memset(lbB.ap, 0.0), writes=[lbB])
        mk.op("dve", lambda en: en.memset(lbT.ap, 0.0), writes=[lbT])
    else:
        l0 = mk.sb([64, 256], F32, "l0")
        l0T = mk.sb([64, 4], F32, "l0T")
        mk.dma("sp", l0.ap, bcast_rows(lbl_d.ap[0, :], 64), reads=[lbl_d], writes=[l0])
        mk.dma("sp", lbB.ap, bcast_rows(lbl_d.ap[1, :], 64), reads=[lbl_d], writes=[lbB])
        mk.dma("sp", l0T.ap, lbl_d.ap[0, :].rearrange("(h d) -> d h", h=4), reads=[lbl_d], writes=[l0T], allow_slow_non_contiguous=True)
        mk.dma("sp", lbT.ap, lbl_d.ap[1, :].rearrange("(h d) -> d h", h=4), reads=[lbl_d], writes=[lbT], allow_slow_non_contiguous=True)
        mk.op("dve", lambda en: en.tensor_tensor(lbB.ap, lbB.ap, l0.ap, ALU.subtract), reads=[lbB, l0], writes=[lbB])
        mk.op("act", lambda en: en.activation(out=lbB.ap, in_=lbB.ap, func=AF.Sigmoid), reads=[lbB], writes=[lbB])
        mk.op("dve", lambda en: en.tensor_tensor(lbT.ap, lbT.ap, l0T.ap, ALU.subtract), reads=[lbT, l0T], writes=[lbT])
        mk.op("act", lambda en: en.activation(out=lbT.ap, in_=lbT.ap, func=AF.Sigmoid), reads=[lbT], writes=[lbT])
    mk.op("dve", lambda en: en.tensor_scalar(omlbB.ap, lbB.ap, -1.0, 1.0, ALU.mult, ALU.add), reads=[lbB], writes=[omlbB])
    mk.op("dve", lambda en: en.tensor_scalar(omlbT.ap, lbT.ap, -1.0, 1.0, ALU.mult, ALU.add), reads=[lbT], writes=[omlbT])
    S = mk.sb([64, H, 64], F32, "S")
    mk.op("dve", lambda en: en.memset(S.ap, 0.0), writes=[S])
    NB = 2 if defer else 3
    def mkb(shape, name, n=NB, dt=F32):
        return [mk.sb(shape, dt, name) for _ in range(n)]
    qz, fz, fzt, vt, gz = mkb([64, H, 64], "qz"), mkb([64, H, 64], "fz"), mkb([64, 256], "fzt"), mkb([64, 256], "vt"), mkb([64, 256], "gz")
    qT, kT, lf = mkb([64, H, 64], "qT"), mkb([64, H, 64], "kT"), mkb([64, 256], "lf")
    bT, d1, e1, e2, e3 = mkb([64, H, 64], "bT"), mkb([64, H, 64], "d1"), mkb([64, H, 64], "e1"), mkb([64, H, 64], "e2"), mkb([64, H, 64], "e3")
    qtl, ktl, qe, kdT, kd = mkb([64, H, 64], "qtl"), mkb([64, H, 64], "ktl"), mkb([64, H, 64], "qe"), mkb([64, H, 64], "kdT"), mkb([64, H, 64], "kd")
    AT, ebl = mkb([64, H, 64], "AT"), mkb([64, 4], "ebl")
    of, scr, st, yo = mkb([64, 256], "of"), mkb([64, 256], "scr"), mkb([64, 12], "st"), mkb([64, 256], "yo")
    NPS = 2 if defer else 8
    PPS = 2 if defer else 4
    pss = [mk.ps([128, 512], F32, "psm") for _ in range(NPS)]
    ipc = [0, 0]
    def body(c):
        i = c % NB
        t0 = c * C
        pset = 0 if defer else c % 2
        def nps():
            p = pss[pset * PPS + ipc[pset] % PPS]
            ipc[pset] += 1
            return p
        mk.dma("sp", qz[i].ap, zfm_d.ap[r_hq:r_hq + 256, t0:t0 + C].rearrange("(h d) t -> d h t", h=H), reads=[zfm_d], writes=[qz[i]])
        mk.dma("sp", fz[i].ap, zfm_d.ap[r_hf:r_hf + 256, t0:t0 + C].rearrange("(h d) t -> d h t", h=H), reads=[zfm_d], writes=[fz[i]])
        mk.dma("sp", fzt[i].ap, ztm_d.ap[t0:t0 + C, c_hf:c_hf + 256], reads=[ztm_d], writes=[fzt[i]])
        mk.dma("sp", vt[i].ap, ztm_d.ap[t0:t0 + C, c_hi:c_hi + 256], reads=[ztm_d], writes=[vt[i]])
        mk.dma("sp", gz[i].ap, ztm_d.ap[t0:t0 + C, c_hg:c_hg + 256], reads=[ztm_d], writes=[gz[i]])
        yield
        exp_silu(mk, qT[i], qT[i].ap, qz[i], qz[i].ap, e1[i], e1[i].ap)
        exp_sigmoid(mk, kT[i], kT[i].ap, fz[i], fz[i].ap, neg=True, eng2="pool")
        mk.op("dve", lambda en, i=i: en.tensor_tensor(kT[i].ap, kT[i].ap, bc_last(omlbT.ap, 64), ALU.mult), reads=[kT[i], omlbT], writes=[kT[i]])
        yield
        exp_sigmoid(mk, lf[i], lf[i].ap, fzt[i], fzt[i].ap, eng2="pool")
        mk.op("dve", lambda en, i=i: en.tensor_tensor(lf[i].ap, lf[i].ap, omlbB.ap, ALU.mult), reads=[lf[i], omlbB], writes=[lf[i]])
        mk.op("dve", lambda en, i=i: en.tensor_tensor(lf[i].ap, lf[i].ap, lbB.ap, ALU.add), reads=[lf[i], lbB], writes=[lf[i]])
        mk.op("dve", lambda en, i=i: en.tensor_scalar(lf[i].ap, lf[i].ap, TINY, None, ALU.max), reads=[lf[i]], writes=[lf[i]])
        mk.op("act", lambda en, i=i: en.activation(out=lf[i].ap, in_=lf[i].ap, func=AF.Ln), reads=[lf[i]], writes=[lf[i]])
        yield
        psb = nps()
        for h in range(H):
            mk.mm(psb.ap[0:64, h * 64:(h + 1) * 64], [(lf[i].ap[:, h * 64:(h + 1) * 64], tri.ap)], reads=[lf[i], tri], tr=psb)
        mk.op("act", lambda en, i=i, psb=psb: en.copy(bT[i].ap, psb.ap[0:64, 0:256].rearrange("p (h t) -> p h t", h=H)), reads=[psb], writes=[bT[i]])
        mk.op("dve", lambda en, i=i: en.tensor_tensor(d1[i].ap, bT[i].ap, bc_last(bT[i].ap[:, :, 31], 64), ALU.subtract), reads=[bT[i]], writes=[d1[i]])
        mk.op("act", lambda en, i=i: en.activation(out=e1[i].ap, in_=d1[i].ap, func=AF.Exp), reads=[d1[i]], writes=[e1[i]])
        mk.op("act", lambda en, i=i: en.activation(out=e2[i].ap, in_=d1[i].ap, func=AF.Exp, scale=-1.0), reads=[d1[i]], writes=[e2[i]])
        mk.op("dve", lambda en, i=i: en.tensor_tensor(qtl[i].ap, qT[i].ap, e1[i].ap, ALU.mult), reads=[qT[i], e1[i]], writes=[qtl[i]])
        mk.op("dve", lambda en, i=i: en.tensor_tensor(ktl[i].ap, kT[i].ap, e2[i].ap, ALU.mult), reads=[kT[i], e2[i]], writes=[ktl[i]])
        mk.op("act", lambda en, i=i: en.activation(out=e1[i].ap, in_=bT[i].ap, func=AF.Exp), reads=[bT[i]], writes=[e1[i]])
        mk.op("dve", lambda en, i=i: en.tensor_tensor(qe[i].ap, qT[i].ap, e1[i].ap, ALU.mult), reads=[qT[i], e1[i]], writes=[qe[i]])
        mk.op("dve", lambda en, i=i: en.tensor_tensor(d1[i].ap, bT[i].ap, bc_last(bT[i].ap[:, :, 63], 64), ALU.subtract), reads=[bT[i]], writes=[d1[i]])
        mk.op("act", lambda en, i=i: en.activation(out=e3[i].ap, in_=d1[i].ap, func=AF.Exp, scale=-1.0), reads=[d1[i]], writes=[e3[i]])
        mk.op("dve", lambda en, i=i: en.tensor_tensor(kdT[i].ap, kT[i].ap, e3[i].ap, ALU.mult), reads=[kT[i], e3[i]], writes=[kdT[i]])
        mk.op("act", lambda en, i=i: en.activation(out=ebl[i].ap, in_=bT[i].ap[:, :, 63], func=AF.Exp), reads=[bT[i]], writes=[ebl[i]])
        yield
        pst = nps()
        for h in range(H):
            mk.op("pe", lambda pe, h=h, i=i, pst=pst: pe.transpose(pst.ap[0:64, h * 64:(h + 1) * 64], kdT[i].ap[:, h, :], identf.ap[0:64, 0:64]),
                  reads=[kdT[i], identf], writes=[pst], inc=(h == H - 1))
        mk.op("act", lambda en, i=i, pst=pst: en.copy(kd[i].ap, pst.ap[0:64, 0:256].rearrange("p (h t) -> p h t", h=H)), reads=[pst], writes=[kd[i]])
        yield
        psa = nps()
        for h in range(H):
            mk.mm(psa.ap[0:64, h * 64:(h + 1) * 64], [(ktl[i].ap[:, h, :], qtl[i].ap[:, h, :])], reads=[ktl[i], qtl[i]], tr=psa)
        mk.op("dve", lambda en, i=i, psa=psa: en.tensor_tensor(AT[i].ap, psa.ap[0:64, 0:256].rearrange("p (h t) -> p h t", h=H), bc_mid(tri.ap, H), ALU.mult),
              reads=[psa, tri], writes=[AT[i]])
        if c > 0:
            yield ("wait", ("S", c - 1))
        pso = nps()
        for h in range(H):
            mk.mm(pso.ap[0:64, h * 64:(h + 1) * 64], [(AT[i].ap[:, h, :], vt[i].ap[:, h * 64:(h + 1) * 64]), (qe[i].ap[:, h, :], S.ap[:, h, :])],
                  reads=[AT[i], vt[i], qe[i], S], tr=pso)
        pss_ = nps()
        for h in range(H):
            mk.mm(pss_.ap[0:64, h * 64:(h + 1) * 64], [(kd[i].ap[:, h, :], vt[i].ap[:, h * 64:(h + 1) * 64])], reads=[kd[i], vt[i]], tr=pss_)
        mk.op("dve", lambda en, i=i: en.tensor_tensor(S.ap, S.ap, bc_last(ebl[i].ap, 64), ALU.mult), reads=[S, ebl[i]], writes=[S])
        mk.op("dve", lambda en, pss_=pss_: en.tensor_tensor(S.ap, S.ap, pss_.ap[0:64, 0:256].rearrange("p (h t) -> p h t", h=H), ALU.add), reads=[S, pss_], writes=[S])
        yield ("done", ("S", c))
        mk.op("act", lambda en, i=i, pso=pso: en.copy(of[i].ap, pso.ap[0:64, 0:256]), reads=[pso], writes=[of[i]])
        head_rmsnorm(mk, of[i], H, 64, gO, yo[i], scr[i], st[i], np_=64)
        exp_sigmoid(mk, gz[i], gz[i].ap, gz[i], gz[i].ap, eng2="pool")
        mk.op("dve", lambda en, i=i: en.tensor_tensor(yo[i].ap, yo[i].ap, gz[i].ap, ALU.mult), reads=[yo[i], gz[i]], writes=[yo[i]])
        mk.dma("pool", y_d.ap[t0:t0 + C, ycol:ycol + 256], yo[i].ap, reads=[yo[i]], writes=[y_d])
    if defer:
        return (body(c) for c in range(T // C)), (lambda: mk.phase_end())
    run_pipelined((body(c) for c in range(T // C)), 2)
    mk.phase_end()


def stage_gdn(mk, ztm_d, c_cz, c_cb, c_ca, zfm_d, r_qkv, y_d, ycol, T, cw_d, alog_d, dtb_d, onorm_d, cst, bg=None):
    H, C = 4, 128
    mk.phase_begin()
    def cload(name):
        b = mk.sb([128, 128], F32, name)
        mk.dma("sp", b.ap, cst[name].ap, reads=[cst[name]], writes=[b])
        return b
    triI, lowS, negtriS, neglowS, id128, ones = (cload(n) for n in ("g_triI", "g_lowS", "g_negtriS", "g_neglowS", "g_id", "g_ones"))
    gO = mk.sb([128, 64], F32, "gO")
    mk.dma("sp", gO.ap, bcast_rows(onorm_d.ap, 128), reads=[onorm_d], writes=[gO])
    cw = mk.sb([64, 12, 4], F32, "cw")
    mk.dma("sp", cw.ap, cw_d.ap.rearrange("(b d) k -> d b k", d=64), reads=[cw_d], writes=[cw])
    negA = mk.sb([128, 4], F32, "negA")
    dtb = mk.sb([128, 4], F32, "dtb")
    mk.dma("sp", negA.ap, bcast_rows(alog_d.ap, 128), reads=[alog_d], writes=[negA])
    mk.dma("sp", dtb.ap, bcast_rows(dtb_d.ap, 128), reads=[dtb_d], writes=[dtb])
    mk.op("act", lambda en: en.activation(out=negA.ap, in_=negA.ap, func=AF.Exp), reads=[negA], writes=[negA])
    mk.op("dve", lambda en: en.tensor_scalar(negA.ap, negA.ap, -1.0, None, ALU.mult), reads=[negA], writes=[negA])
    S = mk.sb([64, H, 64], F32, "S")
    mk.op("dve", lambda en: en.memset(S.ap, 0.0), writes=[S])
    NB = 2
    def mkb(shape, name, n=NB, dt=F32):
        return [mk.sb(list(shape), dt, name) for _ in range(n)]
    FM = [64, H, C]
    TM = [128, H, 64]
    SQ = [128, H, C]
    u = mkb([64, 12, C + 3], "u")
    cz, sc = mkb([128, 256], "cz"), mkb([128, 8], "sc")
    tk = [mkb([64, 12, C], "tk%d" % k) for k in range(2)]
    qkv, sq, rinv = mkb([64, 12, C], "qkv"), mkb([64, 8, C], "sq"), mkb([64, 8, C], "rinv")
    kv_tm = mkb([128, 8, 64], "kvtm")
    beta, g, sp1, sp2 = mkb([128, 4], "beta"), mkb([128, 4], "g"), mkb([128, 4], "sp1"), mkb([128, 4], "sp2")
    rep = mkb([128, 8, 64], "rep")
    eb, bb = mkb(FM, "eb"), mkb(FM, "bb")
    kbT, kbeT, qeT = mkb(FM, "kbT"), mkb(FM, "kbeT"), mkb(FM, "qeT")
    Gt = mkb(SQ, "Gt")
    ET, E, ETi = mkb(SQ, "ET"), mkb(SQ, "E"), mkb(SQ, "ETi")
    NUs, NLs = [mkb(SQ, "NU%d" % j) for j in range(2)], [mkb(SQ, "NL%d" % j) for j in range(2)]
    P = mkb(SQ, "P")
    QKT = mkb(SQ, "QKT")
    bv, rhs, vn, kd, elb = mkb(TM, "bv"), mkb(TM, "rhs"), mkb(TM, "vn"), mkb(TM, "kd"), mkb([128, 4], "elb")
    of, scr, st, yo = mkb([128, 256], "of"), mkb([128, 256], "scr"), mkb([128, 12], "st"), mkb([128, 256], "yo")
    PPS = 3 if bg is not None else 4
    pss = [mk.ps([128, 512], F32, "psm") for _ in range(2 * PPS)]
    ipc = [0, 0]
    psq = lambda p: p.ap[:, 0:H * C].rearrange("p (h t) -> p h t", h=H)
    ptm = lambda p: p.ap[:, 0:H * 64].rearrange("p (h t) -> p h t", h=H)
    pfm = lambda p: p.ap[0:64, 0:H * C].rearrange("p (h t) -> p h t", h=H)

    def body(c):
        i = c % NB
        t0 = c * C
        U = u[i]
        NU = [NUs[0][i], NUs[1][i]]
        NL = [NLs[0][i], NLs[1][i]]
        pset = c % 2
        def nps():
            p = pss[pset * PPS + ipc[pset] % PPS]
            ipc[pset] += 1
            return p
        if c == 0:
            mk.op("pool", lambda en, U=U: en.memset(U.ap[:, :, 0:3], 0.0), writes=[U])
            mk.dma("sp", U.ap[:, :, 3:C + 3], zfm_d.ap[r_qkv:r_qkv + 768, 0:C].rearrange("(b d) t -> d b t", d=64), reads=[zfm_d], writes=[U])
        else:
            mk.dma("sp", U.ap, zfm_d.ap[r_qkv:r_qkv + 768, t0 - 3:t0 + C].rearrange("(b d) t -> d b t", d=64), reads=[zfm_d], writes=[U])
        mk.dma("sp", cz[i].ap, ztm_d.ap[t0:t0 + C, c_cz:c_cz + 256], reads=[ztm_d], writes=[cz[i]])
        mk.dma("sp", sc[i].ap[:, 0:4], ztm_d.ap[t0:t0 + C, c_cb:c_cb + 4], reads=[ztm_d], writes=[sc[i]])
        mk.dma("sp", sc[i].ap[:, 4:8], ztm_d.ap[t0:t0 + C, c_ca:c_ca + 4], reads=[ztm_d], writes=[sc[i]])
        yield
        a0, a1 = tk[0][i], tk[1][i]
        mk.op("dve", lambda en: en.tensor_tensor(a0.ap, U.ap[:, :, 0:C], bc_last(cw.ap[:, :, 0], C), ALU.mult), reads=[U, cw], writes=[a0])
        mk.op("pool", lambda en: en.tensor_tensor(a1.ap, U.ap[:, :, 1:C + 1], bc_last(cw.ap[:, :, 1], C), ALU.mult), reads=[U, cw], writes=[a1])
        mk.op("dve", lambda en: en.tensor_tensor(a0.ap, a0.ap, a1.ap, ALU.add), reads=[a0, a1], writes=[a0])
        mk.op("pool", lambda en: en.tensor_tensor(a1.ap, U.ap[:, :, 2:C + 2], bc_last(cw.ap[:, :, 2], C), ALU.mult), reads=[U, cw], writes=[a1])
        mk.op("dve", lambda en: en.tensor_tensor(a0.ap, a0.ap, a1.ap, ALU.add), reads=[a0, a1], writes=[a0])
        mk.op("pool", lambda en: en.tensor_tensor(a1.ap, U.ap[:, :, 3:C + 3], bc_last(cw.ap[:, :, 3], C), ALU.mult), reads=[U, cw], writes=[a1])
        mk.op("dve", lambda en: en.tensor_tensor(a0.ap, a0.ap, a1.ap, ALU.add), reads=[a0, a1], writes=[a0])
        mk.op("act", lambda en: en.activation(out=qkv[i].ap, in_=a0.ap, func=AF.Silu), reads=[a0], writes=[qkv[i]])
        yield
        mk.op("pool", lambda en: en.tensor_tensor(sq[i].ap, qkv[i].ap[:, 0:8, :], qkv[i].ap[:, 0:8, :], ALU.mult), reads=[qkv[i]], writes=[sq[i]])
        for half in range(2):
            pn = nps()
            mk.mm(pn.ap[0:64, 0:512], [(ones.ap[0:64, 0:64], sq[i].ap[:, half * 4:half * 4 + 4, :].rearrange("p b t -> p (b t)"))], reads=[ones, sq[i]], tr=pn)
            mk.op("act", lambda en, pn=pn, half=half: en.activation(out=rinv[i].ap[:, half * 4:half * 4 + 4, :], in_=pfm(pn), func=AF.Ln, bias=EPS), reads=[pn], writes=[rinv[i]])
        mk.op("act", lambda en: en.activation(out=rinv[i].ap, in_=rinv[i].ap, func=AF.Exp, scale=-0.5), reads=[rinv[i]], writes=[rinv[i]])
        mk.op("dve", lambda en: en.scalar_tensor_tensor(qkv[i].ap[:, 0:4, :], qkv[i].ap[:, 0:4, :], 64 ** -0.5, rinv[i].ap[:, 0:4, :], ALU.mult, ALU.mult),
              reads=[qkv[i], rinv[i]], writes=[qkv[i]])
        mk.op("dve", lambda en: en.tensor_tensor(qkv[i].ap[:, 4:8, :], qkv[i].ap[:, 4:8, :], rinv[i].ap[:, 4:8, :], ALU.mult),
              reads=[qkv[i], rinv[i]], writes=[qkv[i]])
        qT = lambda h: qkv[i].ap[:, h, :]
        kT = lambda h: qkv[i].ap[:, 4 + h, :]
        yield
        pt = nps()
        for b_ in range(8):
            mk.op("pe", lambda pe, b_=b_: pe.transpose(pt.ap[:, b_ * 64:(b_ + 1) * 64], qkv[i].ap[:, 4 + b_, :], id128.ap[0:64, 0:64]),
                  reads=[qkv[i], id128], writes=[pt], inc=(b_ == 7))
        mk.op("act", lambda en: en.copy(kv_tm[i].ap, pt.ap[:, :].rearrange("p (b d) -> p b d", b=8)), reads=[pt], writes=[kv_tm[i]])
        yield
        exp_sigmoid(mk, beta[i], beta[i].ap, sc[i], sc[i].ap[:, 0:4])
        mk.op("dve", lambda en: en.tensor_tensor(sp1[i].ap, sc[i].ap[:, 4:8], dtb.ap, ALU.add), reads=[sc[i], dtb], writes=[sp1[i]])
        mk.op("dve", lambda en: en.tensor_scalar(sp2[i].ap, sp1[i].ap, -1.0, None, ALU.mult), reads=[sp1[i]], writes=[sp2[i]])
        mk.op("dve", lambda en: en.tensor_tensor(sp2[i].ap, sp2[i].ap, sp1[i].ap, ALU.max), reads=[sp1[i], sp2[i]], writes=[sp2[i]])
        mk.op("act", lambda en: en.activation(out=sp2[i].ap, in_=sp2[i].ap, func=AF.Exp, scale=-1.0), reads=[sp2[i]], writes=[sp2[i]])
        mk.op("act", lambda en: en.activation(out=sp2[i].ap, in_=sp2[i].ap, func=AF.Ln, bias=1.0), reads=[sp2[i]], writes=[sp2[i]])
        mk.op("dve", lambda en: en.tensor_scalar(sp1[i].ap, sp1[i].ap, 0.0, None, ALU.max), reads=[sp1[i]], writes=[sp1[i]])
        mk.op("dve", lambda en: en.tensor_tensor(sp1[i].ap, sp1[i].ap, sp2[i].ap, ALU.add), reads=[sp1[i], sp2[i]], writes=[sp1[i]])
        mk.op("dve", lambda en: en.tensor_tensor(g[i].ap, sp1[i].ap, negA.ap, ALU.mult), reads=[sp1[i], negA], writes=[g[i]])
        yield
        mk.op("pool", lambda en: en.tensor_copy(rep[i].ap[:, 0:4, :], bc_last(beta[i].ap, 64)), reads=[beta[i]], writes=[rep[i]])
        mk.op("pool", lambda en: en.tensor_copy(rep[i].ap[:, 4:8, :], bc_last(g[i].ap, 64)), reads=[g[i]], writes=[rep[i]])
        pb1, pb2 = nps(), nps()
        for h in range(H):
            mk.mm(pb1.ap[0:64, h * C:(h + 1) * C], [(rep[i].ap[:, h, :], id128.ap)], reads=[rep[i], id128], tr=pb1)
        for h in range(H):
            mk.mm(pb2.ap[0:64, h * C:(h + 1) * C], [(rep[i].ap[:, 4 + h, :], triI.ap)], reads=[rep[i], triI], tr=pb2)
        mk.op("act", lambda en: en.copy(bb[i].ap, pfm(pb1)), reads=[pb1], writes=[bb[i]])
        mk.op("act", lambda en: en.activation(out=eb[i].ap, in_=pfm(pb2), func=AF.Exp), reads=[pb2], writes=[eb[i]])
        mk.op("dve", lambda en: en.tensor_tensor(kbT[i].ap, qkv[i].ap[:, 4:8, :], bb[i].ap, ALU.mult), reads=[qkv[i], bb[i]], writes=[kbT[i]])
        mk.op("dve", lambda en: en.tensor_tensor(kbeT[i].ap, kbT[i].ap, eb[i].ap, ALU.mult), reads=[kbT[i], eb[i]], writes=[kbeT[i]])
        mk.op("dve", lambda en: en.tensor_tensor(qeT[i].ap, qkv[i].ap[:, 0:4, :], eb[i].ap, ALU.mult), reads=[qkv[i], eb[i]], writes=[qeT[i]])
        yield
        mk.op("pool", lambda en: en.tensor_tensor(Gt[i].ap, bc_mid(triI.ap, H), bc_last(g[i].ap, C), ALU.mult), reads=[triI, g[i]], writes=[Gt[i]])
        pdT, pd = nps(), nps()
        for h in range(H):
            mk.mm(pdT.ap[:, h * C:(h + 1) * C], [(lowS.ap, Gt[i].ap[:, h, :])], reads=[lowS, Gt[i]], tr=pdT)
        for h in range(H):
            mk.mm(pd.ap[:, h * C:(h + 1) * C], [(Gt[i].ap[:, h, :], lowS.ap)], reads=[lowS, Gt[i]], tr=pd)
        mk.op("act", lambda en: en.activation(out=ET[i].ap, in_=psq(pdT), func=AF.Exp), reads=[pdT], writes=[ET[i]])
        mk.op("act", lambda en: en.activation(out=E[i].ap, in_=psq(pd), func=AF.Exp), reads=[pd], writes=[E[i]])
        mk.op("pool", lambda en: en.tensor_tensor(ETi[i].ap, ET[i].ap, bc_mid(triI.ap, H), ALU.mult), reads=[ET[i], triI], writes=[ETi[i]])
        mk.op("dve", lambda en: en.tensor_tensor(ET[i].ap, ET[i].ap, bc_mid(negtriS.ap, H), ALU.mult), reads=[ET[i], negtriS], writes=[ET[i]])
        mk.op("pool", lambda en: en.tensor_tensor(E[i].ap, E[i].ap, bc_mid(neglowS.ap, H), ALU.mult), reads=[E[i], neglowS], writes=[E[i]])
        yield
        pc = nps()
        mk.mm(pc.ap[:, 0:4], [(lowS.ap, g[i].ap)], reads=[lowS, g[i]], tr=pc)
        mk.op("act", lambda en: en.activation(out=elb[i].ap, in_=pc.ap[:, 0:4], func=AF.Exp), reads=[pc], writes=[elb[i]])
        mk.op("dve", lambda en: en.tensor_tensor(kd[i].ap, kv_tm[i].ap[:, 0:4, :], bc_last(elb[i].ap, 64), ALU.mult), reads=[kv_tm[i], elb[i]], writes=[kd[i]])
        mk.op("dve", lambda en: en.tensor_tensor(bv[i].ap, kv_tm[i].ap[:, 4:8, :], bc_last(beta[i].ap, 64), ALU.mult), reads=[kv_tm[i], beta[i]], writes=[bv[i]])
        yield
        pu, pl, pq = nps(), nps(), nps()
        for h in range(H):
            mk.mm(pu.ap[:, h * C:(h + 1) * C], [(kT(h), kbT[i].ap[:, h, :])], reads=[qkv[i], kbT[i]], tr=pu)
        for h in range(H):
            mk.mm(pl.ap[:, h * C:(h + 1) * C], [(kbT[i].ap[:, h, :], kT(h))], reads=[qkv[i], kbT[i]], tr=pl)
        for h in range(H):
            mk.mm(pq.ap[:, h * C:(h + 1) * C], [(kT(h), qT(h))], reads=[qkv[i]], tr=pq)
        mk.op("dve", lambda en: en.tensor_tensor(NU[0].ap, psq(pu), ET[i].ap, ALU.mult), reads=[pu, ET[i]], writes=[NU[0]])
        mk.op("dve", lambda en: en.tensor_tensor(NL[0].ap, psq(pl), E[i].ap, ALU.mult), reads=[pl, E[i]], writes=[NL[0]])
        mk.op("dve", lambda en: en.tensor_tensor(QKT[i].ap, psq(pq), ETi[i].ap, ALU.mult), reads=[pq, ETi[i]], writes=[QKT[i]])
        mk.op("pool", lambda en: en.tensor_tensor(P[i].ap, NU[0].ap, bc_mid(id128.ap, H), ALU.add), reads=[NU[0], id128], writes=[P[i]])
        yield
        cur = 0
        for j in range(1, 7):
            nxt = 1 - cur
            pu, pl = nps(), nps()
            for h in range(H):
                mk.mm(pu.ap[:, h * C:(h + 1) * C], [(NL[cur].ap[:, h, :], NU[cur].ap[:, h, :])], reads=[NL[cur], NU[cur]], tr=pu)
            for h in range(H):
                mk.mm(pl.ap[:, h * C:(h + 1) * C], [(NU[cur].ap[:, h, :], NL[cur].ap[:, h, :])], reads=[NL[cur], NU[cur]], tr=pl)
            mk.op("act", lambda en, nxt=nxt, pu=pu: en.copy(NU[nxt].ap, psq(pu)), reads=[pu], writes=[NU[nxt]])
            mk.op("dve", lambda en, nxt=nxt, pl=pl: en.tensor_copy(NL[nxt].ap, psq(pl)), reads=[pl], writes=[NL[nxt]])
            pp = nps()
            for h in range(H):
                mk.mm(pp.ap[:, h * C:(h + 1) * C], [(NL[nxt].ap[:, h, :], P[i].ap[:, h, :])], reads=[NL[nxt], P[i]], tr=pp)
            mk.op("dve", lambda en, pp=pp: en.tensor_tensor(P[i].ap, P[i].ap, psq(pp), ALU.add), reads=[P[i], pp], writes=[P[i]])
            cur = nxt
            yield
        if c > 0:
            yield ("wait", ("S", c - 1))
        p1 = nps()
        for h in range(H):
            mk.mm(p1.ap[:, h * 64:(h + 1) * 64], [(kbeT[i].ap[:, h, :], S.ap[:, h, :])], reads=[kbeT[i], S], tr=p1)
        mk.op("dve", lambda en: en.tensor_tensor(rhs[i].ap, bv[i].ap, ptm(p1), ALU.subtract), reads=[bv[i], p1], writes=[rhs[i]])
        p2 = nps()
        for h in range(H):
            mk.mm(p2.ap[:, h * 64:(h + 1) * 64], [(P[i].ap[:, h, :], rhs[i].ap[:, h, :])], reads=[P[i], rhs[i]], tr=p2)
        mk.op("act", lambda en: en.copy(vn[i].ap, ptm(p2)), reads=[p2], writes=[vn[i]])
        po, p4 = nps(), nps()
        for h in range(H):
            mk.mm(po.ap[:, h * 64:(h + 1) * 64], [(qeT[i].ap[:, h, :], S.ap[:, h, :]), (QKT[i].ap[:, h, :], vn[i].ap[:, h, :])],
                  reads=[qeT[i], S, QKT[i], vn[i]], tr=po)
        for h in range(H):
            mk.mm(p4.ap[0:64, h * 64:(h + 1) * 64], [(kd[i].ap[:, h, :], vn[i].ap[:, h, :])], reads=[kd[i], vn[i]], tr=p4)
        mk.op("dve", lambda en: en.tensor_tensor(S.ap, S.ap, bc_last(eb[i].ap[:, :, C - 1], 64), ALU.mult), reads=[S, eb[i]], writes=[S])
        mk.op("dve", lambda en: en.tensor_tensor(S.ap, S.ap, p4.ap[0:64, 0:256].rearrange("p (h t) -> p h t", h=H), ALU.add), reads=[S, p4], writes=[S])
        yield ("done", ("S", c))
        mk.op("act", lambda en: en.copy(of[i].ap, po.ap[:, 0:256]), reads=[po], writes=[of[i]])
        head_rmsnorm(mk, of[i], H, 64, gO, yo[i], scr[i], st[i])
        exp_silu(mk, cz[i], cz[i].ap, cz[i], cz[i].ap, scr[i], scr[i].ap)
        mk.op("dve", lambda en: en.tensor_tensor(yo[i].ap, yo[i].ap, cz[i].ap, ALU.mult), reads=[yo[i], cz[i]], writes=[yo[i]])
        mk.dma("pool", y_d.ap[t0:t0 + C, ycol:ycol + 256], yo[i].ap, reads=[yo[i]], writes=[y_d])
    if bg is not None:
        run_pipelined_multi([{"gens": (body(c) for c in range(T // C)), "width": 2}, {"gens": bg, "width": 1, "period": 2}])
    else:
        run_pipelined((body(c) for c in range(T // C)), 2)
    mk.phase_end()


def gdn_consts():
    a = np.arange(64)
    triI = (a[:, None] <= a[None, :]).astype(np.float32)
    b = np.arange(128)
    gI = (b[:, None] <= b[None, :]).astype(np.float32)
    gS = (b[:, None] < b[None, :]).astype(np.float32)
    gL = (b[:, None] > b[None, :]).astype(np.float32)
    return {"triI": triI, "g_triI": gI, "g_lowS": gL, "g_negtriS": -gS, "g_neglowS": -gL,
            "g_id": np.eye(128, dtype=np.float32), "g_ones": np.ones((128, 128), np.float32)}


def nsa_consts(T):
    t = np.arange(T)
    inv = 1.0 / (10000.0 ** (np.arange(0, 64, 2, dtype=np.float32) / 64))
    ang = t[:, None].astype(np.float32) * inv[None, :].astype(np.float32)
    ncp = T // 16
    n = np.arange(ncp)
    ncmp = (T - 32) // 16 + 1
    cm = ((16 * n[None, :] + 31 <= t[:, None]) & (n[None, :] < ncmp)).astype(np.float32)
    ns = T // 64
    j = np.arange(ns)[None, :]
    cur = (t // 64)[:, None]
    valid = j <= cur
    forced = (j == 0) | (j == cur) | (j == cur - 1)
    m1 = (valid & ~forced).astype(np.float32)
    c2 = (1e6 * (valid & forced) - 1.0 * (~valid)).astype(np.float32)
    a = np.arange(128)
    return {"cos": np.cos(ang).astype(np.float32), "sin": np.sin(ang).astype(np.float32),
            "cmpm": cm, "cmpmT": np.ascontiguousarray(cm.T), "selm1": m1, "selc2": c2,
            "causT": (a[:, None] <= a[None, :]).astype(np.float32), "farT": (a[:, None] > a[None, :]).astype(np.float32),
            "esel": (np.arange(ns)[:, None] == (t // 64)[None, :]).astype(np.float32)}


def nsa_n1(mk, ztm_d, c_nq, c_nk, c_nv, T, qn_d, kn_d, cst, ident_d, QT_d, KT_d, VA_d, defer=False):
    NT = T // 128
    mk.phase_begin()
    ident, identf = load_consts_ident(mk, ident_d)
    gqk = mk.sb([128, 14, 64], F32, "gqk")
    mk.dma("sp", gqk.ap[:, 0:8, :], bc_mid(bcast_rows(qn_d.ap, 128), 8), reads=[qn_d], writes=[gqk])
    for ty in range(3):
        mk.dma("sp", gqk.ap[:, 8 + 2 * ty:10 + 2 * ty, :], bc_mid(bcast_rows(kn_d.ap[ty, :], 128), 2), reads=[kn_d], writes=[gqk])
    mk.op("dve", lambda en: en.tensor_scalar(gqk.ap[:, 0:8, :], gqk.ap[:, 0:8, :], 64 ** -0.5, None, ALU.mult), reads=[gqk], writes=[gqk])
    NB = 2 if defer else 3
    xin = [mk.sb([128, 14 * 64], F32, "xin") for _ in range(NB)]
    vin = [mk.sb([128, 256], F32, "vin") for _ in range(NB)]
    cs = [mk.sb([128, 64], F32, "cs") for _ in range(NB)]
    sq = [mk.sb([128, 14 * 64], F32, "sq") for _ in range(NB)]
    st = [mk.sb([128, 42], F32, "st") for _ in range(NB)]
    xn = [mk.sb([128, 14, 64], F32, "xn") for _ in range(NB)]
    r1 = [mk.sb([128, 14, 32], F32, "r1") for _ in range(NB)]
    r2 = [mk.sb([128, 14, 32], F32, "r2") for _ in range(NB)]
    xr = [mk.sb([128, 14, 64], BF16, "xr") for _ in range(NB)]
    xT = [mk.sb([64, 14, 128], BF16, "xT") for _ in range(NB)]
    va = [mk.sb([128, 4, 65], BF16, "va") for _ in range(NB)]
    NPS_ = 1 if defer else NB
    psA = [mk.ps([64, 7, 128], BF16, "psA") for _ in range(NPS_)]
    psB = [mk.ps([64, 7, 128], BF16, "psB") for _ in range(NPS_)]
    def n1body(tt):
        i = tt % NB
        t0 = tt * 128
        X = xin[i]
        mk.dma("sp", X.ap[:, 0:512], ztm_d.ap[t0:t0 + 128, c_nq:c_nq + 512], reads=[ztm_d], writes=[X])
        mk.dma("sp", X.ap[:, 512:896], ztm_d.ap[t0:t0 + 128, c_nk:c_nk + 384], reads=[ztm_d], writes=[X])
        mk.dma("sp", vin[i].ap, ztm_d.ap[t0:t0 + 128, c_nv:c_nv + 256], reads=[ztm_d], writes=[vin[i]])
        mk.dma("sp", cs[i].ap[:, 0:32], cst["cos"].ap[t0:t0 + 128, :], reads=[cst["cos"]], writes=[cs[i]])
        mk.dma("sp", cs[i].ap[:, 32:64], cst["sin"].ap[t0:t0 + 128, :], reads=[cst["sin"]], writes=[cs[i]])
        yield
        X3 = X.ap[:, :].rearrange("p (h d) -> p h d", h=14)
        S3 = sq[i].ap[:, :].rearrange("p (h d) -> p h d", h=14)
        mk.op("act", lambda en, i=i, X=X: en.activation(out=sq[i].ap, in_=X.ap, func=AF.Square), reads=[X], writes=[sq[i]])
        mk.op("dve", lambda en, i=i, S3=S3: en.tensor_reduce(st[i].ap[:, 0:14], S3, AX.X, ALU.add), reads=[sq[i]], writes=[st[i]])
        mk.op("act", lambda en, i=i: en.activation(out=st[i].ap[:, 14:28], in_=st[i].ap[:, 0:14], func=AF.Ln, scale=1.0 / 64, bias=EPS),
              reads=[st[i]], writes=[st[i]])
        mk.op("act", lambda en, i=i: en.activation(out=st[i].ap[:, 28:42], in_=st[i].ap[:, 14:28], func=AF.Exp, scale=-0.5), reads=[st[i]], writes=[st[i]])
        mk.op("dve", lambda en, i=i, X3=X3: en.tensor_tensor(xn[i].ap, X3, bc_last(st[i].ap[:, 28:42], 64), ALU.mult), reads=[X, st[i]], writes=[xn[i]])
        mk.op("pool", lambda en, i=i: en.tensor_tensor(xn[i].ap, xn[i].ap, gqk.ap, ALU.mult), reads=[xn[i], gqk], writes=[xn[i]])
        yield
        cb_ = bc_mid(cs[i].ap[:, 0:32], 14)
        sb_ = bc_mid(cs[i].ap[:, 32:64], 14)
        x1 = xn[i].ap[:, :, 0:32]
        x2 = xn[i].ap[:, :, 32:64]
        mk.op("dve", lambda en, i=i, x1=x1, cb_=cb_: en.tensor_tensor(r1[i].ap, x1, cb_, ALU.mult), reads=[xn[i], cs[i]], writes=[r1[i]])
        mk.op("pool", lambda en, i=i, x2=x2, sb_=sb_: en.tensor_tensor(r2[i].ap, x2, sb_, ALU.mult), reads=[xn[i], cs[i]], writes=[r2[i]])
        mk.op("dve", lambda en, i=i: en.tensor_tensor(xr[i].ap[:, :, 0:32], r1[i].ap, r2[i].ap, ALU.subtract), reads=[r1[i], r2[i]], writes=[xr[i]])
        mk.op("dve", lambda en, i=i, x2=x2, cb_=cb_: en.tensor_tensor(r1[i].ap, x2, cb_, ALU.mult), reads=[xn[i], cs[i]], writes=[r1[i]])
        mk.op("pool", lambda en, i=i, x1=x1, sb_=sb_: en.tensor_tensor(r2[i].ap, x1, sb_, ALU.mult), reads=[xn[i], cs[i]], writes=[r2[i]])
        mk.op("dve", lambda en, i=i: en.tensor_tensor(xr[i].ap[:, :, 32:64], r1[i].ap, r2[i].ap, ALU.add), reads=[r1[i], r2[i]], writes=[xr[i]])
        yield
        for half, psx in ((0, psA[i % NPS_]), (1, psB[i % NPS_])):
            for hh in range(7):
                h = half * 7 + hh
                mk.op("pe", lambda pe, h=h, hh=hh, i=i, psx=psx: pe.transpose(psx.ap[:, hh, :], xr[i].ap[:, h, :], ident.ap),
                      reads=[xr[i], ident], writes=[psx], inc=(hh == 6))
            if half == 0:
                mk.op("act", lambda en, i=i, psx=psx: en.copy(xT[i].ap[:, 0:7, :], psx.ap), reads=[psx], writes=[xT[i]])
            else:
                mk.op("dve", lambda en, i=i, psx=psx: en.tensor_copy(xT[i].ap[:, 7:14, :], psx.ap), reads=[psx], writes=[xT[i]])
        yield
        mk.dma("pool", QT_d.ap[:, :, t0:t0 + 128], xT[i].ap[:, 0:8, :], reads=[xT[i]], writes=[QT_d])
        mk.dma("pool", KT_d.ap[:, :, t0:t0 + 128], xT[i].ap[:, 8:14, :], reads=[xT[i]], writes=[KT_d])
        mk.op("pool", lambda en, i=i: en.memset(va[i].ap[:, :, 64:65], 1.0), writes=[va[i]])
        mk.op("act", lambda en, i=i: en.copy(va[i].ap[:, :, 0:64], vin[i].ap[:, :].rearrange("p (h d) -> p h d", h=4)), reads=[vin[i]], writes=[va[i]])
        mk.dma("pool", VA_d.ap[t0:t0 + 128, :, :], va[i].ap, reads=[va[i]], writes=[VA_d])
    if defer:
        return (n1body(tt) for tt in range(NT)), (lambda: mk.phase_end())
    run_pipelined((n1body(tt) for tt in range(NT)), NB, stagger=2)
    mk.phase_end()
    return None, None


def stage_nsa(mk, ztm_d, c_nq, c_nk, c_nv, c_ng, zfm_d, r_vc, y_d, T, qn_d, kn_d, posk_d, posv_d, w1k_d, w2k_d, w1v_d, w2v_d,
              cst, ident_d, QT_d, KT_d, VA_d, bg=None, skip_n1=False):
    NT = T // 128
    ncp = T // 16
    ncmp = (T - 32) // 16 + 1
    ns = T // 64
    mk.phase_begin()
    ident, identf = load_consts_ident(mk, ident_d)
    if not skip_n1:
        mk.mark("nsa_N1")
        nsa_n1(mk, ztm_d, c_nq, c_nk, c_nv, T, qn_d, kn_d, cst, ident_d, QT_d, KT_d, VA_d)
    mk.mark("nsa_res")
    KsT = mk.sb([128, 2, T], BF16, "KsT")
    KwT = mk.sb([128, 2, T], BF16, "KwT")
    mk.op("pool", lambda en: en.memset(KsT.ap[64:128, :, :], 0.0), writes=[KsT])
    mk.op("pool", lambda en: en.memset(KwT.ap[64:128, :, :], 0.0), writes=[KwT])
    mk.dma("sp", KsT.ap[0:64, :, :], KT_d.ap[:, 2:4, :], reads=[KT_d], writes=[KsT])
    mk.dma("sp", KwT.ap[0:64, :, :], KT_d.ap[:, 4:6, :], reads=[KT_d], writes=[KwT])
    VA = mk.sb([128, NT, 4, 65], BF16, "VA")
    mk.dma("sp", VA.ap, VA_d.ap.rearrange("(n p) f e -> p n f e", p=128), reads=[VA_d], writes=[VA])
    kcT = mk.sb([128, 2, ncp], BF16, "kcT")
    vcA = mk.sb([128, 2, ncp // 128, 65], BF16, "vcA")
    mk.op("pool", lambda en: en.memset(kcT.ap, 0.0), writes=[kcT])
    mk.op("pool", lambda en: en.memset(vcA.ap, 0.0), writes=[vcA])
    mk.op("pool", lambda en: en.memset(vcA.ap[:, :, :, 64:65], 1.0), writes=[vcA])
    mk.mark("nsa_N2")
    mk.phase_begin()
    pss = [mk.ps([128, 512], F32, "psm") for _ in range(4)]
    for which, (w1_d, w2_d, pos_d) in enumerate(((w1k_d, w2k_d, posk_d), (w1v_d, w2v_d, posv_d))):
        if which == 1:
            mk.phase_end()
        mk.phase_begin()
        w1f = mk.sb([64, 32, 128], F32, "w1f")
        w1 = mk.sb([64, 32, 128], BF16, "w1")
        mk.dma("sp", w1f.ap, w1_d.ap.rearrange("(l d) h -> d l h", d=64), reads=[w1_d], writes=[w1f])
        mk.op("pool", lambda en, w1=w1, w1f=w1f: en.tensor_copy(w1.ap, w1f.ap), reads=[w1f], writes=[w1])
        w2f = mk.sb([128, 64], F32, "w2f")
        w2 = mk.sb([128, 64], BF16, "w2")
        mk.dma("sp", w2f.ap, w2_d.ap, reads=[w2_d], writes=[w2f])
        mk.op("pool", lambda en, w2=w2, w2f=w2f: en.tensor_copy(w2.ap, w2f.ap), reads=[w2f], writes=[w2])
        posf = mk.sb([64, 32], F32, "posf")
        posb = mk.sb([64, 32], BF16, "posb")
        mk.dma("sp", posf.ap, pos_d.ap.rearrange("l d -> d l"), reads=[pos_d], writes=[posf], allow_slow_non_contiguous=True)
        mk.op("pool", lambda en, posb=posb, posf=posf: en.tensor_copy(posb.ap, posf.ap), reads=[posf], writes=[posb])
        bias = mk.sb([128, 1], F32, "bias")
        pbias = pss[0]
        mk.mm(pbias.ap[:, 0:1], [(w1.ap[:, l, :], posb.ap[:, l:l + 1]) for l in range(32)], reads=[w1, posb], tr=pbias)
        mk.op("act", lambda en, bias=bias, pbias=pbias: en.copy(bias.ap, pbias.ap[:, 0:1]), reads=[pbias], writes=[bias])
        for g in range(2):
            XT = mk.sb([64, T], BF16, "XT")
            if which == 0:
                mk.dma("sp", XT.ap, KT_d.ap[:, g, :], reads=[KT_d], writes=[XT])
            else:
                XTf = mk.sb([64, T], F32, "XTf")
                mk.dma("sp", XTf.ap, zfm_d.ap[r_vc + g * 64:r_vc + (g + 1) * 64, :], reads=[zfm_d], writes=[XTf])
                mk.op("pool", lambda en, XT=XT, XTf=XTf: en.tensor_copy(XT.ap, XTf.ap), reads=[XTf], writes=[XT])
            ph = pss[1 + g]
            mk.mm(ph.ap[:, 0:ncmp], [(w1.ap[:, l, :], XT.ap[:, l:l + 16 * (ncmp - 1) + 1:16]) for l in range(32)], reads=[w1, XT], tr=ph)
            xs = mk.sb([128, ncp], F32, "xs")
            x2 = mk.sb([128, ncp], F32, "x2")
            ge = mk.sb([128, ncp], BF16, "ge")
            mk.op("pool", lambda en, ge=ge: en.memset(ge.ap, 0.0), writes=[ge])
            mk.op("act", lambda en, xs=xs, ph=ph, bias=bias: en.activation(out=xs.ap[:, 0:ncmp], in_=ph.ap[:, 0:ncmp], func=AF.Identity, bias=bias.ap[:, 0:1]),
                  reads=[ph, bias], writes=[xs])
            mk.op("pool", lambda en, xs=xs, x2=x2: en.tensor_tensor(x2.ap[:, 0:ncmp], xs.ap[:, 0:ncmp], xs.ap[:, 0:ncmp], ALU.mult), reads=[xs], writes=[x2])
            mk.op("dve", lambda en, x2=x2: en.tensor_scalar(x2.ap[:, 0:ncmp], x2.ap[:, 0:ncmp], 0.044715, 1.0, ALU.mult, ALU.add), reads=[x2], writes=[x2])
            mk.op("dve", lambda en, xs=xs, x2=x2: en.tensor_tensor(x2.ap[:, 0:ncmp], x2.ap[:, 0:ncmp], xs.ap[:, 0:ncmp], ALU.mult), reads=[xs, x2], writes=[x2])
            mk.op("act", lambda en, x2=x2: en.activation(out=x2.ap[:, 0:ncmp], in_=x2.ap[:, 0:ncmp], func=AF.Sigmoid, scale=1.5957691216057308),
                  reads=[x2], writes=[x2])
            mk.op("dve", lambda en, xs=xs, x2=x2, ge=ge: en.tensor_tensor(ge.ap[:, 0:ncmp], x2.ap[:, 0:ncmp], xs.ap[:, 0:ncmp], ALU.mult), reads=[xs, x2], writes=[ge])
            po = pss[3]
            if which == 0:
                mk.mm(po.ap[0:64, 0:ncmp], [(w2.ap, ge.ap[:, 0:ncmp])], reads=[w2, ge], tr=po)
                mk.op("act", lambda en, g=g, po=po: en.copy(kcT.ap[0:64, g, 0:ncmp], po.ap[0:64, 0:ncmp]), reads=[po], writes=[kcT])
            else:
                for nt in range(ncp // 128):
                    mk.mm(po.ap[:, nt * 64:(nt + 1) * 64], [(ge.ap[:, nt * 128:(nt + 1) * 128], w2.ap)], reads=[w2, ge], tr=po)
                mk.op("act", lambda en, g=g, po=po: en.copy(vcA.ap[:, g, :, 0:64], po.ap[:, 0:(ncp // 128) * 64].rearrange("p (n d) -> p n d", d=64)),
                      reads=[po], writes=[vcA])
    mk.phase_end()
    mk.phase_end()
    mk.mark("nsa_N3")
    mk.phase_begin()
    causT = mk.sb([128, 128], F32, "causT")
    farT = mk.sb([128, 128], F32, "farT")
    mk.dma("sp", causT.ap, cst["causT"].ap, reads=[cst["causT"]], writes=[causT])
    mk.dma("sp", farT.ap, cst["farT"].ap, reads=[cst["farT"]], writes=[farT])
    esel = mk.sb([ns, T], BF16, "esel")
    mk.phase_begin()
    eself = mk.sb([ns, T], F32, "eself")
    mk.dma("sp", eself.ap, cst["esel"].ap, reads=[cst["esel"]], writes=[eself])
    mk.op("pool", lambda en: en.tensor_copy(esel.ap, eself.ap), reads=[eself], writes=[esel])
    mk.phase_end()
    NQ = ncp // 128
    NS = 2
    psS = [mk.ps([128, 512], F32, "psS") for _ in range(NS)]
    psO = [mk.ps([128, 512], F32, "psO") for _ in range(NS)]
    psT = [[mk.ps([128, 512], F32, "psST") for _ in range(1 if bg is not None else 2)] for _ in range(NS)]
    psM = psS
    mk.ndram = getattr(mk, "ndram", 0) + 1
    selTd = [mk.dram("selTd%d_%d" % (mk.ndram, j), [ns, 128], BF16) for j in range(NS)]
    def mkb(shape, name, dt=F32, n=NS):
        return [mk.sb(list(shape), dt, name) for _ in range(n)]
    QT = mkb([128, 4, 128], "QT", BF16)
    for q_ in QT:
        mk.op("pool", lambda en, q_=q_: en.memset(q_.ap[64:128, :, :], 0.0), writes=[q_])
    gl = mkb([128, 24], "gl")
    cm, cmT = mkb([128, ncp], "cm"), mkb([128, NQ, 128], "cmT")
    m1t, c2t = mkb([128, ns], "m1t"), mkb([128, ns], "c2t")
    sS = mkb([128, 4, ncp], "sS")
    sE = sS
    stt = mkb([128, 16], "stt")
    ph = mkb([128, ncp + 1], "ph")
    imp, wk = mkb([128, ns], "imp"), mkb([128, ns], "wk")
    m8 = mkb([128, 16], "m8")
    selb = mkb([128, ns], "selb")
    selT = mkb([ns, 128], "selT", BF16)
    eT = [mkb([128, 4, 128], "eT", BF16, 3) for _ in range(NS)]
    causB = mk.sb([128, 128], BF16, "causB")
    farB = mk.sb([128, 128], BF16, "farB")
    mk.op("dve", lambda en: en.tensor_copy(causB.ap, causT.ap), reads=[causT], writes=[causB])
    mk.op("dve", lambda en: en.tensor_copy(farB.ap, farT.ap), reads=[farT], writes=[farB])
    cmTb = mkb([128, NQ, 128], "cmTb", BF16)
    bm = [mkb([128, 128], "bm", BF16, 2) for _ in range(NS)]
    rr = mkb([128, 12], "rr")
    acc = mkb([128, 4, 64], "acc")
    tmpo = mkb([128, 4, 64], "tmpo")
    PTall = mkb([128, NT, 4, 128], "PTall", BF16)

    def attn_branch(i, tiles, Ops):
        PA = PTall[i]
        QTi = QT[i]
        n = len(tiles)
        for idx, (KT_ap, V_ap, mask_fn) in enumerate(tiles):
            pt = psT[i][idx % len(psT[i])]
            mk.mm(pt.ap, [(KT_ap, QTi.ap[:, :, :].rearrange("p h q -> p (h q)"))], reads=[QTi, KsT, KwT, kcT], tr=pt)
            e = eT[i][idx % 3]
            p4 = pt.ap[:, :].rearrange("p (h q) -> p h q", h=4)
            msk = mask_fn(idx)
            if msk is None:
                mk.op("act", lambda en, idx=idx, p4=p4: en.activation(out=PA.ap[:, idx, :, :], in_=p4, func=AF.Exp), reads=[pt], writes=[PA])
            else:
                mk.op("act", lambda en, e=e, p4=p4: en.activation(out=e.ap, in_=p4, func=AF.Exp), reads=[pt], writes=[e])
                mb, mreads = msk
                mk.op("dve", lambda en, idx=idx, e=e, mb=mb: en.tensor_tensor(PA.ap[:, idx, :, :], e.ap, bc_mid(mb, 4), ALU.mult), reads=[e] + mreads, writes=[PA])
            yield
        for h in range(4):
            for idx, (KT_ap, V_ap, mask_fn) in enumerate(tiles):
                mk.op("pe", lambda pe, h=h, idx=idx, V_ap=V_ap: pe.matmul(Ops.ap[:, h * 65:(h + 1) * 65], PA.ap[:, idx, h, :], V_ap, start=(idx == 0), stop=(idx == n - 1)),
                      reads=[PA, VA, vcA], writes=[Ops], inc=(idx == n - 1))
            yield

    def combine(i, g, x, first):
        Ops = psO[i]
        O3 = Ops.ap[:, 0:260].rearrange("p (h e) -> p h e", e=65)
        mk.op("dve", lambda en: en.tensor_scalar(rr[i].ap[:, 4 * x:4 * x + 4], O3[:, :, 64], 1e-30, None, ALU.max), reads=[Ops], writes=[rr[i]])
        mk.op("dve", lambda en: en.reciprocal(rr[i].ap[:, 4 * x:4 * x + 4], rr[i].ap[:, 4 * x:4 * x + 4]), reads=[rr[i]], writes=[rr[i]])
        gx = gl[i].ap[:, g * 12:(g + 1) * 12].rearrange("p (h x) -> p h x", x=3)[:, :, x]
        mk.op("dve", lambda en: en.tensor_tensor(rr[i].ap[:, 4 * x:4 * x + 4], rr[i].ap[:, 4 * x:4 * x + 4], gx, ALU.mult), reads=[rr[i], gl[i]], writes=[rr[i]])
        if first:
            mk.op("dve", lambda en: en.tensor_tensor(acc[i].ap, O3[:, :, 0:64], bc_last(rr[i].ap[:, 4 * x:4 * x + 4], 64), ALU.mult), reads=[Ops, rr[i]], writes=[acc[i]])
        else:
            mk.op("dve", lambda en: en.tensor_tensor(tmpo[i].ap, O3[:, :, 0:64], bc_last(rr[i].ap[:, 4 * x:4 * x + 4], 64), ALU.mult), reads=[Ops, rr[i]], writes=[tmpo[i]])
            mk.op("pool", lambda en: en.tensor_tensor(acc[i].ap, acc[i].ap, tmpo[i].ap, ALU.add), reads=[acc[i], tmpo[i]], writes=[acc[i]])

    def body(it, g, qt):
        i = it % NS
        t0 = qt * 128
        mk.dma("sp", QT[i].ap[0:64, :, :], QT_d.ap[:, 4 * g:4 * g + 4, t0:t0 + 128], reads=[QT_d], writes=[QT[i]])
        mk.dma("sp", gl[i].ap, ztm_d.ap[t0:t0 + 128, c_ng:c_ng + 24], reads=[ztm_d], writes=[gl[i]])
        mk.dma("sp", cm[i].ap, cst["cmpm"].ap[t0:t0 + 128, :], reads=[cst["cmpm"]], writes=[cm[i]])
        mk.dma("sp", cmT[i].ap, cst["cmpmT"].ap[:, t0:t0 + 128].rearrange("(n p) q -> p n q", p=128), reads=[cst["cmpmT"]], writes=[cmT[i]])
        mk.dma("sp", m1t[i].ap, cst["selm1"].ap[t0:t0 + 128, :], reads=[cst["selm1"]], writes=[m1t[i]])
        mk.dma("sp", c2t[i].ap, cst["selc2"].ap[t0:t0 + 128, :], reads=[cst["selc2"]], writes=[c2t[i]])
        exp_sigmoid(mk, gl[i], gl[i].ap, gl[i], gl[i].ap)
        mk.op("pool", lambda en: en.tensor_copy(cmTb[i].ap, cmT[i].ap), reads=[cmT[i]], writes=[cmTb[i]])
        yield
        tl = []
        for kt in range(max(0, qt - 4), qt + 1):
            if kt == qt:
                mf = lambda k: (causB.ap, [causB])
            elif kt == qt - 4:
                mf = lambda k: (farB.ap, [farB])
            else:
                mf = lambda k: None
            tl.append((KwT.ap[:, g, kt * 128:(kt + 1) * 128], VA.ap[:, kt, 2 + g, :], mf))
        yield from attn_branch(i, tl, psO[i])
        combine(i, g, 2, True)
        yield
        for hp in range(2):
            for hh in range(2):
                mk.mm(psS[i].ap[:, hh * ncp:(hh + 1) * ncp], [(QT[i].ap[:, hp * 2 + hh, :], kcT.ap[:, g, :])], reads=[QT[i], kcT], tr=psS[i])
            mk.op("act", lambda en, hp=hp: en.copy(sS[i].ap[:, hp * 2:hp * 2 + 2, :], psS[i].ap[:, 0:2 * ncp].rearrange("p (h n) -> p h n", h=2)),
                  reads=[psS[i]], writes=[sS[i]])
        yield
        mk.op("dve", lambda en: en.tensor_reduce(stt[i].ap[:, 0:4], sS[i].ap, AX.X, ALU.max), reads=[sS[i]], writes=[stt[i]])
        mk.op("dve", lambda en: en.tensor_tensor(sS[i].ap, sS[i].ap, bc_last(stt[i].ap[:, 0:4], ncp), ALU.subtract), reads=[sS[i], stt[i]], writes=[sS[i]])
        mk.op("act", lambda en: en.activation(out=sE[i].ap, in_=sS[i].ap, func=AF.Exp), reads=[sS[i]], writes=[sE[i]])
        mk.op("pool", lambda en: en.tensor_tensor(sE[i].ap, sE[i].ap, bc_mid(cm[i].ap, 4), ALU.mult), reads=[sE[i], cm[i]], writes=[sE[i]])
        yield
        mk.op("dve", lambda en: en.tensor_reduce(stt[i].ap[:, 4:8], sE[i].ap, AX.X, ALU.add), reads=[sE[i]], writes=[stt[i]])
        mk.op("dve", lambda en: en.tensor_scalar(stt[i].ap[:, 4:8], stt[i].ap[:, 4:8], 1e-30, None, ALU.max), reads=[stt[i]], writes=[stt[i]])
        mk.op("dve", lambda en: en.reciprocal(stt[i].ap[:, 8:12], stt[i].ap[:, 4:8]), reads=[stt[i]], writes=[stt[i]])
        mk.op("dve", lambda en: en.tensor_tensor(sE[i].ap, sE[i].ap, bc_last(stt[i].ap[:, 8:12], ncp), ALU.mult), reads=[sE[i], stt[i]], writes=[sE[i]])
        yield
        mk.op("pool", lambda en: en.memset(ph[i].ap[:, 0:1], 0.0), writes=[ph[i]])
        mk.op("dve", lambda en: en.tensor_reduce(ph[i].ap[:, 1:ncp + 1], sE[i].ap[:, :, :].rearrange("p h n -> p n h"), AX.X, ALU.add),
              reads=[sE[i]], writes=[ph[i]])
        mk.op("dve", lambda en: en.tensor_reduce(imp[i].ap, ph[i].ap[:, 1:ncp + 1].rearrange("p (j f) -> p j f", f=4), AX.X, ALU.add),
              reads=[ph[i]], writes=[imp[i]])
        mk.op("dve", lambda en: en.tensor_reduce(wk[i].ap, ph[i].ap[:, 0:ncp].rearrange("p (j f) -> p j f", f=4), AX.X, ALU.add),
              reads=[ph[i]], writes=[wk[i]])
        yield
        mk.op("dve", lambda en: en.tensor_tensor(imp[i].ap, imp[i].ap, wk[i].ap, ALU.add), reads=[imp[i], wk[i]], writes=[imp[i]])
        mk.op("dve", lambda en: en.scalar_tensor_tensor(imp[i].ap, imp[i].ap, 16.0, m1t[i].ap, ALU.mult, ALU.mult), reads=[imp[i], m1t[i]], writes=[imp[i]])
        mk.op("dve", lambda en: en.tensor_tensor(imp[i].ap, imp[i].ap, c2t[i].ap, ALU.add), reads=[imp[i], c2t[i]], writes=[imp[i]])
        yield
        mk.op("dve", lambda en: en.max(out=m8[i].ap[:, 0:8], in_=imp[i].ap), reads=[imp[i]], writes=[m8[i]])
        mk.op("dve", lambda en: en.match_replace(out=wk[i].ap, in_to_replace=m8[i].ap[:, 0:8], in_values=imp[i].ap, imm_value=-2.0),
              reads=[imp[i], m8[i]], writes=[wk[i]])
        mk.op("dve", lambda en: en.max(out=m8[i].ap[:, 8:16], in_=wk[i].ap), reads=[wk[i]], writes=[m8[i]])
        yield
        mk.op("dve", lambda en: en.tensor_reduce(stt[i].ap[:, 12:13], m8[i].ap[:, 8:16], AX.X, ALU.min), reads=[m8[i]], writes=[stt[i]])
        mk.op("dve", lambda en: en.tensor_scalar(selb[i].ap, imp[i].ap, stt[i].ap[:, 12:13], None, ALU.is_ge), reads=[imp[i], stt[i]], writes=[selb[i]])
        mk.op("pe", lambda pe: pe.transpose(psM[i].ap[0:ns, 0:128], selb[i].ap, identf.ap), reads=[selb[i], identf], writes=[psM[i]])
        mk.op("act", lambda en: en.copy(selT[i].ap, psM[i].ap[0:ns, 0:128]), reads=[psM[i]], writes=[selT[i]])
        yield
        yield from attn_branch(i, [(kcT.ap[:, g, nt * 128:(nt + 1) * 128], vcA.ap[:, g, nt, :], (lambda k, nt=nt: (cmTb[i].ap[:, nt, :], [cmTb[i]])))
                                  for nt in range(NQ)], psO[i])
        combine(i, g, 0, False)
        yield
        tl = []
        for kt in range(qt + 1):
            def mf(k, kt=kt):
                b = bm[i][k % 2]
                mk.mm(psM[i].ap[:, 0:128], [(esel.ap[:, kt * 128:(kt + 1) * 128], selT[i].ap)], reads=[esel, selT[i]], tr=psM[i])
                if kt == qt:
                    mk.op("dve", lambda en: en.tensor_tensor(b.ap, psM[i].ap[:, 0:128], causT.ap, ALU.mult), reads=[psM[i], causT], writes=[b])
                else:
                    mk.op("act", lambda en: en.copy(b.ap, psM[i].ap[:, 0:128]), reads=[psM[i]], writes=[b])
                return (b.ap, [b])
            tl.append((KsT.ap[:, g, kt * 128:(kt + 1) * 128], VA.ap[:, kt, g, :], mf))
        yield from attn_branch(i, tl, psO[i])
        combine(i, g, 1, False)
        mk.dma("pool", y_d.ap[t0:t0 + 128, g * 256:(g + 1) * 256], acc[i].ap[:, :, :].rearrange("p h d -> p (h d)"), reads=[acc[i]], writes=[y_d])

    order = [(g, qt) for qt in range(NT) for g in range(2)]
    if bg is not None:
        run_pipelined_multi([{"gens": (body(it, g, qt) for it, (g, qt) in enumerate(order)), "width": NS},
                             {"gens": bg, "width": 1, "period": 4}])
    else:
        run_pipelined((body(it, g, qt) for it, (g, qt) in enumerate(order)), NS, stagger=14)
    mk.phase_end()
    mk.phase_end()


T_SEQ = 4096
TAIL_T = 2176
NTM, NFM = 2208, 1408
C_NQ, C_NK, C_NV, C_NG, C_HF, C_HI, C_HG, C_CZ, C_CB, C_CA = 0, 512, 896, 1152, 1176, 1432, 1688, 1944, 2200, 2204
R_VC, R_HQ, R_HF, R_QKV = 0, 128, 384, 640
_r = lambda a, b: list(range(a, b))
W_IN_COLS = (_r(0, 512) + _r(512, 640) + _r(768, 896) + _r(1024, 1152) + _r(896, 1024) + _r(1152, 1280) + _r(1280, 1304)
             + _r(1560, 1816) + _r(1816, 2072) + _r(2072, 2328) + _r(3096, 3352) + _r(3352, 3356) + _r(3356, 3360)
             + _r(640, 768) + _r(1304, 1560) + _r(1560, 1816) + _r(2328, 3096))
assert len(W_IN_COLS) == NTM + NFM

LAYER_KEYS = ["norm_mix", "w_in_sel", "w_gate", "nsa_q_norm", "nsa_k_norm", "cmp_pos_k", "cmp_pos_v", "cmp_k_w1", "cmp_k_w2",
              "cmp_v_w1", "cmp_v_w2", "hgrn_out_norm", "gdn_conv_t", "gdn_a_log", "gdn_dt_bias", "gdn_out_norm",
              "w_branch_a", "w_branch_b", "w_branch_c", "w_mix_out", "norm_cross", "xattn_wq", "xattn_q_norm", "xattn_k_norm",
              "xattn_wo", "norm_ffn", "ffn_w_up", "ffn_conv_t", "ffn_w_down"]


def const_arrays(T):
    c = dict(gdn_consts())
    c.update(nsa_consts(T))
    c["ident"] = np.eye(128, dtype=np.float32)
    return c


def layer_arrays(inputs, l):
    f = lambda a: np.ascontiguousarray(np.asarray(a, dtype=np.float32))
    w_in = np.asarray(inputs["w_in"][l])
    d = {"w_in_sel": f(w_in[:, W_IN_COLS]), "w_gate": f(w_in[:, 3360:6432]),
         "gdn_conv_t": f(np.asarray(inputs["gdn_conv"][l]).T), "ffn_conv_t": f(np.asarray(inputs["ffn_conv"][l]).T)}
    for k in LAYER_KEYS:
        if k not in d:
            d[k] = f(inputs[k][l])
    return d


def build_program(T=T_SEQ, depth=2, shapes=None, stop_after=None):
    mk = MK()
    mk.live.append([])
    ext = lambda name, shape: mk.dram(name, shape, kind="ExternalInput")
    x_in = ext("x", [T, D])
    mem_in = ext("mem", [256, D])
    mem_norm = ext("mem_norm", [D])
    mem_w_kv = ext("mem_w_kv", [D, D])
    lbl = ext("hgrn_lb_logits", [2, 256])
    cst = {k: ext("c_" + k, list(v.shape)) for k, v in const_arrays(T).items()}
    L = []
    for l in range(depth):
        L.append({k: ext("L%d_%s" % (l, k), list(shapes[k])) for k in LAYER_KEYS})
    out = mk.dram("out", [TAIL_T, D], kind="ExternalOutput")
    hsel = ext("hsel", [1])
    xh = [mk.dram("xh%d" % j, [TAIL_T, D]) for j in range(3)]
    xs = [mk.dram("xA", [T, D]), mk.dram("xB", [T, D])]
    Ztm = mk.dram("Ztm", [T, NTM])
    Zfm = mk.dram("Zfm", [NFM, T])
    Y = mk.dram("Y", [T, D])
    MKV = mk.dram("MKV", [256, D])
    QT_d = mk.dram("QT", [64, 8, T], BF16)
    KT_d = mk.dram("KT", [64, 6, T], BF16)
    VA_d = mk.dram("VA", [T, 4, 65], BF16)
    ident = cst["ident"]
    stage_proj(mk, mem_in, mem_norm, mem_w_kv, MKV, None, 256, D, 0, ident)
    cur = x_in
    nstage = 0
    def nxt(last=False):
        return out if last else xs[nstage % 2]
    for l in range(depth):
        W = L[l]
        mk.mark("stage_proj")
        stage_proj(mk, cur, W["norm_mix"], W["w_in_sel"], Ztm, Zfm, T, NTM, NFM, ident)
        mk.mark("stage_gdn")
        n1_gens, n1_fin = nsa_n1(mk, Ztm, C_NQ, C_NK, C_NV, T, W["nsa_q_norm"], W["nsa_k_norm"], cst, ident, QT_d, KT_d, VA_d, defer=True)
        stage_gdn(mk, Ztm, C_CZ, C_CB, C_CA, Zfm, R_QKV, Y, 768, T, W["gdn_conv_t"], W["gdn_a_log"], W["gdn_dt_bias"], W["gdn_out_norm"], cst, bg=n1_gens)
        n1_fin()
        mk.mark("stage_nsa")
        hg_gens, hg_fin = stage_hgrn(mk, Ztm, C_HF, C_HI, C_HG, Zfm, R_HQ, R_HF, Y, 512, T, lbl, l, W["hgrn_out_norm"], cst["triI"], ident, defer=True)
        stage_nsa(mk, Ztm, C_NQ, C_NK, C_NV, C_NG, Zfm, R_VC, Y, T, W["nsa_q_norm"], W["nsa_k_norm"], W["cmp_pos_k"], W["cmp_pos_v"],
                  W["cmp_k_w1"], W["cmp_k_w2"], W["cmp_v_w1"], W["cmp_v_w2"], cst, ident, QT_d, KT_d, VA_d, bg=hg_gens, skip_n1=True)
        hg_fin()
        if stop_after == ("mix", l):
            mk.dma("sp", out.ap, Y.ap, reads=[Y], writes=[out])
            break
        last = (l == depth - 1)
        Tt = TAIL_T if last else T
        xin, yin = cur, Y
        if last:
            mk.mark("stage_select")
            xin, yin = xh[0], xh[1]
            stage_select(mk, cur, xin, Tt, T - Tt, hsel)
            stage_select(mk, Y, yin, Tt, T - Tt, hsel)
        x1 = xh[2] if last else nxt(); nstage += 1
        mk.mark("stage_merge")
        stage_merge(mk, xin, yin, x1, Tt, W["norm_mix"], W["w_gate"], W["w_branch_a"], W["w_branch_b"], W["w_branch_c"], W["w_mix_out"], ident)
        x2 = xh[0] if last else nxt(); nstage += 1
        mk.mark("stage_xattn")
        stage_xattn(mk, x1, x2, Tt, W["norm_cross"], MKV, W["xattn_wq"], W["xattn_q_norm"], W["xattn_k_norm"], W["xattn_wo"], ident)
        x3 = out if last else nxt(); nstage += 1
        mk.mark("stage_ffn")
        stage_ffn(mk, x2, x3, Tt, W["norm_ffn"], W["ffn_w_up"], W["ffn_conv_t"], W["ffn_w_down"], ident)
        cur = x3
    mk.mark("end")
    mk.finish()
    return mk


_PROG = {}


def kernel(**inputs):
    x = np.asarray(inputs["x"], dtype=np.float32)
    B, T, _ = x.shape
    depth = np.asarray(inputs["w_in"]).shape[0]
    layers = [layer_arrays(inputs, l) for l in range(depth)]
    shapes = {k: v.shape for k, v in layers[0].items()}
    key = (T, depth)
    if key not in _PROG:
        _PROG[key] = build_program(T, depth, shapes)
    mk = _PROG[key]
    f = lambda a: np.ascontiguousarray(np.asarray(a, dtype=np.float32))
    common = {"mem_norm": f(inputs["mem_norm"]), "mem_w_kv": f(inputs["mem_w_kv"]), "hgrn_lb_logits": f(inputs["hgrn_lb_logits"])}
    for k, v in const_arrays(T).items():
        common["c_" + k] = v
    for l in range(depth):
        for k, v in layers[l].items():
            common["L%d_%s" % (l, k)] = v
    n = 8
    in_maps = []
    for c in range(n):
        b = c % B
        m = dict(common)
        m["x"] = f(x[b])
        m["mem"] = f(np.asarray(inputs["mem"])[b])
        m["hsel"] = np.full((1,), float(c // B), np.float32)
        in_maps.append(m)
    res = run_bass_kernel_spmd(mk.nc, in_maps, core_ids=list(range(n)))
    Th = T // 2
    outs = []
    for b in range(B):
        lo = np.asarray(res.results[b]["out"], dtype=np.float32)[0:Th]
        hi = np.asarray(res.results[B + b]["out"], dtype=np.float32)[TAIL_T - Th:TAIL_T]
        outs.append(np.concatenate([lo, hi], axis=0))
    return np.stack(outs, axis=0)
```

```python
import numpy as np
import concourse.bass as bass
import concourse.mybir as mybir
from concourse.bass_utils import run_bass_kernel_spmd

F32 = mybir.dt.float32
BF16 = mybir.dt.bfloat16
AF = mybir.ActivationFunctionType
ALU = mybir.AluOpType
AX = mybir.AxisListType
F32R = mybir.dt.float32r
FP32R = False


class Buf:
    __slots__ = ("ap", "w", "r", "name")

    def __init__(self, ap, name):
        self.ap = ap
        self.name = name
        self.w = None
        self.r = {}

    def __getitem__(self, k):
        return self.ap[k]


class MK:
    NDMA = 6

    def __init__(self):
        self.nc = bass.Bass("TRN2", target_bir_lowering=False)
        nc = self.nc
        self.eng = {"pe": nc.tensor, "act": nc.scalar, "dve": nc.vector, "pool": nc.gpsimd, "sp": nc.sync}
        self.sem = {}
        self.cnt = {}
        for e in self.eng:
            self.sem[e] = nc.alloc_semaphore("s_" + e)
            self.cnt[e] = 0
        self.dq = {}
        for q in ("sp", "pool", "act"):
            ks = []
            for i in range(self.NDMA):
                k = "d_%s%d" % (q, i)
                self.sem[k] = nc.alloc_semaphore(k)
                self.cnt[k] = 0
                ks.append(k)
            self.dq[q] = [ks, 0]
        self.obs = {e: {} for e in self.eng}
        self.nbuf = 0
        self.live = []
        self.n_ins = 0
        self.marks = []
        self._pm = None

    def dram(self, name, shape, dt=F32, kind="Internal"):
        t = self.nc.dram_tensor(name, list(shape), dt, kind=kind)
        return Buf(t.ap(), name)

    def sb(self, shape, dt=F32, name=None):
        self.nbuf += 1
        name = "%s_%d" % (name or "sb", self.nbuf)
        g = self.nc.sbuf_tensor(name, list(shape), dt)
        h = g.__enter__()
        self.live[-1].append(g)
        return Buf(h.ap(), name)

    def ps(self, shape, dt=F32, name=None):
        self.nbuf += 1
        name = "%s_%d" % (name or "ps", self.nbuf)
        g = self.nc.psum_tensor(name, list(shape), dt)
        h = g.__enter__()
        self.live[-1].append(g)
        return Buf(h.ap(), name)

    def phase_begin(self):
        self.live.append([])

    def phase_end(self):
        self.barrier()
        for g in reversed(self.live.pop()):
            g.__exit__(None, None, None)

    def _wait(self, e, evs):
        eng = self.eng[e]
        ob = self.obs[e]
        for k, v in evs:
            if k == e and e == "pe":
                continue
            if ob.get(k, 0) < v:
                eng.wait_ge(self.sem[k], v)
                ob[k] = v

    def _deps(self, e, reads, writes):
        evs = []
        for t in reads:
            if t.w is not None:
                evs.append(t.w)
        for t in writes:
            if t.w is not None:
                evs.append(t.w)
            for k, v in t.r.items():
                if k != e:
                    evs.append((k, v))
        return evs

    def _mark(self, ev, reads, writes):
        k, v = ev
        for t in reads:
            if t.r.get(k, 0) < v:
                t.r[k] = v
        for t in writes:
            t.w = ev
            t.r = {}

    def op(self, e, fn, reads=(), writes=(), inc=True):
        self._wait(e, self._deps(e, reads, writes))
        ins = fn(self.eng[e])
        self.n_ins += 1
        self._domark(ins)
        if inc:
            self.cnt[e] += 1
            ins.then_inc(self.sem[e], 1)
            ev = (e, self.cnt[e])
        else:
            ev = (e, self.cnt[e] + 1)
        self._mark(ev, reads, writes)
        return ins

    def dma(self, q, out, in_, reads=(), writes=(), **kw):
        ks, i = self.dq[q]
        k = ks[i % len(ks)]
        self.dq[q][1] = i + 1
        evs = self._deps(q, reads, writes)
        if self.cnt[k] > 0:
            evs.append((k, self.cnt[k]))
        self._wait(q, evs)
        ins = self.eng[q].dma_start(out=out, in_=in_, **kw)
        self.n_ins += 1
        self._domark(ins)
        self.cnt[k] += 16
        ins.then_inc(self.sem[k], 16)
        self._mark((k, self.cnt[k]), reads, writes)
        return ins

    def mark(self, name):
        self._pm = name

    def _domark(self, ins):
        if self._pm is not None:
            try:
                self.marks.append((self._pm, ins.ins.name))
            except Exception as ex:
                self.marks.append((self._pm, repr(ex)))
            self._pm = None

    def barrier(self):
        allev = [(k, v) for k, v in self.cnt.items() if v > 0]
        for e in self.eng:
            self._wait(e, [(k, v) for k, v in allev if k != e])

    def finish(self):
        self.barrier()

    def mm(self, out, pairs, reads, tr=None):
        n = len(pairs)
        if FP32R:
            pairs = [((l.bitcast(F32R) if l.dtype == F32 else l), (r.bitcast(F32R) if r.dtype == F32 else r)) for l, r in pairs]
        for i, (l, r) in enumerate(pairs):
            self.op("pe", lambda pe, l=l, r=r, i=i: pe.matmul(out, l, r, start=(i == 0), stop=(i == n - 1)),
                    reads=reads, writes=[tr], inc=(i == n - 1))


def run_pipelined(gens, width, stagger=0):
    it = iter(gens)
    active = []
    done = set()
    exhausted = False
    head = stagger
    while True:
        while not exhausted and len(active) < (1 if head > 0 else width):
            g = next(it, None)
            if g is None:
                exhausted = True
                break
            active.append([g, None])
        if not active:
            break
        progressed = False
        for slot in list(active):
            g, blk = slot
            if blk is not None and blk not in done:
                continue
            slot[1] = None
            progressed = True
            try:
                r = next(g)
            except StopIteration:
                active.remove(slot)
                continue
            if isinstance(r, tuple):
                if r[0] == "wait":
                    slot[1] = r[1]
                elif r[0] == "done":
                    done.add(r[1])
        if head > 0:
            head -= 1
        assert progressed, "pipeline deadlock"


def run_pipelined_multi(streams):
    st = [{"it": iter(x["gens"]), "w": x["width"], "p": x.get("period", 1), "active": [], "ex": False} for x in streams]
    done = set()
    rnd = 0
    while True:
        alive = False
        for S_ in st:
            while not S_["ex"] and len(S_["active"]) < S_["w"]:
                g = next(S_["it"], None)
                if g is None:
                    S_["ex"] = True
                    break
                S_["active"].append([g, None])
            if S_["active"]:
                alive = True
        if not alive:
            break
        progressed = False
        only_bg = all((not S_["active"]) for S_ in st if S_["p"] == 1)
        for S_ in st:
            if S_["p"] > 1 and (rnd % S_["p"]) != 0 and not only_bg:
                continue
            for slot in list(S_["active"]):
                g, blk = slot
                if blk is not None and blk not in done:
                    continue
                slot[1] = None
                progressed = True
                try:
                    r = next(g)
                except StopIteration:
                    S_["active"].remove(slot)
                    continue
                if isinstance(r, tuple):
                    if r[0] == "wait":
                        slot[1] = r[1]
                    elif r[0] == "done":
                        done.add(r[1])
        rnd += 1
        assert progressed or any(S_["p"] > 1 for S_ in st), "pipeline deadlock"


D = 1024
EPS = 1e-6
TINY = 1e-20


def bcast_rows(ap1d, n):
    return ap1d.partition_broadcast(n)


def load_weight_bf16(mk, w_d, K, N, name, q="sp", chunk=2048):
    kc = K // 128
    wb = mk.sb([128, kc, N], BF16, name)
    mk.phase_begin()
    stg = [mk.sb([128, min(N, chunk)], F32, "wstg") for _ in range(4)]
    i = 0
    for k in range(kc):
        for c0 in range(0, N, chunk):
            c1 = min(N, c0 + chunk)
            s = stg[i % 4]
            mk.dma(("sp", "act")[i % 2], s.ap[:, 0:c1 - c0], w_d.ap[k * 128:(k + 1) * 128, c0:c1], reads=[w_d], writes=[s])
            e = "pool" if i % 2 == 0 else "dve"
            mk.op(e, lambda en, s=s, k=k, c0=c0, c1=c1: en.tensor_copy(wb.ap[:, k, c0:c1], s.ap[:, 0:c1 - c0]),
                  reads=[s], writes=[wb])
            i += 1
    mk.phase_end()
    return wb


def rmsnorm_to_fm(mk, xt, gB, ident, hT, col0, scr, ssb, hb, psT):
    mk.op("act", lambda en: en.activation(out=scr.ap, in_=xt.ap, func=AF.Square, accum_out=ssb.ap[:, 0:1]),
          reads=[xt], writes=[scr, ssb])
    mk.op("act", lambda en: en.activation(out=ssb.ap[:, 1:2], in_=ssb.ap[:, 0:1], func=AF.Sqrt, scale=1.0 / D, bias=EPS),
          reads=[ssb], writes=[ssb])
    mk.op("dve", lambda en: en.reciprocal(ssb.ap[:, 2:3], ssb.ap[:, 1:2]), reads=[ssb], writes=[ssb])
    mk.op("dve", lambda en: en.scalar_tensor_tensor(hb.ap, xt.ap, ssb.ap[:, 2:3], gB.ap, ALU.mult, ALU.mult),
          reads=[xt, ssb, gB], writes=[hb])
    tm_to_fm(mk, hb, ident, hT, col0, psT)


def tm_to_fm(mk, hb, ident, hT, col0, psT, nk=8):
    for k in range(nk):
        mk.op("pe", lambda pe, k=k: pe.transpose(psT.ap[:, k, :], hb.ap[:, k * 128:(k + 1) * 128], ident.ap),
              reads=[hb, ident], writes=[psT], inc=(k == nk - 1))
    mk.op("act", lambda en: en.copy(hT.ap[:, 0:nk, col0:col0 + 128], psT.ap[:, 0:nk, :]), reads=[psT], writes=[hT])


def norm_block(mk, jobs, gB, ident, slots):
    run_pipelined(norm_gens(mk, jobs, gB, ident, slots), len(slots), stagger=1)


def norm_gens(mk, jobs, gB, ident, slots):
    def gen(idx, xt, src, hT, col0, kind):
        sl = slots[idx % len(slots)]
        junk, ssb, hb, psT = sl["junk"], sl["ssb"], sl["hb"], sl["psT"]
        if src is not None:
            mk.dma("sp", xt.ap, src[1], reads=[src[0]], writes=[xt])
        if kind == "norm":
            mk.op("act", lambda en: en.activation(out=junk.ap, in_=xt.ap, func=AF.Square, accum_out=ssb.ap[:, 0:1]),
                  reads=[xt], writes=[junk, ssb])
            yield
            mk.op("act", lambda en: en.activation(out=ssb.ap[:, 1:2], in_=ssb.ap[:, 0:1], func=AF.Ln, scale=1.0 / D, bias=EPS),
                  reads=[ssb], writes=[ssb])
            mk.op("act", lambda en: en.activation(out=ssb.ap[:, 2:3], in_=ssb.ap[:, 1:2], func=AF.Exp, scale=-0.5), reads=[ssb], writes=[ssb])
            mk.op("dve", lambda en: en.scalar_tensor_tensor(hb.ap, xt.ap, ssb.ap[:, 2:3], gB.ap, ALU.mult, ALU.mult),
                  reads=[xt, ssb, gB], writes=[hb])
        else:
            mk.op("pool", lambda en: en.tensor_copy(hb.ap, xt.ap), reads=[xt], writes=[hb])
        yield
        for k in range(8):
            mk.op("pe", lambda pe, k=k: pe.transpose(psT.ap[:, k, :], hb.ap[:, k * 128:(k + 1) * 128], ident.ap),
                  reads=[hb, ident], writes=[psT], inc=(k == 7))
        yield
        mk.op("act", lambda en: en.copy(hT.ap[:, 0:8, col0:col0 + 128], psT.ap[:, 0:8, :]), reads=[psT], writes=[hT])
    return (gen(i, *j) for i, j in enumerate(jobs))


def norm_slots(mk, n, psTs):
    return [{"junk": mk.sb([128, D], BF16, "junk"), "ssb": mk.sb([128, 4], F32, "ssb"), "hb": mk.sb([128, D], BF16, "hb"), "psT": psTs[i]}
            for i in range(n)]


def stage_proj(mk, x_d, g_d, w_d, ztm_d, zfm_d, T, ntm, nfm, ident_d):
    mk.phase_begin()
    N = ntm + nfm
    wb = load_weight_bf16(mk, w_d, D, N, "w_in")
    gB = mk.sb([128, D], F32, "gB")
    mk.dma("sp", gB.ap, bcast_rows(g_d.ap, 128), reads=[g_d], writes=[gB])
    identf = mk.sb([128, 128], F32, "identf")
    ident = mk.sb([128, 128], BF16, "ident")
    mk.dma("sp", identf.ap, ident_d.ap, reads=[ident_d], writes=[identf])
    mk.op("dve", lambda en: en.tensor_copy(ident.ap, identf.ap), reads=[identf], writes=[ident])
    xts = [mk.sb([128, D], F32, "xt") for _ in range(2)]
    TB = min(512, T)
    NTB = TB // 128
    hTs = [mk.sb([128, 8, TB], BF16, "hT") for _ in range(2)]
    psTs = [mk.ps([128, 8, 128], BF16, "psT") for _ in range(2)]
    pss = [mk.ps([128, 512], F32, "psm") for _ in range(4)]
    nsl = norm_slots(mk, 2, psTs)
    ofm = [mk.sb([128, 512], F32, "ofm") for _ in range(3)]
    otm = [mk.sb([128, max(ntm, 1)], F32, "otm") for _ in range(2)]
    it = 0
    ip = 0
    io = 0
    nblk = T // TB

    def mk_jobs(tb):
        jobs = []
        for j in range(NTB):
            t0 = tb * TB + j * 128
            jobs.append((xts[(tb * NTB + j) % 2], (x_d, x_d.ap[t0:t0 + 128, :]), hTs[tb % 2], j * 128, "norm"))
        return jobs

    cnt = {"ip": 0, "io": 0}

    def mm_gen(tb):
        hT = hTs[tb % 2]
        for c in range(nfm // 128):
            ps = pss[cnt["ip"] % 4]
            cnt["ip"] += 1
            c0 = ntm + c * 128
            mk.mm(ps.ap[:, 0:TB], [(wb.ap[:, k, c0:c0 + 128], hT.ap[:, k, :]) for k in range(8)], reads=[wb, hT], tr=ps)
            o = ofm[cnt["io"] % 3]
            e = "act" if cnt["io"] % 2 == 0 else "dve"
            cnt["io"] += 1
            if e == "act":
                mk.op("act", lambda en, o=o, ps=ps: en.copy(o.ap[:, 0:TB], ps.ap[:, 0:TB]), reads=[ps], writes=[o])
            else:
                mk.op("dve", lambda en, o=o, ps=ps: en.tensor_copy(o.ap[:, 0:TB], ps.ap[:, 0:TB]), reads=[ps], writes=[o])
            mk.dma("pool", zfm_d.ap[c * 128:(c + 1) * 128, tb * TB:(tb + 1) * TB], o.ap[:, 0:TB], reads=[o], writes=[zfm_d])
            yield
        for j in range(NTB):
            o = otm[(tb * NTB + j) % 2]
            t0 = tb * TB + j * 128
            for c0 in range(0, ntm, 512):
                c1 = min(ntm, c0 + 512)
                ps = pss[cnt["ip"] % 4]
                cnt["ip"] += 1
                mk.mm(ps.ap[:, 0:c1 - c0], [(hT.ap[:, k, j * 128:(j + 1) * 128], wb.ap[:, k, c0:c1]) for k in range(8)],
                      reads=[wb, hT], tr=ps)
                e = "act" if cnt["io"] % 2 == 0 else "dve"
                cnt["io"] += 1
                if e == "act":
                    mk.op("act", lambda en, o=o, ps=ps, c0=c0, c1=c1: en.copy(o.ap[:, c0:c1], ps.ap[:, 0:c1 - c0]),
                          reads=[ps], writes=[o])
                else:
                    mk.op("dve", lambda en, o=o, ps=ps, c0=c0, c1=c1: en.tensor_copy(o.ap[:, c0:c1], ps.ap[:, 0:c1 - c0]),
                          reads=[ps], writes=[o])
                yield
            if ntm:
                mk.dma("pool", ztm_d.ap[t0:t0 + 128, :], o.ap, reads=[o], writes=[ztm_d])

    norm_block(mk, mk_jobs(0), gB, ident, nsl)
    for tb in range(nblk):
        streams = [{"gens": [mm_gen(tb)], "width": 1}]
        if tb + 1 < nblk:
            streams.append({"gens": norm_gens(mk, mk_jobs(tb + 1), gB, ident, nsl), "width": 2, "period": 2})
        run_pipelined_multi(streams)
    mk.phase_end()


def evac(mk, i, out_ap, in_ap, reads, writes):
    if i % 2 == 0:
        mk.op("act", lambda en: en.copy(out_ap, in_ap), reads=reads, writes=writes)
    else:
        mk.op("dve", lambda en: en.tensor_copy(out_ap, in_ap), reads=reads, writes=writes)


def load_consts_ident(mk, ident_d):
    identf = mk.sb([128, 128], F32, "identf")
    ident = mk.sb([128, 128], BF16, "ident")
    mk.dma("sp", identf.ap, ident_d.ap, reads=[ident_d], writes=[identf])
    mk.op("dve", lambda en: en.tensor_copy(ident.ap, identf.ap), reads=[identf], writes=[ident])
    return ident, identf


def tok_blocks(T, full=512):
    out = []
    rem = T % full
    pos = 0
    if rem:
        out.append((0, rem // 128))
        pos = rem
    while pos < T:
        out.append((pos, full // 128))
        pos += full
    return out


def stage_merge(mk, x_d, y_d, xo_d, T, gn_d, wg_d, wa_d, wb_d, wc_d, wmix_d, ident_d):
    mk.phase_begin()
    wg = load_weight_bf16(mk, wg_d, D, 3072, "wg")
    wa = load_weight_bf16(mk, wa_d, 512, D, "wa")
    wbb = load_weight_bf16(mk, wb_d, 256, D, "wb")
    wc = load_weight_bf16(mk, wc_d, 256, D, "wc")
    wmix = load_weight_bf16(mk, wmix_d, D, D, "wmix")
    gB = mk.sb([128, D], F32, "gB")
    mk.dma("sp", gB.ap, bcast_rows(gn_d.ap, 128), reads=[gn_d], writes=[gB])
    ident, _ = load_consts_ident(mk, ident_d)
    xts = [mk.sb([128, D], F32, "xt") for _ in range(4)]
    yts = [mk.sb([128, D], F32, "yt") for _ in range(2)]
    hT = mk.sb([128, 8, 512], BF16, "hT")
    yT = mk.sb([128, 8, 512], BF16, "yT")
    mT = mk.sb([128, 8, 512], BF16, "mT")
    sg = [mk.sb([128, 512], F32, "sg") for _ in range(3)]
    acc = [mk.sb([128, 512], F32, "acc") for _ in range(2)]
    tmp = [mk.sb([128, 512], F32, "tmp") for _ in range(2)]
    psTs = [mk.ps([128, 8, 128], BF16, "psT") for _ in range(2)]
    pss = [mk.ps([128, 512], F32, "psm") for _ in range(5)]
    nsl = norm_slots(mk, 2, psTs)
    ip = 0
    it = 0
    for (tbase, nt) in tok_blocks(T):
        Wd = nt * 128
        jobs = []
        for j in range(nt):
            t0 = tbase + j * 128
            jobs.append((xts[j], (x_d, x_d.ap[t0:t0 + 128, :]), hT, j * 128, "norm"))
            jobs.append((yts[j % 2], (y_d, y_d.ap[t0:t0 + 128, :]), yT, j * 128, "cast"))
        norm_block(mk, jobs, gB, ident, nsl)
        for c in range(8):
            a = acc[c % 2]
            for b, (wbr, koff, nk) in enumerate(((wa, 0, 4), (wbb, 4, 2), (wc, 6, 2))):
                psg = pss[ip % 5]
                ip += 1
                g0 = b * 1024 + c * 128
                mk.mm(psg.ap[:, 0:Wd], [(wg.ap[:, k, g0:g0 + 128], hT.ap[:, k, 0:Wd]) for k in range(8)], reads=[wg, hT], tr=psg)
                s = sg[b]
                mk.op("act", lambda en, s=s, psg=psg, Wd=Wd: en.activation(out=s.ap[:, 0:Wd], in_=psg.ap[:, 0:Wd], func=AF.Sigmoid),
                      reads=[psg], writes=[s])
                psb = pss[ip % 5]
                ip += 1
                mk.mm(psb.ap[:, 0:Wd], [(wbr.ap[:, k, c * 128:(c + 1) * 128], yT.ap[:, koff + k, 0:Wd]) for k in range(nk)],
                      reads=[wbr, yT], tr=psb)
                if b == 0:
                    mk.op("dve", lambda en, a=a, s=s, psb=psb, Wd=Wd: en.tensor_tensor(a.ap[:, 0:Wd], s.ap[:, 0:Wd], psb.ap[:, 0:Wd], ALU.mult),
                          reads=[s, psb], writes=[a])
                else:
                    t = tmp[b % 2]
                    mk.op("dve", lambda en, t=t, s=s, psb=psb, Wd=Wd: en.tensor_tensor(t.ap[:, 0:Wd], s.ap[:, 0:Wd], psb.ap[:, 0:Wd], ALU.mult),
                          reads=[s, psb], writes=[t])
                    if b == 1:
                        mk.op("pool", lambda en, a=a, t=t, Wd=Wd: en.tensor_tensor(a.ap[:, 0:Wd], a.ap[:, 0:Wd], t.ap[:, 0:Wd], ALU.add),
                              reads=[a, t], writes=[a])
                    else:
                        mk.op("pool", lambda en, a=a, t=t, c=c, Wd=Wd: en.tensor_tensor(mT.ap[:, c, 0:Wd], a.ap[:, 0:Wd], t.ap[:, 0:Wd], ALU.add),
                              reads=[a, t], writes=[mT])
        for j in range(nt):
            xt = xts[j]
            t0 = tbase + j * 128
            for c0 in (0, 512):
                ps = pss[ip % 5]
                ip += 1
                mk.mm(ps.ap, [(mT.ap[:, k, j * 128:(j + 1) * 128], wmix.ap[:, k, c0:c0 + 512]) for k in range(8)],
                      reads=[mT, wmix], tr=ps)
                mk.op("dve", lambda en, xt=xt, ps=ps, c0=c0: en.tensor_tensor(xt.ap[:, c0:c0 + 512], xt.ap[:, c0:c0 + 512], ps.ap, ALU.add),
                      reads=[xt, ps], writes=[xt])
            mk.dma("pool", xo_d.ap[t0:t0 + 128, :], xt.ap, reads=[xt], writes=[xo_d])
    mk.phase_end()


def stage_select(mk, src_d, dst_d, Tt, off, hsel_d):
    mk.phase_begin()
    hB = mk.sb([128, 1], F32, "hB")
    mk.dma("sp", hB.ap, bcast_rows(hsel_d.ap, 128), reads=[hsel_d], writes=[hB])
    W = 3
    lo = [mk.sb([128, D], F32, "lo") for _ in range(W)]
    hi = [mk.sb([128, D], F32, "hi") for _ in range(W)]
    Th = off
    for tt in range(Tt // 128):
        k = tt % W
        t0 = tt * 128
        mk.dma("sp", lo[k].ap, src_d.ap[t0:t0 + 128, :], reads=[src_d], writes=[lo[k]])
        mk.dma("act", hi[k].ap, src_d.ap[Th + t0:Th + t0 + 128, :], reads=[src_d], writes=[hi[k]])
        eng = "dve" if tt % 2 == 0 else "pool"
        mk.op(eng, lambda en, k=k: en.tensor_tensor(hi[k].ap, hi[k].ap, lo[k].ap, ALU.subtract), reads=[hi[k], lo[k]], writes=[hi[k]])
        mk.op("dve", lambda en, k=k: en.scalar_tensor_tensor(lo[k].ap, hi[k].ap, hB.ap[:, 0:1], lo[k].ap, ALU.mult, ALU.add),
              reads=[hi[k], lo[k], hB], writes=[lo[k]])
        mk.dma("pool", dst_d.ap[t0:t0 + 128, :], lo[k].ap, reads=[lo[k]], writes=[dst_d])
    mk.phase_end()


def stage_ffn(mk, x_d, xo_d, T, gn_d, wup_d, cw_d, wd_d, ident_d):
    TB = 512
    NT = TB // 128
    mk.phase_begin()
    wup = load_weight_bf16(mk, wup_d, D, 5632, "wup")
    wd = load_weight_bf16(mk, wd_d, 2816, D, "wd")
    gB = mk.sb([128, D], F32, "gB")
    mk.dma("sp", gB.ap, bcast_rows(gn_d.ap, 128), reads=[gn_d], writes=[gB])
    cw = mk.sb([128, 44, 3], F32, "cw")
    mk.dma("sp", cw.ap, cw_d.ap.rearrange("(c p) k -> p c k", p=128), reads=[cw_d], writes=[cw])
    ident, _ = load_consts_ident(mk, ident_d)
    halos = [mk.sb([128, 2], F32, "halo") for _ in range(44)]
    for hb_ in halos:
        mk.op("pool", lambda en, hb_=hb_: en.memset(hb_.ap, 0.0), writes=[hb_])
    xts = [mk.sb([128, D], F32, "xt") for _ in range(2)]
    hT = mk.sb([128, 8, TB], BF16, "hT")
    W = 2
    us = [[mk.sb([128, TB + 2], F32, "u") for _ in range(2)] for _ in range(W)]
    cas = [mk.sb([128, TB], F32, "ca") for _ in range(W)]
    cbs = [mk.sb([128, TB], F32, "cb") for _ in range(W)]
    gTs = [mk.sb([128, TB], BF16, "gT") for _ in range(22)]
    psTs = [mk.ps([128, 8, 128], BF16, "psT") for _ in range(1)]
    nsl = norm_slots(mk, 1, psTs)
    psu = [[mk.ps([128, 512], F32, "psu") for _ in range(2)] for _ in range(W)]
    psd = mk.ps([128, 512], F32, "psd")
    it = 0
    nit = [0]
    for (tbase, nt) in tok_blocks(T, TB):
        Wd = nt * 128
        jobs = []
        for j in range(nt):
            t0 = tbase + j * 128
            jobs.append((xts[j % 2], (x_d, x_d.ap[t0:t0 + 128, :]), hT, j * 128, "norm"))
        norm_block(mk, jobs, gB, ident, nsl)

        def cbody(c, k):
            cv = []
            for half, ct in enumerate((c, c + 22)):
                ps = psu[k][half]
                mk.mm(ps.ap[:, 0:Wd], [(wup.ap[:, kk, ct * 128:(ct + 1) * 128], hT.ap[:, kk, 0:Wd]) for kk in range(8)],
                      reads=[wup, hT], tr=ps)
                u = us[k][half]
                hl = halos[ct]
                mk.op("act", lambda en, u=u, hl=hl: en.copy(u.ap[:, 0:2], hl.ap), reads=[hl], writes=[u])
                mk.op("act", lambda en, u=u, ps=ps: en.copy(u.ap[:, 2:Wd + 2], ps.ap[:, 0:Wd]), reads=[ps], writes=[u])
                o = (cas if half == 0 else cbs)[k]
                mk.op("act", lambda en, o=o, ps=ps, ct=ct: en.activation(out=o.ap[:, 0:Wd], in_=ps.ap[:, 0:Wd], func=AF.Copy, scale=cw.ap[:, ct, 2:3]),
                      reads=[ps, cw], writes=[o])
                yield
                mk.op("act", lambda en, u=u, hl=hl: en.copy(hl.ap, u.ap[:, Wd:Wd + 2]), reads=[u], writes=[hl])
                mk.op("dve", lambda en, o=o, u=u, ct=ct: en.scalar_tensor_tensor(o.ap[:, 0:Wd], u.ap[:, 1:Wd + 1], cw.ap[:, ct, 1:2], o.ap[:, 0:Wd], ALU.mult, ALU.add),
                      reads=[u, cw, o], writes=[o])
                mk.op("dve", lambda en, o=o, u=u, ct=ct: en.scalar_tensor_tensor(o.ap[:, 0:Wd], u.ap[:, 0:Wd], cw.ap[:, ct, 0:1], o.ap[:, 0:Wd], ALU.mult, ALU.add),
                      reads=[u, cw, o], writes=[o])
                cv.append(o)
                yield
            a_, b_ = cv
            mk.op("act", lambda en, a_=a_: en.activation(out=a_.ap[:, 0:Wd], in_=a_.ap[:, 0:Wd], func=AF.Silu), reads=[a_], writes=[a_])
            yield
            mk.op("dve", lambda en, a_=a_, b_=b_, c=c: en.tensor_tensor(gTs[c].ap[:, 0:Wd], a_.ap[:, 0:Wd], b_.ap[:, 0:Wd], ALU.mult),
                  reads=[a_, b_], writes=[gTs[c]])

        def gens():
            for c in range(22):
                k = nit[0] % W
                nit[0] += 1
                yield cbody(c, k)
        run_pipelined(gens(), W, stagger=2)
        for j in range(nt):
            t0 = tbase + j * 128
            xt = xts[j % 2]
            mk.dma("sp", xt.ap, x_d.ap[t0:t0 + 128, :], reads=[x_d], writes=[xt])
            for c0 in (0, 512):
                ps = psd
                mk.mm(ps.ap, [(gTs[k].ap[:, j * 128:(j + 1) * 128], wd.ap[:, k, c0:c0 + 512]) for k in range(22)],
                      reads=gTs + [wd], tr=ps)
                mk.op("dve", lambda en, xt=xt, ps=ps, c0=c0: en.tensor_tensor(xt.ap[:, c0:c0 + 512], xt.ap[:, c0:c0 + 512], ps.ap, ALU.add),
                      reads=[xt, ps], writes=[xt])
            mk.dma("pool", xo_d.ap[t0:t0 + 128, :], xt.ap, reads=[xt], writes=[xo_d])
    mk.phase_end()

def bc_last(ap2, n):
    return ap2.unsqueeze(2).to_broadcast([ap2.shape[0], ap2.shape[1], n])


def bc_mid(ap2, h):
    return ap2.unsqueeze(1).to_broadcast([ap2.shape[0], h, ap2.shape[1]])


def exp_sigmoid(mk, out_b, out_ap, in_b, in_ap, neg=False, eng2="dve"):
    mk.op("act", lambda en: en.activation(out=out_ap, in_=in_ap, func=AF.Exp, scale=(1.0 if neg else -1.0)), reads=[in_b], writes=[out_b])
    mk.op("act", lambda en: en.activation(out=out_ap, in_=out_ap, func=AF.Ln, bias=1.0), reads=[out_b], writes=[out_b])
    mk.op("act", lambda en: en.activation(out=out_ap, in_=out_ap, func=AF.Exp, scale=-1.0), reads=[out_b], writes=[out_b])


def exp_silu(mk, out_b, out_ap, in_b, in_ap, tmp_b, tmp_ap):
    exp_sigmoid(mk, tmp_b, tmp_ap, in_b, in_ap)
    mk.op("dve", lambda en: en.tensor_tensor(out_ap, in_ap, tmp_ap, ALU.mult), reads=[in_b, tmp_b], writes=[out_b])


def head_rmsnorm(mk, src, H, dh, gB, out, scr, st, scale=1.0, np_=128):
    n = H * dh
    s3 = lambda b: b.ap[0:np_, 0:n].rearrange("p (h d) -> p h d", h=H)
    mk.op("pool", lambda en: en.tensor_tensor(scr.ap[0:np_, 0:n], src.ap[0:np_, 0:n], src.ap[0:np_, 0:n], ALU.mult),
          reads=[src], writes=[scr])
    mk.op("dve", lambda en: en.tensor_reduce(st.ap[0:np_, 0:H], s3(scr), AX.X, ALU.add), reads=[scr], writes=[st])
    mk.op("act", lambda en: en.activation(out=st.ap[0:np_, H:2 * H], in_=st.ap[0:np_, 0:H], func=AF.Ln, scale=1.0 / dh, bias=EPS),
          reads=[st], writes=[st])
    mk.op("act", lambda en: en.activation(out=st.ap[0:np_, 2 * H:3 * H], in_=st.ap[0:np_, H:2 * H], func=AF.Exp, scale=-0.5),
          reads=[st], writes=[st])
    mk.op("dve", lambda en: en.tensor_tensor(s3(scr), s3(src), bc_last(st.ap[0:np_, 2 * H:3 * H], dh), ALU.mult),
          reads=[src, st], writes=[scr])
    mk.op("dve", lambda en: en.scalar_tensor_tensor(s3(out), s3(scr), float(scale), bc_mid(gB.ap[0:np_, 0:dh], H), ALU.mult, ALU.mult),
          reads=[scr, gB], writes=[out])


def stage_xattn(mk, x_d, xo_d, T, gn_d, mkv_d, wq_d, qn_d, kn_d, wo_d, ident_d):
    mk.phase_begin()
    H, DH, M = 4, 128, 256
    wq = load_weight_bf16(mk, wq_d, D, 512, "wq")
    wo = load_weight_bf16(mk, wo_d, 512, D, "wo")
    gB = mk.sb([128, D], F32, "gB")
    mk.dma("sp", gB.ap, bcast_rows(gn_d.ap, 128), reads=[gn_d], writes=[gB])
    gq = mk.sb([128, DH], F32, "gq")
    mk.dma("sp", gq.ap, bcast_rows(qn_d.ap, 128), reads=[qn_d], writes=[gq])
    gk = mk.sb([128, DH], F32, "gk")
    mk.dma("sp", gk.ap, bcast_rows(kn_d.ap, 128), reads=[kn_d], writes=[gk])
    ident, _ = load_consts_ident(mk, ident_d)
    kT = mk.sb([128, H, M], BF16, "kT")
    vaug = mk.sb([128, 2, H, DH + 1], BF16, "vaug")
    scr = mk.sb([128, D], F32, "scr")
    st = mk.sb([128, 12], F32, "st")
    psTs = [mk.ps([128, 8, 128], BF16, "psT") for _ in range(2)]
    pss = [mk.ps([128, 512], F32, "psm") for _ in range(5)]
    hbs = [mk.sb([128, D], BF16, "hb") for _ in range(2)]
    mt_t = mk.sb([128, D], F32, "memt")
    mk.op("pool", lambda en: en.memset(vaug.ap, 1.0), writes=[vaug])
    for mt in range(2):
        mk.dma("sp", mt_t.ap, mkv_d.ap[mt * 128:(mt + 1) * 128, :], reads=[mkv_d], writes=[mt_t])
        hb = hbs[mt]
        head_rmsnorm(mk, mt_t, H, DH, gk, hb, scr, st)
        for h in range(H):
            mk.op("pe", lambda pe, h=h, hb=hb: pe.transpose(psTs[0].ap[:, h, :], hb.ap[:, h * DH:(h + 1) * DH], ident.ap),
                  reads=[hb, ident], writes=[psTs[0]], inc=(h == H - 1))
        mk.op("act", lambda en, mt=mt: en.copy(kT.ap[:, :, mt * 128:(mt + 1) * 128], psTs[0].ap[:, 0:H, :]),
              reads=[psTs[0]], writes=[kT])
        mk.op("dve", lambda en, mt=mt: en.tensor_copy(vaug.ap[:, mt, :, 0:DH], mt_t.ap[:, 512:1024].rearrange("p (h d) -> p h d", h=H)),
              reads=[mt_t], writes=[vaug])
    xts = [mk.sb([128, D], F32, "xt") for _ in range(4)]
    nsl = [{"junk": mk.sb([128, D], BF16, "junk"), "ssb": mk.sb([128, 4], F32, "ssb"), "hb": hbs[i_], "psT": psTs[i_]} for i_ in range(2)]
    hT = mk.sb([128, 8, 512], BF16, "hT")
    W = 2
    qf = [mk.sb([128, 512], F32, "qf") for _ in range(W)]
    qb = [mk.sb([128, 512], BF16, "qb") for _ in range(W)]
    scrs = [mk.sb([128, 512], F32, "scrq") for _ in range(W)]
    sts = [mk.sb([128, 12], F32, "stq") for _ in range(W)]
    qT = mk.sb([128, H, 512], BF16, "qT")
    PT = mk.sb([128, H * 2, 512], BF16, "PT")
    of = [mk.sb([128, 512], F32, "of") for _ in range(W)]
    ob = [mk.sb([128, 512], BF16, "ob") for _ in range(W)]
    oT = [mk.sb([128, H, 128], BF16, "oT") for _ in range(W)]
    rd = [mk.sb([128, 4], F32, "rd") for _ in range(W)]
    pq = [pss[0], pss[1]]
    po = [[pss[2], pss[3]], [pss[4], pss[0]]]
    it = 0
    for (tbase, nt) in tok_blocks(T):
        Wd = nt * 128
        jobs = []
        for j in range(nt):
            t0 = tbase + j * 128
            jobs.append((xts[j], (x_d, x_d.ap[t0:t0 + 128, :]), hT, j * 128, "norm"))
        norm_block(mk, jobs, gB, ident, nsl)

        def qbody(j):
            k = j % W
            ps = pq[k]
            mk.mm(ps.ap, [(hT.ap[:, kk, j * 128:(j + 1) * 128], wq.ap[:, kk, :]) for kk in range(8)], reads=[hT, wq], tr=ps)
            q = qf[k]
            mk.op("act", lambda en: en.copy(q.ap, ps.ap), reads=[ps], writes=[q])
            yield
            qq = qb[k]
            head_rmsnorm(mk, q, H, DH, gq, qq, scrs[k], sts[k], scale=DH ** -0.5)
            yield
            psT = psTs[k]
            for h in range(H):
                mk.op("pe", lambda pe, h=h: pe.transpose(psT.ap[:, h, :], qq.ap[:, h * DH:(h + 1) * DH], ident.ap),
                      reads=[qq, ident], writes=[psT], inc=(h == H - 1))
            mk.op("act", lambda en: en.copy(qT.ap[:, :, j * 128:(j + 1) * 128], psT.ap[:, 0:H, :]),
                  reads=[psT], writes=[qT])
        run_pipelined((qbody(j) for j in range(nt)), W, stagger=1)
        ip = 0
        for h in range(H):
            for mt in range(2):
                ps = pss[ip % 2]
                ip += 1
                mk.mm(ps.ap[:, 0:Wd], [(kT.ap[:, h, mt * 128:(mt + 1) * 128], qT.ap[:, h, 0:Wd])], reads=[kT, qT], tr=ps)
                mk.op("act", lambda en, ps=ps, h=h, mt=mt, Wd=Wd: en.activation(out=PT.ap[:, h * 2 + mt, 0:Wd], in_=ps.ap[:, 0:Wd], func=AF.Exp),
                      reads=[ps], writes=[PT])

        def obody(j):
            k = j % W
            xt = xts[j]
            t0 = tbase + j * 128
            o_f = of[k]
            r = rd[k]
            for hp in range(2):
                ps = po[k][hp]
                for hh in range(2):
                    h = hp * 2 + hh
                    mk.mm(ps.ap[:, hh * 129:(hh + 1) * 129],
                          [(PT.ap[:, h * 2 + mt, j * 128:(j + 1) * 128], vaug.ap[:, mt, h, :]) for mt in range(2)],
                          reads=[PT, vaug], tr=ps)
                for hh in range(2):
                    h = hp * 2 + hh
                    mk.op("dve", lambda en, ps=ps, hh=hh, h=h: en.reciprocal(r.ap[:, h:h + 1], ps.ap[:, hh * 129 + 128:hh * 129 + 129]),
                          reads=[ps], writes=[r])
                    mk.op("dve", lambda en, ps=ps, hh=hh, h=h: en.tensor_scalar(
                        o_f.ap[:, h * 128:(h + 1) * 128], ps.ap[:, hh * 129:hh * 129 + 128], r.ap[:, h:h + 1], None, ALU.mult),
                        reads=[ps, r], writes=[o_f])
                yield
            o_b = ob[k]
            mk.op("pool", lambda en: en.tensor_copy(o_b.ap, o_f.ap), reads=[o_f], writes=[o_b])
            psT = psTs[k]
            for h in range(H):
                mk.op("pe", lambda pe, h=h: pe.transpose(psT.ap[:, h, :], o_b.ap[:, h * DH:(h + 1) * DH], ident.ap),
                      reads=[o_b, ident], writes=[psT], inc=(h == H - 1))
            o_T = oT[k]
            mk.op("act", lambda en: en.copy(o_T.ap, psT.ap[:, 0:H, :]), reads=[psT], writes=[o_T])
            yield
            for ci, c0 in enumerate((0, 512)):
                ps = po[k][ci]
                mk.mm(ps.ap, [(o_T.ap[:, h, :], wo.ap[:, h, c0:c0 + 512]) for h in range(H)], reads=[o_T, wo], tr=ps)
                mk.op("dve", lambda en, ps=ps, c0=c0: en.tensor_tensor(xt.ap[:, c0:c0 + 512], xt.ap[:, c0:c0 + 512], ps.ap, ALU.add),
                      reads=[xt, ps], writes=[xt])
                yield
            mk.dma("pool", xo_d.ap[t0:t0 + 128, :], xt.ap, reads=[xt], writes=[xo_d])
        run_pipelined((obody(j) for j in range(nt)), W, stagger=1)
    mk.phase_end()


def stage_hgrn(mk, ztm_d, c_hf, c_hi, c_hg, zfm_d, r_hq, r_hf, y_d, ycol, T, lbl_d, layer, onorm_d, tri_d, ident_d, defer=False):
    H, DK, C = 4, 64, 64
    mk.phase_begin()
    tri = mk.sb([64, 64], F32, "tri")
    mk.dma("sp", tri.ap, tri_d.ap, reads=[tri_d], writes=[tri])
    identf = mk.sb([128, 128], F32, "identf")
    mk.dma("sp", identf.ap, ident_d.ap, reads=[ident_d], writes=[identf])
    gO = mk.sb([64, 64], F32, "gO")
    mk.dma("sp", gO.ap, bcast_rows(onorm_d.ap, 64), reads=[onorm_d], writes=[gO])
    lbB = mk.sb([64, 256], F32, "lbB")
    omlbB = mk.sb([64, 256], F32, "omlbB")
    lbT = mk.sb([64, 4], F32, "lbT")
    omlbT = mk.sb([64, 4], F32, "omlbT")
    if layer == 0:
        mk.op("dve", lambda en: en.memset(lbB.ap, 0.0), writes=[lbB])
        mk.op("dve", lambda en: en.memset(lbT.ap, 0.0), writes=[lbT])
    else:
        l0 = mk.sb([64, 256], F32, "l0")
        l0T = mk.sb([64, 4], F32, "l0T")
        mk.dma("sp", l0.ap, bcast_rows(lbl_d.ap[0, :], 64), reads=[lbl_d], writes=[l0])
        mk.dma("sp", lbB.ap, bcast_rows(lbl_d.ap[1, :], 64), reads=[lbl_d], writes=[lbB])
        mk.dma("sp", l0T.ap, lbl_d.ap[0, :].rearrange("(h d) -> d h", h=4), reads=[lbl_d], writes=[l0T], allow_slow_non_contiguous=True)
        mk.dma("sp", lbT.ap, lbl_d.ap[1, :].rearrange("(h d) -> d h", h=4), reads=[lbl_d], writes=[lbT], allow_slow_non_contiguous=True)
        mk.op("dve", lambda en: en.tensor_tensor(lbB.ap, lbB.ap, l0.ap, ALU.subtract), reads=[lbB, l0], writes=[lbB])
        mk.op("act", lambda en: en.activation(out=lbB.ap, in_=lbB.ap, func=AF.Sigmoid), reads=[lbB], writes=[lbB])
        mk.op("dve", lambda en: en.tensor_tensor(lbT.ap, lbT.ap, l0T.ap, ALU.subtract), reads=[lbT, l0T], writes=[lbT])
        mk.op("act", lambda en: en.activation(out=lbT.ap, in_=lbT.ap, func=AF.Sigmoid), reads=[lbT], writes=[lbT])
    mk.op("dve", lambda en: en.tensor_scalar(omlbB.ap, lbB.ap, -1.0, 1.0, ALU.mult, ALU.add), reads=[lbB], writes=[omlbB])
    mk.op("dve", lambda en: en.tensor_scalar(omlbT.ap, lbT.ap, -1.0, 1.0, ALU.mult, ALU.add), reads=[lbT], writes=[omlbT])
    S = mk.sb([64, H, 64], F32, "S")
    mk.op("dve", lambda en: en.memset(S.ap, 0.0), writes=[S])
    NB = 2 if defer else 3
    def mkb(shape, name, n=NB, dt=F32):
        return [mk.sb(shape, dt, name) for _ in range(n)]
    qz, fz, fzt, vt, gz = mkb([64, H, 64], "qz"), mkb([64, H, 64], "fz"), mkb([64, 256], "fzt"), mkb([64, 256], "vt"), mkb([64, 256], "gz")
    qT, kT, lf = mkb([64, H, 64], "qT"), mkb([64, H, 64], "kT"), mkb([64, 256], "lf")
    bT, d1, e1, e2, e3 = mkb([64, H, 64], "bT"), mkb([64, H, 64], "d1"), mkb([64, H, 64], "e1"), mkb([64, H, 64], "e2"), mkb([64, H, 64], "e3")
    qtl, ktl, qe, kdT, kd = mkb([64, H, 64], "qtl"), mkb([64, H, 64], "ktl"), mkb([64, H, 64], "qe"), mkb([64, H, 64], "kdT"), mkb([64, H, 64], "kd")
    AT, ebl = mkb([64, H, 64], "AT"), mkb([64, 4], "ebl")
    of, scr, st, yo = mkb([64, 256], "of"), mkb([64, 256], "scr"), mkb([64, 12], "st"), mkb([64, 256], "yo")
    NPS = 2 if defer else 8
    PPS = 2 if defer else 4
    pss = [mk.ps([128, 512], F32, "psm") for _ in range(NPS)]
    ipc = [0, 0]
    def body(c):
        i = c % NB
        t0 = c * C
        pset = 0 if defer else c % 2
        def nps():
            p = pss[pset * PPS + ipc[pset] % PPS]
            ipc[pset] += 1
            return p
        mk.dma("sp", qz[i].ap, zfm_d.ap[r_hq:r_hq + 256, t0:t0 + C].rearrange("(h d) t -> d h t", h=H), reads=[zfm_d], writes=[qz[i]])
        mk.dma("sp", fz[i].ap, zfm_d.ap[r_hf:r_hf + 256, t0:t0 + C].rearrange("(h d) t -> d h t", h=H), reads=[zfm_d], writes=[fz[i]])
        mk.dma("sp", fzt[i].ap, ztm_d.ap[t0:t0 + C, c_hf:c_hf + 256], reads=[ztm_d], writes=[fzt[i]])
        mk.dma("sp", vt[i].ap, ztm_d.ap[t0:t0 + C, c_hi:c_hi + 256], reads=[ztm_d], writes=[vt[i]])
        mk.dma("sp", gz[i].ap, ztm_d.ap[t0:t0 + C, c_hg:c_hg + 256], reads=[ztm_d], writes=[gz[i]])
        yield
        exp_silu(mk, qT[i], qT[i].ap, qz[i], qz[i].ap, e1[i], e1[i].ap)
        exp_sigmoid(mk, kT[i], kT[i].ap, fz[i], fz[i].ap, neg=True, eng2="pool")
        mk.op("dve", lambda en, i=i: en.tensor_tensor(kT[i].ap, kT[i].ap, bc_last(omlbT.ap, 64), ALU.mult), reads=[kT[i], omlbT], writes=[kT[i]])
        yield
        exp_sigmoid(mk, lf[i], lf[i].ap, fzt[i], fzt[i].ap, eng2="pool")
        mk.op("dve", lambda en, i=i: en.tensor_tensor(lf[i].ap, lf[i].ap, omlbB.ap, ALU.mult), reads=[lf[i], omlbB], writes=[lf[i]])
        mk.op("dve", lambda en, i=i: en.tensor_tensor(lf[i].ap, lf[i].ap, lbB.ap, ALU.add), reads=[lf[i], lbB], writes=[lf[i]])
        mk.op("dve", lambda en, i=i: en.tensor_scalar(lf[i].ap, lf[i].ap, TINY, None, ALU.max), reads=[lf[i]], writes=[lf[i]])
        mk.op("act", lambda en, i=i: en.activation(out=lf[i].ap, in_=lf[i].ap, func=AF.Ln), reads=[lf[i]], writes=[lf[i]])
        yield
        psb = nps()
        for h in range(H):
            mk.mm(psb.ap[0:64, h * 64:(h + 1) * 64], [(lf[i].ap[:, h * 64:(h + 1) * 64], tri.ap)], reads=[lf[i], tri], tr=psb)
        mk.op("act", lambda en, i=i, psb=psb: en.copy(bT[i].ap, psb.ap[0:64, 0:256].rearrange("p (h t) -> p h t", h=H)), reads=[psb], writes=[bT[i]])
        mk.op("dve", lambda en, i=i: en.tensor_tensor(d1[i].ap, bT[i].ap, bc_last(bT[i].ap[:, :, 31], 64), ALU.subtract), reads=[bT[i]], writes=[d1[i]])
        mk.op("act", lambda en, i=i: en.activation(out=e1[i].ap, in_=d1[i].ap, func=AF.Exp), reads=[d1[i]], writes=[e1[i]])
        mk.op("act", lambda en, i=i: en.activation(out=e2[i].ap, in_=d1[i].ap, func=AF.Exp, scale=-1.0), reads=[d1[i]], writes=[e2[i]])
        mk.op("dve", lambda en, i=i: en.tensor_tensor(qtl[i].ap, qT[i].ap, e1[i].ap, ALU.mult), reads=[qT[i], e1[i]], writes=[qtl[i]])
        mk.op("dve", lambda en, i=i: en.tensor_tensor(ktl[i].ap, kT[i].ap, e2[i].ap, ALU.mult), reads=[kT[i], e2[i]], writes=[ktl[i]])
        mk.op("act", lambda en, i=i: en.activation(out=e1[i].ap, in_=bT[i].ap, func=AF.Exp), reads=[bT[i]], writes=[e1[i]])
        mk.op("dve", lambda en, i=i: en.tensor_tensor(qe[i].ap, qT[i].ap, e1[i].ap, ALU.mult), reads=[qT[i], e1[i]], writes=[qe[i]])
        mk.op("dve", lambda en, i=i: en.tensor_tensor(d1[i].ap, bT[i].ap, bc_last(bT[i].ap[:, :, 63], 64), ALU.subtract), reads=[bT[i]], writes=[d1[i]])
        mk.op("act", lambda en, i=i: en.activation(out=e3[i].ap, in_=d1[i].ap, func=AF.Exp, scale=-1.0), reads=[d1[i]], writes=[e3[i]])
        mk.op("dve", lambda en, i=i: en.tensor_tensor(kdT[i].ap, kT[i].ap, e3[i].ap, ALU.mult), reads=[kT[i], e3[i]], writes=[kdT[i]])
        mk.op("act", lambda en, i=i: en.activation(out=ebl[i].ap, in_=bT[i].ap[:, :, 63], func=AF.Exp), reads=[bT[i]], writes=[ebl[i]])
        yield
        pst = nps()
        for h in range(H):
            mk.op("pe", lambda pe, h=h, i=i, pst=pst: pe.transpose(pst.ap[0:64, h * 64:(h + 1) * 64], kdT[i].ap[:, h, :], identf.ap[0:64, 0:64]),
                  reads=[kdT[i], identf], writes=[pst], inc=(h == H - 1))
        mk.op("act", lambda en, i=i, pst=pst: en.copy(kd[i].ap, pst.ap[0:64, 0:256].rearrange("p (h t) -> p h t", h=H)), reads=[pst], writes=[kd[i]])
        yield
        psa = nps()
        for h in range(H):
            mk.mm(psa.ap[0:64, h * 64:(h + 1) * 64], [(ktl[i].ap[:, h, :], qtl[i].ap[:, h, :])], reads=[ktl[i], qtl[i]], tr=psa)
        mk.op("dve", lambda en, i=i, psa=psa: en.tensor_tensor(AT[i].ap, psa.ap[0:64, 0:256].rearrange("p (h t) -> p h t", h=H), bc_mid(tri.ap, H), ALU.mult),
              reads=[psa, tri], writes=[AT[i]])
        if c > 0:
            yield ("wait", ("S", c - 1))
        pso = nps()
        for h in range(H):
            mk.mm(pso.ap[0:64, h * 64:(h + 1) * 64], [(AT[i].ap[:, h, :], vt[i].ap[:, h * 64:(h + 1) * 64]), (qe[i].ap[:, h, :], S.ap[:, h, :])],
                  reads=[AT[i], vt[i], qe[i], S], tr=pso)
        pss_ = nps()
        for h in range(H):
            mk.mm(pss_.ap[0:64, h * 64:(h + 1) * 64], [(kd[i].ap[:, h, :], vt[i].ap[:, h * 64:(h + 1) * 64])], reads=[kd[i], vt[i]], tr=pss_)
        mk.op("dve", lambda en, i=i: en.tensor_tensor(S.ap, S.ap, bc_last(ebl[i].ap, 64), ALU.mult), reads=[S, ebl[i]], writes=[S])
        mk.op("dve", lambda en, pss_=pss_: en.tensor_tensor(S.ap, S.ap, pss_.ap[0:64, 0:256].rearrange("p (h t) -> p h t", h=H), ALU.add), reads=[S, pss_], writes=[S])
        yield ("done", ("S", c))
        mk.op("act", lambda en, i=i, pso=pso: en.copy(of[i].ap, pso.ap[0:64, 0:256]), reads=[pso], writes=[of[i]])
        head_rmsnorm(mk, of[i], H, 64, gO, yo[i], scr[i], st[i], np_=64)
        exp_sigmoid(mk, gz[i], gz[i].ap, gz[i], gz[i].ap, eng2="pool")
        mk.op("dve", lambda en, i=i: en.tensor_tensor(yo[i].ap, yo[i].ap, gz[i].ap, ALU.mult), reads=[yo[i], gz[i]], writes=[yo[i]])
        mk.dma("pool", y_d.ap[t0:t0 + C, ycol:ycol + 256], yo[i].ap, reads=[yo[i]], writes=[y_d])
    if defer:
        return (body(c) for c in range(T // C)), (lambda: mk.phase_end())
    run_pipelined((body(c) for c in range(T // C)), 2)
    mk.phase_end()


def stage_gdn(mk, ztm_d, c_cz, c_cb, c_ca, zfm_d, r_qkv, y_d, ycol, T, cw_d, alog_d, dtb_d, onorm_d, cst, bg=None):
    H, C = 4, 128
    mk.phase_begin()
    def cload(name):
        b = mk.sb([128, 128], F32, name)
        mk.dma("sp", b.ap, cst[name].ap, reads=[cst[name]], writes=[b])
        return b
    triI, lowS, negtriS, neglowS, id128, ones = (cload(n) for n in ("g_triI", "g_lowS", "g_negtriS", "g_neglowS", "g_id", "g_ones"))
    gO = mk.sb([128, 64], F32, "gO")
    mk.dma("sp", gO.ap, bcast_rows(onorm_d.ap, 128), reads=[onorm_d], writes=[gO])
    cw = mk.sb([64, 12, 4], F32, "cw")
    mk.dma("sp", cw.ap, cw_d.ap.rearrange("(b d) k -> d b k", d=64), reads=[cw_d], writes=[cw])
    negA = mk.sb([128, 4], F32, "negA")
    dtb = mk.sb([128, 4], F32, "dtb")
    mk.dma("sp", negA.ap, bcast_rows(alog_d.ap, 128), reads=[alog_d], writes=[negA])
    mk.dma("sp", dtb.ap, bcast_rows(dtb_d.ap, 128), reads=[dtb_d], writes=[dtb])
    mk.op("act", lambda en: en.activation(out=negA.ap, in_=negA.ap, func=AF.Exp), reads=[negA], writes=[negA])
    mk.op("dve", lambda en: en.tensor_scalar(negA.ap, negA.ap, -1.0, None, ALU.mult), reads=[negA], writes=[negA])
    S = mk.sb([64, H, 64], F32, "S")
    mk.op("dve", lambda en: en.memset(S.ap, 0.0), writes=[S])
    NB = 2
    def mkb(shape, name, n=NB, dt=F32):
        return [mk.sb(list(shape), dt, name) for _ in range(n)]
    FM = [64, H, C]
    TM = [128, H, 64]
    SQ = [128, H, C]
    u = mkb([64, 12, C + 3], "u")
    cz, sc = mkb([128, 256], "cz"), mkb([128, 8], "sc")
    tk = [mkb([64, 12, C], "tk%d" % k) for k in range(2)]
    qkv, sq, rinv = mkb([64, 12, C], "qkv"), mkb([64, 8, C], "sq"), mkb([64, 8, C], "rinv")
    kv_tm = mkb([128, 8, 64], "kvtm")
    beta, g, sp1, sp2 = mkb([128, 4], "beta"), mkb([128, 4], "g"), mkb([128, 4], "sp1"), mkb([128, 4], "sp2")
    rep = mkb([128, 8, 64], "rep")
    eb, bb = mkb(FM, "eb"), mkb(FM, "bb")
    kbT, kbeT, qeT = mkb(FM, "kbT"), mkb(FM, "kbeT"), mkb(FM, "qeT")
    Gt = mkb(SQ, "Gt")
    ET, E, ETi = mkb(SQ, "ET"), mkb(SQ, "E"), mkb(SQ, "ETi")
    NUs, NLs = [mkb(SQ, "NU%d" % j) for j in range(2)], [mkb(SQ, "NL%d" % j) for j in range(2)]
    P = mkb(SQ, "P")
    QKT = mkb(SQ, "QKT")
    bv, rhs, vn, kd, elb = mkb(TM, "bv"), mkb(TM, "rhs"), mkb(TM, "vn"), mkb(TM, "kd"), mkb([128, 4], "elb")
    of, scr, st, yo = mkb([128, 256], "of"), mkb([128, 256], "scr"), mkb([128, 12], "st"), mkb([128, 256], "yo")
    PPS = 3 if bg is not None else 4
    pss = [mk.ps([128, 512], F32, "psm") for _ in range(2 * PPS)]
    ipc = [0, 0]
    psq = lambda p: p.ap[:, 0:H * C].rearrange("p (h t) -> p h t", h=H)
    ptm = lambda p: p.ap[:, 0:H * 64].rearrange("p (h t) -> p h t", h=H)
    pfm = lambda p: p.ap[0:64, 0:H * C].rearrange("p (h t) -> p h t", h=H)

    def body(c):
        i = c % NB
        t0 = c * C
        U = u[i]
        NU = [NUs[0][i], NUs[1][i]]
        NL = [NLs[0][i], NLs[1][i]]
        pset = c % 2
        def nps():
            p = pss[pset * PPS + ipc[pset] % PPS]
            ipc[pset] += 1
            return p
        if c == 0:
            mk.op("pool", lambda en, U=U: en.memset(U.ap[:, :, 0:3], 0.0), writes=[U])
            mk.dma("sp", U.ap[:, :, 3:C + 3], zfm_d.ap[r_qkv:r_qkv + 768, 0:C].rearrange("(b d) t -> d b t", d=64), reads=[zfm_d], writes=[U])
        else:
            mk.dma("sp", U.ap, zfm_d.ap[r_qkv:r_qkv + 768, t0 - 3:t0 + C].rearrange("(b d) t -> d b t", d=64), reads=[zfm_d], writes=[U])
        mk.dma("sp", cz[i].ap, ztm_d.ap[t0:t0 + C, c_cz:c_cz + 256], reads=[ztm_d], writes=[cz[i]])
        mk.dma("sp", sc[i].ap[:, 0:4], ztm_d.ap[t0:t0 + C, c_cb:c_cb + 4], reads=[ztm_d], writes=[sc[i]])
        mk.dma("sp", sc[i].ap[:, 4:8], ztm_d.ap[t0:t0 + C, c_ca:c_ca + 4], reads=[ztm_d], writes=[sc[i]])
        yield
        a0, a1 = tk[0][i], tk[1][i]
        mk.op("dve", lambda en: en.tensor_tensor(a0.ap, U.ap[:, :, 0:C], bc_last(cw.ap[:, :, 0], C), ALU.mult), reads=[U, cw], writes=[a0])
        mk.op("pool", lambda en: en.tensor_tensor(a1.ap, U.ap[:, :, 1:C + 1], bc_last(cw.ap[:, :, 1], C), ALU.mult), reads=[U, cw], writes=[a1])
        mk.op("dve", lambda en: en.tensor_tensor(a0.ap, a0.ap, a1.ap, ALU.add), reads=[a0, a1], writes=[a0])
        mk.op("pool", lambda en: en.tensor_tensor(a1.ap, U.ap[:, :, 2:C + 2], bc_last(cw.ap[:, :, 2], C), ALU.mult), reads=[U, cw], writes=[a1])
        mk.op("dve", lambda en: en.tensor_tensor(a0.ap, a0.ap, a1.ap, ALU.add), reads=[a0, a1], writes=[a0])
        mk.op("pool", lambda en: en.tensor_tensor(a1.ap, U.ap[:, :, 3:C + 3], bc_last(cw.ap[:, :, 3], C), ALU.mult), reads=[U, cw], writes=[a1])
        mk.op("dve", lambda en: en.tensor_tensor(a0.ap, a0.ap, a1.ap, ALU.add), reads=[a0, a1], writes=[a0])
        mk.op("act", lambda en: en.activation(out=qkv[i].ap, in_=a0.ap, func=AF.Silu), reads=[a0], writes=[qkv[i]])
        yield
        mk.op("pool", lambda en: en.tensor_tensor(sq[i].ap, qkv[i].ap[:, 0:8, :], qkv[i].ap[:, 0:8, :], ALU.mult), reads=[qkv[i]], writes=[sq[i]])
        for half in range(2):
            pn = nps()
            mk.mm(pn.ap[0:64, 0:512], [(ones.ap[0:64, 0:64], sq[i].ap[:, half * 4:half * 4 + 4, :].rearrange("p b t -> p (b t)"))], reads=[ones, sq[i]], tr=pn)
            mk.op("act", lambda en, pn=pn, half=half: en.activation(out=rinv[i].ap[:, half * 4:half * 4 + 4, :], in_=pfm(pn), func=AF.Ln, bias=EPS), reads=[pn], writes=[rinv[i]])
        mk.op("act", lambda en: en.activation(out=rinv[i].ap, in_=rinv[i].ap, func=AF.Exp, scale=-0.5), reads=[rinv[i]], writes=[rinv[i]])
        mk.op("dve", lambda en: en.scalar_tensor_tensor(qkv[i].ap[:, 0:4, :], qkv[i].ap[:, 0:4, :], 64 ** -0.5, rinv[i].ap[:, 0:4, :], ALU.mult, ALU.mult),
              reads=[qkv[i], rinv[i]], writes=[qkv[i]])
        mk.op("dve", lambda en: en.tensor_tensor(qkv[i].ap[:, 4:8, :], qkv[i].ap[:, 4:8, :], rinv[i].ap[:, 4:8, :], ALU.mult),
              reads=[qkv[i], rinv[i]], writes=[qkv[i]])
        qT = lambda h: qkv[i].ap[:, h, :]
        kT = lambda h: qkv[i].ap[:, 4 + h, :]
        yield
        pt = nps()
        for b_ in range(8):
            mk.op("pe", lambda pe, b_=b_: pe.transpose(pt.ap[:, b_ * 64:(b_ + 1) * 64], qkv[i].ap[:, 4 + b_, :], id128.ap[0:64, 0:64]),
                  reads=[qkv[i], id128], writes=[pt], inc=(b_ == 7))
        mk.op("act", lambda en: en.copy(kv_tm[i].ap, pt.ap[:, :].rearrange("p (b d) -> p b d", b=8)), reads=[pt], writes=[kv_tm[i]])
        yield
        exp_sigmoid(mk, beta[i], beta[i].ap, sc[i], sc[i].ap[:, 0:4])
        mk.op("dve", lambda en: en.tensor_tensor(sp1[i].ap, sc[i].ap[:, 4:8], dtb.ap, ALU.add), reads=[sc[i], dtb], writes=[sp1[i]])
        mk.op("dve", lambda en: en.tensor_scalar(sp2[i].ap, sp1[i].ap, -1.0, None, ALU.mult), reads=[sp1[i]], writes=[sp2[i]])
        mk.op("dve", lambda en: en.tensor_tensor(sp2[i].ap, sp2[i].ap, sp1[i].ap, ALU.max), reads=[sp1[i], sp2[i]], writes=[sp2[i]])
        mk.op("act", lambda en: en.activation(out=sp2[i].ap, in_=sp2[i].ap, func=AF.Exp, scale=-1.0), reads=[sp2[i]], writes=[sp2[i]])
        mk.op("act", lambda en: en.activation(out=sp2[i].ap, in_=sp2[i].ap, func=AF.Ln, bias=1.0), reads=[sp2[i]], writes=[sp2[i]])
        mk.op("dve", lambda en: en.tensor_scalar(sp1[i].ap, sp1[i].ap, 0.0, None, ALU.max), reads=[sp1[i]], writes=[sp1[i]])
        mk.op("dve", lambda en: en.tensor_tensor(sp1[i].ap, sp1[i].ap, sp2[i].ap, ALU.add), reads=[sp1[i], sp2[i]], writes=[sp1[i]])
        mk.op("dve", lambda en: en.tensor_tensor(g[i].ap, sp1[i].ap, negA.ap, ALU.mult), reads=[sp1[i], negA], writes=[g[i]])
        yield
        mk.op("pool", lambda en: en.tensor_copy(rep[i].ap[:, 0:4, :], bc_last(beta[i].ap, 64)), reads=[beta[i]], writes=[rep[i]])
        mk.op("pool", lambda en: en.tensor_copy(rep[i].ap[:, 4:8, :], bc_last(g[i].ap, 64)), reads=[g[i]], writes=[rep[i]])
        pb1, pb2 = nps(), nps()
        for h in range(H):
            mk.mm(pb1.ap[0:64, h * C:(h + 1) * C], [(rep[i].ap[:, h, :], id128.ap)], reads=[rep[i], id128], tr=pb1)
        for h in range(H):
            mk.mm(pb2.ap[0:64, h * C:(h + 1) * C], [(rep[i].ap[:, 4 + h, :], triI.ap)], reads=[rep[i], triI], tr=pb2)
        mk.op("act", lambda en: en.copy(bb[i].ap, pfm(pb1)), reads=[pb1], writes=[bb[i]])
        mk.op("act", lambda en: en.activation(out=eb[i].ap, in_=pfm(pb2), func=AF.Exp), reads=[pb2], writes=[eb[i]])
        mk.op("dve", lambda en: en.tensor_tensor(kbT[i].ap, qkv[i].ap[:, 4:8, :], bb[i].ap, ALU.mult), reads=[qkv[i], bb[i]], writes=[kbT[i]])
        mk.op("dve", lambda en: en.tensor_tensor(kbeT[i].ap, kbT[i].ap, eb[i].ap, ALU.mult), reads=[kbT[i], eb[i]], writes=[kbeT[i]])
        mk.op("dve", lambda en: en.tensor_tensor(qeT[i].ap, qkv[i].ap[:, 0:4, :], eb[i].ap, ALU.mult), reads=[qkv[i], eb[i]], writes=[qeT[i]])
        yield
        mk.op("pool", lambda en: en.tensor_tensor(Gt[i].ap, bc_mid(triI.ap, H), bc_last(g[i].ap, C), ALU.mult), reads=[triI, g[i]], writes=[Gt[i]])
        pdT, pd = nps(), nps()
        for h in range(H):
            mk.mm(pdT.ap[:, h * C:(h + 1) * C], [(lowS.ap, Gt[i].ap[:, h, :])], reads=[lowS, Gt[i]], tr=pdT)
        for h in range(H):
            mk.mm(pd.ap[:, h * C:(h + 1) * C], [(Gt[i].ap[:, h, :], lowS.ap)], reads=[lowS, Gt[i]], tr=pd)
        mk.op("act", lambda en: en.activation(out=ET[i].ap, in_=psq(pdT), func=AF.Exp), reads=[pdT], writes=[ET[i]])
        mk.op("act", lambda en: en.activation(out=E[i].ap, in_=psq(pd), func=AF.Exp), reads=[pd], writes=[E[i]])
        mk.op("pool", lambda en: en.tensor_tensor(ETi[i].ap, ET[i].ap, bc_mid(triI.ap, H), ALU.mult), reads=[ET[i], triI], writes=[ETi[i]])
        mk.op("dve", lambda en: en.tensor_tensor(ET[i].ap, ET[i].ap, bc_mid(negtriS.ap, H), ALU.mult), reads=[ET[i], negtriS], writes=[ET[i]])
        mk.op("pool", lambda en: en.tensor_tensor(E[i].ap, E[i].ap, bc_mid(neglowS.ap, H), ALU.mult), reads=[E[i], neglowS], writes=[E[i]])
        yield
        pc = nps()
        mk.mm(pc.ap[:, 0:4], [(lowS.ap, g[i].ap)], reads=[lowS, g[i]], tr=pc)
        mk.op("act", lambda en: en.activation(out=elb[i].ap, in_=pc.ap[:, 0:4], func=AF.Exp), reads=[pc], writes=[elb[i]])
        mk.op("dve", lambda en: en.tensor_tensor(kd[i].ap, kv_tm[i].ap[:, 0:4, :], bc_last(elb[i].ap, 64), ALU.mult), reads=[kv_tm[i], elb[i]], writes=[kd[i]])
        mk.op("dve", lambda en: en.tensor_tensor(bv[i].ap, kv_tm[i].ap[:, 4:8, :], bc_last(beta[i].ap, 64), ALU.mult), reads=[kv_tm[i], beta[i]], writes=[bv[i]])
        yield
        pu, pl, pq = nps(), nps(), nps()
        for h in range(H):
            mk.mm(pu.ap[:, h * C:(h + 1) * C], [(kT(h), kbT[i].ap[:, h, :])], reads=[qkv[i], kbT[i]], tr=pu)
        for h in range(H):
            mk.mm(pl.ap[:, h * C:(h + 1) * C], [(kbT[i].ap[:, h, :], kT(h))], reads=[qkv[i], kbT[i]], tr=pl)
        for h in range(H):
            mk.mm(pq.ap[:, h * C:(h + 1) * C], [(kT(h), qT(h))], reads=[qkv[i]], tr=pq)
        mk.op("dve", lambda en: en.tensor_tensor(NU[0].ap, psq(pu), ET[i].ap, ALU.mult), reads=[pu, ET[i]], writes=[NU[0]])
        mk.op("dve", lambda en: en.tensor_tensor(NL[0].ap, psq(pl), E[i].ap, ALU.mult), reads=[pl, E[i]], writes=[NL[0]])
        mk.op("dve", lambda en: en.tensor_tensor(QKT[i].ap, psq(pq), ETi[i].ap, ALU.mult), reads=[pq, ETi[i]], writes=[QKT[i]])
        mk.op("pool", lambda en: en.tensor_tensor(P[i].ap, NU[0].ap, bc_mid(id128.ap, H), ALU.add), reads=[NU[0], id128], writes=[P[i]])
        yield
        cur = 0
        for j in range(1, 7):
            nxt = 1 - cur
            pu, pl = nps(), nps()
            for h in range(H):
                mk.mm(pu.ap[:, h * C:(h + 1) * C], [(NL[cur].ap[:, h, :], NU[cur].ap[:, h, :])], reads=[NL[cur], NU[cur]], tr=pu)
            for h in range(H):
                mk.mm(pl.ap[:, h * C:(h + 1) * C], [(NU[cur].ap[:, h, :], NL[cur].ap[:, h, :])], reads=[NL[cur], NU[cur]], tr=pl)
            mk.op("act", lambda en, nxt=nxt, pu=pu: en.copy(NU[nxt].ap, psq(pu)), reads=[pu], writes=[NU[nxt]])
            mk.op("dve", lambda en, nxt=nxt, pl=pl: en.tensor_copy(NL[nxt].ap, psq(pl)), reads=[pl], writes=[NL[nxt]])
            pp = nps()
            for h in range(H):
                mk.mm(pp.ap[:, h * C:(h + 1) * C], [(NL[nxt].ap[:, h, :], P[i].ap[:, h, :])], reads=[NL[nxt], P[i]], tr=pp)
            mk.op("dve", lambda en, pp=pp: en.tensor_tensor(P[i].ap, P[i].ap, psq(pp), ALU.add), reads=[P[i], pp], writes=[P[i]])
            cur = nxt
            yield
        if c > 0:
            yield ("wait", ("S", c - 1))
        p1 = nps()
        for h in range(H):
            mk.mm(p1.ap[:, h * 64:(h + 1) * 64], [(kbeT[i].ap[:, h, :], S.ap[:, h, :])], reads=[kbeT[i], S], tr=p1)
        mk.op("dve", lambda en: en.tensor_tensor(rhs[i].ap, bv[i].ap, ptm(p1), ALU.subtract), reads=[bv[i], p1], writes=[rhs[i]])
        p2 = nps()
        for h in range(H):
            mk.mm(p2.ap[:, h * 64:(h + 1) * 64], [(P[i].ap[:, h, :], rhs[i].ap[:, h, :])], reads=[P[i], rhs[i]], tr=p2)
        mk.op("act", lambda en: en.copy(vn[i].ap, ptm(p2)), reads=[p2], writes=[vn[i]])
        po, p4 = nps(), nps()
        for h in range(H):
            mk.mm(po.ap[:, h * 64:(h + 1) * 64], [(qeT[i].ap[:, h, :], S.ap[:, h, :]), (QKT[i].ap[:, h, :], vn[i].ap[:, h, :])],
                  reads=[qeT[i], S, QKT[i], vn[i]], tr=po)
        for h in range(H):
            mk.mm(p4.ap[0:64, h * 64:(h + 1) * 64], [(kd[i].ap[:, h, :], vn[i].ap[:, h, :])], reads=[kd[i], vn[i]], tr=p4)
        mk.op("dve", lambda en: en.tensor_tensor(S.ap, S.ap, bc_last(eb[i].ap[:, :, C - 1], 64), ALU.mult), reads=[S, eb[i]], writes=[S])
        mk.op("dve", lambda en: en.tensor_tensor(S.ap, S.ap, p4.ap[0:64, 0:256].rearrange("p (h t) -> p h t", h=H), ALU.add), reads=[S, p4], writes=[S])
        yield ("done", ("S", c))
        mk.op("act", lambda en: en.copy(of[i].ap, po.ap[:, 0:256]), reads=[po], writes=[of[i]])
        head_rmsnorm(mk, of[i], H, 64, gO, yo[i], scr[i], st[i])
        exp_silu(mk, cz[i], cz[i].ap, cz[i], cz[i].ap, scr[i], scr[i].ap)
        mk.op("dve", lambda en: en.tensor_tensor(yo[i].ap, yo[i].ap, cz[i].ap, ALU.mult), reads=[yo[i], cz[i]], writes=[yo[i]])
        mk.dma("pool", y_d.ap[t0:t0 + C, ycol:ycol + 256], yo[i].ap, reads=[yo[i]], writes=[y_d])
    if bg is not None:
        run_pipelined_multi([{"gens": (body(c) for c in range(T // C)), "width": 2}, {"gens": bg, "width": 1, "period": 2}])
    else:
        run_pipelined((body(c) for c in range(T // C)), 2)
    mk.phase_end()


def gdn_consts():
    a = np.arange(64)
    triI = (a[:, None] <= a[None, :]).astype(np.float32)
    b = np.arange(128)
    gI = (b[:, None] <= b[None, :]).astype(np.float32)
    gS = (b[:, None] < b[None, :]).astype(np.float32)
    gL = (b[:, None] > b[None, :]).astype(np.float32)
    return {"triI": triI, "g_triI": gI, "g_lowS": gL, "g_negtriS": -gS, "g_neglowS": -gL,
            "g_id": np.eye(128, dtype=np.float32), "g_ones": np.ones((128, 128), np.float32)}


def nsa_consts(T):
    t = np.arange(T)
    inv = 1.0 / (10000.0 ** (np.arange(0, 64, 2, dtype=np.float32) / 64))
    ang = t[:, None].astype(np.float32) * inv[None, :].astype(np.float32)
    ncp = T // 16
    n = np.arange(ncp)
    ncmp = (T - 32) // 16 + 1
    cm = ((16 * n[None, :] + 31 <= t[:, None]) & (n[None, :] < ncmp)).astype(np.float32)
    ns = T // 64
    j = np.arange(ns)[None, :]
    cur = (t // 64)[:, None]
    valid = j <= cur
    forced = (j == 0) | (j == cur) | (j == cur - 1)
    m1 = (valid & ~forced).astype(np.float32)
    c2 = (1e6 * (valid & forced) - 1.0 * (~valid)).astype(np.float32)
    a = np.arange(128)
    return {"cos": np.cos(ang).astype(np.float32), "sin": np.sin(ang).astype(np.float32),
            "cmpm": cm, "cmpmT": np.ascontiguousarray(cm.T), "selm1": m1, "selc2": c2,
            "causT": (a[:, None] <= a[None, :]).astype(np.float32), "farT": (a[:, None] > a[None, :]).astype(np.float32),
            "esel": (np.arange(ns)[:, None] == (t // 64)[None, :]).astype(np.float32)}


def nsa_n1(mk, ztm_d, c_nq, c_nk, c_nv, T, qn_d, kn_d, cst, ident_d, QT_d, KT_d, VA_d, defer=False):
    NT = T // 128
    mk.phase_begin()
    ident, identf = load_consts_ident(mk, ident_d)
    gqk = mk.sb([128, 14, 64], F32, "gqk")
    mk.dma("sp", gqk.ap[:, 0:8, :], bc_mid(bcast_rows(qn_d.ap, 128), 8), reads=[qn_d], writes=[gqk])
    for ty in range(3):
        mk.dma("sp", gqk.ap[:, 8 + 2 * ty:10 + 2 * ty, :], bc_mid(bcast_rows(kn_d.ap[ty, :], 128), 2), reads=[kn_d], writes=[gqk])
    mk.op("dve", lambda en: en.tensor_scalar(gqk.ap[:, 0:8, :], gqk.ap[:, 0:8, :], 64 ** -0.5, None, ALU.mult), reads=[gqk], writes=[gqk])
    NB = 2 if defer else 3
    xin = [mk.sb([128, 14 * 64], F32, "xin") for _ in range(NB)]
    vin = [mk.sb([128, 256], F32, "vin") for _ in range(NB)]
    cs = [mk.sb([128, 64], F32, "cs") for _ in range(NB)]
    sq = [mk.sb([128, 14 * 64], F32, "sq") for _ in range(NB)]
    st = [mk.sb([128, 42], F32, "st") for _ in range(NB)]
    xn = [mk.sb([128, 14, 64], F32, "xn") for _ in range(NB)]
    r1 = [mk.sb([128, 14, 32], F32, "r1") for _ in range(NB)]
    r2 = [mk.sb([128, 14, 32], F32, "r2") for _ in range(NB)]
    xr = [mk.sb([128, 14, 64], BF16, "xr") for _ in range(NB)]
    xT = [mk.sb([64, 14, 128], BF16, "xT") for _ in range(NB)]
    va = [mk.sb([128, 4, 65], BF16, "va") for _ in range(NB)]
    NPS_ = 1 if defer else NB
    psA = [mk.ps([64, 7, 128], BF16, "psA") for _ in range(NPS_)]
    psB = [mk.ps([64, 7, 128], BF16, "psB") for _ in range(NPS_)]
    def n1body(tt):
        i = tt % NB
        t0 = tt * 128
        X = xin[i]
        mk.dma("sp", X.ap[:, 0:512], ztm_d.ap[t0:t0 + 128, c_nq:c_nq + 512], reads=[ztm_d], writes=[X])
        mk.dma("sp", X.ap[:, 512:896], ztm_d.ap[t0:t0 + 128, c_nk:c_nk + 384], reads=[ztm_d], writes=[X])
        mk.dma("sp", vin[i].ap, ztm_d.ap[t0:t0 + 128, c_nv:c_nv + 256], reads=[ztm_d], writes=[vin[i]])
        mk.dma("sp", cs[i].ap[:, 0:32], cst["cos"].ap[t0:t0 + 128, :], reads=[cst["cos"]], writes=[cs[i]])
        mk.dma("sp", cs[i].ap[:, 32:64], cst["sin"].ap[t0:t0 + 128, :], reads=[cst["sin"]], writes=[cs[i]])
        yield
        X3 = X.ap[:, :].rearrange("p (h d) -> p h d", h=14)
        S3 = sq[i].ap[:, :].rearrange("p (h d) -> p h d", h=14)
        mk.op("act", lambda en, i=i, X=X: en.activation(out=sq[i].ap, in_=X.ap, func=AF.Square), reads=[X], writes=[sq[i]])
        mk.op("dve", lambda en, i=i, S3=S3: en.tensor_reduce(st[i].ap[:, 0:14], S3, AX.X, ALU.add), reads=[sq[i]], writes=[st[i]])
        mk.op("act", lambda en, i=i: en.activation(out=st[i].ap[:, 14:28], in_=st[i].ap[:, 0:14], func=AF.Ln, scale=1.0 / 64, bias=EPS),
              reads=[st[i]], writes=[st[i]])
        mk.op("act", lambda en, i=i: en.activation(out=st[i].ap[:, 28:42], in_=st[i].ap[:, 14:28], func=AF.Exp, scale=-0.5), reads=[st[i]], writes=[st[i]])
        mk.op("dve", lambda en, i=i, X3=X3: en.tensor_tensor(xn[i].ap, X3, bc_last(st[i].ap[:, 28:42], 64), ALU.mult), reads=[X, st[i]], writes=[xn[i]])
        mk.op("pool", lambda en, i=i: en.tensor_tensor(xn[i].ap, xn[i].ap, gqk.ap, ALU.mult), reads=[xn[i], gqk], writes=[xn[i]])
        yield
        cb_ = bc_mid(cs[i].ap[:, 0:32], 14)
        sb_ = bc_mid(cs[i].ap[:, 32:64], 14)
        x1 = xn[i].ap[:, :, 0:32]
        x2 = xn[i].ap[:, :, 32:64]
        mk.op("dve", lambda en, i=i, x1=x1, cb_=cb_: en.tensor_tensor(r1[i].ap, x1, cb_, ALU.mult), reads=[xn[i], cs[i]], writes=[r1[i]])
        mk.op("pool", lambda en, i=i, x2=x2, sb_=sb_: en.tensor_tensor(r2[i].ap, x2, sb_, ALU.mult), reads=[xn[i], cs[i]], writes=[r2[i]])
        mk.op("dve", lambda en, i=i: en.tensor_tensor(xr[i].ap[:, :, 0:32], r1[i].ap, r2[i].ap, ALU.subtract), reads=[r1[i], r2[i]], writes=[xr[i]])
        mk.op("dve", lambda en, i=i, x2=x2, cb_=cb_: en.tensor_tensor(r1[i].ap, x2, cb_, ALU.mult), reads=[xn[i], cs[i]], writes=[r1[i]])
        mk.op("pool", lambda en, i=i, x1=x1, sb_=sb_: en.tensor_tensor(r2[i].ap, x1, sb_, ALU.mult), reads=[xn[i], cs[i]], writes=[r2[i]])
        mk.op("dve", lambda en, i=i: en.tensor_tensor(xr[i].ap[:, :, 32:64], r1[i].ap, r2[i].ap, ALU.add), reads=[r1[i], r2[i]], writes=[xr[i]])
        yield
        for half, psx in ((0, psA[i % NPS_]), (1, psB[i % NPS_])):
            for hh in range(7):
                h = half * 7 + hh
                mk.op("pe", lambda pe, h=h, hh=hh, i=i, psx=psx: pe.transpose(psx.ap[:, hh, :], xr[i].ap[:, h, :], ident.ap),
                      reads=[xr[i], ident], writes=[psx], inc=(hh == 6))
            if half == 0:
                mk.op("act", lambda en, i=i, psx=psx: en.copy(xT[i].ap[:, 0:7, :], psx.ap), reads=[psx], writes=[xT[i]])
            else:
                mk.op("dve", lambda en, i=i, psx=psx: en.tensor_copy(xT[i].ap[:, 7:14, :], psx.ap), reads=[psx], writes=[xT[i]])
        yield
        mk.dma("pool", QT_d.ap[:, :, t0:t0 + 128], xT[i].ap[:, 0:8, :], reads=[xT[i]], writes=[QT_d])
        mk.dma("pool", KT_d.ap[:, :, t0:t0 + 128], xT[i].ap[:, 8:14, :], reads=[xT[i]], writes=[KT_d])
        mk.op("pool", lambda en, i=i: en.memset(va[i].ap[:, :, 64:65], 1.0), writes=[va[i]])
        mk.op("act", lambda en, i=i: en.copy(va[i].ap[:, :, 0:64], vin[i].ap[:, :].rearrange("p (h d) -> p h d", h=4)), reads=[vin[i]], writes=[va[i]])
        mk.dma("pool", VA_d.ap[t0:t0 + 128, :, :], va[i].ap, reads=[va[i]], writes=[VA_d])
    if defer:
        return (n1body(tt) for tt in range(NT)), (lambda: mk.phase_end())
    run_pipelined((n1body(tt) for tt in range(NT)), NB, stagger=2)
    mk.phase_end()
    return None, None


def stage_nsa(mk, ztm_d, c_nq, c_nk, c_nv, c_ng, zfm_d, r_vc, y_d, T, qn_d, kn_d, posk_d, posv_d, w1k_d, w2k_d, w1v_d, w2v_d,
              cst, ident_d, QT_d, KT_d, VA_d, bg=None, skip_n1=False):
    NT = T // 128
    ncp = T // 16
    ncmp = (T - 32) // 16 + 1
    ns = T // 64
    mk.phase_begin()
    ident, identf = load_consts_ident(mk, ident_d)
    if not skip_n1:
        mk.mark("nsa_N1")
        nsa_n1(mk, ztm_d, c_nq, c_nk, c_nv, T, qn_d, kn_d, cst, ident_d, QT_d, KT_d, VA_d)
    mk.mark("nsa_res")
    KsT = mk.sb([128, 2, T], BF16, "KsT")
    KwT = mk.sb([128, 2, T], BF16, "KwT")
    mk.op("pool", lambda en: en.memset(KsT.ap[64:128, :, :], 0.0), writes=[KsT])
    mk.op("pool", lambda en: en.memset(KwT.ap[64:128, :, :], 0.0), writes=[KwT])
    mk.dma("sp", KsT.ap[0:64, :, :], KT_d.ap[:, 2:4, :], reads=[KT_d], writes=[KsT])
    mk.dma("sp", KwT.ap[0:64, :, :], KT_d.ap[:, 4:6, :], reads=[KT_d], writes=[KwT])
    VA = mk.sb([128, NT, 4, 65], BF16, "VA")
    mk.dma("sp", VA.ap, VA_d.ap.rearrange("(n p) f e -> p n f e", p=128), reads=[VA_d], writes=[VA])
    kcT = mk.sb([128, 2, ncp], BF16, "kcT")
    vcA = mk.sb([128, 2, ncp // 128, 65], BF16, "vcA")
    mk.op("pool", lambda en: en.memset(kcT.ap, 0.0), writes=[kcT])
    mk.op("pool", lambda en: en.memset(vcA.ap, 0.0), writes=[vcA])
    mk.op("pool", lambda en: en.memset(vcA.ap[:, :, :, 64:65], 1.0), writes=[vcA])
    mk.mark("nsa_N2")
    mk.phase_begin()
    pss = [mk.ps([128, 512], F32, "psm") for _ in range(4)]
    for which, (w1_d, w2_d, pos_d) in enumerate(((w1k_d, w2k_d, posk_d), (w1v_d, w2v_d, posv_d))):
        if which == 1:
            mk.phase_end()
        mk.phase_begin()
        w1f = mk.sb([64, 32, 128], F32, "w1f")
        w1 = mk.sb([64, 32, 128], BF16, "w1")
        mk.dma("sp", w1f.ap, w1_d.ap.rearrange("(l d) h -> d l h", d=64), reads=[w1_d], writes=[w1f])
        mk.op("pool", lambda en, w1=w1, w1f=w1f: en.tensor_copy(w1.ap, w1f.ap), reads=[w1f], writes=[w1])
        w2f = mk.sb([128, 64], F32, "w2f")
        w2 = mk.sb([128, 64], BF16, "w2")
        mk.dma("sp", w2f.ap, w2_d.ap, reads=[w2_d], writes=[w2f])
        mk.op("pool", lambda en, w2=w2, w2f=w2f: en.tensor_copy(w2.ap, w2f.ap), reads=[w2f], writes=[w2])
        posf = mk.sb([64, 32], F32, "posf")
        posb = mk.sb([64, 32], BF16, "posb")
        mk.dma("sp", posf.ap, pos_d.ap.rearrange("l d -> d l"), reads=[pos_d], writes=[posf], allow_slow_non_contiguous=True)
        mk.op("pool", lambda en, posb=posb, posf=posf: en.tensor_copy(posb.ap, posf.ap), reads=[posf], writes=[posb])
        bias = mk.sb([128, 1], F32, "bias")
        pbias = pss[0]
        mk.mm(pbias.ap[:, 0:1], [(w1.ap[:, l, :], posb.ap[:, l:l + 1]) for l in range(32)], reads=[w1, posb], tr=pbias)
        mk.op("act", lambda en, bias=bias, pbias=pbias: en.copy(bias.ap, pbias.ap[:, 0:1]), reads=[pbias], writes=[bias])
        for g in range(2):
            XT = mk.sb([64, T], BF16, "XT")
            if which == 0:
                mk.dma("sp", XT.ap, KT_d.ap[:, g, :], reads=[KT_d], writes=[XT])
            else:
                XTf = mk.sb([64, T], F32, "XTf")
                mk.dma("sp", XTf.ap, zfm_d.ap[r_vc + g * 64:r_vc + (g + 1) * 64, :], reads=[zfm_d], writes=[XTf])
                mk.op("pool", lambda en, XT=XT, XTf=XTf: en.tensor_copy(XT.ap, XTf.ap), reads=[XTf], writes=[XT])
            ph = pss[1 + g]
            mk.mm(ph.ap[:, 0:ncmp], [(w1.ap[:, l, :], XT.ap[:, l:l + 16 * (ncmp - 1) + 1:16]) for l in range(32)], reads=[w1, XT], tr=ph)
            xs = mk.sb([128, ncp], F32, "xs")
            x2 = mk.sb([128, ncp], F32, "x2")
            ge = mk.sb([128, ncp], BF16, "ge")
            mk.op("pool", lambda en, ge=ge: en.memset(ge.ap, 0.0), writes=[ge])
            mk.op("act", lambda en, xs=xs, ph=ph, bias=bias: en.activation(out=xs.ap[:, 0:ncmp], in_=ph.ap[:, 0:ncmp], func=AF.Identity, bias=bias.ap[:, 0:1]),
                  reads=[ph, bias], writes=[xs])
            mk.op("pool", lambda en, xs=xs, x2=x2: en.tensor_tensor(x2.ap[:, 0:ncmp], xs.ap[:, 0:ncmp], xs.ap[:, 0:ncmp], ALU.mult), reads=[xs], writes=[x2])
            mk.op("dve", lambda en, x2=x2: en.tensor_scalar(x2.ap[:, 0:ncmp], x2.ap[:, 0:ncmp], 0.044715, 1.0, ALU.mult, ALU.add), reads=[x2], writes=[x2])
            mk.op("dve", lambda en, xs=xs, x2=x2: en.tensor_tensor(x2.ap[:, 0:ncmp], x2.ap[:, 0:ncmp], xs.ap[:, 0:ncmp], ALU.mult), reads=[xs, x2], writes=[x2])
            mk.op("act", lambda en, x2=x2: en.activation(out=x2.ap[:, 0:ncmp], in_=x2.ap[:, 0:ncmp], func=AF.Sigmoid, scale=1.5957691216057308),
                  reads=[x2], writes=[x2])
            mk.op("dve", lambda en, xs=xs, x2=x2, ge=ge: en.tensor_tensor(ge.ap[:, 0:ncmp], x2.ap[:, 0:ncmp], xs.ap[:, 0:ncmp], ALU.mult), reads=[xs, x2], writes=[ge])
            po = pss[3]
            if which == 0:
                mk.mm(po.ap[0:64, 0:ncmp], [(w2.ap, ge.ap[:, 0:ncmp])], reads=[w2, ge], tr=po)
                mk.op("act", lambda en, g=g, po=po: en.copy(kcT.ap[0:64, g, 0:ncmp], po.ap[0:64, 0:ncmp]), reads=[po], writes=[kcT])
            else:
                for nt in range(ncp // 128):
                    mk.mm(po.ap[:, nt * 64:(nt + 1) * 64], [(ge.ap[:, nt * 128:(nt + 1) * 128], w2.ap)], reads=[w2, ge], tr=po)
                mk.op("act", lambda en, g=g, po=po: en.copy(vcA.ap[:, g, :, 0:64], po.ap[:, 0:(ncp // 128) * 64].rearrange("p (n d) -> p n d", d=64)),
                      reads=[po], writes=[vcA])
    mk.phase_end()
    mk.phase_end()
    mk.mark("nsa_N3")
    mk.phase_begin()
    causT = mk.sb([128, 128], F32, "causT")
    farT = mk.sb([128, 128], F32, "farT")
    mk.dma("sp", causT.ap, cst["causT"].ap, reads=[cst["causT"]], writes=[causT])
    mk.dma("sp", farT.ap, cst["farT"].ap, reads=[cst["farT"]], writes=[farT])
    esel = mk.sb([ns, T], BF16, "esel")
    mk.phase_begin()
    eself = mk.sb([ns, T], F32, "eself")
    mk.dma("sp", eself.ap, cst["esel"].ap, reads=[cst["esel"]], writes=[eself])
    mk.op("pool", lambda en: en.tensor_copy(esel.ap, eself.ap), reads=[eself], writes=[esel])
    mk.phase_end()
    NQ = ncp // 128
    NS = 2
    psS = [mk.ps([128, 512], F32, "psS") for _ in range(NS)]
    psO = [mk.ps([128, 512], F32, "psO") for _ in range(NS)]
    psT = [[mk.ps([128, 512], F32, "psST") for _ in range(1 if bg is not None else 2)] for _ in range(NS)]
    psM = psS
    mk.ndram = getattr(mk, "ndram", 0) + 1
    selTd = [mk.dram("selTd%d_%d" % (mk.ndram, j), [ns, 128], BF16) for j in range(NS)]
    def mkb(shape, name, dt=F32, n=NS):
        return [mk.sb(list(shape), dt, name) for _ in range(n)]
    QT = mkb([128, 4, 128], "QT", BF16)
    for q_ in QT:
        mk.op("pool", lambda en, q_=q_: en.memset(q_.ap[64:128, :, :], 0.0), writes=[q_])
    gl = mkb([128, 24], "gl")
    cm, cmT = mkb([128, ncp], "cm"), mkb([128, NQ, 128], "cmT")
    m1t, c2t = mkb([128, ns], "m1t"), mkb([128, ns], "c2t")
    sS = mkb([128, 4, ncp], "sS")
    sE = sS
    stt = mkb([128, 16], "stt")
    ph = mkb([128, ncp + 1], "ph")
    imp, wk = mkb([128, ns], "imp"), mkb([128, ns], "wk")
    m8 = mkb([128, 16], "m8")
    selb = mkb([128, ns], "selb")
    selT = mkb([ns, 128], "selT", BF16)
    eT = [mkb([128, 4, 128], "eT", BF16, 3) for _ in range(NS)]
    causB = mk.sb([128, 128], BF16, "causB")
    farB = mk.sb([128, 128], BF16, "farB")
    mk.op("dve", lambda en: en.tensor_copy(causB.ap, causT.ap), reads=[causT], writes=[causB])
    mk.op("dve", lambda en: en.tensor_copy(farB.ap, farT.ap), reads=[farT], writes=[farB])
    cmTb = mkb([128, NQ, 128], "cmTb", BF16)
    bm = [mkb([128, 128], "bm", BF16, 2) for _ in range(NS)]
    rr = mkb([128, 12], "rr")
    acc = mkb([128, 4, 64], "acc")
    tmpo = mkb([128, 4, 64], "tmpo")
    PTall = mkb([128, NT, 4, 128], "PTall", BF16)

    def attn_branch(i, tiles, Ops):
        PA = PTall[i]
        QTi = QT[i]
        n = len(tiles)
        for idx, (KT_ap, V_ap, mask_fn) in enumerate(tiles):
            pt = psT[i][idx % len(psT[i])]
            mk.mm(pt.ap, [(KT_ap, QTi.ap[:, :, :].rearrange("p h q -> p (h q)"))], reads=[QTi, KsT, KwT, kcT], tr=pt)
            e = eT[i][idx % 3]
            p4 = pt.ap[:, :].rearrange("p (h q) -> p h q", h=4)
            msk = mask_fn(idx)
            if msk is None:
                mk.op("act", lambda en, idx=idx, p4=p4: en.activation(out=PA.ap[:, idx, :, :], in_=p4, func=AF.Exp), reads=[pt], writes=[PA])
            else:
                mk.op("act", lambda en, e=e, p4=p4: en.activation(out=e.ap, in_=p4, func=AF.Exp), reads=[pt], writes=[e])
                mb, mreads = msk
                mk.op("dve", lambda en, idx=idx, e=e, mb=mb: en.tensor_tensor(PA.ap[:, idx, :, :], e.ap, bc_mid(mb, 4), ALU.mult), reads=[e] + mreads, writes=[PA])
            yield
        for h in range(4):
            for idx, (KT_ap, V_ap, mask_fn) in enumerate(tiles):
                mk.op("pe", lambda pe, h=h, idx=idx, V_ap=V_ap: pe.matmul(Ops.ap[:, h * 65:(h + 1) * 65], PA.ap[:, idx, h, :], V_ap, start=(idx == 0), stop=(idx == n - 1)),
                      reads=[PA, VA, vcA], writes=[Ops], inc=(idx == n - 1))
            yield

    def combine(i, g, x, first):
        Ops = psO[i]
        O3 = Ops.ap[:, 0:260].rearrange("p (h e) -> p h e", e=65)
        mk.op("dve", lambda en: en.tensor_scalar(rr[i].ap[:, 4 * x:4 * x + 4], O3[:, :, 64], 1e-30, None, ALU.max), reads=[Ops], writes=[rr[i]])
        mk.op("dve", lambda en: en.reciprocal(rr[i].ap[:, 4 * x:4 * x + 4], rr[i].ap[:, 4 * x:4 * x + 4]), reads=[rr[i]], writes=[rr[i]])
        gx = gl[i].ap[:, g * 12:(g + 1) * 12].rearrange("p (h x) -> p h x", x=3)[:, :, x]
        mk.op("dve", lambda en: en.tensor_tensor(rr[i].ap[:, 4 * x:4 * x + 4], rr[i].ap[:, 4 * x:4 * x + 4], gx, ALU.mult), reads=[rr[i], gl[i]], writes=[rr[i]])
        if first:
            mk.op("dve", lambda en: en.tensor_tensor(acc[i].ap, O3[:, :, 0:64], bc_last(rr[i].ap[:, 4 * x:4 * x + 4], 64), ALU.mult), reads=[Ops, rr[i]], writes=[acc[i]])
        else:
            mk.op("dve", lambda en: en.tensor_tensor(tmpo[i].ap, O3[:, :, 0:64], bc_last(rr[i].ap[:, 4 * x:4 * x + 4], 64), ALU.mult), reads=[Ops, rr[i]], writes=[tmpo[i]])
            mk.op("pool", lambda en: en.tensor_tensor(acc[i].ap, acc[i].ap, tmpo[i].ap, ALU.add), reads=[acc[i], tmpo[i]], writes=[acc[i]])

    def body(it, g, qt):
        i = it % NS
        t0 = qt * 128
        mk.dma("sp", QT[i].ap[0:64, :, :], QT_d.ap[:, 4 * g:4 * g + 4, t0:t0 + 128], reads=[QT_d], writes=[QT[i]])
        mk.dma("sp", gl[i].ap, ztm_d.ap[t0:t0 + 128, c_ng:c_ng + 24], reads=[ztm_d], writes=[gl[i]])
        mk.dma("sp", cm[i].ap, cst["cmpm"].ap[t0:t0 + 128, :], reads=[cst["cmpm"]], writes=[cm[i]])
        mk.dma("sp", cmT[i].ap, cst["cmpmT"].ap[:, t0:t0 + 128].rearrange("(n p) q -> p n q", p=128), reads=[cst["cmpmT"]], writes=[cmT[i]])
        mk.dma("sp", m1t[i].ap, cst["selm1"].ap[t0:t0 + 128, :], reads=[cst["selm1"]], writes=[m1t[i]])
        mk.dma("sp", c2t[i].ap, cst["selc2"].ap[t0:t0 + 128, :], reads=[cst["selc2"]], writes=[c2t[i]])
        exp_sigmoid(mk, gl[i], gl[i].ap, gl[i], gl[i].ap)
        mk.op("pool", lambda en: en.tensor_copy(cmTb[i].ap, cmT[i].ap), reads=[cmT[i]], writes=[cmTb[i]])
        yield
        tl = []
        for kt in range(max(0, qt - 4), qt + 1):
            if kt == qt:
                mf = lambda k: (causB.ap, [causB])
            elif kt == qt - 4:
                mf = lambda k: (farB.ap, [farB])
            else:
                mf = lambda k: None
            tl.append((KwT.ap[:, g, kt * 128:(kt + 1) * 128], VA.ap[:, kt, 2 + g, :], mf))
        yield from attn_branch(i, tl, psO[i])
        combine(i, g, 2, True)
        yield
        for hp in range(2):
            for hh in range(2):
                mk.mm(psS[i].ap[:, hh * ncp:(hh + 1) * ncp], [(QT[i].ap[:, hp * 2 + hh, :], kcT.ap[:, g, :])], reads=[QT[i], kcT], tr=psS[i])
            mk.op("act", lambda en, hp=hp: en.copy(sS[i].ap[:, hp * 2:hp * 2 + 2, :], psS[i].ap[:, 0:2 * ncp].rearrange("p (h n) -> p h n", h=2)),
                  reads=[psS[i]], writes=[sS[i]])
        yield
        mk.op("dve", lambda en: en.tensor_reduce(stt[i].ap[:, 0:4], sS[i].ap, AX.X, ALU.max), reads=[sS[i]], writes=[stt[i]])
        mk.op("dve", lambda en: en.tensor_tensor(sS[i].ap, sS[i].ap, bc_last(stt[i].ap[:, 0:4], ncp), ALU.subtract), reads=[sS[i], stt[i]], writes=[sS[i]])
        mk.op("act", lambda en: en.activation(out=sE[i].ap, in_=sS[i].ap, func=AF.Exp), reads=[sS[i]], writes=[sE[i]])
        mk.op("pool", lambda en: en.tensor_tensor(sE[i].ap, sE[i].ap, bc_mid(cm[i].ap, 4), ALU.mult), reads=[sE[i], cm[i]], writes=[sE[i]])
        yield
        mk.op("dve", lambda en: en.tensor_reduce(stt[i].ap[:, 4:8], sE[i].ap, AX.X, ALU.add), reads=[sE[i]], writes=[stt[i]])
        mk.op("dve", lambda en: en.tensor_scalar(stt[i].ap[:, 4:8], stt[i].ap[:, 4:8], 1e-30, None, ALU.max), reads=[stt[i]], writes=[stt[i]])
        mk.op("dve", lambda en: en.reciprocal(stt[i].ap[:, 8:12], stt[i].ap[:, 4:8]), reads=[stt[i]], writes=[stt[i]])
        mk.op("dve", lambda en: en.tensor_tensor(sE[i].ap, sE[i].ap, bc_last(stt[i].ap[:, 8:12], ncp), ALU.mult), reads=[sE[i], stt[i]], writes=[sE[i]])
        yield
        mk.op("pool", lambda en: en.memset(ph[i].ap[:, 0:1], 0.0), writes=[ph[i]])
        mk.op("dve", lambda en: en.tensor_reduce(ph[i].ap[:, 1:ncp + 1], sE[i].ap[:, :, :].rearrange("p h n -> p n h"), AX.X, ALU.add),
              reads=[sE[i]], writes=[ph[i]])
        mk.op("dve", lambda en: en.tensor_reduce(imp[i].ap, ph[i].ap[:, 1:ncp + 1].rearrange("p (j f) -> p j f", f=4), AX.X, ALU.add),
              reads=[ph[i]], writes=[imp[i]])
        mk.op("dve", lambda en: en.tensor_reduce(wk[i].ap, ph[i].ap[:, 0:ncp].rearrange("p (j f) -> p j f", f=4), AX.X, ALU.add),
              reads=[ph[i]], writes=[wk[i]])
        yield
        mk.op("dve", lambda en: en.tensor_tensor(imp[i].ap, imp[i].ap, wk[i].ap, ALU.add), reads=[imp[i], wk[i]], writes=[imp[i]])
        mk.op("dve", lambda en: en.scalar_tensor_tensor(imp[i].ap, imp[i].ap, 16.0, m1t[i].ap, ALU.mult, ALU.mult), reads=[imp[i], m1t[i]], writes=[imp[i]])
        mk.op("dve", lambda en: en.tensor_tensor(imp[i].ap, imp[i].ap, c2t[i].ap, ALU.add), reads=[imp[i], c2t[i]], writes=[imp[i]])
        yield
        mk.op("dve", lambda en: en.max(out=m8[i].ap[:, 0:8], in_=imp[i].ap), reads=[imp[i]], writes=[m8[i]])
        mk.op("dve", lambda en: en.match_replace(out=wk[i].ap, in_to_replace=m8[i].ap[:, 0:8], in_values=imp[i].ap, imm_value=-2.0),
              reads=[imp[i], m8[i]], writes=[wk[i]])
        mk.op("dve", lambda en: en.max(out=m8[i].ap[:, 8:16], in_=wk[i].ap), reads=[wk[i]], writes=[m8[i]])
        yield
        mk.op("dve", lambda en: en.tensor_reduce(stt[i].ap[:, 12:13], m8[i].ap[:, 8:16], AX.X, ALU.min), reads=[m8[i]], writes=[stt[i]])
        mk.op("dve", lambda en: en.tensor_scalar(selb[i].ap, imp[i].ap, stt[i].ap[:, 12:13], None, ALU.is_ge), reads=[imp[i], stt[i]], writes=[selb[i]])
        mk.op("pe", lambda pe: pe.transpose(psM[i].ap[0:ns, 0:128], selb[i].ap, identf.ap), reads=[selb[i], identf], writes=[psM[i]])
        mk.op("act", lambda en: en.copy(selT[i].ap, psM[i].ap[0:ns, 0:128]), reads=[psM[i]], writes=[selT[i]])
        yield
        yield from attn_branch(i, [(kcT.ap[:, g, nt * 128:(nt + 1) * 128], vcA.ap[:, g, nt, :], (lambda k, nt=nt: (cmTb[i].ap[:, nt, :], [cmTb[i]])))
                                  for nt in range(NQ)], psO[i])
        combine(i, g, 0, False)
        yield
        tl = []
        for kt in range(qt + 1):
            def mf(k, kt=kt):
                b = bm[i][k % 2]
                mk.mm(psM[i].ap[:, 0:128], [(esel.ap[:, kt * 128:(kt + 1) * 128], selT[i].ap)], reads=[esel, selT[i]], tr=psM[i])
                if kt == qt:
                    mk.op("dve", lambda en: en.tensor_tensor(b.ap, psM[i].ap[:, 0:128], causT.ap, ALU.mult), reads=[psM[i], causT], writes=[b])
                else:
                    mk.op("act", lambda en: en.copy(b.ap, psM[i].ap[:, 0:128]), reads=[psM[i]], writes=[b])
                return (b.ap, [b])
            tl.append((KsT.ap[:, g, kt * 128:(kt + 1) * 128], VA.ap[:, kt, g, :], mf))
        yield from attn_branch(i, tl, psO[i])
        combine(i, g, 1, False)
        mk.dma("pool", y_d.ap[t0:t0 + 128, g * 256:(g + 1) * 256], acc[i].ap[:, :, :].rearrange("p h d -> p (h d)"), reads=[acc[i]], writes=[y_d])

    order = [(g, qt) for qt in range(NT) for g in range(2)]
    if bg is not None:
        run_pipelined_multi([{"gens": (body(it, g, qt) for it, (g, qt) in enumerate(order)), "width": NS},
                             {"gens": bg, "width": 1, "period": 3}])
    else:
        run_pipelined((body(it, g, qt) for it, (g, qt) in enumerate(order)), NS, stagger=14)
    mk.phase_end()
    mk.phase_end()


T_SEQ = 4096
TAIL_T = 2176
NTM, NFM = 2208, 1408
C_NQ, C_NK, C_NV, C_NG, C_HF, C_HI, C_HG, C_CZ, C_CB, C_CA = 0, 512, 896, 1152, 1176, 1432, 1688, 1944, 2200, 2204
R_VC, R_HQ, R_HF, R_QKV = 0, 128, 384, 640
_r = lambda a, b: list(range(a, b))
W_IN_COLS = (_r(0, 512) + _r(512, 640) + _r(768, 896) + _r(1024, 1152) + _r(896, 1024) + _r(1152, 1280) + _r(1280, 1304)
             + _r(1560, 1816) + _r(1816, 2072) + _r(2072, 2328) + _r(3096, 3352) + _r(3352, 3356) + _r(3356, 3360)
             + _r(640, 768) + _r(1304, 1560) + _r(1560, 1816) + _r(2328, 3096))
assert len(W_IN_COLS) == NTM + NFM

LAYER_KEYS = ["norm_mix", "w_in_sel", "w_gate", "nsa_q_norm", "nsa_k_norm", "cmp_pos_k", "cmp_pos_v", "cmp_k_w1", "cmp_k_w2",
              "cmp_v_w1", "cmp_v_w2", "hgrn_out_norm", "gdn_conv_t", "gdn_a_log", "gdn_dt_bias", "gdn_out_norm",
              "w_branch_a", "w_branch_b", "w_branch_c", "w_mix_out", "norm_cross", "xattn_wq", "xattn_q_norm", "xattn_k_norm",
              "xattn_wo", "norm_ffn", "ffn_w_up", "ffn_conv_t", "ffn_w_down"]


def const_arrays(T):
    c = dict(gdn_consts())
    c.update(nsa_consts(T))
    c["ident"] = np.eye(128, dtype=np.float32)
    return c


def layer_arrays(inputs, l):
    f = lambda a: np.ascontiguousarray(np.asarray(a, dtype=np.float32))
    w_in = np.asarray(inputs["w_in"][l])
    d = {"w_in_sel": f(w_in[:, W_IN_COLS]), "w_gate": f(w_in[:, 3360:6432]),
         "gdn_conv_t": f(np.asarray(inputs["gdn_conv"][l]).T), "ffn_conv_t": f(np.asarray(inputs["ffn_conv"][l]).T)}
    for k in LAYER_KEYS:
        if k not in d:
            d[k] = f(inputs[k][l])
    return d


def build_program(T=T_SEQ, depth=2, shapes=None, stop_after=None):
    mk = MK()
    mk.live.append([])
    ext = lambda name, shape: mk.dram(name, shape, kind="ExternalInput")
    x_in = ext("x", [T, D])
    mem_in = ext("mem", [256, D])
    mem_norm = ext("mem_norm", [D])
    mem_w_kv = ext("mem_w_kv", [D, D])
    lbl = ext("hgrn_lb_logits", [2, 256])
    cst = {k: ext("c_" + k, list(v.shape)) for k, v in const_arrays(T).items()}
    L = []
    for l in range(depth):
        L.append({k: ext("L%d_%s" % (l, k), list(shapes[k])) for k in LAYER_KEYS})
    out = mk.dram("out", [TAIL_T, D], kind="ExternalOutput")
    hsel = ext("hsel", [1])
    xh = [mk.dram("xh%d" % j, [TAIL_T, D]) for j in range(3)]
    xs = [mk.dram("xA", [T, D]), mk.dram("xB", [T, D])]
    Ztm = mk.dram("Ztm", [T, NTM])
    Zfm = mk.dram("Zfm", [NFM, T])
    Y = mk.dram("Y", [T, D])
    MKV = mk.dram("MKV", [256, D])
    QT_d = mk.dram("QT", [64, 8, T], BF16)
    KT_d = mk.dram("KT", [64, 6, T], BF16)
    VA_d = mk.dram("VA", [T, 4, 65], BF16)
    ident = cst["ident"]
    stage_proj(mk, mem_in, mem_norm, mem_w_kv, MKV, None, 256, D, 0, ident)
    cur = x_in
    nstage = 0
    def nxt(last=False):
        return out if last else xs[nstage % 2]
    for l in range(depth):
        W = L[l]
        mk.mark("stage_proj")
        stage_proj(mk, cur, W["norm_mix"], W["w_in_sel"], Ztm, Zfm, T, NTM, NFM, ident)
        mk.mark("stage_gdn")
        n1_gens, n1_fin = nsa_n1(mk, Ztm, C_NQ, C_NK, C_NV, T, W["nsa_q_norm"], W["nsa_k_norm"], cst, ident, QT_d, KT_d, VA_d, defer=True)
        stage_gdn(mk, Ztm, C_CZ, C_CB, C_CA, Zfm, R_QKV, Y, 768, T, W["gdn_conv_t"], W["gdn_a_log"], W["gdn_dt_bias"], W["gdn_out_norm"], cst, bg=n1_gens)
        n1_fin()
        mk.mark("stage_nsa")
        hg_gens, hg_fin = stage_hgrn(mk, Ztm, C_HF, C_HI, C_HG, Zfm, R_HQ, R_HF, Y, 512, T, lbl, l, W["hgrn_out_norm"], cst["triI"], ident, defer=True)
        stage_nsa(mk, Ztm, C_NQ, C_NK, C_NV, C_NG, Zfm, R_VC, Y, T, W["nsa_q_norm"], W["nsa_k_norm"], W["cmp_pos_k"], W["cmp_pos_v"],
                  W["cmp_k_w1"], W["cmp_k_w2"], W["cmp_v_w1"], W["cmp_v_w2"], cst, ident, QT_d, KT_d, VA_d, bg=hg_gens, skip_n1=True)
        hg_fin()
        if stop_after == ("mix", l):
            mk.dma("sp", out.ap, Y.ap, reads=[Y], writes=[out])
            break
        last = (l == depth - 1)
        Tt = TAIL_T if last else T
        xin, yin = cur, Y
        if last:
            mk.mark("stage_select")
            xin, yin = xh[0], xh[1]
            stage_select(mk, cur, xin, Tt, T - Tt, hsel)
            stage_select(mk, Y, yin, Tt, T - Tt, hsel)
        x1 = xh[2] if last else nxt(); nstage += 1
        mk.mark("stage_merge")
        stage_merge(mk, xin, yin, x1, Tt, W["norm_mix"], W["w_gate"], W["w_branch_a"], W["w_branch_b"], W["w_branch_c"], W["w_mix_out"], ident)
        x2 = xh[0] if last else nxt(); nstage += 1
        mk.mark("stage_xattn")
        stage_xattn(mk, x1, x2, Tt, W["norm_cross"], MKV, W["xattn_wq"], W["xattn_q_norm"], W["xattn_k_norm"], W["xattn_wo"], ident)
        x3 = out if last else nxt(); nstage += 1
        mk.mark("stage_ffn")
        stage_ffn(mk, x2, x3, Tt, W["norm_ffn"], W["ffn_w_up"], W["ffn_conv_t"], W["ffn_w_down"], ident)
        cur = x3
    mk.mark("end")
    mk.finish()
    return mk


_PROG = {}


def kernel(**inputs):
    x = np.asarray(inputs["x"], dtype=np.float32)
    B, T, _ = x.shape
    depth = np.asarray(inputs["w_in"]).shape[0]
    layers = [layer_arrays(inputs, l) for l in range(depth)]
    shapes = {k: v.shape for k, v in layers[0].items()}
    key = (T, depth)
    if key not in _PROG:
        _PROG[key] = build_program(T, depth, shapes)
    mk = _PROG[key]
    f = lambda a: np.ascontiguousarray(np.asarray(a, dtype=np.float32))
    common = {"mem_norm": f(inputs["mem_norm"]), "mem_w_kv": f(inputs["mem_w_kv"]), "hgrn_lb_logits": f(inputs["hgrn_lb_logits"])}
    for k, v in const_arrays(T).items():
        common["c_" + k] = v
    for l in range(depth):
        for k, v in layers[l].items():
            common["L%d_%s" % (l, k)] = v
    n = 8
    in_maps = []
    for c in range(n):
        b = c % B
        m = dict(common)
        m["x"] = f(x[b])
        m["mem"] = f(np.asarray(inputs["mem"])[b])
        m["hsel"] = np.full((1,), float(c // B), np.float32)
        in_maps.append(m)
    res = run_bass_kernel_spmd(mk.nc, in_maps, core_ids=list(range(n)))
    Th = T // 2
    outs = []
    for b in range(B):
        lo = np.asarray(res.results[b]["out"], dtype=np.float32)[0:Th]
        hi = np.asarray(res.results[B + b]["out"], dtype=np.float32)[TAIL_T - Th:TAIL_T]
        outs.append(np.concatenate([lo, hi], axis=0))
    return np.stack(outs, axis=0)
```

```python
import numpy as np
import concourse.bass as bass
import concourse.mybir as mybir
from concourse.bass_utils import run_bass_kernel_spmd

F32 = mybir.dt.float32
BF16 = mybir.dt.bfloat16
AF = mybir.ActivationFunctionType
ALU = mybir.AluOpType
AX = mybir.AxisListType
F32R = mybir.dt.float32r
FP32R = False


class Buf:
    __slots__ = ("ap", "w", "r", "name")

    def __init__(self, ap, name):
        self.ap = ap
        self.name = name
        self.w = None
        self.r = {}

    def __getitem__(self, k):
        return self.ap[k]


class MK:
    NDMA = 6

    def __init__(self):
        self.nc = bass.Bass("TRN2", target_bir_lowering=False)
        nc = self.nc
        self.eng = {"pe": nc.tensor, "act": nc.scalar, "dve": nc.vector, "pool": nc.gpsimd, "sp": nc.sync}
        self.sem = {}
        self.cnt = {}
        for e in self.eng:
            self.sem[e] = nc.alloc_semaphore("s_" + e)
            self.cnt[e] = 0
        self.dq = {}
        for q in ("sp", "pool", "act"):
            ks = []
            for i in range(self.NDMA):
                k = "d_%s%d" % (q, i)
                self.sem[k] = nc.alloc_semaphore(k)
                self.cnt[k] = 0
                ks.append(k)
            self.dq[q] = [ks, 0]
        self.obs = {e: {} for e in self.eng}
        self.nbuf = 0
        self.live = []
        self.n_ins = 0
        self.marks = []
        self._pm = None

    def dram(self, name, shape, dt=F32, kind="Internal"):
        t = self.nc.dram_tensor(name, list(shape), dt, kind=kind)
        return Buf(t.ap(), name)

    def sb(self, shape, dt=F32, name=None):
        self.nbuf += 1
        name = "%s_%d" % (name or "sb", self.nbuf)
        g = self.nc.sbuf_tensor(name, list(shape), dt)
        h = g.__enter__()
        self.live[-1].append(g)
        return Buf(h.ap(), name)

    def ps(self, shape, dt=F32, name=None):
        self.nbuf += 1
        name = "%s_%d" % (name or "ps", self.nbuf)
        g = self.nc.psum_tensor(name, list(shape), dt)
        h = g.__enter__()
        self.live[-1].append(g)
        return Buf(h.ap(), name)

    def phase_begin(self):
        self.live.append([])

    def phase_end(self):
        self.barrier()
        for g in reversed(self.live.pop()):
            g.__exit__(None, None, None)

    def _wait(self, e, evs):
        eng = self.eng[e]
        ob = self.obs[e]
        for k, v in evs:
            if k == e and e == "pe":
                continue
            if ob.get(k, 0) < v:
                eng.wait_ge(self.sem[k], v)
                ob[k] = v

    def _deps(self, e, reads, writes):
        evs = []
        for t in reads:
            if t.w is not None:
                evs.append(t.w)
        for t in writes:
            if t.w is not None:
                evs.append(t.w)
            for k, v in t.r.items():
                if k != e:
                    evs.append((k, v))
        return evs

    def _mark(self, ev, reads, writes):
        k, v = ev
        for t in reads:
            if t.r.get(k, 0) < v:
                t.r[k] = v
        for t in writes:
            t.w = ev
            t.r = {}

    def op(self, e, fn, reads=(), writes=(), inc=True):
        self._wait(e, self._deps(e, reads, writes))
        ins = fn(self.eng[e])
        self.n_ins += 1
        self._domark(ins)
        if inc:
            self.cnt[e] += 1
            ins.then_inc(self.sem[e], 1)
            ev = (e, self.cnt[e])
        else:
            ev = (e, self.cnt[e] + 1)
        self._mark(ev, reads, writes)
        return ins

    def dma(self, q, out, in_, reads=(), writes=(), **kw):
        ks, i = self.dq[q]
        k = ks[i % len(ks)]
        self.dq[q][1] = i + 1
        evs = self._deps(q, reads, writes)
        if self.cnt[k] > 0:
            evs.append((k, self.cnt[k]))
        self._wait(q, evs)
        ins = self.eng[q].dma_start(out=out, in_=in_, **kw)
        self.n_ins += 1
        self._domark(ins)
        self.cnt[k] += 16
        ins.then_inc(self.sem[k], 16)
        self._mark((k, self.cnt[k]), reads, writes)
        return ins

    def mark(self, name):
        self._pm = name

    def _domark(self, ins):
        if self._pm is not None:
            try:
                self.marks.append((self._pm, ins.ins.name))
            except Exception as ex:
                self.marks.append((self._pm, repr(ex)))
            self._pm = None

    def barrier(self):
        allev = [(k, v) for k, v in self.cnt.items() if v > 0]
        for e in self.eng:
            self._wait(e, [(k, v) for k, v in allev if k != e])

    def finish(self):
        self.barrier()

    def mm(self, out, pairs, reads, tr=None):
        n = len(pairs)
        if FP32R:
            pairs = [((l.bitcast(F32R) if l.dtype == F32 else l), (r.bitcast(F32R) if r.dtype == F32 else r)) for l, r in pairs]
        for i, (l, r) in enumerate(pairs):
            self.op("pe", lambda pe, l=l, r=r, i=i: pe.matmul(out, l, r, start=(i == 0), stop=(i == n - 1)),
                    reads=reads, writes=[tr], inc=(i == n - 1))


def run_pipelined(gens, width, stagger=0):
    it = iter(gens)
    active = []
    done = set()
    exhausted = False
    head = stagger
    while True:
        while not exhausted and len(active) < (1 if head > 0 else width):
            g = next(it, None)
            if g is None:
                exhausted = True
                break
            active.append([g, None])
        if not active:
            break
        progressed = False
        for slot in list(active):
            g, blk = slot
            if blk is not None and blk not in done:
                continue
            slot[1] = None
            progressed = True
            try:
                r = next(g)
            except StopIteration:
                active.remove(slot)
                continue
            if isinstance(r, tuple):
                if r[0] == "wait":
                    slot[1] = r[1]
                elif r[0] == "done":
                    done.add(r[1])
        if head > 0:
            head -= 1
        assert progressed, "pipeline deadlock"


def run_pipelined_multi(streams):
    st = [{"it": iter(x["gens"]), "w": x["width"], "p": x.get("period", 1), "active": [], "ex": False} for x in streams]
    done = set()
    rnd = 0
    while True:
        alive = False
        for S_ in st:
            while not S_["ex"] and len(S_["active"]) < S_["w"]:
                g = next(S_["it"], None)
                if g is None:
                    S_["ex"] = True
                    break
                S_["active"].append([g, None])
            if S_["active"]:
                alive = True
        if not alive:
            break
        progressed = False
        only_bg = all((not S_["active"]) for S_ in st if S_["p"] == 1)
        for S_ in st:
            if S_["p"] > 1 and (rnd % S_["p"]) != 0 and not only_bg:
                continue
            for slot in list(S_["active"]):
                g, blk = slot
                if blk is not None and blk not in done:
                    continue
                slot[1] = None
                progressed = True
                try:
                    r = next(g)
                except StopIteration:
                    S_["active"].remove(slot)
                    continue
                if isinstance(r, tuple):
                    if r[0] == "wait":
                        slot[1] = r[1]
                    elif r[0] == "done":
                        done.add(r[1])
        rnd += 1
        assert progressed or any(S_["p"] > 1 for S_ in st), "pipeline deadlock"


D = 1024
EPS = 1e-6
TINY = 1e-20


def bcast_rows(ap1d, n):
    return ap1d.partition_broadcast(n)


def load_weight_bf16(mk, w_d, K, N, name, q="sp", chunk=2048):
    kc = K // 128
    wb = mk.sb([128, kc, N], BF16, name)
    mk.phase_begin()
    stg = [mk.sb([128, min(N, chunk)], F32, "wstg") for _ in range(4)]
    i = 0
    for k in range(kc):
        for c0 in range(0, N, chunk):
            c1 = min(N, c0 + chunk)
            s = stg[i % 4]
            mk.dma(("sp", "act")[i % 2], s.ap[:, 0:c1 - c0], w_d.ap[k * 128:(k + 1) * 128, c0:c1], reads=[w_d], writes=[s])
            e = "pool" if i % 2 == 0 else "dve"
            mk.op(e, lambda en, s=s, k=k, c0=c0, c1=c1: en.tensor_copy(wb.ap[:, k, c0:c1], s.ap[:, 0:c1 - c0]),
                  reads=[s], writes=[wb])
            i += 1
    mk.phase_end()
    return wb


def rmsnorm_to_fm(mk, xt, gB, ident, hT, col0, scr, ssb, hb, psT):
    mk.op("act", lambda en: en.activation(out=scr.ap, in_=xt.ap, func=AF.Square, accum_out=ssb.ap[:, 0:1]),
          reads=[xt], writes=[scr, ssb])
    mk.op("act", lambda en: en.activation(out=ssb.ap[:, 1:2], in_=ssb.ap[:, 0:1], func=AF.Sqrt, scale=1.0 / D, bias=EPS),
          reads=[ssb], writes=[ssb])
    mk.op("dve", lambda en: en.reciprocal(ssb.ap[:, 2:3], ssb.ap[:, 1:2]), reads=[ssb], writes=[ssb])
    mk.op("dve", lambda en: en.scalar_tensor_tensor(hb.ap, xt.ap, ssb.ap[:, 2:3], gB.ap, ALU.mult, ALU.mult),
          reads=[xt, ssb, gB], writes=[hb])
    tm_to_fm(mk, hb, ident, hT, col0, psT)


def tm_to_fm(mk, hb, ident, hT, col0, psT, nk=8):
    for k in range(nk):
        mk.op("pe", lambda pe, k=k: pe.transpose(psT.ap[:, k, :], hb.ap[:, k * 128:(k + 1) * 128], ident.ap),
              reads=[hb, ident], writes=[psT], inc=(k == nk - 1))
    mk.op("act", lambda en: en.copy(hT.ap[:, 0:nk, col0:col0 + 128], psT.ap[:, 0:nk, :]), reads=[psT], writes=[hT])


def norm_block(mk, jobs, gB, ident, slots):
    run_pipelined(norm_gens(mk, jobs, gB, ident, slots), len(slots), stagger=1)


def norm_gens(mk, jobs, gB, ident, slots):
    def gen(idx, xt, src, hT, col0, kind):
        sl = slots[idx % len(slots)]
        junk, ssb, hb, psT = sl["junk"], sl["ssb"], sl["hb"], sl["psT"]
        if src is not None:
            mk.dma("sp", xt.ap, src[1], reads=[src[0]], writes=[xt])
        if kind == "norm":
            mk.op("act", lambda en: en.activation(out=junk.ap, in_=xt.ap, func=AF.Square, accum_out=ssb.ap[:, 0:1]),
                  reads=[xt], writes=[junk, ssb])
            yield
            mk.op("act", lambda en: en.activation(out=ssb.ap[:, 1:2], in_=ssb.ap[:, 0:1], func=AF.Ln, scale=1.0 / D, bias=EPS),
                  reads=[ssb], writes=[ssb])
            mk.op("act", lambda en: en.activation(out=ssb.ap[:, 2:3], in_=ssb.ap[:, 1:2], func=AF.Exp, scale=-0.5), reads=[ssb], writes=[ssb])
            mk.op("dve", lambda en: en.scalar_tensor_tensor(hb.ap, xt.ap, ssb.ap[:, 2:3], gB.ap, ALU.mult, ALU.mult),
                  reads=[xt, ssb, gB], writes=[hb])
        else:
            mk.op("pool", lambda en: en.tensor_copy(hb.ap, xt.ap), reads=[xt], writes=[hb])
        yield
        for k in range(8):
            mk.op("pe", lambda pe, k=k: pe.transpose(psT.ap[:, k, :], hb.ap[:, k * 128:(k + 1) * 128], ident.ap),
                  reads=[hb, ident], writes=[psT], inc=(k == 7))
        yield
        mk.op("act", lambda en: en.copy(hT.ap[:, 0:8, col0:col0 + 128], psT.ap[:, 0:8, :]), reads=[psT], writes=[hT])
    return (gen(i, *j) for i, j in enumerate(jobs))


def norm_slots(mk, n, psTs):
    return [{"junk": mk.sb([128, D], BF16, "junk"), "ssb": mk.sb([128, 4], F32, "ssb"), "hb": mk.sb([128, D], BF16, "hb"), "psT": psTs[i]}
            for i in range(n)]


def stage_proj(mk, x_d, g_d, w_d, ztm_d, zfm_d, T, ntm, nfm, ident_d):
    mk.phase_begin()
    N = ntm + nfm
    wb = load_weight_bf16(mk, w_d, D, N, "w_in")
    gB = mk.sb([128, D], F32, "gB")
    mk.dma("sp", gB.ap, bcast_rows(g_d.ap, 128), reads=[g_d], writes=[gB])
    identf = mk.sb([128, 128], F32, "identf")
    ident = mk.sb([128, 128], BF16, "ident")
    mk.dma("sp", identf.ap, ident_d.ap, reads=[ident_d], writes=[identf])
    mk.op("dve", lambda en: en.tensor_copy(ident.ap, identf.ap), reads=[identf], writes=[ident])
    xts = [mk.sb([128, D], F32, "xt") for _ in range(2)]
    TB = min(512, T)
    NTB = TB // 128
    hTs = [mk.sb([128, 8, TB], BF16, "hT") for _ in range(2)]
    psTs = [mk.ps([128, 8, 128], BF16, "psT") for _ in range(2)]
    pss = [mk.ps([128, 512], F32, "psm") for _ in range(4)]
    nsl = norm_slots(mk, 2, psTs)
    ofm = [mk.sb([128, 512], F32, "ofm") for _ in range(3)]
    otm = [mk.sb([128, max(ntm, 1)], F32, "otm") for _ in range(2)]
    it = 0
    ip = 0
    io = 0
    nblk = T // TB

    def mk_jobs(tb):
        jobs = []
        for j in range(NTB):
            t0 = tb * TB + j * 128
            jobs.append((xts[(tb * NTB + j) % 2], (x_d, x_d.ap[t0:t0 + 128, :]), hTs[tb % 2], j * 128, "norm"))
        return jobs

    cnt = {"ip": 0, "io": 0}

    def mm_gen(tb):
        hT = hTs[tb % 2]
        for c in range(nfm // 128):
            ps = pss[cnt["ip"] % 4]
            cnt["ip"] += 1
            c0 = ntm + c * 128
            mk.mm(ps.ap[:, 0:TB], [(wb.ap[:, k, c0:c0 + 128], hT.ap[:, k, :]) for k in range(8)], reads=[wb, hT], tr=ps)
            o = ofm[cnt["io"] % 3]
            e = "act" if cnt["io"] % 2 == 0 else "dve"
            cnt["io"] += 1
            if e == "act":
                mk.op("act", lambda en, o=o, ps=ps: en.copy(o.ap[:, 0:TB], ps.ap[:, 0:TB]), reads=[ps], writes=[o])
            else:
                mk.op("dve", lambda en, o=o, ps=ps: en.tensor_copy(o.ap[:, 0:TB], ps.ap[:, 0:TB]), reads=[ps], writes=[o])
            mk.dma("pool", zfm_d.ap[c * 128:(c + 1) * 128, tb * TB:(tb + 1) * TB], o.ap[:, 0:TB], reads=[o], writes=[zfm_d])
            yield
        for j in range(NTB):
            o = otm[(tb * NTB + j) % 2]
            t0 = tb * TB + j * 128
            for c0 in range(0, ntm, 512):
                c1 = min(ntm, c0 + 512)
                ps = pss[cnt["ip"] % 4]
                cnt["ip"] += 1
                mk.mm(ps.ap[:, 0:c1 - c0], [(hT.ap[:, k, j * 128:(j + 1) * 128], wb.ap[:, k, c0:c1]) for k in range(8)],
                      reads=[wb, hT], tr=ps)
                e = "act" if cnt["io"] % 2 == 0 else "dve"
                cnt["io"] += 1
                if e == "act":
                    mk.op("act", lambda en, o=o, ps=ps, c0=c0, c1=c1: en.copy(o.ap[:, c0:c1], ps.ap[:, 0:c1 - c0]),
                          reads=[ps], writes=[o])
                else:
                    mk.op("dve", lambda en, o=o, ps=ps, c0=c0, c1=c1: en.tensor_copy(o.ap[:, c0:c1], ps.ap[:, 0:c1 - c0]),
                          reads=[ps], writes=[o])
                yield
            if ntm:
                mk.dma("pool", ztm_d.ap[t0:t0 + 128, :], o.ap, reads=[o], writes=[ztm_d])

    norm_block(mk, mk_jobs(0), gB, ident, nsl)
    for tb in range(nblk):
        streams = [{"gens": [mm_gen(tb)], "width": 1}]
        if tb + 1 < nblk:
            streams.append({"gens": norm_gens(mk, mk_jobs(tb + 1), gB, ident, nsl), "width": 2, "period": 2})
        run_pipelined_multi(streams)
    mk.phase_end()


def evac(mk, i, out_ap, in_ap, reads, writes):
    if i % 2 == 0:
        mk.op("act", lambda en: en.copy(out_ap, in_ap), reads=reads, writes=writes)
    else:
        mk.op("dve", lambda en: en.tensor_copy(out_ap, in_ap), reads=reads, writes=writes)


def load_consts_ident(mk, ident_d):
    identf = mk.sb([128, 128], F32, "identf")
    ident = mk.sb([128, 128], BF16, "ident")
    mk.dma("sp", identf.ap, ident_d.ap, reads=[ident_d], writes=[identf])
    mk.op("dve", lambda en: en.tensor_copy(ident.ap, identf.ap), reads=[identf], writes=[ident])
    return ident, identf


def tok_blocks(T, full=512):
    out = []
    rem = T % full
    pos = 0
    if rem:
        out.append((0, rem // 128))
        pos = rem
    while pos < T:
        out.append((pos, full // 128))
        pos += full
    return out


def stage_merge(mk, x_d, y_d, xo_d, T, gn_d, wg_d, wa_d, wb_d, wc_d, wmix_d, ident_d):
    mk.phase_begin()
    wg = load_weight_bf16(mk, wg_d, D, 3072, "wg")
    wa = load_weight_bf16(mk, wa_d, 512, D, "wa")
    wbb = load_weight_bf16(mk, wb_d, 256, D, "wb")
    wc = load_weight_bf16(mk, wc_d, 256, D, "wc")
    wmix = load_weight_bf16(mk, wmix_d, D, D, "wmix")
    gB = mk.sb([128, D], F32, "gB")
    mk.dma("sp", gB.ap, bcast_rows(gn_d.ap, 128), reads=[gn_d], writes=[gB])
    ident, _ = load_consts_ident(mk, ident_d)
    xts = [mk.sb([128, D], F32, "xt") for _ in range(8)]
    yts = [mk.sb([128, D], F32, "yt") for _ in range(2)]
    hTs = [mk.sb([128, 8, 512], BF16, "hT") for _ in range(2)]
    yTs = [mk.sb([128, 8, 512], BF16, "yT") for _ in range(2)]
    mT = mk.sb([128, 8, 512], BF16, "mT")
    sg = [mk.sb([128, 512], F32, "sg") for _ in range(3)]
    acc = [mk.sb([128, 512], F32, "acc") for _ in range(2)]
    tmp = [mk.sb([128, 512], F32, "tmp") for _ in range(2)]
    psTs = [mk.ps([128, 8, 128], BF16, "psT") for _ in range(2)]
    pss = [mk.ps([128, 512], F32, "psm") for _ in range(5)]
    nsl = norm_slots(mk, 2, psTs)
    blocks = tok_blocks(T)

    def mk_jobs(bi):
        tbase, nt = blocks[bi]
        jobs = []
        for j in range(nt):
            t0 = tbase + j * 128
            jobs.append((xts[(bi % 2) * 4 + j], (x_d, x_d.ap[t0:t0 + 128, :]), hTs[bi % 2], j * 128, "norm"))
            jobs.append((yts[j % 2], (y_d, y_d.ap[t0:t0 + 128, :]), yTs[bi % 2], j * 128, "cast"))
        return jobs

    cnt = {"ip": 0}

    def mm_gen(bi):
        tbase, nt = blocks[bi]
        Wd = nt * 128
        hT, yT = hTs[bi % 2], yTs[bi % 2]
        for c in range(8):
            a = acc[c % 2]
            for b, (wbr, koff, nk) in enumerate(((wa, 0, 4), (wbb, 4, 2), (wc, 6, 2))):
                psg = pss[cnt["ip"] % 5]
                cnt["ip"] += 1
                g0 = b * 1024 + c * 128
                mk.mm(psg.ap[:, 0:Wd], [(wg.ap[:, k, g0:g0 + 128], hT.ap[:, k, 0:Wd]) for k in range(8)], reads=[wg, hT], tr=psg)
                s = sg[b]
                mk.op("act", lambda en, s=s, psg=psg, Wd=Wd: en.activation(out=s.ap[:, 0:Wd], in_=psg.ap[:, 0:Wd], func=AF.Sigmoid),
                      reads=[psg], writes=[s])
                psb = pss[cnt["ip"] % 5]
                cnt["ip"] += 1
                mk.mm(psb.ap[:, 0:Wd], [(wbr.ap[:, k, c * 128:(c + 1) * 128], yT.ap[:, koff + k, 0:Wd]) for k in range(nk)],
                      reads=[wbr, yT], tr=psb)
                if b == 0:
                    mk.op("dve", lambda en, a=a, s=s, psb=psb, Wd=Wd: en.tensor_tensor(a.ap[:, 0:Wd], s.ap[:, 0:Wd], psb.ap[:, 0:Wd], ALU.mult),
                          reads=[s, psb], writes=[a])
                else:
                    t = tmp[b % 2]
                    mk.op("dve", lambda en, t=t, s=s, psb=psb, Wd=Wd: en.tensor_tensor(t.ap[:, 0:Wd], s.ap[:, 0:Wd], psb.ap[:, 0:Wd], ALU.mult),
                          reads=[s, psb], writes=[t])
                    if b == 1:
                        mk.op("pool", lambda en, a=a, t=t, Wd=Wd: en.tensor_tensor(a.ap[:, 0:Wd], a.ap[:, 0:Wd], t.ap[:, 0:Wd], ALU.add),
                              reads=[a, t], writes=[a])
                    else:
                        mk.op("pool", lambda en, a=a, t=t, c=c, Wd=Wd: en.tensor_tensor(mT.ap[:, c, 0:Wd], a.ap[:, 0:Wd], t.ap[:, 0:Wd], ALU.add),
                              reads=[a, t], writes=[mT])
                yield
        for j in range(nt):
            xt = xts[(bi % 2) * 4 + j]
            t0 = tbase + j * 128
            for c0 in (0, 512):
                ps = pss[cnt["ip"] % 5]
                cnt["ip"] += 1
                mk.mm(ps.ap, [(mT.ap[:, k, j * 128:(j + 1) * 128], wmix.ap[:, k, c0:c0 + 512]) for k in range(8)],
                      reads=[mT, wmix], tr=ps)
                mk.op("dve", lambda en, xt=xt, ps=ps, c0=c0: en.tensor_tensor(xt.ap[:, c0:c0 + 512], xt.ap[:, c0:c0 + 512], ps.ap, ALU.add),
                      reads=[xt, ps], writes=[xt])
                yield
            mk.dma("pool", xo_d.ap[t0:t0 + 128, :], xt.ap, reads=[xt], writes=[xo_d])

    norm_block(mk, mk_jobs(0), gB, ident, nsl)
    for bi in range(len(blocks)):
        streams = [{"gens": [mm_gen(bi)], "width": 1}]
        if bi + 1 < len(blocks):
            streams.append({"gens": norm_gens(mk, mk_jobs(bi + 1), gB, ident, nsl), "width": 2, "period": 2})
        run_pipelined_multi(streams)
    mk.phase_end()


def stage_select(mk, src_d, dst_d, Tt, off, hsel_d):
    mk.phase_begin()
    hB = mk.sb([128, 1], F32, "hB")
    mk.dma("sp", hB.ap, bcast_rows(hsel_d.ap, 128), reads=[hsel_d], writes=[hB])
    W = 3
    lo = [mk.sb([128, D], F32, "lo") for _ in range(W)]
    hi = [mk.sb([128, D], F32, "hi") for _ in range(W)]
    Th = off
    for tt in range(Tt // 128):
        k = tt % W
        t0 = tt * 128
        mk.dma("sp", lo[k].ap, src_d.ap[t0:t0 + 128, :], reads=[src_d], writes=[lo[k]])
        mk.dma("act", hi[k].ap, src_d.ap[Th + t0:Th + t0 + 128, :], reads=[src_d], writes=[hi[k]])
        eng = "dve" if tt % 2 == 0 else "pool"
        mk.op(eng, lambda en, k=k: en.tensor_tensor(hi[k].ap, hi[k].ap, lo[k].ap, ALU.subtract), reads=[hi[k], lo[k]], writes=[hi[k]])
        mk.op("dve", lambda en, k=k: en.scalar_tensor_tensor(lo[k].ap, hi[k].ap, hB.ap[:, 0:1], lo[k].ap, ALU.mult, ALU.add),
              reads=[hi[k], lo[k], hB], writes=[lo[k]])
        mk.dma("pool", dst_d.ap[t0:t0 + 128, :], lo[k].ap, reads=[lo[k]], writes=[dst_d])
    mk.phase_end()


def stage_ffn(mk, x_d, xo_d, T, gn_d, wup_d, cw_d, wd_d, ident_d):
    TB = 512
    NT = TB // 128
    mk.phase_begin()
    wup = load_weight_bf16(mk, wup_d, D, 5632, "wup")
    wd = load_weight_bf16(mk, wd_d, 2816, D, "wd")
    gB = mk.sb([128, D], F32, "gB")
    mk.dma("sp", gB.ap, bcast_rows(gn_d.ap, 128), reads=[gn_d], writes=[gB])
    cw = mk.sb([128, 44, 3], F32, "cw")
    mk.dma("sp", cw.ap, cw_d.ap.rearrange("(c p) k -> p c k", p=128), reads=[cw_d], writes=[cw])
    ident, _ = load_consts_ident(mk, ident_d)
    halos = [mk.sb([128, 2], F32, "halo") for _ in range(44)]
    for hb_ in halos:
        mk.op("pool", lambda en, hb_=hb_: en.memset(hb_.ap, 0.0), writes=[hb_])
    xts = [mk.sb([128, D], F32, "xt") for _ in range(2)]
    hT = mk.sb([128, 8, TB], BF16, "hT")
    W = 2
    us = [[mk.sb([128, TB + 2], F32, "u") for _ in range(2)] for _ in range(W)]
    cas = [mk.sb([128, TB], F32, "ca") for _ in range(W)]
    cbs = [mk.sb([128, TB], F32, "cb") for _ in range(W)]
    gTs = [mk.sb([128, TB], BF16, "gT") for _ in range(22)]
    psTs = [mk.ps([128, 8, 128], BF16, "psT") for _ in range(1)]
    nsl = norm_slots(mk, 1, psTs)
    psu = [[mk.ps([128, 512], F32, "psu") for _ in range(2)] for _ in range(W)]
    psd = mk.ps([128, 512], F32, "psd")
    it = 0
    nit = [0]
    for (tbase, nt) in tok_blocks(T, TB):
        Wd = nt * 128
        jobs = []
        for j in range(nt):
            t0 = tbase + j * 128
            jobs.append((xts[j % 2], (x_d, x_d.ap[t0:t0 + 128, :]), hT, j * 128, "norm"))
        norm_block(mk, jobs, gB, ident, nsl)

        def cbody(c, k):
            cv = []
            for half, ct in enumerate((c, c + 22)):
                ps = psu[k][half]
                mk.mm(ps.ap[:, 0:Wd], [(wup.ap[:, kk, ct * 128:(ct + 1) * 128], hT.ap[:, kk, 0:Wd]) for kk in range(8)],
                      reads=[wup, hT], tr=ps)
                u = us[k][half]
                hl = halos[ct]
                mk.op("act", lambda en, u=u, hl=hl: en.copy(u.ap[:, 0:2], hl.ap), reads=[hl], writes=[u])
                mk.op("act", lambda en, u=u, ps=ps: en.copy(u.ap[:, 2:Wd + 2], ps.ap[:, 0:Wd]), reads=[ps], writes=[u])
                o = (cas if half == 0 else cbs)[k]
                mk.op("act", lambda en, o=o, ps=ps, ct=ct: en.activation(out=o.ap[:, 0:Wd], in_=ps.ap[:, 0:Wd], func=AF.Copy, scale=cw.ap[:, ct, 2:3]),
                      reads=[ps, cw], writes=[o])
                yield
                mk.op("act", lambda en, u=u, hl=hl: en.copy(hl.ap, u.ap[:, Wd:Wd + 2]), reads=[u], writes=[hl])
                mk.op("dve", lambda en, o=o, u=u, ct=ct: en.scalar_tensor_tensor(o.ap[:, 0:Wd], u.ap[:, 1:Wd + 1], cw.ap[:, ct, 1:2], o.ap[:, 0:Wd], ALU.mult, ALU.add),
                      reads=[u, cw, o], writes=[o])
                mk.op("dve", lambda en, o=o, u=u, ct=ct: en.scalar_tensor_tensor(o.ap[:, 0:Wd], u.ap[:, 0:Wd], cw.ap[:, ct, 0:1], o.ap[:, 0:Wd], ALU.mult, ALU.add),
                      reads=[u, cw, o], writes=[o])
                cv.append(o)
                yield
            a_, b_ = cv
            mk.op("act", lambda en, a_=a_: en.activation(out=a_.ap[:, 0:Wd], in_=a_.ap[:, 0:Wd], func=AF.Silu), reads=[a_], writes=[a_])
            yield
            mk.op("dve", lambda en, a_=a_, b_=b_, c=c: en.tensor_tensor(gTs[c].ap[:, 0:Wd], a_.ap[:, 0:Wd], b_.ap[:, 0:Wd], ALU.mult),
                  reads=[a_, b_], writes=[gTs[c]])

        def gens():
            for c in range(22):
                k = nit[0] % W
                nit[0] += 1
                yield cbody(c, k)
        run_pipelined(gens(), W, stagger=2)
        for j in range(nt):
            t0 = tbase + j * 128
            xt = xts[j % 2]
            mk.dma("sp", xt.ap, x_d.ap[t0:t0 + 128, :], reads=[x_d], writes=[xt])
            for c0 in (0, 512):
                ps = psd
                mk.mm(ps.ap, [(gTs[k].ap[:, j * 128:(j + 1) * 128], wd.ap[:, k, c0:c0 + 512]) for k in range(22)],
                      reads=gTs + [wd], tr=ps)
                mk.op("dve", lambda en, xt=xt, ps=ps, c0=c0: en.tensor_tensor(xt.ap[:, c0:c0 + 512], xt.ap[:, c0:c0 + 512], ps.ap, ALU.add),
                      reads=[xt, ps], writes=[xt])
            mk.dma("pool", xo_d.ap[t0:t0 + 128, :], xt.ap, reads=[xt], writes=[xo_d])
    mk.phase_end()

def bc_last(ap2, n):
    return ap2.unsqueeze(2).to_broadcast([ap2.shape[0], ap2.shape[1], n])


def bc_mid(ap2, h):
    return ap2.unsqueeze(1).to_broadcast([ap2.shape[0], h, ap2.shape[1]])


def exp_sigmoid(mk, out_b, out_ap, in_b, in_ap, neg=False, eng2="dve"):
    mk.op("act", lambda en: en.activation(out=out_ap, in_=in_ap, func=AF.Exp, scale=(1.0 if neg else -1.0)), reads=[in_b], writes=[out_b])
    mk.op("act", lambda en: en.activation(out=out_ap, in_=out_ap, func=AF.Ln, bias=1.0), reads=[out_b], writes=[out_b])
    mk.op("act", lambda en: en.activation(out=out_ap, in_=out_ap, func=AF.Exp, scale=-1.0), reads=[out_b], writes=[out_b])


def exp_silu(mk, out_b, out_ap, in_b, in_ap, tmp_b, tmp_ap):
    exp_sigmoid(mk, tmp_b, tmp_ap, in_b, in_ap)
    mk.op("dve", lambda en: en.tensor_tensor(out_ap, in_ap, tmp_ap, ALU.mult), reads=[in_b, tmp_b], writes=[out_b])


def head_rmsnorm(mk, src, H, dh, gB, out, scr, st, scale=1.0, np_=128):
    n = H * dh
    s3 = lambda b: b.ap[0:np_, 0:n].rearrange("p (h d) -> p h d", h=H)
    mk.op("pool", lambda en: en.tensor_tensor(scr.ap[0:np_, 0:n], src.ap[0:np_, 0:n], src.ap[0:np_, 0:n], ALU.mult),
          reads=[src], writes=[scr])
    mk.op("dve", lambda en: en.tensor_reduce(st.ap[0:np_, 0:H], s3(scr), AX.X, ALU.add), reads=[scr], writes=[st])
    mk.op("act", lambda en: en.activation(out=st.ap[0:np_, H:2 * H], in_=st.ap[0:np_, 0:H], func=AF.Ln, scale=1.0 / dh, bias=EPS),
          reads=[st], writes=[st])
    mk.op("act", lambda en: en.activation(out=st.ap[0:np_, 2 * H:3 * H], in_=st.ap[0:np_, H:2 * H], func=AF.Exp, scale=-0.5),
          reads=[st], writes=[st])
    mk.op("dve", lambda en: en.tensor_tensor(s3(scr), s3(src), bc_last(st.ap[0:np_, 2 * H:3 * H], dh), ALU.mult),
          reads=[src, st], writes=[scr])
    mk.op("dve", lambda en: en.scalar_tensor_tensor(s3(out), s3(scr), float(scale), bc_mid(gB.ap[0:np_, 0:dh], H), ALU.mult, ALU.mult),
          reads=[scr, gB], writes=[out])


def stage_xattn(mk, x_d, xo_d, T, gn_d, mkv_d, wq_d, qn_d, kn_d, wo_d, ident_d):
    mk.phase_begin()
    H, DH, M = 4, 128, 256
    wq = load_weight_bf16(mk, wq_d, D, 512, "wq")
    wo = load_weight_bf16(mk, wo_d, 512, D, "wo")
    gB = mk.sb([128, D], F32, "gB")
    mk.dma("sp", gB.ap, bcast_rows(gn_d.ap, 128), reads=[gn_d], writes=[gB])
    gq = mk.sb([128, DH], F32, "gq")
    mk.dma("sp", gq.ap, bcast_rows(qn_d.ap, 128), reads=[qn_d], writes=[gq])
    gk = mk.sb([128, DH], F32, "gk")
    mk.dma("sp", gk.ap, bcast_rows(kn_d.ap, 128), reads=[kn_d], writes=[gk])
    ident, _ = load_consts_ident(mk, ident_d)
    kT = mk.sb([128, H, M], BF16, "kT")
    vaug = mk.sb([128, 2, H, DH + 1], BF16, "vaug")
    scr = mk.sb([128, D], F32, "scr")
    st = mk.sb([128, 12], F32, "st")
    psTs = [mk.ps([128, 8, 128], BF16, "psT") for _ in range(2)]
    pss = [mk.ps([128, 512], F32, "psm") for _ in range(5)]
    hbs = [mk.sb([128, D], BF16, "hb") for _ in range(2)]
    mt_t = mk.sb([128, D], F32, "memt")
    mk.op("pool", lambda en: en.memset(vaug.ap, 1.0), writes=[vaug])
    for mt in range(2):
        mk.dma("sp", mt_t.ap, mkv_d.ap[mt * 128:(mt + 1) * 128, :], reads=[mkv_d], writes=[mt_t])
        hb = hbs[mt]
        head_rmsnorm(mk, mt_t, H, DH, gk, hb, scr, st)
        for h in range(H):
            mk.op("pe", lambda pe, h=h, hb=hb: pe.transpose(psTs[0].ap[:, h, :], hb.ap[:, h * DH:(h + 1) * DH], ident.ap),
                  reads=[hb, ident], writes=[psTs[0]], inc=(h == H - 1))
        mk.op("act", lambda en, mt=mt: en.copy(kT.ap[:, :, mt * 128:(mt + 1) * 128], psTs[0].ap[:, 0:H, :]),
              reads=[psTs[0]], writes=[kT])
        mk.op("dve", lambda en, mt=mt: en.tensor_copy(vaug.ap[:, mt, :, 0:DH], mt_t.ap[:, 512:1024].rearrange("p (h d) -> p h d", h=H)),
              reads=[mt_t], writes=[vaug])
    xts = [mk.sb([128, D], F32, "xt") for _ in range(4)]
    nsl = [{"junk": mk.sb([128, D], BF16, "junk"), "ssb": mk.sb([128, 4], F32, "ssb"), "hb": hbs[i_], "psT": psTs[i_]} for i_ in range(2)]
    hT = mk.sb([128, 8, 512], BF16, "hT")
    W = 2
    qf = [mk.sb([128, 512], F32, "qf") for _ in range(W)]
    qb = [mk.sb([128, 512], BF16, "qb") for _ in range(W)]
    scrs = [mk.sb([128, 512], F32, "scrq") for _ in range(W)]
    sts = [mk.sb([128, 12], F32, "stq") for _ in range(W)]
    qT = mk.sb([128, H, 512], BF16, "qT")
    PT = mk.sb([128, H * 2, 512], BF16, "PT")
    of = [mk.sb([128, 512], F32, "of") for _ in range(W)]
    ob = [mk.sb([128, 512], BF16, "ob") for _ in range(W)]
    oT = [mk.sb([128, H, 128], BF16, "oT") for _ in range(W)]
    rd = [mk.sb([128, 4], F32, "rd") for _ in range(W)]
    pq = [pss[0], pss[1]]
    po = [[pss[2], pss[3]], [pss[4], pss[0]]]
    it = 0
    for (tbase, nt) in tok_blocks(T):
        Wd = nt * 128
        jobs = []
        for j in range(nt):
            t0 = tbase + j * 128
            jobs.append((xts[j], (x_d, x_d.ap[t0:t0 + 128, :]), hT, j * 128, "norm"))
        norm_block(mk, jobs, gB, ident, nsl)

        def qbody(j):
            k = j % W
            ps = pq[k]
            mk.mm(ps.ap, [(hT.ap[:, kk, j * 128:(j + 1) * 128], wq.ap[:, kk, :]) for kk in range(8)], reads=[hT, wq], tr=ps)
            q = qf[k]
            mk.op("act", lambda en: en.copy(q.ap, ps.ap), reads=[ps], writes=[q])
            yield
            qq = qb[k]
            head_rmsnorm(mk, q, H, DH, gq, qq, scrs[k], sts[k], scale=DH ** -0.5)
            yield
            psT = psTs[k]
            for h in range(H):
                mk.op("pe", lambda pe, h=h: pe.transpose(psT.ap[:, h, :], qq.ap[:, h * DH:(h + 1) * DH], ident.ap),
                      reads=[qq, ident], writes=[psT], inc=(h == H - 1))
            mk.op("act", lambda en: en.copy(qT.ap[:, :, j * 128:(j + 1) * 128], psT.ap[:, 0:H, :]),
                  reads=[psT], writes=[qT])
        run_pipelined((qbody(j) for j in range(nt)), W, stagger=1)
        ip = 0
        for h in range(H):
            for mt in range(2):
                ps = pss[ip % 2]
                ip += 1
                mk.mm(ps.ap[:, 0:Wd], [(kT.ap[:, h, mt * 128:(mt + 1) * 128], qT.ap[:, h, 0:Wd])], reads=[kT, qT], tr=ps)
                mk.op("act", lambda en, ps=ps, h=h, mt=mt, Wd=Wd: en.activation(out=PT.ap[:, h * 2 + mt, 0:Wd], in_=ps.ap[:, 0:Wd], func=AF.Exp),
                      reads=[ps], writes=[PT])

        def obody(j):
            k = j % W
            xt = xts[j]
            t0 = tbase + j * 128
            o_f = of[k]
            r = rd[k]
            for hp in range(2):
                ps = po[k][hp]
                for hh in range(2):
                    h = hp * 2 + hh
                    mk.mm(ps.ap[:, hh * 129:(hh + 1) * 129],
                          [(PT.ap[:, h * 2 + mt, j * 128:(j + 1) * 128], vaug.ap[:, mt, h, :]) for mt in range(2)],
                          reads=[PT, vaug], tr=ps)
                for hh in range(2):
                    h = hp * 2 + hh
                    mk.op("dve", lambda en, ps=ps, hh=hh, h=h: en.reciprocal(r.ap[:, h:h + 1], ps.ap[:, hh * 129 + 128:hh * 129 + 129]),
                          reads=[ps], writes=[r])
                    mk.op("dve", lambda en, ps=ps, hh=hh, h=h: en.tensor_scalar(
                        o_f.ap[:, h * 128:(h + 1) * 128], ps.ap[:, hh * 129:hh * 129 + 128], r.ap[:, h:h + 1], None, ALU.mult),
                        reads=[ps, r], writes=[o_f])
                yield
            o_b = ob[k]
            mk.op("pool", lambda en: en.tensor_copy(o_b.ap, o_f.ap), reads=[o_f], writes=[o_b])
            psT = psTs[k]
            for h in range(H):
                mk.op("pe", lambda pe, h=h: pe.transpose(psT.ap[:, h, :], o_b.ap[:, h * DH:(h + 1) * DH], ident.ap),
                      reads=[o_b, ident], writes=[psT], inc=(h == H - 1))
            o_T = oT[k]
            mk.op("act", lambda en: en.copy(o_T.ap, psT.ap[:, 0:H, :]), reads=[psT], writes=[o_T])
            yield
            for ci, c0 in enumerate((0, 512)):
                ps = po[k][ci]
                mk.mm(ps.ap, [(o_T.ap[:, h, :], wo.ap[:, h, c0:c0 + 512]) for h in range(H)], reads=[o_T, wo], tr=ps)
                mk.op("dve", lambda en, ps=ps, c0=c0: en.tensor_tensor(xt.ap[:, c0:c0 + 512], xt.ap[:, c0:c0 + 512], ps.ap, ALU.add),
                      reads=[xt, ps], writes=[xt])
                yield
            mk.dma("pool", xo_d.ap[t0:t0 + 128, :], xt.ap, reads=[xt], writes=[xo_d])
        run_pipelined((obody(j) for j in range(nt)), W, stagger=1)
    mk.phase_end()


def stage_hgrn(mk, ztm_d, c_hf, c_hi, c_hg, zfm_d, r_hq, r_hf, y_d, ycol, T, lbl_d, layer, onorm_d, tri_d, ident_d, defer=False):
    H, DK, C = 4, 64, 64
    mk.phase_begin()
    tri = mk.sb([64, 64], F32, "tri")
    mk.dma("sp", tri.ap, tri_d.ap, reads=[tri_d], writes=[tri])
    identf = mk.sb([128, 128], F32, "identf")
    mk.dma("sp", identf.ap, ident_d.ap, reads=[ident_d], writes=[identf])
    gO = mk.sb([64, 64], F32, "gO")
    mk.dma("sp", gO.ap, bcast_rows(onorm_d.ap, 64), reads=[onorm_d], writes=[gO])
    lbB = mk.sb([64, 256], F32, "lbB")
    omlbB = mk.sb([64, 256], F32, "omlbB")
    lbT = mk.sb([64, 4], F32, "lbT")
    omlbT = mk.sb([64, 4], F32, "omlbT")
    if layer == 0:
        mk.op("dve", lambda en: en.memset(lbB.ap, 0.0), writes=[lbB])
        mk.op("dve", lambda en: en.memset(lbT.ap, 0.0), writes=[lbT])
    else:
        l0 = mk.sb([64, 256], F32, "l0")
        l0T = mk.sb([64, 4], F32, "l0T")
        mk.dma("sp", l0.ap, bcast_rows(lbl_d.ap[0, :], 64), reads=[lbl_d], writes=[l0])
        mk.dma("sp", lbB.ap, bcast_rows(lbl_d.ap[1, :], 64), reads=[lbl_d], writes=[lbB])
        mk.dma("sp", l0T.ap, lbl_d.ap[0, :].rearrange("(h d) -> d h", h=4), reads=[lbl_d], writes=[l0T], allow_slow_non_contiguous=True)
        mk.dma("sp", lbT.ap, lbl_d.ap[1, :].rearrange("(h d) -> d h", h=4), reads=[lbl_d], writes=[lbT], allow_slow_non_contiguous=True)
        mk.op("dve", lambda en: en.tensor_tensor(lbB.ap, lbB.ap, l0.ap, ALU.subtract), reads=[lbB, l0], writes=[lbB])
        mk.op("act", lambda en: en.activation(out=lbB.ap, in_=lbB.ap, func=AF.Sigmoid), reads=[lbB], writes=[lbB])
        mk.op("dve", lambda en: en.tensor_tensor(lbT.ap, lbT.ap, l0T.ap, ALU.subtract), reads=[lbT, l0T], writes=[lbT])
        mk.op("act", lambda en: en.activation(out=lbT.ap, in_=lbT.ap, func=AF.Sigmoid), reads=[lbT], writes=[lbT])
    mk.op("dve", lambda en: en.tensor_scalar(omlbB.ap, lbB.ap, -1.0, 1.0, ALU.mult, ALU.add), reads=[lbB], writes=[omlbB])
    mk.op("dve", lambda en: en.tensor_scalar(omlbT.ap, lbT.ap, -1.0, 1.0, ALU.mult, ALU.add), reads=[lbT], writes=[omlbT])
    S = mk.sb([64, H, 64], F32, "S")
    mk.op("dve", lambda en: en.memset(S.ap, 0.0), writes=[S])
    NB = 2 if defer else 3
    def mkb(shape, name, n=NB, dt=F32):
        return [mk.sb(shape, dt, name) for _ in range(n)]
    qz, fz, fzt, vt, gz = mkb([64, H, 64], "qz"), mkb([64, H, 64], "fz"), mkb([64, 256], "fzt"), mkb([64, 256], "vt"), mkb([64, 256], "gz")
    qT, kT, lf = mkb([64, H, 64], "qT"), mkb([64, H, 64], "kT"), mkb([64, 256], "lf")
    bT, d1, e1, e2, e3 = mkb([64, H, 64], "bT"), mkb([64, H, 64], "d1"), mkb([64, H, 64], "e1"), mkb([64, H, 64], "e2"), mkb([64, H, 64], "e3")
    qtl, ktl, qe, kdT, kd = mkb([64, H, 64], "qtl"), mkb([64, H, 64], "ktl"), mkb([64, H, 64], "qe"), mkb([64, H, 64], "kdT"), mkb([64, H, 64], "kd")
    AT, ebl = mkb([64, H, 64], "AT"), mkb([64, 4], "ebl")
    of, scr, st, yo = mkb([64, 256], "of"), mkb([64, 256], "scr"), mkb([64, 12], "st"), mkb([64, 256], "yo")
    NPS = 2 if defer else 8
    PPS = 2 if defer else 4
    pss = [mk.ps([128, 512], F32, "psm") for _ in range(NPS)]
    ipc = [0, 0]
    def body(c):
        i = c % NB
        t0 = c * C
        pset = 0 if defer else c % 2
        def nps():
            p = pss[pset * PPS + ipc[pset] % PPS]
            ipc[pset] += 1
            return p
        mk.dma("sp", qz[i].ap, zfm_d.ap[r_hq:r_hq + 256, t0:t0 + C].rearrange("(h d) t -> d h t", h=H), reads=[zfm_d], writes=[qz[i]])
        mk.dma("sp", fz[i].ap, zfm_d.ap[r_hf:r_hf + 256, t0:t0 + C].rearrange("(h d) t -> d h t", h=H), reads=[zfm_d], writes=[fz[i]])
        mk.dma("sp", fzt[i].ap, ztm_d.ap[t0:t0 + C, c_hf:c_hf + 256], reads=[ztm_d], writes=[fzt[i]])
        mk.dma("sp", vt[i].ap, ztm_d.ap[t0:t0 + C, c_hi:c_hi + 256], reads=[ztm_d], writes=[vt[i]])
        mk.dma("sp", gz[i].ap, ztm_d.ap[t0:t0 + C, c_hg:c_hg + 256], reads=[ztm_d], writes=[gz[i]])
        yield
        exp_silu(mk, qT[i], qT[i].ap, qz[i], qz[i].ap, e1[i], e1[i].ap)
        exp_sigmoid(mk, kT[i], kT[i].ap, fz[i], fz[i].ap, neg=True, eng2="pool")
        mk.op("dve", lambda en, i=i: en.tensor_tensor(kT[i].ap, kT[i].ap, bc_last(omlbT.ap, 64), ALU.mult), reads=[kT[i], omlbT], writes=[kT[i]])
        yield
        exp_sigmoid(mk, lf[i], lf[i].ap, fzt[i], fzt[i].ap, eng2="pool")
        mk.op("dve", lambda en, i=i: en.tensor_tensor(lf[i].ap, lf[i].ap, omlbB.ap, ALU.mult), reads=[lf[i], omlbB], writes=[lf[i]])
        mk.op("dve", lambda en, i=i: en.tensor_tensor(lf[i].ap, lf[i].ap, lbB.ap, ALU.add), reads=[lf[i], lbB], writes=[lf[i]])
        mk.op("dve", lambda en, i=i: en.tensor_scalar(lf[i].ap, lf[i].ap, TINY, None, ALU.max), reads=[lf[i]], writes=[lf[i]])
        mk.op("act", lambda en, i=i: en.activation(out=lf[i].ap, in_=lf[i].ap, func=AF.Ln), reads=[lf[i]], writes=[lf[i]])
        yield
        psb = nps()
        for h in range(H):
            mk.mm(psb.ap[0:64, h * 64:(h + 1) * 64], [(lf[i].ap[:, h * 64:(h + 1) * 64], tri.ap)], reads=[lf[i], tri], tr=psb)
        mk.op("act", lambda en, i=i, psb=psb: en.copy(bT[i].ap, psb.ap[0:64, 0:256].rearrange("p (h t) -> p h t", h=H)), reads=[psb], writes=[bT[i]])
        mk.op("dve", lambda en, i=i: en.tensor_tensor(d1[i].ap, bT[i].ap, bc_last(bT[i].ap[:, :, 31], 64), ALU.subtract), reads=[bT[i]], writes=[d1[i]])
        mk.op("act", lambda en, i=i: en.activation(out=e1[i].ap, in_=d1[i].ap, func=AF.Exp), reads=[d1[i]], writes=[e1[i]])
        mk.op("act", lambda en, i=i: en.activation(out=e2[i].ap, in_=d1[i].ap, func=AF.Exp, scale=-1.0), reads=[d1[i]], writes=[e2[i]])
        mk.op("dve", lambda en, i=i: en.tensor_tensor(qtl[i].ap, qT[i].ap, e1[i].ap, ALU.mult), reads=[qT[i], e1[i]], writes=[qtl[i]])
        mk.op("dve", lambda en, i=i: en.tensor_tensor(ktl[i].ap, kT[i].ap, e2[i].ap, ALU.mult), reads=[kT[i], e2[i]], writes=[ktl[i]])
        mk.op("act", lambda en, i=i: en.activation(out=e1[i].ap, in_=bT[i].ap, func=AF.Exp), reads=[bT[i]], writes=[e1[i]])
        mk.op("dve", lambda en, i=i: en.tensor_tensor(qe[i].ap, qT[i].ap, e1[i].ap, ALU.mult), reads=[qT[i], e1[i]], writes=[qe[i]])
        mk.op("dve", lambda en, i=i: en.tensor_tensor(d1[i].ap, bT[i].ap, bc_last(bT[i].ap[:, :, 63], 64), ALU.subtract), reads=[bT[i]], writes=[d1[i]])
        mk.op("act", lambda en, i=i: en.activation(out=e3[i].ap, in_=d1[i].ap, func=AF.Exp, scale=-1.0), reads=[d1[i]], writes=[e3[i]])
        mk.op("dve", lambda en, i=i: en.tensor_tensor(kdT[i].ap, kT[i].ap, e3[i].ap, ALU.mult), reads=[kT[i], e3[i]], writes=[kdT[i]])
        mk.op("act", lambda en, i=i: en.activation(out=ebl[i].ap, in_=bT[i].ap[:, :, 63], func=AF.Exp), reads=[bT[i]], writes=[ebl[i]])
        yield
        pst = nps()
        for h in range(H):
            mk.op("pe", lambda pe, h=h, i=i, pst=pst: pe.transpose(pst.ap[0:64, h * 64:(h + 1) * 64], kdT[i].ap[:, h, :], identf.ap[0:64, 0:64]),
                  reads=[kdT[i], identf], writes=[pst], inc=(h == H - 1))
        mk.op("act", lambda en, i=i, pst=pst: en.copy(kd[i].ap, pst.ap[0:64, 0:256].rearrange("p (h t) -> p h t", h=H)), reads=[pst], writes=[kd[i]])
        yield
        psa = nps()
        for h in range(H):
            mk.mm(psa.ap[0:64, h * 64:(h + 1) * 64], [(ktl[i].ap[:, h, :], qtl[i].ap[:, h, :])], reads=[ktl[i], qtl[i]], tr=psa)
        mk.op("dve", lambda en, i=i, psa=psa: en.tensor_tensor(AT[i].ap, psa.ap[0:64, 0:256].rearrange("p (h t) -> p h t", h=H), bc_mid(tri.ap, H), ALU.mult),
              reads=[psa, tri], writes=[AT[i]])
        if c > 0:
            yield ("wait", ("S", c - 1))
        pso = nps()
        for h in range(H):
            mk.mm(pso.ap[0:64, h * 64:(h + 1) * 64], [(AT[i].ap[:, h, :], vt[i].ap[:, h * 64:(h + 1) * 64]), (qe[i].ap[:, h, :], S.ap[:, h, :])],
                  reads=[AT[i], vt[i], qe[i], S], tr=pso)
        pss_ = nps()
        for h in range(H):
            mk.mm(pss_.ap[0:64, h * 64:(h + 1) * 64], [(kd[i].ap[:, h, :], vt[i].ap[:, h * 64:(h + 1) * 64])], reads=[kd[i], vt[i]], tr=pss_)
        mk.op("dve", lambda en, i=i: en.tensor_tensor(S.ap, S.ap, bc_last(ebl[i].ap, 64), ALU.mult), reads=[S, ebl[i]], writes=[S])
        mk.op("dve", lambda en, pss_=pss_: en.tensor_tensor(S.ap, S.ap, pss_.ap[0:64, 0:256].rearrange("p (h t) -> p h t", h=H), ALU.add), reads=[S, pss_], writes=[S])
        yield ("done", ("S", c))
        mk.op("act", lambda en, i=i, pso=pso: en.copy(of[i].ap, pso.ap[0:64, 0:256]), reads=[pso], writes=[of[i]])
        head_rmsnorm(mk, of[i], H, 64, gO, yo[i], scr[i], st[i], np_=64)
        exp_sigmoid(mk, gz[i], gz[i].ap, gz[i], gz[i].ap, eng2="pool")
        mk.op("dve", lambda en, i=i: en.tensor_tensor(yo[i].ap, yo[i].ap, gz[i].ap, ALU.mult), reads=[yo[i], gz[i]], writes=[yo[i]])
        mk.dma("pool", y_d.ap[t0:t0 + C, ycol:ycol + 256], yo[i].ap, reads=[yo[i]], writes=[y_d])
    if defer:
        return (body(c) for c in range(T // C)), (lambda: mk.phase_end())
    run_pipelined((body(c) for c in range(T // C)), 2)
    mk.phase_end()


def stage_gdn(mk, ztm_d, c_cz, c_cb, c_ca, zfm_d, r_qkv, y_d, ycol, T, cw_d, alog_d, dtb_d, onorm_d, cst, bg=None):
    H, C = 4, 128
    mk.phase_begin()
    def cload(name):
        b = mk.sb([128, 128], F32, name)
        mk.dma("sp", b.ap, cst[name].ap, reads=[cst[name]], writes=[b])
        return b
    triI, lowS, negtriS, neglowS, id128, ones = (cload(n) for n in ("g_triI", "g_lowS", "g_negtriS", "g_neglowS", "g_id", "g_ones"))
    gO = mk.sb([128, 64], F32, "gO")
    mk.dma("sp", gO.ap, bcast_rows(onorm_d.ap, 128), reads=[onorm_d], writes=[gO])
    cw = mk.sb([64, 12, 4], F32, "cw")
    mk.dma("sp", cw.ap, cw_d.ap.rearrange("(b d) k -> d b k", d=64), reads=[cw_d], writes=[cw])
    negA = mk.sb([128, 4], F32, "negA")
    dtb = mk.sb([128, 4], F32, "dtb")
    mk.dma("sp", negA.ap, bcast_rows(alog_d.ap, 128), reads=[alog_d], writes=[negA])
    mk.dma("sp", dtb.ap, bcast_rows(dtb_d.ap, 128), reads=[dtb_d], writes=[dtb])
    mk.op("act", lambda en: en.activation(out=negA.ap, in_=negA.ap, func=AF.Exp), reads=[negA], writes=[negA])
    mk.op("dve", lambda en: en.tensor_scalar(negA.ap, negA.ap, -1.0, None, ALU.mult), reads=[negA], writes=[negA])
    S = mk.sb([64, H, 64], F32, "S")
    mk.op("dve", lambda en: en.memset(S.ap, 0.0), writes=[S])
    NB = 2
    def mkb(shape, name, n=NB, dt=F32):
        return [mk.sb(list(shape), dt, name) for _ in range(n)]
    FM = [64, H, C]
    TM = [128, H, 64]
    SQ = [128, H, C]
    u = mkb([64, 12, C + 3], "u")
    cz, sc = mkb([128, 256], "cz"), mkb([128, 8], "sc")
    tk = [mkb([64, 12, C], "tk%d" % k) for k in range(2)]
    qkv, sq, rinv = mkb([64, 12, C], "qkv"), mkb([64, 8, C], "sq"), mkb([64, 8, C], "rinv")
    kv_tm = mkb([128, 8, 64], "kvtm")
    beta, g, sp1, sp2 = mkb([128, 4], "beta"), mkb([128, 4], "g"), mkb([128, 4], "sp1"), mkb([128, 4], "sp2")
    rep = mkb([128, 8, 64], "rep")
    eb, bb = mkb(FM, "eb"), mkb(FM, "bb")
    kbT, kbeT, qeT = mkb(FM, "kbT"), mkb(FM, "kbeT"), mkb(FM, "qeT")
    Gt = mkb(SQ, "Gt")
    ET, E, ETi = mkb(SQ, "ET"), mkb(SQ, "E"), mkb(SQ, "ETi")
    NUs, NLs = [mkb(SQ, "NU%d" % j) for j in range(2)], [mkb(SQ, "NL%d" % j) for j in range(2)]
    P = mkb(SQ, "P")
    QKT = mkb(SQ, "QKT")
    bv, rhs, vn, kd, elb = mkb(TM, "bv"), mkb(TM, "rhs"), mkb(TM, "vn"), mkb(TM, "kd"), mkb([128, 4], "elb")
    of, scr, st, yo = mkb([128, 256], "of"), mkb([128, 256], "scr"), mkb([128, 12], "st"), mkb([128, 256], "yo")
    PPS = 3 if bg is not None else 4
    pss = [mk.ps([128, 512], F32, "psm") for _ in range(2 * PPS)]
    ipc = [0, 0]
    psq = lambda p: p.ap[:, 0:H * C].rearrange("p (h t) -> p h t", h=H)
    ptm = lambda p: p.ap[:, 0:H * 64].rearrange("p (h t) -> p h t", h=H)
    pfm = lambda p: p.ap[0:64, 0:H * C].rearrange("p (h t) -> p h t", h=H)

    def body(c):
        i = c % NB
        t0 = c * C
        U = u[i]
        NU = [NUs[0][i], NUs[1][i]]
        NL = [NLs[0][i], NLs[1][i]]
        pset = c % 2
        def nps():
            p = pss[pset * PPS + ipc[pset] % PPS]
            ipc[pset] += 1
            return p
        if c == 0:
            mk.op("pool", lambda en, U=U: en.memset(U.ap[:, :, 0:3], 0.0), writes=[U])
            mk.dma("sp", U.ap[:, :, 3:C + 3], zfm_d.ap[r_qkv:r_qkv + 768, 0:C].rearrange("(b d) t -> d b t", d=64), reads=[zfm_d], writes=[U])
        else:
            mk.dma("sp", U.ap, zfm_d.ap[r_qkv:r_qkv + 768, t0 - 3:t0 + C].rearrange("(b d) t -> d b t", d=64), reads=[zfm_d], writes=[U])
        mk.dma("sp", cz[i].ap, ztm_d.ap[t0:t0 + C, c_cz:c_cz + 256], reads=[ztm_d], writes=[cz[i]])
        mk.dma("sp", sc[i].ap[:, 0:4], ztm_d.ap[t0:t0 + C, c_cb:c_cb + 4], reads=[ztm_d], writes=[sc[i]])
        mk.dma("sp", sc[i].ap[:, 4:8], ztm_d.ap[t0:t0 + C, c_ca:c_ca + 4], reads=[ztm_d], writes=[sc[i]])
        yield
        a0, a1 = tk[0][i], tk[1][i]
        mk.op("dve", lambda en: en.tensor_tensor(a0.ap, U.ap[:, :, 0:C], bc_last(cw.ap[:, :, 0], C), ALU.mult), reads=[U, cw], writes=[a0])
        mk.op("pool", lambda en: en.tensor_tensor(a1.ap, U.ap[:, :, 1:C + 1], bc_last(cw.ap[:, :, 1], C), ALU.mult), reads=[U, cw], writes=[a1])
        mk.op("dve", lambda en: en.tensor_tensor(a0.ap, a0.ap, a1.ap, ALU.add), reads=[a0, a1], writes=[a0])
        mk.op("pool", lambda en: en.tensor_tensor(a1.ap, U.ap[:, :, 2:C + 2], bc_last(cw.ap[:, :, 2], C), ALU.mult), reads=[U, cw], writes=[a1])
        mk.op("dve", lambda en: en.tensor_tensor(a0.ap, a0.ap, a1.ap, ALU.add), reads=[a0, a1], writes=[a0])
        mk.op("pool", lambda en: en.tensor_tensor(a1.ap, U.ap[:, :, 3:C + 3], bc_last(cw.ap[:, :, 3], C), ALU.mult), reads=[U, cw], writes=[a1])
        mk.op("dve", lambda en: en.tensor_tensor(a0.ap, a0.ap, a1.ap, ALU.add), reads=[a0, a1], writes=[a0])
        mk.op("act", lambda en: en.activation(out=qkv[i].ap, in_=a0.ap, func=AF.Silu), reads=[a0], writes=[qkv[i]])
        yield
        mk.op("pool", lambda en: en.tensor_tensor(sq[i].ap, qkv[i].ap[:, 0:8, :], qkv[i].ap[:, 0:8, :], ALU.mult), reads=[qkv[i]], writes=[sq[i]])
        for half in range(2):
            pn = nps()
            mk.mm(pn.ap[0:64, 0:512], [(ones.ap[0:64, 0:64], sq[i].ap[:, half * 4:half * 4 + 4, :].rearrange("p b t -> p (b t)"))], reads=[ones, sq[i]], tr=pn)
            mk.op("act", lambda en, pn=pn, half=half: en.activation(out=rinv[i].ap[:, half * 4:half * 4 + 4, :], in_=pfm(pn), func=AF.Ln, bias=EPS), reads=[pn], writes=[rinv[i]])
        mk.op("act", lambda en: en.activation(out=rinv[i].ap, in_=rinv[i].ap, func=AF.Exp, scale=-0.5), reads=[rinv[i]], writes=[rinv[i]])
        mk.op("dve", lambda en: en.scalar_tensor_tensor(qkv[i].ap[:, 0:4, :], qkv[i].ap[:, 0:4, :], 64 ** -0.5, rinv[i].ap[:, 0:4, :], ALU.mult, ALU.mult),
              reads=[qkv[i], rinv[i]], writes=[qkv[i]])
        mk.op("dve", lambda en: en.tensor_tensor(qkv[i].ap[:, 4:8, :], qkv[i].ap[:, 4:8, :], rinv[i].ap[:, 4:8, :], ALU.mult),
              reads=[qkv[i], rinv[i]], writes=[qkv[i]])
        qT = lambda h: qkv[i].ap[:, h, :]
        kT = lambda h: qkv[i].ap[:, 4 + h, :]
        yield
        pt = nps()
        for b_ in range(8):
            mk.op("pe", lambda pe, b_=b_: pe.transpose(pt.ap[:, b_ * 64:(b_ + 1) * 64], qkv[i].ap[:, 4 + b_, :], id128.ap[0:64, 0:64]),
                  reads=[qkv[i], id128], writes=[pt], inc=(b_ == 7))
        mk.op("act", lambda en: en.copy(kv_tm[i].ap, pt.ap[:, :].rearrange("p (b d) -> p b d", b=8)), reads=[pt], writes=[kv_tm[i]])
        yield
        exp_sigmoid(mk, beta[i], beta[i].ap, sc[i], sc[i].ap[:, 0:4])
        mk.op("dve", lambda en: en.tensor_tensor(sp1[i].ap, sc[i].ap[:, 4:8], dtb.ap, ALU.add), reads=[sc[i], dtb], writes=[sp1[i]])
        mk.op("dve", lambda en: en.tensor_scalar(sp2[i].ap, sp1[i].ap, -1.0, None, ALU.mult), reads=[sp1[i]], writes=[sp2[i]])
        mk.op("dve", lambda en: en.tensor_tensor(sp2[i].ap, sp2[i].ap, sp1[i].ap, ALU.max), reads=[sp1[i], sp2[i]], writes=[sp2[i]])
        mk.op("act", lambda en: en.activation(out=sp2[i].ap, in_=sp2[i].ap, func=AF.Exp, scale=-1.0), reads=[sp2[i]], writes=[sp2[i]])
        mk.op("act", lambda en: en.activation(out=sp2[i].ap, in_=sp2[i].ap, func=AF.Ln, bias=1.0), reads=[sp2[i]], writes=[sp2[i]])
        mk.op("dve", lambda en: en.tensor_scalar(sp1[i].ap, sp1[i].ap, 0.0, None, ALU.max), reads=[sp1[i]], writes=[sp1[i]])
        mk.op("dve", lambda en: en.tensor_tensor(sp1[i].ap, sp1[i].ap, sp2[i].ap, ALU.add), reads=[sp1[i], sp2[i]], writes=[sp1[i]])
        mk.op("dve", lambda en: en.tensor_tensor(g[i].ap, sp1[i].ap, negA.ap, ALU.mult), reads=[sp1[i], negA], writes=[g[i]])
        yield
        mk.op("pool", lambda en: en.tensor_copy(rep[i].ap[:, 0:4, :], bc_last(beta[i].ap, 64)), reads=[beta[i]], writes=[rep[i]])
        mk.op("pool", lambda en: en.tensor_copy(rep[i].ap[:, 4:8, :], bc_last(g[i].ap, 64)), reads=[g[i]], writes=[rep[i]])
        pb1, pb2 = nps(), nps()
        for h in range(H):
            mk.mm(pb1.ap[0:64, h * C:(h + 1) * C], [(rep[i].ap[:, h, :], id128.ap)], reads=[rep[i], id128], tr=pb1)
        for h in range(H):
            mk.mm(pb2.ap[0:64, h * C:(h + 1) * C], [(rep[i].ap[:, 4 + h, :], triI.ap)], reads=[rep[i], triI], tr=pb2)
        mk.op("act", lambda en: en.copy(bb[i].ap, pfm(pb1)), reads=[pb1], writes=[bb[i]])
        mk.op("act", lambda en: en.activation(out=eb[i].ap, in_=pfm(pb2), func=AF.Exp), reads=[pb2], writes=[eb[i]])
        mk.op("dve", lambda en: en.tensor_tensor(kbT[i].ap, qkv[i].ap[:, 4:8, :], bb[i].ap, ALU.mult), reads=[qkv[i], bb[i]], writes=[kbT[i]])
        mk.op("dve", lambda en: en.tensor_tensor(kbeT[i].ap, kbT[i].ap, eb[i].ap, ALU.mult), reads=[kbT[i], eb[i]], writes=[kbeT[i]])
        mk.op("dve", lambda en: en.tensor_tensor(qeT[i].ap, qkv[i].ap[:, 0:4, :], eb[i].ap, ALU.mult), reads=[qkv[i], eb[i]], writes=[qeT[i]])
        yield
        mk.op("pool", lambda en: en.tensor_tensor(Gt[i].ap, bc_mid(triI.ap, H), bc_last(g[i].ap, C), ALU.mult), reads=[triI, g[i]], writes=[Gt[i]])
        pdT, pd = nps(), nps()
        for h in range(H):
            mk.mm(pdT.ap[:, h * C:(h + 1) * C], [(lowS.ap, Gt[i].ap[:, h, :])], reads=[lowS, Gt[i]], tr=pdT)
        for h in range(H):
            mk.mm(pd.ap[:, h * C:(h + 1) * C], [(Gt[i].ap[:, h, :], lowS.ap)], reads=[lowS, Gt[i]], tr=pd)
        mk.op("act", lambda en: en.activation(out=ET[i].ap, in_=psq(pdT), func=AF.Exp), reads=[pdT], writes=[ET[i]])
        mk.op("act", lambda en: en.activation(out=E[i].ap, in_=psq(pd), func=AF.Exp), reads=[pd], writes=[E[i]])
        mk.op("pool", lambda en: en.tensor_tensor(ETi[i].ap, ET[i].ap, bc_mid(triI.ap, H), ALU.mult), reads=[ET[i], triI], writes=[ETi[i]])
        mk.op("dve", lambda en: en.tensor_tensor(ET[i].ap, ET[i].ap, bc_mid(negtriS.ap, H), ALU.mult), reads=[ET[i], negtriS], writes=[ET[i]])
        mk.op("pool", lambda en: en.tensor_tensor(E[i].ap, E[i].ap, bc_mid(neglowS.ap, H), ALU.mult), reads=[E[i], neglowS], writes=[E[i]])
        yield
        pc = nps()
        mk.mm(pc.ap[:, 0:4], [(lowS.ap, g[i].ap)], reads=[lowS, g[i]], tr=pc)
        mk.op("act", lambda en: en.activation(out=elb[i].ap, in_=pc.ap[:, 0:4], func=AF.Exp), reads=[pc], writes=[elb[i]])
        mk.op("dve", lambda en: en.tensor_tensor(kd[i].ap, kv_tm[i].ap[:, 0:4, :], bc_last(elb[i].ap, 64), ALU.mult), reads=[kv_tm[i], elb[i]], writes=[kd[i]])
        mk.op("dve", lambda en: en.tensor_tensor(bv[i].ap, kv_tm[i].ap[:, 4:8, :], bc_last(beta[i].ap, 64), ALU.mult), reads=[kv_tm[i], beta[i]], writes=[bv[i]])
        yield
        pu, pl, pq = nps(), nps(), nps()
        for h in range(H):
            mk.mm(pu.ap[:, h * C:(h + 1) * C], [(kT(h), kbT[i].ap[:, h, :])], reads=[qkv[i], kbT[i]], tr=pu)
        for h in range(H):
            mk.mm(pl.ap[:, h * C:(h + 1) * C], [(kbT[i].ap[:, h, :], kT(h))], reads=[qkv[i], kbT[i]], tr=pl)
        for h in range(H):
            mk.mm(pq.ap[:, h * C:(h + 1) * C], [(kT(h), qT(h))], reads=[qkv[i]], tr=pq)
        mk.op("dve", lambda en: en.tensor_tensor(NU[0].ap, psq(pu), ET[i].ap, ALU.mult), reads=[pu, ET[i]], writes=[NU[0]])
        mk.op("dve", lambda en: en.tensor_tensor(NL[0].ap, psq(pl), E[i].ap, ALU.mult), reads=[pl, E[i]], writes=[NL[0]])
        mk.op("dve", lambda en: en.tensor_tensor(QKT[i].ap, psq(pq), ETi[i].ap, ALU.mult), reads=[pq, ETi[i]], writes=[QKT[i]])
        mk.op("pool", lambda en: en.tensor_tensor(P[i].ap, NU[0].ap, bc_mid(id128.ap, H), ALU.add), reads=[NU[0], id128], writes=[P[i]])
        yield
        cur = 0
        for j in range(1, 7):
            nxt = 1 - cur
            pu, pl = nps(), nps()
            for h in range(H):
                mk.mm(pu.ap[:, h * C:(h + 1) * C], [(NL[cur].ap[:, h, :], NU[cur].ap[:, h, :])], reads=[NL[cur], NU[cur]], tr=pu)
            for h in range(H):
                mk.mm(pl.ap[:, h * C:(h + 1) * C], [(NU[cur].ap[:, h, :], NL[cur].ap[:, h, :])], reads=[NL[cur], NU[cur]], tr=pl)
            mk.op("act", lambda en, nxt=nxt, pu=pu: en.copy(NU[nxt].ap, psq(pu)), reads=[pu], writes=[NU[nxt]])
            mk.op("dve", lambda en, nxt=nxt, pl=pl: en.tensor_copy(NL[nxt].ap, psq(pl)), reads=[pl], writes=[NL[nxt]])
            pp = nps()
            for h in range(H):
                mk.mm(pp.ap[:, h * C:(h + 1) * C], [(NL[nxt].ap[:, h, :], P[i].ap[:, h, :])], reads=[NL[nxt], P[i]], tr=pp)
            mk.op("dve", lambda en, pp=pp: en.tensor_tensor(P[i].ap, P[i].ap, psq(pp), ALU.add), reads=[P[i], pp], writes=[P[i]])
            cur = nxt
            yield
        if c > 0:
            yield ("wait", ("S", c - 1))
        p1 = nps()
        for h in range(H):
            mk.mm(p1.ap[:, h * 64:(h + 1) * 64], [(kbeT[i].ap[:, h, :], S.ap[:, h, :])], reads=[kbeT[i], S], tr=p1)
        mk.op("dve", lambda en: en.tensor_tensor(rhs[i].ap, bv[i].ap, ptm(p1), ALU.subtract), reads=[bv[i], p1], writes=[rhs[i]])
        p2 = nps()
        for h in range(H):
            mk.mm(p2.ap[:, h * 64:(h + 1) * 64], [(P[i].ap[:, h, :], rhs[i].ap[:, h, :])], reads=[P[i], rhs[i]], tr=p2)
        mk.op("act", lambda en: en.copy(vn[i].ap, ptm(p2)), reads=[p2], writes=[vn[i]])
        po, p4 = nps(), nps()
        for h in range(H):
            mk.mm(po.ap[:, h * 64:(h + 1) * 64], [(qeT[i].ap[:, h, :], S.ap[:, h, :]), (QKT[i].ap[:, h, :], vn[i].ap[:, h, :])],
                  reads=[qeT[i], S, QKT[i], vn[i]], tr=po)
        for h in range(H):
            mk.mm(p4.ap[0:64, h * 64:(h + 1) * 64], [(kd[i].ap[:, h, :], vn[i].ap[:, h, :])], reads=[kd[i], vn[i]], tr=p4)
        mk.op("dve", lambda en: en.tensor_tensor(S.ap, S.ap, bc_last(eb[i].ap[:, :, C - 1], 64), ALU.mult), reads=[S, eb[i]], writes=[S])
        mk.op("dve", lambda en: en.tensor_tensor(S.ap, S.ap, p4.ap[0:64, 0:256].rearrange("p (h t) -> p h t", h=H), ALU.add), reads=[S, p4], writes=[S])
        yield ("done", ("S", c))
        mk.op("act", lambda en: en.copy(of[i].ap, po.ap[:, 0:256]), reads=[po], writes=[of[i]])
        head_rmsnorm(mk, of[i], H, 64, gO, yo[i], scr[i], st[i])
        exp_silu(mk, cz[i], cz[i].ap, cz[i], cz[i].ap, scr[i], scr[i].ap)
        mk.op("dve", lambda en: en.tensor_tensor(yo[i].ap, yo[i].ap, cz[i].ap, ALU.mult), reads=[yo[i], cz[i]], writes=[yo[i]])
        mk.dma("pool", y_d.ap[t0:t0 + C, ycol:ycol + 256], yo[i].ap, reads=[yo[i]], writes=[y_d])
    if bg is not None:
        run_pipelined_multi([{"gens": (body(c) for c in range(T // C)), "width": 2}, {"gens": bg, "width": 1, "period": 2}])
    else:
        run_pipelined((body(c) for c in range(T // C)), 2)
    mk.phase_end()


def gdn_consts():
    a = np.arange(64)
    triI = (a[:, None] <= a[None, :]).astype(np.float32)
    b = np.arange(128)
    gI = (b[:, None] <= b[None, :]).astype(np.float32)
    gS = (b[:, None] < b[None, :]).astype(np.float32)
    gL = (b[:, None] > b[None, :]).astype(np.float32)
    return {"triI": triI, "g_triI": gI, "g_lowS": gL, "g_negtriS": -gS, "g_neglowS": -gL,
            "g_id": np.eye(128, dtype=np.float32), "g_ones": np.ones((128, 128), np.float32)}


def nsa_consts(T):
    t = np.arange(T)
    inv = 1.0 / (10000.0 ** (np.arange(0, 64, 2, dtype=np.float32) / 64))
    ang = t[:, None].astype(np.float32) * inv[None, :].astype(np.float32)
    ncp = T // 16
    n = np.arange(ncp)
    ncmp = (T - 32) // 16 + 1
    cm = ((16 * n[None, :] + 31 <= t[:, None]) & (n[None, :] < ncmp)).astype(np.float32)
    ns = T // 64
    j = np.arange(ns)[None, :]
    cur = (t // 64)[:, None]
    valid = j <= cur
    forced = (j == 0) | (j == cur) | (j == cur - 1)
    m1 = (valid & ~forced).astype(np.float32)
    c2 = (1e6 * (valid & forced) - 1.0 * (~valid)).astype(np.float32)
    a = np.arange(128)
    return {"cos": np.cos(ang).astype(np.float32), "sin": np.sin(ang).astype(np.float32),
            "cmpm": cm, "cmpmT": np.ascontiguousarray(cm.T), "selm1": m1, "selc2": c2,
            "causT": (a[:, None] <= a[None, :]).astype(np.float32), "farT": (a[:, None] > a[None, :]).astype(np.float32),
            "esel": (np.arange(ns)[:, None] == (t // 64)[None, :]).astype(np.float32)}


def nsa_n1(mk, ztm_d, c_nq, c_nk, c_nv, T, qn_d, kn_d, cst, ident_d, QT_d, KT_d, VA_d, defer=False):
    NT = T // 128
    mk.phase_begin()
    ident, identf = load_consts_ident(mk, ident_d)
    gqk = mk.sb([128, 14, 64], F32, "gqk")
    mk.dma("sp", gqk.ap[:, 0:8, :], bc_mid(bcast_rows(qn_d.ap, 128), 8), reads=[qn_d], writes=[gqk])
    for ty in range(3):
        mk.dma("sp", gqk.ap[:, 8 + 2 * ty:10 + 2 * ty, :], bc_mid(bcast_rows(kn_d.ap[ty, :], 128), 2), reads=[kn_d], writes=[gqk])
    mk.op("dve", lambda en: en.tensor_scalar(gqk.ap[:, 0:8, :], gqk.ap[:, 0:8, :], 64 ** -0.5, None, ALU.mult), reads=[gqk], writes=[gqk])
    NB = 2 if defer else 3
    xin = [mk.sb([128, 14 * 64], F32, "xin") for _ in range(NB)]
    vin = [mk.sb([128, 256], F32, "vin") for _ in range(NB)]
    cs = [mk.sb([128, 64], F32, "cs") for _ in range(NB)]
    sq = [mk.sb([128, 14 * 64], F32, "sq") for _ in range(NB)]
    st = [mk.sb([128, 42], F32, "st") for _ in range(NB)]
    xn = [mk.sb([128, 14, 64], F32, "xn") for _ in range(NB)]
    r1 = [mk.sb([128, 14, 32], F32, "r1") for _ in range(NB)]
    r2 = [mk.sb([128, 14, 32], F32, "r2") for _ in range(NB)]
    xr = [mk.sb([128, 14, 64], BF16, "xr") for _ in range(NB)]
    xT = [mk.sb([64, 14, 128], BF16, "xT") for _ in range(NB)]
    va = [mk.sb([128, 4, 65], BF16, "va") for _ in range(NB)]
    NPS_ = 1 if defer else NB
    psA = [mk.ps([64, 7, 128], BF16, "psA") for _ in range(NPS_)]
    psB = [mk.ps([64, 7, 128], BF16, "psB") for _ in range(NPS_)]
    def n1body(tt):
        i = tt % NB
        t0 = tt * 128
        X = xin[i]
        mk.dma("sp", X.ap[:, 0:512], ztm_d.ap[t0:t0 + 128, c_nq:c_nq + 512], reads=[ztm_d], writes=[X])
        mk.dma("sp", X.ap[:, 512:896], ztm_d.ap[t0:t0 + 128, c_nk:c_nk + 384], reads=[ztm_d], writes=[X])
        mk.dma("sp", vin[i].ap, ztm_d.ap[t0:t0 + 128, c_nv:c_nv + 256], reads=[ztm_d], writes=[vin[i]])
        mk.dma("sp", cs[i].ap[:, 0:32], cst["cos"].ap[t0:t0 + 128, :], reads=[cst["cos"]], writes=[cs[i]])
        mk.dma("sp", cs[i].ap[:, 32:64], cst["sin"].ap[t0:t0 + 128, :], reads=[cst["sin"]], writes=[cs[i]])
        yield
        X3 = X.ap[:, :].rearrange("p (h d) -> p h d", h=14)
        S3 = sq[i].ap[:, :].rearrange("p (h d) -> p h d", h=14)
        mk.op("act", lambda en, i=i, X=X: en.activation(out=sq[i].ap, in_=X.ap, func=AF.Square), reads=[X], writes=[sq[i]])
        mk.op("dve", lambda en, i=i, S3=S3: en.tensor_reduce(st[i].ap[:, 0:14], S3, AX.X, ALU.add), reads=[sq[i]], writes=[st[i]])
        mk.op("act", lambda en, i=i: en.activation(out=st[i].ap[:, 14:28], in_=st[i].ap[:, 0:14], func=AF.Ln, scale=1.0 / 64, bias=EPS),
              reads=[st[i]], writes=[st[i]])
        mk.op("act", lambda en, i=i: en.activation(out=st[i].ap[:, 28:42], in_=st[i].ap[:, 14:28], func=AF.Exp, scale=-0.5), reads=[st[i]], writes=[st[i]])
        mk.op("dve", lambda en, i=i, X3=X3: en.tensor_tensor(xn[i].ap, X3, bc_last(st[i].ap[:, 28:42], 64), ALU.mult), reads=[X, st[i]], writes=[xn[i]])
        mk.op("pool", lambda en, i=i: en.tensor_tensor(xn[i].ap, xn[i].ap, gqk.ap, ALU.mult), reads=[xn[i], gqk], writes=[xn[i]])
        yield
        cb_ = bc_mid(cs[i].ap[:, 0:32], 14)
        sb_ = bc_mid(cs[i].ap[:, 32:64], 14)
        x1 = xn[i].ap[:, :, 0:32]
        x2 = xn[i].ap[:, :, 32:64]
        mk.op("dve", lambda en, i=i, x1=x1, cb_=cb_: en.tensor_tensor(r1[i].ap, x1, cb_, ALU.mult), reads=[xn[i], cs[i]], writes=[r1[i]])
        mk.op("pool", lambda en, i=i, x2=x2, sb_=sb_: en.tensor_tensor(r2[i].ap, x2, sb_, ALU.mult), reads=[xn[i], cs[i]], writes=[r2[i]])
        mk.op("dve", lambda en, i=i: en.tensor_tensor(xr[i].ap[:, :, 0:32], r1[i].ap, r2[i].ap, ALU.subtract), reads=[r1[i], r2[i]], writes=[xr[i]])
        mk.op("dve", lambda en, i=i, x2=x2, cb_=cb_: en.tensor_tensor(r1[i].ap, x2, cb_, ALU.mult), reads=[xn[i], cs[i]], writes=[r1[i]])
        mk.op("pool", lambda en, i=i, x1=x1, sb_=sb_: en.tensor_tensor(r2[i].ap, x1, sb_, ALU.mult), reads=[xn[i], cs[i]], writes=[r2[i]])
        mk.op("dve", lambda en, i=i: en.tensor_tensor(xr[i].ap[:, :, 32:64], r1[i].ap, r2[i].ap, ALU.add), reads=[r1[i], r2[i]], writes=[xr[i]])
        yield
        for half, psx in ((0, psA[i % NPS_]), (1, psB[i % NPS_])):
            for hh in range(7):
                h = half * 7 + hh
                mk.op("pe", lambda pe, h=h, hh=hh, i=i, psx=psx: pe.transpose(psx.ap[:, hh, :], xr[i].ap[:, h, :], ident.ap),
                      reads=[xr[i], ident], writes=[psx], inc=(hh == 6))
            if half == 0:
                mk.op("act", lambda en, i=i, psx=psx: en.copy(xT[i].ap[:, 0:7, :], psx.ap), reads=[psx], writes=[xT[i]])
            else:
                mk.op("dve", lambda en, i=i, psx=psx: en.tensor_copy(xT[i].ap[:, 7:14, :], psx.ap), reads=[psx], writes=[xT[i]])
        yield
        mk.dma("pool", QT_d.ap[:, :, t0:t0 + 128], xT[i].ap[:, 0:8, :], reads=[xT[i]], writes=[QT_d])
        mk.dma("pool", KT_d.ap[:, :, t0:t0 + 128], xT[i].ap[:, 8:14, :], reads=[xT[i]], writes=[KT_d])
        mk.op("pool", lambda en, i=i: en.memset(va[i].ap[:, :, 64:65], 1.0), writes=[va[i]])
        mk.op("act", lambda en, i=i: en.copy(va[i].ap[:, :, 0:64], vin[i].ap[:, :].rearrange("p (h d) -> p h d", h=4)), reads=[vin[i]], writes=[va[i]])
        mk.dma("pool", VA_d.ap[t0:t0 + 128, :, :], va[i].ap, reads=[va[i]], writes=[VA_d])
    if defer:
        return (n1body(tt) for tt in range(NT)), (lambda: mk.phase_end())
    run_pipelined((n1body(tt) for tt in range(NT)), NB, stagger=2)
    mk.phase_end()
    return None, None


def stage_nsa(mk, ztm_d, c_nq, c_nk, c_nv, c_ng, zfm_d, r_vc, y_d, T, qn_d, kn_d, posk_d, posv_d, w1k_d, w2k_d, w1v_d, w2v_d,
              cst, ident_d, QT_d, KT_d, VA_d, bg=None, skip_n1=False):
    NT = T // 128
    ncp = T // 16
    ncmp = (T - 32) // 16 + 1
    ns = T // 64
    mk.phase_begin()
    ident, identf = load_consts_ident(mk, ident_d)
    if not skip_n1:
        mk.mark("nsa_N1")
        nsa_n1(mk, ztm_d, c_nq, c_nk, c_nv, T, qn_d, kn_d, cst, ident_d, QT_d, KT_d, VA_d)
    mk.mark("nsa_res")
    KsT = mk.sb([128, 2, T], BF16, "KsT")
    KwT = mk.sb([128, 2, T], BF16, "KwT")
    mk.op("pool", lambda en: en.memset(KsT.ap[64:128, :, :], 0.0), writes=[KsT])
    mk.op("pool", lambda en: en.memset(KwT.ap[64:128, :, :], 0.0), writes=[KwT])
    mk.dma("sp", KsT.ap[0:64, :, :], KT_d.ap[:, 2:4, :], reads=[KT_d], writes=[KsT])
    mk.dma("sp", KwT.ap[0:64, :, :], KT_d.ap[:, 4:6, :], reads=[KT_d], writes=[KwT])
    VA = mk.sb([128, NT, 4, 65], BF16, "VA")
    mk.dma("sp", VA.ap, VA_d.ap.rearrange("(n p) f e -> p n f e", p=128), reads=[VA_d], writes=[VA])
    kcT = mk.sb([128, 2, ncp], BF16, "kcT")
    vcA = mk.sb([128, 2, ncp // 128, 65], BF16, "vcA")
    mk.op("pool", lambda en: en.memset(kcT.ap, 0.0), writes=[kcT])
    mk.op("pool", lambda en: en.memset(vcA.ap, 0.0), writes=[vcA])
    mk.op("pool", lambda en: en.memset(vcA.ap[:, :, :, 64:65], 1.0), writes=[vcA])
    mk.mark("nsa_N2")
    mk.phase_begin()
    pss = [mk.ps([128, 512], F32, "psm") for _ in range(4)]
    for which, (w1_d, w2_d, pos_d) in enumerate(((w1k_d, w2k_d, posk_d), (w1v_d, w2v_d, posv_d))):
        if which == 1:
            mk.phase_end()
        mk.phase_begin()
        w1f = mk.sb([64, 32, 128], F32, "w1f")
        w1 = mk.sb([64, 32, 128], BF16, "w1")
        mk.dma("sp", w1f.ap, w1_d.ap.rearrange("(l d) h -> d l h", d=64), reads=[w1_d], writes=[w1f])
        mk.op("pool", lambda en, w1=w1, w1f=w1f: en.tensor_copy(w1.ap, w1f.ap), reads=[w1f], writes=[w1])
        w2f = mk.sb([128, 64], F32, "w2f")
        w2 = mk.sb([128, 64], BF16, "w2")
        mk.dma("sp", w2f.ap, w2_d.ap, reads=[w2_d], writes=[w2f])
        mk.op("pool", lambda en, w2=w2, w2f=w2f: en.tensor_copy(w2.ap, w2f.ap), reads=[w2f], writes=[w2])
        posf = mk.sb([64, 32], F32, "posf")
        posb = mk.sb([64, 32], BF16, "posb")
        mk.dma("sp", posf.ap, pos_d.ap.rearrange("l d -> d l"), reads=[pos_d], writes=[posf], allow_slow_non_contiguous=True)
        mk.op("pool", lambda en, posb=posb, posf=posf: en.tensor_copy(posb.ap, posf.ap), reads=[posf], writes=[posb])
        bias = mk.sb([128, 1], F32, "bias")
        pbias = pss[0]
        mk.mm(pbias.ap[:, 0:1], [(w1.ap[:, l, :], posb.ap[:, l:l + 1]) for l in range(32)], reads=[w1, posb], tr=pbias)
        mk.op("act", lambda en, bias=bias, pbias=pbias: en.copy(bias.ap, pbias.ap[:, 0:1]), reads=[pbias], writes=[bias])
        for g in range(2):
            XT = mk.sb([64, T], BF16, "XT")
            if which == 0:
                mk.dma("sp", XT.ap, KT_d.ap[:, g, :], reads=[KT_d], writes=[XT])
            else:
                XTf = mk.sb([64, T], F32, "XTf")
                mk.dma("sp", XTf.ap, zfm_d.ap[r_vc + g * 64:r_vc + (g + 1) * 64, :], reads=[zfm_d], writes=[XTf])
                mk.op("pool", lambda en, XT=XT, XTf=XTf: en.tensor_copy(XT.ap, XTf.ap), reads=[XTf], writes=[XT])
            ph = pss[1 + g]
            mk.mm(ph.ap[:, 0:ncmp], [(w1.ap[:, l, :], XT.ap[:, l:l + 16 * (ncmp - 1) + 1:16]) for l in range(32)], reads=[w1, XT], tr=ph)
            xs = mk.sb([128, ncp], F32, "xs")
            x2 = mk.sb([128, ncp], F32, "x2")
            ge = mk.sb([128, ncp], BF16, "ge")
            mk.op("pool", lambda en, ge=ge: en.memset(ge.ap, 0.0), writes=[ge])
            mk.op("act", lambda en, xs=xs, ph=ph, bias=bias: en.activation(out=xs.ap[:, 0:ncmp], in_=ph.ap[:, 0:ncmp], func=AF.Identity, bias=bias.ap[:, 0:1]),
                  reads=[ph, bias], writes=[xs])
            mk.op("pool", lambda en, xs=xs, x2=x2: en.tensor_tensor(x2.ap[:, 0:ncmp], xs.ap[:, 0:ncmp], xs.ap[:, 0:ncmp], ALU.mult), reads=[xs], writes=[x2])
            mk.op("dve", lambda en, x2=x2: en.tensor_scalar(x2.ap[:, 0:ncmp], x2.ap[:, 0:ncmp], 0.044715, 1.0, ALU.mult, ALU.add), reads=[x2], writes=[x2])
            mk.op("dve", lambda en, xs=xs, x2=x2: en.tensor_tensor(x2.ap[:, 0:ncmp], x2.ap[:, 0:ncmp], xs.ap[:, 0:ncmp], ALU.mult), reads=[xs, x2], writes=[x2])
            mk.op("act", lambda en, x2=x2: en.activation(out=x2.ap[:, 0:ncmp], in_=x2.ap[:, 0:ncmp], func=AF.Sigmoid, scale=1.5957691216057308),
                  reads=[x2], writes=[x2])
            mk.op("dve", lambda en, xs=xs, x2=x2, ge=ge: en.tensor_tensor(ge.ap[:, 0:ncmp], x2.ap[:, 0:ncmp], xs.ap[:, 0:ncmp], ALU.mult), reads=[xs, x2], writes=[ge])
            po = pss[3]
            if which == 0:
                mk.mm(po.ap[0:64, 0:ncmp], [(w2.ap, ge.ap[:, 0:ncmp])], reads=[w2, ge], tr=po)
                mk.op("act", lambda en, g=g, po=po: en.copy(kcT.ap[0:64, g, 0:ncmp], po.ap[0:64, 0:ncmp]), reads=[po], writes=[kcT])
            else:
                for nt in range(ncp // 128):
                    mk.mm(po.ap[:, nt * 64:(nt + 1) * 64], [(ge.ap[:, nt * 128:(nt + 1) * 128], w2.ap)], reads=[w2, ge], tr=po)
                mk.op("act", lambda en, g=g, po=po: en.copy(vcA.ap[:, g, :, 0:64], po.ap[:, 0:(ncp // 128) * 64].rearrange("p (n d) -> p n d", d=64)),
                      reads=[po], writes=[vcA])
    mk.phase_end()
    mk.phase_end()
    mk.mark("nsa_N3")
    mk.phase_begin()
    causT = mk.sb([128, 128], F32, "causT")
    farT = mk.sb([128, 128], F32, "farT")
    mk.dma("sp", causT.ap, cst["causT"].ap, reads=[cst["causT"]], writes=[causT])
    mk.dma("sp", farT.ap, cst["farT"].ap, reads=[cst["farT"]], writes=[farT])
    esel = mk.sb([ns, T], BF16, "esel")
    mk.phase_begin()
    eself = mk.sb([ns, T], F32, "eself")
    mk.dma("sp", eself.ap, cst["esel"].ap, reads=[cst["esel"]], writes=[eself])
    mk.op("pool", lambda en: en.tensor_copy(esel.ap, eself.ap), reads=[eself], writes=[esel])
    mk.phase_end()
    NQ = ncp // 128
    NS = 2
    psS = [mk.ps([128, 512], F32, "psS") for _ in range(NS)]
    psO = [mk.ps([128, 512], F32, "psO") for _ in range(NS)]
    psT = [[mk.ps([128, 512], F32, "psST") for _ in range(1 if bg is not None else 2)] for _ in range(NS)]
    psM = psS
    mk.ndram = getattr(mk, "ndram", 0) + 1
    selTd = [mk.dram("selTd%d_%d" % (mk.ndram, j), [ns, 128], BF16) for j in range(NS)]
    def mkb(shape, name, dt=F32, n=NS):
        return [mk.sb(list(shape), dt, name) for _ in range(n)]
    QT = mkb([128, 4, 128], "QT", BF16)
    for q_ in QT:
        mk.op("pool", lambda en, q_=q_: en.memset(q_.ap[64:128, :, :], 0.0), writes=[q_])
    gl = mkb([128, 24], "gl")
    cm, cmT = mkb([128, ncp], "cm"), mkb([128, NQ, 128], "cmT")
    m1t, c2t = mkb([128, ns], "m1t"), mkb([128, ns], "c2t")
    sS = mkb([128, 4, ncp], "sS")
    sE = sS
    stt = mkb([128, 16], "stt")
    ph = mkb([128, ncp + 1], "ph")
    imp, wk = mkb([128, ns], "imp"), mkb([128, ns], "wk")
    m8 = mkb([128, 16], "m8")
    selb = mkb([128, ns], "selb")
    selT = mkb([ns, 128], "selT", BF16)
    eT = [mkb([128, 4, 128], "eT", BF16, 3) for _ in range(NS)]
    causB = mk.sb([128, 128], BF16, "causB")
    farB = mk.sb([128, 128], BF16, "farB")
    mk.op("dve", lambda en: en.tensor_copy(causB.ap, causT.ap), reads=[causT], writes=[causB])
    mk.op("dve", lambda en: en.tensor_copy(farB.ap, farT.ap), reads=[farT], writes=[farB])
    cmTb = mkb([128, NQ, 128], "cmTb", BF16)
    bm = [mkb([128, 128], "bm", BF16, 2) for _ in range(NS)]
    rr = mkb([128, 12], "rr")
    acc = mkb([128, 4, 64], "acc")
    tmpo = mkb([128, 4, 64], "tmpo")
    PTall = mkb([128, NT, 4, 128], "PTall", BF16)

    def attn_branch(i, tiles, Ops):
        PA = PTall[i]
        QTi = QT[i]
        n = len(tiles)
        for idx, (KT_ap, V_ap, mask_fn) in enumerate(tiles):
            pt = psT[i][idx % len(psT[i])]
            mk.mm(pt.ap, [(KT_ap, QTi.ap[:, :, :].rearrange("p h q -> p (h q)"))], reads=[QTi, KsT, KwT, kcT], tr=pt)
            e = eT[i][idx % 3]
            p4 = pt.ap[:, :].rearrange("p (h q) -> p h q", h=4)
            msk = mask_fn(idx)
            if msk is None:
                mk.op("act", lambda en, idx=idx, p4=p4: en.activation(out=PA.ap[:, idx, :, :], in_=p4, func=AF.Exp), reads=[pt], writes=[PA])
            else:
                mk.op("act", lambda en, e=e, p4=p4: en.activation(out=e.ap, in_=p4, func=AF.Exp), reads=[pt], writes=[e])
                mb, mreads = msk
                mk.op("dve", lambda en, idx=idx, e=e, mb=mb: en.tensor_tensor(PA.ap[:, idx, :, :], e.ap, bc_mid(mb, 4), ALU.mult), reads=[e] + mreads, writes=[PA])
            yield
        for h in range(4):
            for idx, (KT_ap, V_ap, mask_fn) in enumerate(tiles):
                mk.op("pe", lambda pe, h=h, idx=idx, V_ap=V_ap: pe.matmul(Ops.ap[:, h * 65:(h + 1) * 65], PA.ap[:, idx, h, :], V_ap, start=(idx == 0), stop=(idx == n - 1)),
                      reads=[PA, VA, vcA], writes=[Ops], inc=(idx == n - 1))
            yield

    def combine(i, g, x, first):
        Ops = psO[i]
        O3 = Ops.ap[:, 0:260].rearrange("p (h e) -> p h e", e=65)
        mk.op("dve", lambda en: en.tensor_scalar(rr[i].ap[:, 4 * x:4 * x + 4], O3[:, :, 64], 1e-30, None, ALU.max), reads=[Ops], writes=[rr[i]])
        mk.op("dve", lambda en: en.reciprocal(rr[i].ap[:, 4 * x:4 * x + 4], rr[i].ap[:, 4 * x:4 * x + 4]), reads=[rr[i]], writes=[rr[i]])
        gx = gl[i].ap[:, g * 12:(g + 1) * 12].rearrange("p (h x) -> p h x", x=3)[:, :, x]
        mk.op("dve", lambda en: en.tensor_tensor(rr[i].ap[:, 4 * x:4 * x + 4], rr[i].ap[:, 4 * x:4 * x + 4], gx, ALU.mult), reads=[rr[i], gl[i]], writes=[rr[i]])
        if first:
            mk.op("dve", lambda en: en.tensor_tensor(acc[i].ap, O3[:, :, 0:64], bc_last(rr[i].ap[:, 4 * x:4 * x + 4], 64), ALU.mult), reads=[Ops, rr[i]], writes=[acc[i]])
        else:
            mk.op("dve", lambda en: en.tensor_tensor(tmpo[i].ap, O3[:, :, 0:64], bc_last(rr[i].ap[:, 4 * x:4 * x + 4], 64), ALU.mult), reads=[Ops, rr[i]], writes=[tmpo[i]])
            mk.op("pool", lambda en: en.tensor_tensor(acc[i].ap, acc[i].ap, tmpo[i].ap, ALU.add), reads=[acc[i], tmpo[i]], writes=[acc[i]])

    def body(it, g, qt):
        i = it % NS
        t0 = qt * 128
        mk.dma("sp", QT[i].ap[0:64, :, :], QT_d.ap[:, 4 * g:4 * g + 4, t0:t0 + 128], reads=[QT_d], writes=[QT[i]])
        mk.dma("sp", gl[i].ap, ztm_d.ap[t0:t0 + 128, c_ng:c_ng + 24], reads=[ztm_d], writes=[gl[i]])
        mk.dma("sp", cm[i].ap, cst["cmpm"].ap[t0:t0 + 128, :], reads=[cst["cmpm"]], writes=[cm[i]])
        mk.dma("sp", cmT[i].ap, cst["cmpmT"].ap[:, t0:t0 + 128].rearrange("(n p) q -> p n q", p=128), reads=[cst["cmpmT"]], writes=[cmT[i]])
        mk.dma("sp", m1t[i].ap, cst["selm1"].ap[t0:t0 + 128, :], reads=[cst["selm1"]], writes=[m1t[i]])
        mk.dma("sp", c2t[i].ap, cst["selc2"].ap[t0:t0 + 128, :], reads=[cst["selc2"]], writes=[c2t[i]])
        exp_sigmoid(mk, gl[i], gl[i].ap, gl[i], gl[i].ap)
        mk.op("pool", lambda en: en.tensor_copy(cmTb[i].ap, cmT[i].ap), reads=[cmT[i]], writes=[cmTb[i]])
        yield
        tl = []
        for kt in range(max(0, qt - 4), qt + 1):
            if kt == qt:
                mf = lambda k: (causB.ap, [causB])
            elif kt == qt - 4:
                mf = lambda k: (farB.ap, [farB])
            else:
                mf = lambda k: None
            tl.append((KwT.ap[:, g, kt * 128:(kt + 1) * 128], VA.ap[:, kt, 2 + g, :], mf))
        yield from attn_branch(i, tl, psO[i])
        combine(i, g, 2, True)
        yield
        for hp in range(2):
            for hh in range(2):
                mk.mm(psS[i].ap[:, hh * ncp:(hh + 1) * ncp], [(QT[i].ap[:, hp * 2 + hh, :], kcT.ap[:, g, :])], reads=[QT[i], kcT], tr=psS[i])
            mk.op("act", lambda en, hp=hp: en.copy(sS[i].ap[:, hp * 2:hp * 2 + 2, :], psS[i].ap[:, 0:2 * ncp].rearrange("p (h n) -> p h n", h=2)),
                  reads=[psS[i]], writes=[sS[i]])
        yield
        mk.op("dve", lambda en: en.tensor_reduce(stt[i].ap[:, 0:4], sS[i].ap, AX.X, ALU.max), reads=[sS[i]], writes=[stt[i]])
        mk.op("dve", lambda en: en.tensor_tensor(sS[i].ap, sS[i].ap, bc_last(stt[i].ap[:, 0:4], ncp), ALU.subtract), reads=[sS[i], stt[i]], writes=[sS[i]])
        mk.op("act", lambda en: en.activation(out=sE[i].ap, in_=sS[i].ap, func=AF.Exp), reads=[sS[i]], writes=[sE[i]])
        mk.op("pool", lambda en: en.tensor_tensor(sE[i].ap, sE[i].ap, bc_mid(cm[i].ap, 4), ALU.mult), reads=[sE[i], cm[i]], writes=[sE[i]])
        yield
        mk.op("dve", lambda en: en.tensor_reduce(stt[i].ap[:, 4:8], sE[i].ap, AX.X, ALU.add), reads=[sE[i]], writes=[stt[i]])
        mk.op("dve", lambda en: en.tensor_scalar(stt[i].ap[:, 4:8], stt[i].ap[:, 4:8], 1e-30, None, ALU.max), reads=[stt[i]], writes=[stt[i]])
        mk.op("dve", lambda en: en.reciprocal(stt[i].ap[:, 8:12], stt[i].ap[:, 4:8]), reads=[stt[i]], writes=[stt[i]])
        mk.op("dve", lambda en: en.tensor_tensor(sE[i].ap, sE[i].ap, bc_last(stt[i].ap[:, 8:12], ncp), ALU.mult), reads=[sE[i], stt[i]], writes=[sE[i]])
        yield
        mk.op("pool", lambda en: en.memset(ph[i].ap[:, 0:1], 0.0), writes=[ph[i]])
        mk.op("dve", lambda en: en.tensor_reduce(ph[i].ap[:, 1:ncp + 1], sE[i].ap[:, :, :].rearrange("p h n -> p n h"), AX.X, ALU.add),
              reads=[sE[i]], writes=[ph[i]])
        mk.op("dve", lambda en: en.tensor_reduce(imp[i].ap, ph[i].ap[:, 1:ncp + 1].rearrange("p (j f) -> p j f", f=4), AX.X, ALU.add),
              reads=[ph[i]], writes=[imp[i]])
        mk.op("dve", lambda en: en.tensor_reduce(wk[i].ap, ph[i].ap[:, 0:ncp].rearrange("p (j f) -> p j f", f=4), AX.X, ALU.add),
              reads=[ph[i]], writes=[wk[i]])
        yield
        mk.op("dve", lambda en: en.tensor_tensor(imp[i].ap, imp[i].ap, wk[i].ap, ALU.add), reads=[imp[i], wk[i]], writes=[imp[i]])
        mk.op("dve", lambda en: en.scalar_tensor_tensor(imp[i].ap, imp[i].ap, 16.0, m1t[i].ap, ALU.mult, ALU.mult), reads=[imp[i], m1t[i]], writes=[imp[i]])
        mk.op("dve", lambda en: en.tensor_tensor(imp[i].ap, imp[i].ap, c2t[i].ap, ALU.add), reads=[imp[i], c2t[i]], writes=[imp[i]])
        yield
        mk.op("dve", lambda en: en.max(out=m8[i].ap[:, 0:8], in_=imp[i].ap), reads=[imp[i]], writes=[m8[i]])
        mk.op("dve", lambda en: en.match_replace(out=wk[i].ap, in_to_replace=m8[i].ap[:, 0:8], in_values=imp[i].ap, imm_value=-2.0),
              reads=[imp[i], m8[i]], writes=[wk[i]])
        mk.op("dve", lambda en: en.max(out=m8[i].ap[:, 8:16], in_=wk[i].ap), reads=[wk[i]], writes=[m8[i]])
        yield
        mk.op("dve", lambda en: en.tensor_reduce(stt[i].ap[:, 12:13], m8[i].ap[:, 8:16], AX.X, ALU.min), reads=[m8[i]], writes=[stt[i]])
        mk.op("dve", lambda en: en.tensor_scalar(selb[i].ap, imp[i].ap, stt[i].ap[:, 12:13], None, ALU.is_ge), reads=[imp[i], stt[i]], writes=[selb[i]])
        mk.op("pe", lambda pe: pe.transpose(psM[i].ap[0:ns, 0:128], selb[i].ap, identf.ap), reads=[selb[i], identf], writes=[psM[i]])
        mk.op("act", lambda en: en.copy(selT[i].ap, psM[i].ap[0:ns, 0:128]), reads=[psM[i]], writes=[selT[i]])
        yield
        yield from attn_branch(i, [(kcT.ap[:, g, nt * 128:(nt + 1) * 128], vcA.ap[:, g, nt, :], (lambda k, nt=nt: (cmTb[i].ap[:, nt, :], [cmTb[i]])))
                                  for nt in range(NQ)], psO[i])
        combine(i, g, 0, False)
        yield
        tl = []
        for kt in range(qt + 1):
            def mf(k, kt=kt):
                b = bm[i][k % 2]
                mk.mm(psM[i].ap[:, 0:128], [(esel.ap[:, kt * 128:(kt + 1) * 128], selT[i].ap)], reads=[esel, selT[i]], tr=psM[i])
                if kt == qt:
                    mk.op("dve", lambda en: en.tensor_tensor(b.ap, psM[i].ap[:, 0:128], causT.ap, ALU.mult), reads=[psM[i], causT], writes=[b])
                else:
                    mk.op("act", lambda en: en.copy(b.ap, psM[i].ap[:, 0:128]), reads=[psM[i]], writes=[b])
                return (b.ap, [b])
            tl.append((KsT.ap[:, g, kt * 128:(kt + 1) * 128], VA.ap[:, kt, g, :], mf))
        yield from attn_branch(i, tl, psO[i])
        combine(i, g, 1, False)
        mk.dma("pool", y_d.ap[t0:t0 + 128, g * 256:(g + 1) * 256], acc[i].ap[:, :, :].rearrange("p h d -> p (h d)"), reads=[acc[i]], writes=[y_d])

    order = [(g, qt) for qt in range(NT) for g in range(2)]
    if bg is not None:
        run_pipelined_multi([{"gens": (body(it, g, qt) for it, (g, qt) in enumerate(order)), "width": NS},
                             {"gens": bg, "width": 1, "period": 3}])
    else:
        run_pipelined((body(it, g, qt) for it, (g, qt) in enumerate(order)), NS, stagger=14)
    mk.phase_end()
    mk.phase_end()


T_SEQ = 4096
TAIL_T = 2176
NTM, NFM = 2208, 1408
C_NQ, C_NK, C_NV, C_NG, C_HF, C_HI, C_HG, C_CZ, C_CB, C_CA = 0, 512, 896, 1152, 1176, 1432, 1688, 1944, 2200, 2204
R_VC, R_HQ, R_HF, R_QKV = 0, 128, 384, 640
_r = lambda a, b: list(range(a, b))
W_IN_COLS = (_r(0, 512) + _r(512, 640) + _r(768, 896) + _r(1024, 1152) + _r(896, 1024) + _r(1152, 1280) + _r(1280, 1304)
             + _r(1560, 1816) + _r(1816, 2072) + _r(2072, 2328) + _r(3096, 3352) + _r(3352, 3356) + _r(3356, 3360)
             + _r(640, 768) + _r(1304, 1560) + _r(1560, 1816) + _r(2328, 3096))
assert len(W_IN_COLS) == NTM + NFM

LAYER_KEYS = ["norm_mix", "w_in_sel", "w_gate", "nsa_q_norm", "nsa_k_norm", "cmp_pos_k", "cmp_pos_v", "cmp_k_w1", "cmp_k_w2",
              "cmp_v_w1", "cmp_v_w2", "hgrn_out_norm", "gdn_conv_t", "gdn_a_log", "gdn_dt_bias", "gdn_out_norm",
              "w_branch_a", "w_branch_b", "w_branch_c", "w_mix_out", "norm_cross", "xattn_wq", "xattn_q_norm", "xattn_k_norm",
              "xattn_wo", "norm_ffn", "ffn_w_up", "ffn_conv_t", "ffn_w_down"]


def const_arrays(T):
    c = dict(gdn_consts())
    c.update(nsa_consts(T))
    c["ident"] = np.eye(128, dtype=np.float32)
    return c


def layer_arrays(inputs, l):
    f = lambda a: np.ascontiguousarray(np.asarray(a, dtype=np.float32))
    w_in = np.asarray(inputs["w_in"][l])
    d = {"w_in_sel": f(w_in[:, W_IN_COLS]), "w_gate": f(w_in[:, 3360:6432]),
         "gdn_conv_t": f(np.asarray(inputs["gdn_conv"][l]).T), "ffn_conv_t": f(np.asarray(inputs["ffn_conv"][l]).T)}
    for k in LAYER_KEYS:
        if k not in d:
            d[k] = f(inputs[k][l])
    return d


def build_program(T=T_SEQ, depth=2, shapes=None, stop_after=None):
    mk = MK()
    mk.live.append([])
    ext = lambda name, shape: mk.dram(name, shape, kind="ExternalInput")
    x_in = ext("x", [T, D])
    mem_in = ext("mem", [256, D])
    mem_norm = ext("mem_norm", [D])
    mem_w_kv = ext("mem_w_kv", [D, D])
    lbl = ext("hgrn_lb_logits", [2, 256])
    cst = {k: ext("c_" + k, list(v.shape)) for k, v in const_arrays(T).items()}
    L = []
    for l in range(depth):
        L.append({k: ext("L%d_%s" % (l, k), list(shapes[k])) for k in LAYER_KEYS})
    out = mk.dram("out", [TAIL_T, D], kind="ExternalOutput")
    hsel = ext("hsel", [1])
    xh = [mk.dram("xh%d" % j, [TAIL_T, D]) for j in range(3)]
    xs = [mk.dram("xA", [T, D]), mk.dram("xB", [T, D])]
    Ztm = mk.dram("Ztm", [T, NTM])
    Zfm = mk.dram("Zfm", [NFM, T])
    Y = mk.dram("Y", [T, D])
    MKV = mk.dram("MKV", [256, D])
    QT_d = mk.dram("QT", [64, 8, T], BF16)
    KT_d = mk.dram("KT", [64, 6, T], BF16)
    VA_d = mk.dram("VA", [T, 4, 65], BF16)
    ident = cst["ident"]
    stage_proj(mk, mem_in, mem_norm, mem_w_kv, MKV, None, 256, D, 0, ident)
    cur = x_in
    nstage = 0
    def nxt(last=False):
        return out if last else xs[nstage % 2]
    for l in range(depth):
        W = L[l]
        mk.mark("stage_proj")
        stage_proj(mk, cur, W["norm_mix"], W["w_in_sel"], Ztm, Zfm, T, NTM, NFM, ident)
        mk.mark("stage_gdn")
        n1_gens, n1_fin = nsa_n1(mk, Ztm, C_NQ, C_NK, C_NV, T, W["nsa_q_norm"], W["nsa_k_norm"], cst, ident, QT_d, KT_d, VA_d, defer=True)
        stage_gdn(mk, Ztm, C_CZ, C_CB, C_CA, Zfm, R_QKV, Y, 768, T, W["gdn_conv_t"], W["gdn_a_log"], W["gdn_dt_bias"], W["gdn_out_norm"], cst, bg=n1_gens)
        n1_fin()
        mk.mark("stage_nsa")
        hg_gens, hg_fin = stage_hgrn(mk, Ztm, C_HF, C_HI, C_HG, Zfm, R_HQ, R_HF, Y, 512, T, lbl, l, W["hgrn_out_norm"], cst["triI"], ident, defer=True)
        stage_nsa(mk, Ztm, C_NQ, C_NK, C_NV, C_NG, Zfm, R_VC, Y, T, W["nsa_q_norm"], W["nsa_k_norm"], W["cmp_pos_k"], W["cmp_pos_v"],
                  W["cmp_k_w1"], W["cmp_k_w2"], W["cmp_v_w1"], W["cmp_v_w2"], cst, ident, QT_d, KT_d, VA_d, bg=hg_gens, skip_n1=True)
        hg_fin()
        if stop_after == ("mix", l):
            mk.dma("sp", out.ap, Y.ap, reads=[Y], writes=[out])
            break
        last = (l == depth - 1)
        Tt = TAIL_T if last else T
        xin, yin = cur, Y
        if last:
            mk.mark("stage_select")
            xin, yin = xh[0], xh[1]
            stage_select(mk, cur, xin, Tt, T - Tt, hsel)
            stage_select(mk, Y, yin, Tt, T - Tt, hsel)
        x1 = xh[2] if last else nxt(); nstage += 1
        mk.mark("stage_merge")
        stage_merge(mk, xin, yin, x1, Tt, W["norm_mix"], W["w_gate"], W["w_branch_a"], W["w_branch_b"], W["w_branch_c"], W["w_mix_out"], ident)
        x2 = xh[0] if last else nxt(); nstage += 1
        mk.mark("stage_xattn")
        stage_xattn(mk, x1, x2, Tt, W["norm_cross"], MKV, W["xattn_wq"], W["xattn_q_norm"], W["xattn_k_norm"], W["xattn_wo"], ident)
        x3 = out if last else nxt(); nstage += 1
        mk.mark("stage_ffn")
        stage_ffn(mk, x2, x3, Tt, W["norm_ffn"], W["ffn_w_up"], W["ffn_conv_t"], W["ffn_w_down"], ident)
        cur = x3
    mk.mark("end")
    mk.finish()
    return mk


_PROG = {}


def kernel(**inputs):
    x = np.asarray(inputs["x"], dtype=np.float32)
    B, T, _ = x.shape
    depth = np.asarray(inputs["w_in"]).shape[0]
    layers = [layer_arrays(inputs, l) for l in range(depth)]
    shapes = {k: v.shape for k, v in layers[0].items()}
    key = (T, depth)
    if key not in _PROG:
        _PROG[key] = build_program(T, depth, shapes)
    mk = _PROG[key]
    f = lambda a: np.ascontiguousarray(np.asarray(a, dtype=np.float32))
    common = {"mem_norm": f(inputs["mem_norm"]), "mem_w_kv": f(inputs["mem_w_kv"]), "hgrn_lb_logits": f(inputs["hgrn_lb_logits"])}
    for k, v in const_arrays(T).items():
        common["c_" + k] = v
    for l in range(depth):
        for k, v in layers[l].items():
            common["L%d_%s" % (l, k)] = v
    n = 8
    in_maps = []
    for c in range(n):
        b = c % B
        m = dict(common)
        m["x"] = f(x[b])
        m["mem"] = f(np.asarray(inputs["mem"])[b])
        m["hsel"] = np.full((1,), float(c // B), np.float32)
        in_maps.append(m)
    res = run_bass_kernel_spmd(mk.nc, in_maps, core_ids=list(range(n)))
    Th = T // 2
    outs = []
    for b in range(B):
        lo = np.asarray(res.results[b]["out"], dtype=np.float32)[0:Th]
        hi = np.asarray(res.results[B + b]["out"], dtype=np.float32)[TAIL_T - Th:TAIL_T]
        outs.append(np.concatenate([lo, hi], axis=0))
    return np.stack(outs, axis=0)
```

```python
import numpy as np
import concourse.bass as bass
import concourse.mybir as mybir
from concourse.bass_utils import run_bass_kernel_spmd

F32 = mybir.dt.float32
BF16 = mybir.dt.bfloat16
AF = mybir.ActivationFunctionType
ALU = mybir.AluOpType
AX = mybir.AxisListType
F32R = mybir.dt.float32r
FP32R = False


class Buf:
    __slots__ = ("ap", "w", "r", "name")

    def __init__(self, ap, name):
        self.ap = ap
        self.name = name
        self.w = None
        self.r = {}

    def __getitem__(self, k):
        return self.ap[k]


class MK:
    NDMA = 6

    def __init__(self):
        self.nc = bass.Bass("TRN2", target_bir_lowering=False)
        nc = self.nc
        self.eng = {"pe": nc.tensor, "act": nc.scalar, "dve": nc.vector, "pool": nc.gpsimd, "sp": nc.sync}
        self.sem = {}
        self.cnt = {}
        for e in self.eng:
            self.sem[e] = nc.alloc_semaphore("s_" + e)
            self.cnt[e] = 0
        self.dq = {}
        for q in ("sp", "pool", "act"):
            ks = []
            for i in range(self.NDMA):
                k = "d_%s%d" % (q, i)
                self.sem[k] = nc.alloc_semaphore(k)
                self.cnt[k] = 0
                ks.append(k)
            self.dq[q] = [ks, 0]
        self.obs = {e: {} for e in self.eng}
        self.nbuf = 0
        self.live = []
        self.n_ins = 0
        self.marks = []
        self._pm = None

    def dram(self, name, shape, dt=F32, kind="Internal"):
        t = self.nc.dram_tensor(name, list(shape), dt, kind=kind)
        return Buf(t.ap(), name)

    def sb(self, shape, dt=F32, name=None):
        self.nbuf += 1
        name = "%s_%d" % (name or "sb", self.nbuf)
        g = self.nc.sbuf_tensor(name, list(shape), dt)
        h = g.__enter__()
        self.live[-1].append(g)
        return Buf(h.ap(), name)

    def ps(self, shape, dt=F32, name=None):
        self.nbuf += 1
        name = "%s_%d" % (name or "ps", self.nbuf)
        g = self.nc.psum_tensor(name, list(shape), dt)
        h = g.__enter__()
        self.live[-1].append(g)
        return Buf(h.ap(), name)

    def phase_begin(self):
        self.live.append([])

    def phase_end(self):
        self.barrier()
        for g in reversed(self.live.pop()):
            g.__exit__(None, None, None)

    def _wait(self, e, evs):
        eng = self.eng[e]
        ob = self.obs[e]
        for k, v in evs:
            if k == e and e == "pe":
                continue
            if ob.get(k, 0) < v:
                eng.wait_ge(self.sem[k], v)
                ob[k] = v

    def _deps(self, e, reads, writes):
        evs = []
        for t in reads:
            if t.w is not None:
                evs.append(t.w)
        for t in writes:
            if t.w is not None:
                evs.append(t.w)
            for k, v in t.r.items():
                if k != e:
                    evs.append((k, v))
        return evs

    def _mark(self, ev, reads, writes):
        k, v = ev
        for t in reads:
            if t.r.get(k, 0) < v:
                t.r[k] = v
        for t in writes:
            t.w = ev
            t.r = {}

    def op(self, e, fn, reads=(), writes=(), inc=True):
        self._wait(e, self._deps(e, reads, writes))
        ins = fn(self.eng[e])
        self.n_ins += 1
        self._domark(ins)
        if inc:
            self.cnt[e] += 1
            ins.then_inc(self.sem[e], 1)
            ev = (e, self.cnt[e])
        else:
            ev = (e, self.cnt[e] + 1)
        self._mark(ev, reads, writes)
        return ins

    def dma(self, q, out, in_, reads=(), writes=(), **kw):
        ks, i = self.dq[q]
        k = ks[i % len(ks)]
        self.dq[q][1] = i + 1
        evs = self._deps(q, reads, writes)
        if self.cnt[k] > 0:
            evs.append((k, self.cnt[k]))
        self._wait(q, evs)
        ins = self.eng[q].dma_start(out=out, in_=in_, **kw)
        self.n_ins += 1
        self._domark(ins)
        self.cnt[k] += 16
        ins.then_inc(self.sem[k], 16)
        self._mark((k, self.cnt[k]), reads, writes)
        return ins

    def mark(self, name):
        self._pm = name

    def _domark(self, ins):
        if self._pm is not None:
            try:
                self.marks.append((self._pm, ins.ins.name))
            except Exception as ex:
                self.marks.append((self._pm, repr(ex)))
            self._pm = None

    def barrier(self):
        allev = [(k, v) for k, v in self.cnt.items() if v > 0]
        for e in self.eng:
            self._wait(e, [(k, v) for k, v in allev if k != e])

    def finish(self):
        self.barrier()

    def mm(self, out, pairs, reads, tr=None):
        n = len(pairs)
        if FP32R:
            pairs = [((l.bitcast(F32R) if l.dtype == F32 else l), (r.bitcast(F32R) if r.dtype == F32 else r)) for l, r in pairs]
        for i, (l, r) in enumerate(pairs):
            self.op("pe", lambda pe, l=l, r=r, i=i: pe.matmul(out, l, r, start=(i == 0), stop=(i == n - 1)),
                    reads=reads, writes=[tr], inc=(i == n - 1))


def run_pipelined(gens, width, stagger=0):
    it = iter(gens)
    active = []
    done = set()
    exhausted = False
    head = stagger
    while True:
        while not exhausted and len(active) < (1 if head > 0 else width):
            g = next(it, None)
            if g is None:
                exhausted = True
                break
            active.append([g, None])
        if not active:
            break
        progressed = False
        for slot in list(active):
            g, blk = slot
            if blk is not None and blk not in done:
                continue
            slot[1] = None
            progressed = True
            try:
                r = next(g)
            except StopIteration:
                active.remove(slot)
                continue
            if isinstance(r, tuple):
                if r[0] == "wait":
                    slot[1] = r[1]
                elif r[0] == "done":
                    done.add(r[1])
        if head > 0:
            head -= 1
        assert progressed, "pipeline deadlock"


def run_pipelined_multi(streams):
    st = [{"it": iter(x["gens"]), "w": x["width"], "p": x.get("period", 1), "active": [], "ex": False} for x in streams]
    done = set()
    rnd = 0
    while True:
        alive = False
        for S_ in st:
            while not S_["ex"] and len(S_["active"]) < S_["w"]:
                g = next(S_["it"], None)
                if g is None:
                    S_["ex"] = True
                    break
                S_["active"].append([g, None])
            if S_["active"]:
                alive = True
        if not alive:
            break
        progressed = False
        only_bg = all((not S_["active"]) for S_ in st if S_["p"] == 1)
        for S_ in st:
            if S_["p"] > 1 and (rnd % S_["p"]) != 0 and not only_bg:
                continue
            for slot in list(S_["active"]):
                g, blk = slot
                if blk is not None and blk not in done:
                    continue
                slot[1] = None
                progressed = True
                try:
                    r = next(g)
                except StopIteration:
                    S_["active"].remove(slot)
                    continue
                if isinstance(r, tuple):
                    if r[0] == "wait":
                        slot[1] = r[1]
                    elif r[0] == "done":
                        done.add(r[1])
        rnd += 1
        assert progressed or any(S_["p"] > 1 for S_ in st), "pipeline deadlock"


D = 1024
EPS = 1e-6
TINY = 1e-20


def bcast_rows(ap1d, n):
    return ap1d.partition_broadcast(n)


def load_weight_bf16(mk, w_d, K, N, name, q="sp", chunk=2048):
    kc = K // 128
    wb = mk.sb([128, kc, N], BF16, name)
    mk.phase_begin()
    stg = [mk.sb([128, min(N, chunk)], F32, "wstg") for _ in range(4)]
    i = 0
    for k in range(kc):
        for c0 in range(0, N, chunk):
            c1 = min(N, c0 + chunk)
            s = stg[i % 4]
            mk.dma(("sp", "act")[i % 2], s.ap[:, 0:c1 - c0], w_d.ap[k * 128:(k + 1) * 128, c0:c1], reads=[w_d], writes=[s])
            e = "pool" if i % 2 == 0 else "dve"
            mk.op(e, lambda en, s=s, k=k, c0=c0, c1=c1: en.tensor_copy(wb.ap[:, k, c0:c1], s.ap[:, 0:c1 - c0]),
                  reads=[s], writes=[wb])
            i += 1
    mk.phase_end()
    return wb


def rmsnorm_to_fm(mk, xt, gB, ident, hT, col0, scr, ssb, hb, psT):
    mk.op("act", lambda en: en.activation(out=scr.ap, in_=xt.ap, func=AF.Square, accum_out=ssb.ap[:, 0:1]),
          reads=[xt], writes=[scr, ssb])
    mk.op("act", lambda en: en.activation(out=ssb.ap[:, 1:2], in_=ssb.ap[:, 0:1], func=AF.Sqrt, scale=1.0 / D, bias=EPS),
          reads=[ssb], writes=[ssb])
    mk.op("dve", lambda en: en.reciprocal(ssb.ap[:, 2:3], ssb.ap[:, 1:2]), reads=[ssb], writes=[ssb])
    mk.op("dve", lambda en: en.scalar_tensor_tensor(hb.ap, xt.ap, ssb.ap[:, 2:3], gB.ap, ALU.mult, ALU.mult),
          reads=[xt, ssb, gB], writes=[hb])
    tm_to_fm(mk, hb, ident, hT, col0, psT)


def tm_to_fm(mk, hb, ident, hT, col0, psT, nk=8):
    for k in range(nk):
        mk.op("pe", lambda pe, k=k: pe.transpose(psT.ap[:, k, :], hb.ap[:, k * 128:(k + 1) * 128], ident.ap),
              reads=[hb, ident], writes=[psT], inc=(k == nk - 1))
    mk.op("act", lambda en: en.copy(hT.ap[:, 0:nk, col0:col0 + 128], psT.ap[:, 0:nk, :]), reads=[psT], writes=[hT])


def norm_block(mk, jobs, gB, ident, slots):
    run_pipelined(norm_gens(mk, jobs, gB, ident, slots), len(slots), stagger=1)


def norm_gens(mk, jobs, gB, ident, slots):
    def gen(idx, xt, src, hT, col0, kind):
        sl = slots[idx % len(slots)]
        junk, ssb, hb, psT = sl["junk"], sl["ssb"], sl["hb"], sl["psT"]
        if src is not None:
            mk.dma("sp", xt.ap, src[1], reads=[src[0]], writes=[xt])
        if kind == "norm":
            mk.op("act", lambda en: en.activation(out=junk.ap, in_=xt.ap, func=AF.Square, accum_out=ssb.ap[:, 0:1]),
                  reads=[xt], writes=[junk, ssb])
            yield
            mk.op("act", lambda en: en.activation(out=ssb.ap[:, 1:2], in_=ssb.ap[:, 0:1], func=AF.Ln, scale=1.0 / D, bias=EPS),
                  reads=[ssb], writes=[ssb])
            mk.op("act", lambda en: en.activation(out=ssb.ap[:, 2:3], in_=ssb.ap[:, 1:2], func=AF.Exp, scale=-0.5), reads=[ssb], writes=[ssb])
            mk.op("dve", lambda en: en.scalar_tensor_tensor(hb.ap, xt.ap, ssb.ap[:, 2:3], gB.ap, ALU.mult, ALU.mult),
                  reads=[xt, ssb, gB], writes=[hb])
        else:
            mk.op("pool", lambda en: en.tensor_copy(hb.ap, xt.ap), reads=[xt], writes=[hb])
        yield
        for k in range(8):
            mk.op("pe", lambda pe, k=k: pe.transpose(psT.ap[:, k, :], hb.ap[:, k * 128:(k + 1) * 128], ident.ap),
                  reads=[hb, ident], writes=[psT], inc=(k == 7))
        yield
        mk.op("act", lambda en: en.copy(hT.ap[:, 0:8, col0:col0 + 128], psT.ap[:, 0:8, :]), reads=[psT], writes=[hT])
    return (gen(i, *j) for i, j in enumerate(jobs))


def norm_slots(mk, n, psTs):
    return [{"junk": mk.sb([128, D], BF16, "junk"), "ssb": mk.sb([128, 4], F32, "ssb"), "hb": mk.sb([128, D], BF16, "hb"), "psT": psTs[i]}
            for i in range(n)]


def stage_proj(mk, x_d, g_d, w_d, ztm_d, zfm_d, T, ntm, nfm, ident_d):
    mk.phase_begin()
    N = ntm + nfm
    wb = load_weight_bf16(mk, w_d, D, N, "w_in")
    gB = mk.sb([128, D], F32, "gB")
    mk.dma("sp", gB.ap, bcast_rows(g_d.ap, 128), reads=[g_d], writes=[gB])
    identf = mk.sb([128, 128], F32, "identf")
    ident = mk.sb([128, 128], BF16, "ident")
    mk.dma("sp", identf.ap, ident_d.ap, reads=[ident_d], writes=[identf])
    mk.op("dve", lambda en: en.tensor_copy(ident.ap, identf.ap), reads=[identf], writes=[ident])
    xts = [mk.sb([128, D], F32, "xt") for _ in range(2)]
    TB = min(512, T)
    NTB = TB // 128
    hTs = [mk.sb([128, 8, TB], BF16, "hT") for _ in range(2)]
    psTs = [mk.ps([128, 8, 128], BF16, "psT") for _ in range(2)]
    pss = [mk.ps([128, 512], F32, "psm") for _ in range(4)]
    nsl = norm_slots(mk, 2, psTs)
    ofm = [mk.sb([128, 512], F32, "ofm") for _ in range(3)]
    otm = [mk.sb([128, max(ntm, 1)], F32, "otm") for _ in range(2)]
    it = 0
    ip = 0
    io = 0
    nblk = T // TB

    def mk_jobs(tb):
        jobs = []
        for j in range(NTB):
            t0 = tb * TB + j * 128
            jobs.append((xts[(tb * NTB + j) % 2], (x_d, x_d.ap[t0:t0 + 128, :]), hTs[tb % 2], j * 128, "norm"))
        return jobs

    cnt = {"ip": 0, "io": 0}

    def mm_gen(tb):
        hT = hTs[tb % 2]
        for c in range(nfm // 128):
            ps = pss[cnt["ip"] % 4]
            cnt["ip"] += 1
            c0 = ntm + c * 128
            mk.mm(ps.ap[:, 0:TB], [(wb.ap[:, k, c0:c0 + 128], hT.ap[:, k, :]) for k in range(8)], reads=[wb, hT], tr=ps)
            o = ofm[cnt["io"] % 3]
            e = "act" if cnt["io"] % 2 == 0 else "dve"
            cnt["io"] += 1
            if e == "act":
                mk.op("act", lambda en, o=o, ps=ps: en.copy(o.ap[:, 0:TB], ps.ap[:, 0:TB]), reads=[ps], writes=[o])
            else:
                mk.op("dve", lambda en, o=o, ps=ps: en.tensor_copy(o.ap[:, 0:TB], ps.ap[:, 0:TB]), reads=[ps], writes=[o])
            mk.dma("pool", zfm_d.ap[c * 128:(c + 1) * 128, tb * TB:(tb + 1) * TB], o.ap[:, 0:TB], reads=[o], writes=[zfm_d])
            yield
        for j in range(NTB):
            o = otm[(tb * NTB + j) % 2]
            t0 = tb * TB + j * 128
            for c0 in range(0, ntm, 512):
                c1 = min(ntm, c0 + 512)
                ps = pss[cnt["ip"] % 4]
                cnt["ip"] += 1
                mk.mm(ps.ap[:, 0:c1 - c0], [(hT.ap[:, k, j * 128:(j + 1) * 128], wb.ap[:, k, c0:c1]) for k in range(8)],
                      reads=[wb, hT], tr=ps)
                e = "act" if cnt["io"] % 2 == 0 else "dve"
                cnt["io"] += 1
                if e == "act":
                    mk.op("act", lambda en, o=o, ps=ps, c0=c0, c1=c1: en.copy(o.ap[:, c0:c1], ps.ap[:, 0:c1 - c0]),
                          reads=[ps], writes=[o])
                else:
                    mk.op("dve", lambda en, o=o, ps=ps, c0=c0, c1=c1: en.tensor_copy(o.ap[:, c0:c1], ps.ap[:, 0:c1 - c0]),
                          reads=[ps], writes=[o])
                yield
            if ntm:
                mk.dma("pool", ztm_d.ap[t0:t0 + 128, :], o.ap, reads=[o], writes=[ztm_d])

    norm_block(mk, mk_jobs(0), gB, ident, nsl)
    for tb in range(nblk):
        streams = [{"gens": [mm_gen(tb)], "width": 1}]
        if tb + 1 < nblk:
            streams.append({"gens": norm_gens(mk, mk_jobs(tb + 1), gB, ident, nsl), "width": 2, "period": 2})
        run_pipelined_multi(streams)
    mk.phase_end()


def evac(mk, i, out_ap, in_ap, reads, writes):
    if i % 2 == 0:
        mk.op("act", lambda en: en.copy(out_ap, in_ap), reads=reads, writes=writes)
    else:
        mk.op("dve", lambda en: en.tensor_copy(out_ap, in_ap), reads=reads, writes=writes)


def load_consts_ident(mk, ident_d):
    identf = mk.sb([128, 128], F32, "identf")
    ident = mk.sb([128, 128], BF16, "ident")
    mk.dma("sp", identf.ap, ident_d.ap, reads=[ident_d], writes=[identf])
    mk.op("dve", lambda en: en.tensor_copy(ident.ap, identf.ap), reads=[identf], writes=[ident])
    return ident, identf


def tok_blocks(T, full=512):
    out = []
    rem = T % full
    pos = 0
    if rem:
        out.append((0, rem // 128))
        pos = rem
    while pos < T:
        out.append((pos, full // 128))
        pos += full
    return out


def stage_merge(mk, x_d, y_d, xo_d, T, gn_d, wg_d, wa_d, wb_d, wc_d, wmix_d, ident_d):
    mk.phase_begin()
    wg = load_weight_bf16(mk, wg_d, D, 3072, "wg")
    wa = load_weight_bf16(mk, wa_d, 512, D, "wa")
    wbb = load_weight_bf16(mk, wb_d, 256, D, "wb")
    wc = load_weight_bf16(mk, wc_d, 256, D, "wc")
    wmix = load_weight_bf16(mk, wmix_d, D, D, "wmix")
    gB = mk.sb([128, D], F32, "gB")
    mk.dma("sp", gB.ap, bcast_rows(gn_d.ap, 128), reads=[gn_d], writes=[gB])
    ident, _ = load_consts_ident(mk, ident_d)
    xts = [mk.sb([128, D], F32, "xt") for _ in range(8)]
    yts = [mk.sb([128, D], F32, "yt") for _ in range(2)]
    hTs = [mk.sb([128, 8, 512], BF16, "hT") for _ in range(2)]
    yTs = [mk.sb([128, 8, 512], BF16, "yT") for _ in range(2)]
    mT = mk.sb([128, 8, 512], BF16, "mT")
    sg = [mk.sb([128, 512], F32, "sg") for _ in range(3)]
    acc = [mk.sb([128, 512], F32, "acc") for _ in range(2)]
    tmp = [mk.sb([128, 512], F32, "tmp") for _ in range(2)]
    psTs = [mk.ps([128, 8, 128], BF16, "psT") for _ in range(2)]
    pss = [mk.ps([128, 512], F32, "psm") for _ in range(5)]
    nsl = norm_slots(mk, 2, psTs)
    blocks = tok_blocks(T)

    def mk_jobs(bi):
        tbase, nt = blocks[bi]
        jobs = []
        for j in range(nt):
            t0 = tbase + j * 128
            jobs.append((xts[(bi % 2) * 4 + j], (x_d, x_d.ap[t0:t0 + 128, :]), hTs[bi % 2], j * 128, "norm"))
            jobs.append((yts[j % 2], (y_d, y_d.ap[t0:t0 + 128, :]), yTs[bi % 2], j * 128, "cast"))
        return jobs

    cnt = {"ip": 0}

    def mm_gen(bi):
        tbase, nt = blocks[bi]
        Wd = nt * 128
        hT, yT = hTs[bi % 2], yTs[bi % 2]
        for c in range(8):
            a = acc[c % 2]
            for b, (wbr, koff, nk) in enumerate(((wa, 0, 4), (wbb, 4, 2), (wc, 6, 2))):
                psg = pss[cnt["ip"] % 5]
                cnt["ip"] += 1
                g0 = b * 1024 + c * 128
                mk.mm(psg.ap[:, 0:Wd], [(wg.ap[:, k, g0:g0 + 128], hT.ap[:, k, 0:Wd]) for k in range(8)], reads=[wg, hT], tr=psg)
                s = sg[b]
                mk.op("act", lambda en, s=s, psg=psg, Wd=Wd: en.activation(out=s.ap[:, 0:Wd], in_=psg.ap[:, 0:Wd], func=AF.Sigmoid),
                      reads=[psg], writes=[s])
                psb = pss[cnt["ip"] % 5]
                cnt["ip"] += 1
                mk.mm(psb.ap[:, 0:Wd], [(wbr.ap[:, k, c * 128:(c + 1) * 128], yT.ap[:, koff + k, 0:Wd]) for k in range(nk)],
                      reads=[wbr, yT], tr=psb)
                if b == 0:
                    mk.op("dve", lambda en, a=a, s=s, psb=psb, Wd=Wd: en.tensor_tensor(a.ap[:, 0:Wd], s.ap[:, 0:Wd], psb.ap[:, 0:Wd], ALU.mult),
                          reads=[s, psb], writes=[a])
                else:
                    t = tmp[b % 2]
                    mk.op("dve", lambda en, t=t, s=s, psb=psb, Wd=Wd: en.tensor_tensor(t.ap[:, 0:Wd], s.ap[:, 0:Wd], psb.ap[:, 0:Wd], ALU.mult),
                          reads=[s, psb], writes=[t])
                    if b == 1:
                        mk.op("pool", lambda en, a=a, t=t, Wd=Wd: en.tensor_tensor(a.ap[:, 0:Wd], a.ap[:, 0:Wd], t.ap[:, 0:Wd], ALU.add),
                              reads=[a, t], writes=[a])
                    else:
                        mk.op("pool", lambda en, a=a, t=t, c=c, Wd=Wd: en.tensor_tensor(mT.ap[:, c, 0:Wd], a.ap[:, 0:Wd], t.ap[:, 0:Wd], ALU.add),
                              reads=[a, t], writes=[mT])
                yield
        for j in range(nt):
            xt = xts[(bi % 2) * 4 + j]
            t0 = tbase + j * 128
            for c0 in (0, 512):
                ps = pss[cnt["ip"] % 5]
                cnt["ip"] += 1
                mk.mm(ps.ap, [(mT.ap[:, k, j * 128:(j + 1) * 128], wmix.ap[:, k, c0:c0 + 512]) for k in range(8)],
                      reads=[mT, wmix], tr=ps)
                mk.op("dve", lambda en, xt=xt, ps=ps, c0=c0: en.tensor_tensor(xt.ap[:, c0:c0 + 512], xt.ap[:, c0:c0 + 512], ps.ap, ALU.add),
                      reads=[xt, ps], writes=[xt])
                yield
            mk.dma("pool", xo_d.ap[t0:t0 + 128, :], xt.ap, reads=[xt], writes=[xo_d])

    norm_block(mk, mk_jobs(0), gB, ident, nsl)
    for bi in range(len(blocks)):
        streams = [{"gens": [mm_gen(bi)], "width": 1}]
        if bi + 1 < len(blocks):
            streams.append({"gens": norm_gens(mk, mk_jobs(bi + 1), gB, ident, nsl), "width": 2, "period": 2})
        run_pipelined_multi(streams)
    mk.phase_end()


def stage_select(mk, src_d, dst_d, Tt, off, hsel_d):
    mk.phase_begin()
    hB = mk.sb([128, 1], F32, "hB")
    mk.dma("sp", hB.ap, bcast_rows(hsel_d.ap, 128), reads=[hsel_d], writes=[hB])
    W = 3
    lo = [mk.sb([128, D], F32, "lo") for _ in range(W)]
    hi = [mk.sb([128, D], F32, "hi") for _ in range(W)]
    Th = off
    for tt in range(Tt // 128):
        k = tt % W
        t0 = tt * 128
        mk.dma("sp", lo[k].ap, src_d.ap[t0:t0 + 128, :], reads=[src_d], writes=[lo[k]])
        mk.dma("act", hi[k].ap, src_d.ap[Th + t0:Th + t0 + 128, :], reads=[src_d], writes=[hi[k]])
        eng = "dve" if tt % 2 == 0 else "pool"
        mk.op(eng, lambda en, k=k: en.tensor_tensor(hi[k].ap, hi[k].ap, lo[k].ap, ALU.subtract), reads=[hi[k], lo[k]], writes=[hi[k]])
        mk.op("dve", lambda en, k=k: en.scalar_tensor_tensor(lo[k].ap, hi[k].ap, hB.ap[:, 0:1], lo[k].ap, ALU.mult, ALU.add),
              reads=[hi[k], lo[k], hB], writes=[lo[k]])
        mk.dma("pool", dst_d.ap[t0:t0 + 128, :], lo[k].ap, reads=[lo[k]], writes=[dst_d])
    mk.phase_end()


def stage_ffn(mk, x_d, xo_d, T, gn_d, wup_d, cw_d, wd_d, ident_d):
    TB = 512
    NT = TB // 128
    mk.phase_begin()
    wup = load_weight_bf16(mk, wup_d, D, 5632, "wup")
    wd = load_weight_bf16(mk, wd_d, 2816, D, "wd")
    gB = mk.sb([128, D], F32, "gB")
    mk.dma("sp", gB.ap, bcast_rows(gn_d.ap, 128), reads=[gn_d], writes=[gB])
    cw = mk.sb([128, 44, 3], F32, "cw")
    mk.dma("sp", cw.ap, cw_d.ap.rearrange("(c p) k -> p c k", p=128), reads=[cw_d], writes=[cw])
    ident, _ = load_consts_ident(mk, ident_d)
    halos = [mk.sb([128, 2], F32, "halo") for _ in range(44)]
    for hb_ in halos:
        mk.op("pool", lambda en, hb_=hb_: en.memset(hb_.ap, 0.0), writes=[hb_])
    xts = [mk.sb([128, D], F32, "xt") for _ in range(2)]
    hTs = [mk.sb([128, 8, TB], BF16, "hT") for _ in range(2)]
    W = 2
    us = [[mk.sb([128, TB + 2], F32, "u") for _ in range(2)] for _ in range(W)]
    cas = [mk.sb([128, TB], F32, "ca") for _ in range(W)]
    cbs = [mk.sb([128, TB], F32, "cb") for _ in range(W)]
    gTs = [mk.sb([128, TB], BF16, "gT") for _ in range(22)]
    psTs = [mk.ps([128, 8, 128], BF16, "psT") for _ in range(1)]
    nsl = norm_slots(mk, 1, psTs)
    psu = [[mk.ps([128, 512], F32, "psu") for _ in range(2)] for _ in range(W)]
    psd = mk.ps([128, 512], F32, "psd")
    it = 0
    nit = [0]
    blocks = tok_blocks(T, TB)

    def mk_jobs(bi):
        tbase_, nt_ = blocks[bi]
        return [(xts[j % 2], (x_d, x_d.ap[tbase_ + j * 128:tbase_ + j * 128 + 128, :]), hTs[bi % 2], j * 128, "norm") for j in range(nt_)]

    norm_block(mk, mk_jobs(0), gB, ident, nsl)
    for bi, (tbase, nt) in enumerate(blocks):
        Wd = nt * 128
        hT = hTs[bi % 2]

        def cbody(c, k):
            cv = []
            for half, ct in enumerate((c, c + 22)):
                ps = psu[k][half]
                mk.mm(ps.ap[:, 0:Wd], [(wup.ap[:, kk, ct * 128:(ct + 1) * 128], hT.ap[:, kk, 0:Wd]) for kk in range(8)],
                      reads=[wup, hT], tr=ps)
                u = us[k][half]
                hl = halos[ct]
                mk.op("act", lambda en, u=u, hl=hl: en.copy(u.ap[:, 0:2], hl.ap), reads=[hl], writes=[u])
                mk.op("act", lambda en, u=u, ps=ps: en.copy(u.ap[:, 2:Wd + 2], ps.ap[:, 0:Wd]), reads=[ps], writes=[u])
                o = (cas if half == 0 else cbs)[k]
                mk.op("act", lambda en, o=o, ps=ps, ct=ct: en.activation(out=o.ap[:, 0:Wd], in_=ps.ap[:, 0:Wd], func=AF.Copy, scale=cw.ap[:, ct, 2:3]),
                      reads=[ps, cw], writes=[o])
                yield
                mk.op("act", lambda en, u=u, hl=hl: en.copy(hl.ap, u.ap[:, Wd:Wd + 2]), reads=[u], writes=[hl])
                mk.op("dve", lambda en, o=o, u=u, ct=ct: en.scalar_tensor_tensor(o.ap[:, 0:Wd], u.ap[:, 1:Wd + 1], cw.ap[:, ct, 1:2], o.ap[:, 0:Wd], ALU.mult, ALU.add),
                      reads=[u, cw, o], writes=[o])
                mk.op("dve", lambda en, o=o, u=u, ct=ct: en.scalar_tensor_tensor(o.ap[:, 0:Wd], u.ap[:, 0:Wd], cw.ap[:, ct, 0:1], o.ap[:, 0:Wd], ALU.mult, ALU.add),
                      reads=[u, cw, o], writes=[o])
                cv.append(o)
                yield
            a_, b_ = cv
            mk.op("act", lambda en, a_=a_: en.activation(out=a_.ap[:, 0:Wd], in_=a_.ap[:, 0:Wd], func=AF.Silu), reads=[a_], writes=[a_])
            yield
            mk.op("dve", lambda en, a_=a_, b_=b_, c=c: en.tensor_tensor(gTs[c].ap[:, 0:Wd], a_.ap[:, 0:Wd], b_.ap[:, 0:Wd], ALU.mult),
                  reads=[a_, b_], writes=[gTs[c]])

        def gens():
            for c in range(22):
                k = nit[0] % W
                nit[0] += 1
                yield cbody(c, k)
        streams = [{"gens": gens(), "width": W}]
        if bi + 1 < len(blocks):
            streams.append({"gens": norm_gens(mk, mk_jobs(bi + 1), gB, ident, nsl), "width": 1, "period": 3})
        run_pipelined_multi(streams)
        for j in range(nt):
            t0 = tbase + j * 128
            xt = xts[j % 2]
            mk.dma("sp", xt.ap, x_d.ap[t0:t0 + 128, :], reads=[x_d], writes=[xt])
            for c0 in (0, 512):
                ps = psd
                mk.mm(ps.ap, [(gTs[k].ap[:, j * 128:(j + 1) * 128], wd.ap[:, k, c0:c0 + 512]) for k in range(22)],
                      reads=gTs + [wd], tr=ps)
                mk.op("dve", lambda en, xt=xt, ps=ps, c0=c0: en.tensor_tensor(xt.ap[:, c0:c0 + 512], xt.ap[:, c0:c0 + 512], ps.ap, ALU.add),
                      reads=[xt, ps], writes=[xt])
            mk.dma("pool", xo_d.ap[t0:t0 + 128, :], xt.ap, reads=[xt], writes=[xo_d])
    mk.phase_end()

def bc_last(ap2, n):
    return ap2.unsqueeze(2).to_broadcast([ap2.shape[0], ap2.shape[1], n])


def bc_mid(ap2, h):
    return ap2.unsqueeze(1).to_broadcast([ap2.shape[0], h, ap2.shape[1]])


def exp_sigmoid(mk, out_b, out_ap, in_b, in_ap, neg=False, eng2="dve"):
    mk.op("act", lambda en: en.activation(out=out_ap, in_=in_ap, func=AF.Exp, scale=(1.0 if neg else -1.0)), reads=[in_b], writes=[out_b])
    mk.op("act", lambda en: en.activation(out=out_ap, in_=out_ap, func=AF.Ln, bias=1.0), reads=[out_b], writes=[out_b])
    mk.op("act", lambda en: en.activation(out=out_ap, in_=out_ap, func=AF.Exp, scale=-1.0), reads=[out_b], writes=[out_b])


def exp_silu(mk, out_b, out_ap, in_b, in_ap, tmp_b, tmp_ap):
    exp_sigmoid(mk, tmp_b, tmp_ap, in_b, in_ap)
    mk.op("dve", lambda en: en.tensor_tensor(out_ap, in_ap, tmp_ap, ALU.mult), reads=[in_b, tmp_b], writes=[out_b])


def head_rmsnorm(mk, src, H, dh, gB, out, scr, st, scale=1.0, np_=128):
    n = H * dh
    s3 = lambda b: b.ap[0:np_, 0:n].rearrange("p (h d) -> p h d", h=H)
    mk.op("pool", lambda en: en.tensor_tensor(scr.ap[0:np_, 0:n], src.ap[0:np_, 0:n], src.ap[0:np_, 0:n], ALU.mult),
          reads=[src], writes=[scr])
    mk.op("dve", lambda en: en.tensor_reduce(st.ap[0:np_, 0:H], s3(scr), AX.X, ALU.add), reads=[scr], writes=[st])
    mk.op("act", lambda en: en.activation(out=st.ap[0:np_, H:2 * H], in_=st.ap[0:np_, 0:H], func=AF.Ln, scale=1.0 / dh, bias=EPS),
          reads=[st], writes=[st])
    mk.op("act", lambda en: en.activation(out=st.ap[0:np_, 2 * H:3 * H], in_=st.ap[0:np_, H:2 * H], func=AF.Exp, scale=-0.5),
          reads=[st], writes=[st])
    mk.op("dve", lambda en: en.tensor_tensor(s3(scr), s3(src), bc_last(st.ap[0:np_, 2 * H:3 * H], dh), ALU.mult),
          reads=[src, st], writes=[scr])
    mk.op("dve", lambda en: en.scalar_tensor_tensor(s3(out), s3(scr), float(scale), bc_mid(gB.ap[0:np_, 0:dh], H), ALU.mult, ALU.mult),
          reads=[scr, gB], writes=[out])


def stage_xattn(mk, x_d, xo_d, T, gn_d, mkv_d, wq_d, qn_d, kn_d, wo_d, ident_d):
    mk.phase_begin()
    H, DH, M = 4, 128, 256
    wq = load_weight_bf16(mk, wq_d, D, 512, "wq")
    wo = load_weight_bf16(mk, wo_d, 512, D, "wo")
    gB = mk.sb([128, D], F32, "gB")
    mk.dma("sp", gB.ap, bcast_rows(gn_d.ap, 128), reads=[gn_d], writes=[gB])
    gq = mk.sb([128, DH], F32, "gq")
    mk.dma("sp", gq.ap, bcast_rows(qn_d.ap, 128), reads=[qn_d], writes=[gq])
    gk = mk.sb([128, DH], F32, "gk")
    mk.dma("sp", gk.ap, bcast_rows(kn_d.ap, 128), reads=[kn_d], writes=[gk])
    ident, _ = load_consts_ident(mk, ident_d)
    kT = mk.sb([128, H, M], BF16, "kT")
    vaug = mk.sb([128, 2, H, DH + 1], BF16, "vaug")
    scr = mk.sb([128, D], F32, "scr")
    st = mk.sb([128, 12], F32, "st")
    psTs = [mk.ps([128, 8, 128], BF16, "psT") for _ in range(2)]
    pss = [mk.ps([128, 512], F32, "psm") for _ in range(5)]
    hbs = [mk.sb([128, D], BF16, "hb") for _ in range(2)]
    mt_t = mk.sb([128, D], F32, "memt")
    mk.op("pool", lambda en: en.memset(vaug.ap, 1.0), writes=[vaug])
    for mt in range(2):
        mk.dma("sp", mt_t.ap, mkv_d.ap[mt * 128:(mt + 1) * 128, :], reads=[mkv_d], writes=[mt_t])
        hb = hbs[mt]
        head_rmsnorm(mk, mt_t, H, DH, gk, hb, scr, st)
        for h in range(H):
            mk.op("pe", lambda pe, h=h, hb=hb: pe.transpose(psTs[0].ap[:, h, :], hb.ap[:, h * DH:(h + 1) * DH], ident.ap),
                  reads=[hb, ident], writes=[psTs[0]], inc=(h == H - 1))
        mk.op("act", lambda en, mt=mt: en.copy(kT.ap[:, :, mt * 128:(mt + 1) * 128], psTs[0].ap[:, 0:H, :]),
              reads=[psTs[0]], writes=[kT])
        mk.op("dve", lambda en, mt=mt: en.tensor_copy(vaug.ap[:, mt, :, 0:DH], mt_t.ap[:, 512:1024].rearrange("p (h d) -> p h d", h=H)),
              reads=[mt_t], writes=[vaug])
    xts = [mk.sb([128, D], F32, "xt") for _ in range(4)]
    nsl = [{"junk": mk.sb([128, D], BF16, "junk"), "ssb": mk.sb([128, 4], F32, "ssb"), "hb": hbs[i_], "psT": psTs[i_]} for i_ in range(2)]
    hT = mk.sb([128, 8, 512], BF16, "hT")
    W = 2
    qf = [mk.sb([128, 512], F32, "qf") for _ in range(W)]
    qb = [mk.sb([128, 512], BF16, "qb") for _ in range(W)]
    scrs = [mk.sb([128, 512], F32, "scrq") for _ in range(W)]
    sts = [mk.sb([128, 12], F32, "stq") for _ in range(W)]
    qT = mk.sb([128, H, 512], BF16, "qT")
    PT = mk.sb([128, H * 2, 512], BF16, "PT")
    of = [mk.sb([128, 512], F32, "of") for _ in range(W)]
    ob = [mk.sb([128, 512], BF16, "ob") for _ in range(W)]
    oT = [mk.sb([128, H, 128], BF16, "oT") for _ in range(W)]
    rd = [mk.sb([128, 4], F32, "rd") for _ in range(W)]
    pq = [pss[0], pss[1]]
    po = [[pss[2], pss[3]], [pss[4], pss[0]]]
    it = 0
    for (tbase, nt) in tok_blocks(T):
        Wd = nt * 128
        jobs = []
        for j in range(nt):
            t0 = tbase + j * 128
            jobs.append((xts[j], (x_d, x_d.ap[t0:t0 + 128, :]), hT, j * 128, "norm"))
        norm_block(mk, jobs, gB, ident, nsl)

        def qbody(j):
            k = j % W
            ps = pq[k]
            mk.mm(ps.ap, [(hT.ap[:, kk, j * 128:(j + 1) * 128], wq.ap[:, kk, :]) for kk in range(8)], reads=[hT, wq], tr=ps)
            q = qf[k]
            mk.op("act", lambda en: en.copy(q.ap, ps.ap), reads=[ps], writes=[q])
            yield
            qq = qb[k]
            head_rmsnorm(mk, q, H, DH, gq, qq, scrs[k], sts[k], scale=DH ** -0.5)
            yield
            psT = psTs[k]
            for h in range(H):
                mk.op("pe", lambda pe, h=h: pe.transpose(psT.ap[:, h, :], qq.ap[:, h * DH:(h + 1) * DH], ident.ap),
                      reads=[qq, ident], writes=[psT], inc=(h == H - 1))
            mk.op("act", lambda en: en.copy(qT.ap[:, :, j * 128:(j + 1) * 128], psT.ap[:, 0:H, :]),
                  reads=[psT], writes=[qT])
        run_pipelined((qbody(j) for j in range(nt)), W, stagger=1)
        ip = 0
        for h in range(H):
            for mt in range(2):
                ps = pss[ip % 2]
                ip += 1
                mk.mm(ps.ap[:, 0:Wd], [(kT.ap[:, h, mt * 128:(mt + 1) * 128], qT.ap[:, h, 0:Wd])], reads=[kT, qT], tr=ps)
                mk.op("act", lambda en, ps=ps, h=h, mt=mt, Wd=Wd: en.activation(out=PT.ap[:, h * 2 + mt, 0:Wd], in_=ps.ap[:, 0:Wd], func=AF.Exp),
                      reads=[ps], writes=[PT])

        def obody(j):
            k = j % W
            xt = xts[j]
            t0 = tbase + j * 128
            o_f = of[k]
            r = rd[k]
            for hp in range(2):
                ps = po[k][hp]
                for hh in range(2):
                    h = hp * 2 + hh
                    mk.mm(ps.ap[:, hh * 129:(hh + 1) * 129],
                          [(PT.ap[:, h * 2 + mt, j * 128:(j + 1) * 128], vaug.ap[:, mt, h, :]) for mt in range(2)],
                          reads=[PT, vaug], tr=ps)
                for hh in range(2):
                    h = hp * 2 + hh
                    mk.op("dve", lambda en, ps=ps, hh=hh, h=h: en.reciprocal(r.ap[:, h:h + 1], ps.ap[:, hh * 129 + 128:hh * 129 + 129]),
                          reads=[ps], writes=[r])
                    mk.op("dve", lambda en, ps=ps, hh=hh, h=h: en.tensor_scalar(
                        o_f.ap[:, h * 128:(h + 1) * 128], ps.ap[:, hh * 129:hh * 129 + 128], r.ap[:, h:h + 1], None, ALU.mult),
                        reads=[ps, r], writes=[o_f])
                yield
            o_b = ob[k]
            mk.op("pool", lambda en: en.tensor_copy(o_b.ap, o_f.ap), reads=[o_f], writes=[o_b])
            psT = psTs[k]
            for h in range(H):
                mk.op("pe", lambda pe, h=h: pe.transpose(psT.ap[:, h, :], o_b.ap[:, h * DH:(h + 1) * DH], ident.ap),
                      reads=[o_b, ident], writes=[psT], inc=(h == H - 1))
            o_T = oT[k]
            mk.op("act", lambda en: en.copy(o_T.ap, psT.ap[:, 0:H, :]), reads=[psT], writes=[o_T])
            yield
            for ci, c0 in enumerate((0, 512)):
                ps = po[k][ci]
                mk.mm(ps.ap, [(o_T.ap[:, h, :], wo.ap[:, h, c0:c0 + 512]) for h in range(H)], reads=[o_T, wo], tr=ps)
                mk.op("dve", lambda en, ps=ps, c0=c0: en.tensor_tensor(xt.ap[:, c0:c0 + 512], xt.ap[:, c0:c0 + 512], ps.ap, ALU.add),
                      reads=[xt, ps], writes=[xt])
                yield
            mk.dma("pool", xo_d.ap[t0:t0 + 128, :], xt.ap, reads=[xt], writes=[xo_d])
        run_pipelined((obody(j) for j in range(nt)), W, stagger=1)
    mk.phase_end()


def stage_hgrn(mk, ztm_d, c_hf, c_hi, c_hg, zfm_d, r_hq, r_hf, y_d, ycol, T, lbl_d, layer, onorm_d, tri_d, ident_d, defer=False):
    H, DK, C = 4, 64, 64
    mk.phase_begin()
    tri = mk.sb([64, 64], F32, "tri")
    mk.dma("sp", tri.ap, tri_d.ap, reads=[tri_d], writes=[tri])
    identf = mk.sb([128, 128], F32, "identf")
    mk.dma("sp", identf.ap, ident_d.ap, reads=[ident_d], writes=[identf])
    gO = mk.sb([64, 64], F32, "gO")
    mk.dma("sp", gO.ap, bcast_rows(onorm_d.ap, 64), reads=[onorm_d], writes=[gO])
    lbB = mk.sb([64, 256], F32, "lbB")
    omlbB = mk.sb([64, 256], F32, "omlbB")
    lbT = mk.sb([64, 4], F32, "lbT")
    omlbT = mk.sb([64, 4], F32, "omlbT")
    if layer == 0:
        mk.op("dve", lambda en: en.memset(lbB.ap, 0.0), writes=[lbB])
        mk.op("dve", lambda en: en.memset(lbT.ap, 0.0), writes=[lbT])
    else:
        l0 = mk.sb([64, 256], F32, "l0")
        l0T = mk.sb([64, 4], F32, "l0T")
        mk.dma("sp", l0.ap, bcast_rows(lbl_d.ap[0, :], 64), reads=[lbl_d], writes=[l0])
        mk.dma("sp", lbB.ap, bcast_rows(lbl_d.ap[1, :], 64), reads=[lbl_d], writes=[lbB])
        mk.dma("sp", l0T.ap, lbl_d.ap[0, :].rearrange("(h d) -> d h", h=4), reads=[lbl_d], writes=[l0T], allow_slow_non_contiguous=True)
        mk.dma("sp", lbT.ap, lbl_d.ap[1, :].rearrange("(h d) -> d h", h=4), reads=[lbl_d], writes=[lbT], allow_slow_non_contiguous=True)
        mk.op("dve", lambda en: en.tensor_tensor(lbB.ap, lbB.ap, l0.ap, ALU.subtract), reads=[lbB, l0], writes=[lbB])
        mk.op("act", lambda en: en.activation(out=lbB.ap, in_=lbB.ap, func=AF.Sigmoid), reads=[lbB], writes=[lbB])
        mk.op("dve", lambda en: en.tensor_tensor(lbT.ap, lbT.ap, l0T.ap, ALU.subtract), reads=[lbT, l0T], writes=[lbT])
        mk.op("act", lambda en: en.activation(out=lbT.ap, in_=lbT.ap, func=AF.Sigmoid), reads=[lbT], writes=[lbT])
    mk.op("dve", lambda en: en.tensor_scalar(omlbB.ap, lbB.ap, -1.0, 1.0, ALU.mult, ALU.add), reads=[lbB], writes=[omlbB])
    mk.op("dve", lambda en: en.tensor_scalar(omlbT.ap, lbT.ap, -1.0, 1.0, ALU.mult, ALU.add), reads=[lbT], writes=[omlbT])
    S = mk.sb([64, H, 64], F32, "S")
    mk.op("dve", lambda en: en.memset(S.ap, 0.0), writes=[S])
    NB = 2 if defer else 3
    def mkb(shape, name, n=NB, dt=F32):
        return [mk.sb(shape, dt, name) for _ in range(n)]
    qz, fz, fzt, vt, gz = mkb([64, H, 64], "qz"), mkb([64, H, 64], "fz"), mkb([64, 256], "fzt"), mkb([64, 256], "vt"), mkb([64, 256], "gz")
    qT, kT, lf = mkb([64, H, 64], "qT"), mkb([64, H, 64], "kT"), mkb([64, 256], "lf")
    bT, d1, e1, e2, e3 = mkb([64, H, 64], "bT"), mkb([64, H, 64], "d1"), mkb([64, H, 64], "e1"), mkb([64, H, 64], "e2"), mkb([64, H, 64], "e3")
    qtl, ktl, qe, kdT, kd = mkb([64, H, 64], "qtl"), mkb([64, H, 64], "ktl"), mkb([64, H, 64], "qe"), mkb([64, H, 64], "kdT"), mkb([64, H, 64], "kd")
    AT, ebl = mkb([64, H, 64], "AT"), mkb([64, 4], "ebl")
    of, scr, st, yo = mkb([64, 256], "of"), mkb([64, 256], "scr"), mkb([64, 12], "st"), mkb([64, 256], "yo")
    NPS = 2 if defer else 8
    PPS = 2 if defer else 4
    pss = [mk.ps([128, 512], F32, "psm") for _ in range(NPS)]
    ipc = [0, 0]
    def body(c):
        i = c % NB
        t0 = c * C
        pset = 0 if defer else c % 2
        def nps():
            p = pss[pset * PPS + ipc[pset] % PPS]
            ipc[pset] += 1
            return p
        mk.dma("sp", qz[i].ap, zfm_d.ap[r_hq:r_hq + 256, t0:t0 + C].rearrange("(h d) t -> d h t", h=H), reads=[zfm_d], writes=[qz[i]])
        mk.dma("sp", fz[i].ap, zfm_d.ap[r_hf:r_hf + 256, t0:t0 + C].rearrange("(h d) t -> d h t", h=H), reads=[zfm_d], writes=[fz[i]])
        mk.dma("sp", fzt[i].ap, ztm_d.ap[t0:t0 + C, c_hf:c_hf + 256], reads=[ztm_d], writes=[fzt[i]])
        mk.dma("sp", vt[i].ap, ztm_d.ap[t0:t0 + C, c_hi:c_hi + 256], reads=[ztm_d], writes=[vt[i]])
        mk.dma("sp", gz[i].ap, ztm_d.ap[t0:t0 + C, c_hg:c_hg + 256], reads=[ztm_d], writes=[gz[i]])
        yield
        exp_silu(mk, qT[i], qT[i].ap, qz[i], qz[i].ap, e1[i], e1[i].ap)
        exp_sigmoid(mk, kT[i], kT[i].ap, fz[i], fz[i].ap, neg=True, eng2="pool")
        mk.op("dve", lambda en, i=i: en.tensor_tensor(kT[i].ap, kT[i].ap, bc_last(omlbT.ap, 64), ALU.mult), reads=[kT[i], omlbT], writes=[kT[i]])
        yield
        exp_sigmoid(mk, lf[i], lf[i].ap, fzt[i], fzt[i].ap, eng2="pool")
        mk.op("dve", lambda en, i=i: en.tensor_tensor(lf[i].ap, lf[i].ap, omlbB.ap, ALU.mult), reads=[lf[i], omlbB], writes=[lf[i]])
        mk.op("dve", lambda en, i=i: en.tensor_tensor(lf[i].ap, lf[i].ap, lbB.ap, ALU.add), reads=[lf[i], lbB], writes=[lf[i]])
        mk.op("dve", lambda en, i=i: en.tensor_scalar(lf[i].ap, lf[i].ap, TINY, None, ALU.max), reads=[lf[i]], writes=[lf[i]])
        mk.op("act", lambda en, i=i: en.activation(out=lf[i].ap, in_=lf[i].ap, func=AF.Ln), reads=[lf[i]], writes=[lf[i]])
        yield
        psb = nps()
        for h in range(H):
            mk.mm(psb.ap[0:64, h * 64:(h + 1) * 64], [(lf[i].ap[:, h * 64:(h + 1) * 64], tri.ap)], reads=[lf[i], tri], tr=psb)
        mk.op("act", lambda en, i=i, psb=psb: en.copy(bT[i].ap, psb.ap[0:64, 0:256].rearrange("p (h t) -> p h t", h=H)), reads=[psb], writes=[bT[i]])
        mk.op("dve", lambda en, i=i: en.tensor_tensor(d1[i].ap, bT[i].ap, bc_last(bT[i].ap[:, :, 31], 64), ALU.subtract), reads=[bT[i]], writes=[d1[i]])
        mk.op("act", lambda en, i=i: en.activation(out=e1[i].ap, in_=d1[i].ap, func=AF.Exp), reads=[d1[i]], writes=[e1[i]])
        mk.op("act", lambda en, i=i: en.activation(out=e2[i].ap, in_=d1[i].ap, func=AF.Exp, scale=-1.0), reads=[d1[i]], writes=[e2[i]])
        mk.op("dve", lambda en, i=i: en.tensor_tensor(qtl[i].ap, qT[i].ap, e1[i].ap, ALU.mult), reads=[qT[i], e1[i]], writes=[qtl[i]])
        mk.op("dve", lambda en, i=i: en.tensor_tensor(ktl[i].ap, kT[i].ap, e2[i].ap, ALU.mult), reads=[kT[i], e2[i]], writes=[ktl[i]])
        mk.op("act", lambda en, i=i: en.activation(out=e1[i].ap, in_=bT[i].ap, func=AF.Exp), reads=[bT[i]], writes=[e1[i]])
        mk.op("dve", lambda en, i=i: en.tensor_tensor(qe[i].ap, qT[i].ap, e1[i].ap, ALU.mult), reads=[qT[i], e1[i]], writes=[qe[i]])
        mk.op("dve", lambda en, i=i: en.tensor_tensor(d1[i].ap, bT[i].ap, bc_last(bT[i].ap[:, :, 63], 64), ALU.subtract), reads=[bT[i]], writes=[d1[i]])
        mk.op("act", lambda en, i=i: en.activation(out=e3[i].ap, in_=d1[i].ap, func=AF.Exp, scale=-1.0), reads=[d1[i]], writes=[e3[i]])
        mk.op("dve", lambda en, i=i: en.tensor_tensor(kdT[i].ap, kT[i].ap, e3[i].ap, ALU.mult), reads=[kT[i], e3[i]], writes=[kdT[i]])
        mk.op("act", lambda en, i=i: en.activation(out=ebl[i].ap, in_=bT[i].ap[:, :, 63], func=AF.Exp), reads=[bT[i]], writes=[ebl[i]])
        yield
        pst = nps()
        for h in range(H):
            mk.op("pe", lambda pe, h=h, i=i, pst=pst: pe.transpose(pst.ap[0:64, h * 64:(h + 1) * 64], kdT[i].ap[:, h, :], identf.ap[0:64, 0:64]),
                  reads=[kdT[i], identf], writes=[pst], inc=(h == H - 1))
        mk.op("act", lambda en, i=i, pst=pst: en.copy(kd[i].ap, pst.ap[0:64, 0:256].rearrange("p (h t) -> p h t", h=H)), reads=[pst], writes=[kd[i]])
        yield
        psa = nps()
        for h in range(H):
            mk.mm(psa.ap[0:64, h * 64:(h + 1) * 64], [(ktl[i].ap[:, h, :], qtl[i].ap[:, h, :])], reads=[ktl[i], qtl[i]], tr=psa)
        mk.op("dve", lambda en, i=i, psa=psa: en.tensor_tensor(AT[i].ap, psa.ap[0:64, 0:256].rearrange("p (h t) -> p h t", h=H), bc_mid(tri.ap, H), ALU.mult),
              reads=[psa, tri], writes=[AT[i]])
        if c > 0:
            yield ("wait", ("S", c - 1))
        pso = nps()
        for h in range(H):
            mk.mm(pso.ap[0:64, h * 64:(h + 1) * 64], [(AT[i].ap[:, h, :], vt[i].ap[:, h * 64:(h + 1) * 64]), (qe[i].ap[:, h, :], S.ap[:, h, :])],
                  reads=[AT[i], vt[i], qe[i], S], tr=pso)
        pss_ = nps()
        for h in range(H):
            mk.mm(pss_.ap[0:64, h * 64:(h + 1) * 64], [(kd[i].ap[:, h, :], vt[i].ap[:, h * 64:(h + 1) * 64])], reads=[kd[i], vt[i]], tr=pss_)
        mk.op("dve", lambda en, i=i: en.tensor_tensor(S.ap, S.ap, bc_last(ebl[i].ap, 64), ALU.mult), reads=[S, ebl[i]], writes=[S])
        mk.op("dve", lambda en, pss_=pss_: en.tensor_tensor(S.ap, S.ap, pss_.ap[0:64, 0:256].rearrange("p (h t) -> p h t", h=H), ALU.add), reads=[S, pss_], writes=[S])
        yield ("done", ("S", c))
        mk.op("act", lambda en, i=i, pso=pso: en.copy(of[i].ap, pso.ap[0:64, 0:256]), reads=[pso], writes=[of[i]])
        head_rmsnorm(mk, of[i], H, 64, gO, yo[i], scr[i], st[i], np_=64)
        exp_sigmoid(mk, gz[i], gz[i].ap, gz[i], gz[i].ap, eng2="pool")
        mk.op("dve", lambda en, i=i: en.tensor_tensor(yo[i].ap, yo[i].ap, gz[i].ap, ALU.mult), reads=[yo[i], gz[i]], writes=[yo[i]])
        mk.dma("pool", y_d.ap[t0:t0 + C, ycol:ycol + 256], yo[i].ap, reads=[yo[i]], writes=[y_d])
    if defer:
        return (body(c) for c in range(T // C)), (lambda: mk.phase_end())
    run_pipelined((body(c) for c in range(T // C)), 2)
    mk.phase_end()


def stage_gdn(mk, ztm_d, c_cz, c_cb, c_ca, zfm_d, r_qkv, y_d, ycol, T, cw_d, alog_d, dtb_d, onorm_d, cst, bg=None):
    H, C = 4, 128
    mk.phase_begin()
    def cload(name):
        b = mk.sb([128, 128], F32, name)
        mk.dma("sp", b.ap, cst[name].ap, reads=[cst[name]], writes=[b])
        return b
    triI, lowS, negtriS, neglowS, id128, ones = (cload(n) for n in ("g_triI", "g_lowS", "g_negtriS", "g_neglowS", "g_id", "g_ones"))
    gO = mk.sb([128, 64], F32, "gO")
    mk.dma("sp", gO.ap, bcast_rows(onorm_d.ap, 128), reads=[onorm_d], writes=[gO])
    cw = mk.sb([64, 12, 4], F32, "cw")
    mk.dma("sp", cw.ap, cw_d.ap.rearrange("(b d) k -> d b k", d=64), reads=[cw_d], writes=[cw])
    negA = mk.sb([128, 4], F32, "negA")
    dtb = mk.sb([128, 4], F32, "dtb")
    mk.dma("sp", negA.ap, bcast_rows(alog_d.ap, 128), reads=[alog_d], writes=[negA])
    mk.dma("sp", dtb.ap, bcast_rows(dtb_d.ap, 128), reads=[dtb_d], writes=[dtb])
    mk.op("act", lambda en: en.activation(out=negA.ap, in_=negA.ap, func=AF.Exp), reads=[negA], writes=[negA])
    mk.op("dve", lambda en: en.tensor_scalar(negA.ap, negA.ap, -1.0, None, ALU.mult), reads=[negA], writes=[negA])
    S = mk.sb([64, H, 64], F32, "S")
    mk.op("dve", lambda en: en.memset(S.ap, 0.0), writes=[S])
    NB = 2
    def mkb(shape, name, n=NB, dt=F32):
        return [mk.sb(list(shape), dt, name) for _ in range(n)]
    FM = [64, H, C]
    TM = [128, H, 64]
    SQ = [128, H, C]
    u = mkb([64, 12, C + 3], "u")
    cz, sc = mkb([128, 256], "cz"), mkb([128, 8], "sc")
    tk = [mkb([64, 12, C], "tk%d" % k) for k in range(2)]
    qkv, sq, rinv = mkb([64, 12, C], "qkv"), mkb([64, 8, C], "sq"), mkb([64, 8, C], "rinv")
    kv_tm = mkb([128, 8, 64], "kvtm")
    beta, g, sp1, sp2 = mkb([128, 4], "beta"), mkb([128, 4], "g"), mkb([128, 4], "sp1"), mkb([128, 4], "sp2")
    rep = mkb([128, 8, 64], "rep")
    eb, bb = mkb(FM, "eb"), mkb(FM, "bb")
    kbT, kbeT, qeT = mkb(FM, "kbT"), mkb(FM, "kbeT"), mkb(FM, "qeT")
    Gt = mkb(SQ, "Gt")
    ET, E, ETi = mkb(SQ, "ET"), mkb(SQ, "E"), mkb(SQ, "ETi")
    NUs, NLs = [mkb(SQ, "NU%d" % j) for j in range(2)], [mkb(SQ, "NL%d" % j) for j in range(2)]
    P = mkb(SQ, "P")
    QKT = mkb(SQ, "QKT")
    bv, rhs, vn, kd, elb = mkb(TM, "bv"), mkb(TM, "rhs"), mkb(TM, "vn"), mkb(TM, "kd"), mkb([128, 4], "elb")
    of, scr, st, yo = mkb([128, 256], "of"), mkb([128, 256], "scr"), mkb([128, 12], "st"), mkb([128, 256], "yo")
    PPS = 3 if bg is not None else 4
    pss = [mk.ps([128, 512], F32, "psm") for _ in range(2 * PPS)]
    ipc = [0, 0]
    psq = lambda p: p.ap[:, 0:H * C].rearrange("p (h t) -> p h t", h=H)
    ptm = lambda p: p.ap[:, 0:H * 64].rearrange("p (h t) -> p h t", h=H)
    pfm = lambda p: p.ap[0:64, 0:H * C].rearrange("p (h t) -> p h t", h=H)

    def body(c):
        i = c % NB
        t0 = c * C
        U = u[i]
        NU = [NUs[0][i], NUs[1][i]]
        NL = [NLs[0][i], NLs[1][i]]
        pset = c % 2
        def nps():
            p = pss[pset * PPS + ipc[pset] % PPS]
            ipc[pset] += 1
            return p
        if c == 0:
            mk.op("pool", lambda en, U=U: en.memset(U.ap[:, :, 0:3], 0.0), writes=[U])
            mk.dma("sp", U.ap[:, :, 3:C + 3], zfm_d.ap[r_qkv:r_qkv + 768, 0:C].rearrange("(b d) t -> d b t", d=64), reads=[zfm_d], writes=[U])
        else:
            mk.dma("sp", U.ap, zfm_d.ap[r_qkv:r_qkv + 768, t0 - 3:t0 + C].rearrange("(b d) t -> d b t", d=64), reads=[zfm_d], writes=[U])
        mk.dma("sp", cz[i].ap, ztm_d.ap[t0:t0 + C, c_cz:c_cz + 256], reads=[ztm_d], writes=[cz[i]])
        mk.dma("sp", sc[i].ap[:, 0:4], ztm_d.ap[t0:t0 + C, c_cb:c_cb + 4], reads=[ztm_d], writes=[sc[i]])
        mk.dma("sp", sc[i].ap[:, 4:8], ztm_d.ap[t0:t0 + C, c_ca:c_ca + 4], reads=[ztm_d], writes=[sc[i]])
        yield
        a0, a1 = tk[0][i], tk[1][i]
        mk.op("dve", lambda en: en.tensor_tensor(a0.ap, U.ap[:, :, 0:C], bc_last(cw.ap[:, :, 0], C), ALU.mult), reads=[U, cw], writes=[a0])
        mk.op("pool", lambda en: en.tensor_tensor(a1.ap, U.ap[:, :, 1:C + 1], bc_last(cw.ap[:, :, 1], C), ALU.mult), reads=[U, cw], writes=[a1])
        mk.op("dve", lambda en: en.tensor_tensor(a0.ap, a0.ap, a1.ap, ALU.add), reads=[a0, a1], writes=[a0])
        mk.op("pool", lambda en: en.tensor_tensor(a1.ap, U.ap[:, :, 2:C + 2], bc_last(cw.ap[:, :, 2], C), ALU.mult), reads=[U, cw], writes=[a1])
        mk.op("dve", lambda en: en.tensor_tensor(a0.ap, a0.ap, a1.ap, ALU.add), reads=[a0, a1], writes=[a0])
        mk.op("pool", lambda en: en.tensor_tensor(a1.ap, U.ap[:, :, 3:C + 3], bc_last(cw.ap[:, :, 3], C), ALU.mult), reads=[U, cw], writes=[a1])
        mk.op("dve", lambda en: en.tensor_tensor(a0.ap, a0.ap, a1.ap, ALU.add), reads=[a0, a1], writes=[a0])
        mk.op("act", lambda en: en.activation(out=qkv[i].ap, in_=a0.ap, func=AF.Silu), reads=[a0], writes=[qkv[i]])
        yield
        mk.op("pool", lambda en: en.tensor_tensor(sq[i].ap, qkv[i].ap[:, 0:8, :], qkv[i].ap[:, 0:8, :], ALU.mult), reads=[qkv[i]], writes=[sq[i]])
        for half in range(2):
            pn = nps()
            mk.mm(pn.ap[0:64, 0:512], [(ones.ap[0:64, 0:64], sq[i].ap[:, half * 4:half * 4 + 4, :].rearrange("p b t -> p (b t)"))], reads=[ones, sq[i]], tr=pn)
            mk.op("act", lambda en, pn=pn, half=half: en.activation(out=rinv[i].ap[:, half * 4:half * 4 + 4, :], in_=pfm(pn), func=AF.Ln, bias=EPS), reads=[pn], writes=[rinv[i]])
        mk.op("act", lambda en: en.activation(out=rinv[i].ap, in_=rinv[i].ap, func=AF.Exp, scale=-0.5), reads=[rinv[i]], writes=[rinv[i]])
        mk.op("dve", lambda en: en.scalar_tensor_tensor(qkv[i].ap[:, 0:4, :], qkv[i].ap[:, 0:4, :], 64 ** -0.5, rinv[i].ap[:, 0:4, :], ALU.mult, ALU.mult),
              reads=[qkv[i], rinv[i]], writes=[qkv[i]])
        mk.op("dve", lambda en: en.tensor_tensor(qkv[i].ap[:, 4:8, :], qkv[i].ap[:, 4:8, :], rinv[i].ap[:, 4:8, :], ALU.mult),
              reads=[qkv[i], rinv[i]], writes=[qkv[i]])
        qT = lambda h: qkv[i].ap[:, h, :]
        kT = lambda h: qkv[i].ap[:, 4 + h, :]
        yield
        pt = nps()
        for b_ in range(8):
            mk.op("pe", lambda pe, b_=b_: pe.transpose(pt.ap[:, b_ * 64:(b_ + 1) * 64], qkv[i].ap[:, 4 + b_, :], id128.ap[0:64, 0:64]),
                  reads=[qkv[i], id128], writes=[pt], inc=(b_ == 7))
        mk.op("act", lambda en: en.copy(kv_tm[i].ap, pt.ap[:, :].rearrange("p (b d) -> p b d", b=8)), reads=[pt], writes=[kv_tm[i]])
        yield
        exp_sigmoid(mk, beta[i], beta[i].ap, sc[i], sc[i].ap[:, 0:4])
        mk.op("dve", lambda en: en.tensor_tensor(sp1[i].ap, sc[i].ap[:, 4:8], dtb.ap, ALU.add), reads=[sc[i], dtb], writes=[sp1[i]])
        mk.op("dve", lambda en: en.tensor_scalar(sp2[i].ap, sp1[i].ap, -1.0, None, ALU.mult), reads=[sp1[i]], writes=[sp2[i]])
        mk.op("dve", lambda en: en.tensor_tensor(sp2[i].ap, sp2[i].ap, sp1[i].ap, ALU.max), reads=[sp1[i], sp2[i]], writes=[sp2[i]])
        mk.op("act", lambda en: en.activation(out=sp2[i].ap, in_=sp2[i].ap, func=AF.Exp, scale=-1.0), reads=[sp2[i]], writes=[sp2[i]])
        mk.op("act", lambda en: en.activation(out=sp2[i].ap, in_=sp2[i].ap, func=AF.Ln, bias=1.0), reads=[sp2[i]], writes=[sp2[i]])
        mk.op("dve", lambda en: en.tensor_scalar(sp1[i].ap, sp1[i].ap, 0.0, None, ALU.max), reads=[sp1[i]], writes=[sp1[i]])
        mk.op("dve", lambda en: en.tensor_tensor(sp1[i].ap, sp1[i].ap, sp2[i].ap, ALU.add), reads=[sp1[i], sp2[i]], writes=[sp1[i]])
        mk.op("dve", lambda en: en.tensor_tensor(g[i].ap, sp1[i].ap, negA.ap, ALU.mult), reads=[sp1[i], negA], writes=[g[i]])
        yield
        mk.op("pool", lambda en: en.tensor_copy(rep[i].ap[:, 0:4, :], bc_last(beta[i].ap, 64)), reads=[beta[i]], writes=[rep[i]])
        mk.op("pool", lambda en: en.tensor_copy(rep[i].ap[:, 4:8, :], bc_last(g[i].ap, 64)), reads=[g[i]], writes=[rep[i]])
        pb1, pb2 = nps(), nps()
        for h in range(H):
            mk.mm(pb1.ap[0:64, h * C:(h + 1) * C], [(rep[i].ap[:, h, :], id128.ap)], reads=[rep[i], id128], tr=pb1)
        for h in range(H):
            mk.mm(pb2.ap[0:64, h * C:(h + 1) * C], [(rep[i].ap[:, 4 + h, :], triI.ap)], reads=[rep[i], triI], tr=pb2)
        mk.op("act", lambda en: en.copy(bb[i].ap, pfm(pb1)), reads=[pb1], writes=[bb[i]])
        mk.op("act", lambda en: en.activation(out=eb[i].ap, in_=pfm(pb2), func=AF.Exp), reads=[pb2], writes=[eb[i]])
        mk.op("dve", lambda en: en.tensor_tensor(kbT[i].ap, qkv[i].ap[:, 4:8, :], bb[i].ap, ALU.mult), reads=[qkv[i], bb[i]], writes=[kbT[i]])
        mk.op("dve", lambda en: en.tensor_tensor(kbeT[i].ap, kbT[i].ap, eb[i].ap, ALU.mult), reads=[kbT[i], eb[i]], writes=[kbeT[i]])
        mk.op("dve", lambda en: en.tensor_tensor(qeT[i].ap, qkv[i].ap[:, 0:4, :], eb[i].ap, ALU.mult), reads=[qkv[i], eb[i]], writes=[qeT[i]])
        yield
        mk.op("pool", lambda en: en.tensor_tensor(Gt[i].ap, bc_mid(triI.ap, H), bc_last(g[i].ap, C), ALU.mult), reads=[triI, g[i]], writes=[Gt[i]])
        pdT, pd = nps(), nps()
        for h in range(H):
            mk.mm(pdT.ap[:, h * C:(h + 1) * C], [(lowS.ap, Gt[i].ap[:, h, :])], reads=[lowS, Gt[i]], tr=pdT)
        for h in range(H):
            mk.mm(pd.ap[:, h * C:(h + 1) * C], [(Gt[i].ap[:, h, :], lowS.ap)], reads=[lowS, Gt[i]], tr=pd)
        mk.op("act", lambda en: en.activation(out=ET[i].ap, in_=psq(pdT), func=AF.Exp), reads=[pdT], writes=[ET[i]])
        mk.op("act", lambda en: en.activation(out=E[i].ap, in_=psq(pd), func=AF.Exp), reads=[pd], writes=[E[i]])
        mk.op("pool", lambda en: en.tensor_tensor(ETi[i].ap, ET[i].ap, bc_mid(triI.ap, H), ALU.mult), reads=[ET[i], triI], writes=[ETi[i]])
        mk.op("dve", lambda en: en.tensor_tensor(ET[i].ap, ET[i].ap, bc_mid(negtriS.ap, H), ALU.mult), reads=[ET[i], negtriS], writes=[ET[i]])
        mk.op("pool", lambda en: en.tensor_tensor(E[i].ap, E[i].ap, bc_mid(neglowS.ap, H), ALU.mult), reads=[E[i], neglowS], writes=[E[i]])
        yield
        pc = nps()
        mk.mm(pc.ap[:, 0:4], [(lowS.ap, g[i].ap)], reads=[lowS, g[i]], tr=pc)
        mk.op("act", lambda en: en.activation(out=elb[i].ap, in_=pc.ap[:, 0:4], func=AF.Exp), reads=[pc], writes=[elb[i]])
        mk.op("dve", lambda en: en.tensor_tensor(kd[i].ap, kv_tm[i].ap[:, 0:4, :], bc_last(elb[i].ap, 64), ALU.mult), reads=[kv_tm[i], elb[i]], writes=[kd[i]])
        mk.op("dve", lambda en: en.tensor_tensor(bv[i].ap, kv_tm[i].ap[:, 4:8, :], bc_last(beta[i].ap, 64), ALU.mult), reads=[kv_tm[i], beta[i]], writes=[bv[i]])
        yield
        pu, pl, pq = nps(), nps(), nps()
        for h in range(H):
            mk.mm(pu.ap[:, h * C:(h + 1) * C], [(kT(h), kbT[i].ap[:, h, :])], reads=[qkv[i], kbT[i]], tr=pu)
        for h in range(H):
            mk.mm(pl.ap[:, h * C:(h + 1) * C], [(kbT[i].ap[:, h, :], kT(h))], reads=[qkv[i], kbT[i]], tr=pl)
        for h in range(H):
            mk.mm(pq.ap[:, h * C:(h + 1) * C], [(kT(h), qT(h))], reads=[qkv[i]], tr=pq)
        mk.op("dve", lambda en: en.tensor_tensor(NU[0].ap, psq(pu), ET[i].ap, ALU.mult), reads=[pu, ET[i]], writes=[NU[0]])
        mk.op("dve", lambda en: en.tensor_tensor(NL[0].ap, psq(pl), E[i].ap, ALU.mult), reads=[pl, E[i]], writes=[NL[0]])
        mk.op("dve", lambda en: en.tensor_tensor(QKT[i].ap, psq(pq), ETi[i].ap, ALU.mult), reads=[pq, ETi[i]], writes=[QKT[i]])
        mk.op("pool", lambda en: en.tensor_tensor(P[i].ap, NU[0].ap, bc_mid(id128.ap, H), ALU.add), reads=[NU[0], id128], writes=[P[i]])
        yield
        cur = 0
        for j in range(1, 7):
            nxt = 1 - cur
            pu, pl = nps(), nps()
            for h in range(H):
                mk.mm(pu.ap[:, h * C:(h + 1) * C], [(NL[cur].ap[:, h, :], NU[cur].ap[:, h, :])], reads=[NL[cur], NU[cur]], tr=pu)
            for h in range(H):
                mk.mm(pl.ap[:, h * C:(h + 1) * C], [(NU[cur].ap[:, h, :], NL[cur].ap[:, h, :])], reads=[NL[cur], NU[cur]], tr=pl)
            mk.op("act", lambda en, nxt=nxt, pu=pu: en.copy(NU[nxt].ap, psq(pu)), reads=[pu], writes=[NU[nxt]])
            mk.op("dve", lambda en, nxt=nxt, pl=pl: en.tensor_copy(NL[nxt].ap, psq(pl)), reads=[pl], writes=[NL[nxt]])
            pp = nps()
            for h in range(H):
                mk.mm(pp.ap[:, h * C:(h + 1) * C], [(NL[nxt].ap[:, h, :], P[i].ap[:, h, :])], reads=[NL[nxt], P[i]], tr=pp)
            mk.op("dve", lambda en, pp=pp: en.tensor_tensor(P[i].ap, P[i].ap, psq(pp), ALU.add), reads=[P[i], pp], writes=[P[i]])
            cur = nxt
            yield
        if c > 0:
            yield ("wait", ("S", c - 1))
        p1 = nps()
        for h in range(H):
            mk.mm(p1.ap[:, h * 64:(h + 1) * 64], [(kbeT[i].ap[:, h, :], S.ap[:, h, :])], reads=[kbeT[i], S], tr=p1)
        mk.op("dve", lambda en: en.tensor_tensor(rhs[i].ap, bv[i].ap, ptm(p1), ALU.subtract), reads=[bv[i], p1], writes=[rhs[i]])
        p2 = nps()
        for h in range(H):
            mk.mm(p2.ap[:, h * 64:(h + 1) * 64], [(P[i].ap[:, h, :], rhs[i].ap[:, h, :])], reads=[P[i], rhs[i]], tr=p2)
        mk.op("act", lambda en: en.copy(vn[i].ap, ptm(p2)), reads=[p2], writes=[vn[i]])
        po, p4 = nps(), nps()
        for h in range(H):
            mk.mm(po.ap[:, h * 64:(h + 1) * 64], [(qeT[i].ap[:, h, :], S.ap[:, h, :]), (QKT[i].ap[:, h, :], vn[i].ap[:, h, :])],
                  reads=[qeT[i], S, QKT[i], vn[i]], tr=po)
        for h in range(H):
            mk.mm(p4.ap[0:64, h * 64:(h + 1) * 64], [(kd[i].ap[:, h, :], vn[i].ap[:, h, :])], reads=[kd[i], vn[i]], tr=p4)
        mk.op("dve", lambda en: en.tensor_tensor(S.ap, S.ap, bc_last(eb[i].ap[:, :, C - 1], 64), ALU.mult), reads=[S, eb[i]], writes=[S])
        mk.op("dve", lambda en: en.tensor_tensor(S.ap, S.ap, p4.ap[0:64, 0:256].rearrange("p (h t) -> p h t", h=H), ALU.add), reads=[S, p4], writes=[S])
        yield ("done", ("S", c))
        mk.op("act", lambda en: en.copy(of[i].ap, po.ap[:, 0:256]), reads=[po], writes=[of[i]])
        head_rmsnorm(mk, of[i], H, 64, gO, yo[i], scr[i], st[i])
        exp_silu(mk, cz[i], cz[i].ap, cz[i], cz[i].ap, scr[i], scr[i].ap)
        mk.op("dve", lambda en: en.tensor_tensor(yo[i].ap, yo[i].ap, cz[i].ap, ALU.mult), reads=[yo[i], cz[i]], writes=[yo[i]])
        mk.dma("pool", y_d.ap[t0:t0 + C, ycol:ycol + 256], yo[i].ap, reads=[yo[i]], writes=[y_d])
    if bg is not None:
        run_pipelined_multi([{"gens": (body(c) for c in range(T // C)), "width": 2}, {"gens": bg, "width": 1, "period": 2}])
    else:
        run_pipelined((body(c) for c in range(T // C)), 2)
    mk.phase_end()


def gdn_consts():
    a = np.arange(64)
    triI = (a[:, None] <= a[None, :]).astype(np.float32)
    b = np.arange(128)
    gI = (b[:, None] <= b[None, :]).astype(np.float32)
    gS = (b[:, None] < b[None, :]).astype(np.float32)
    gL = (b[:, None] > b[None, :]).astype(np.float32)
    return {"triI": triI, "g_triI": gI, "g_lowS": gL, "g_negtriS": -gS, "g_neglowS": -gL,
            "g_id": np.eye(128, dtype=np.float32), "g_ones": np.ones((128, 128), np.float32)}


def nsa_consts(T):
    t = np.arange(T)
    inv = 1.0 / (10000.0 ** (np.arange(0, 64, 2, dtype=np.float32) / 64))
    ang = t[:, None].astype(np.float32) * inv[None, :].astype(np.float32)
    ncp = T // 16
    n = np.arange(ncp)
    ncmp = (T - 32) // 16 + 1
    cm = ((16 * n[None, :] + 31 <= t[:, None]) & (n[None, :] < ncmp)).astype(np.float32)
    ns = T // 64
    j = np.arange(ns)[None, :]
    cur = (t // 64)[:, None]
    valid = j <= cur
    forced = (j == 0) | (j == cur) | (j == cur - 1)
    m1 = (valid & ~forced).astype(np.float32)
    c2 = (1e6 * (valid & forced) - 1.0 * (~valid)).astype(np.float32)
    a = np.arange(128)
    return {"cos": np.cos(ang).astype(np.float32), "sin": np.sin(ang).astype(np.float32),
            "cmpm": cm, "cmpmT": np.ascontiguousarray(cm.T), "selm1": m1, "selc2": c2,
            "causT": (a[:, None] <= a[None, :]).astype(np.float32), "farT": (a[:, None] > a[None, :]).astype(np.float32),
            "esel": (np.arange(ns)[:, None] == (t // 64)[None, :]).astype(np.float32)}


def nsa_n1(mk, ztm_d, c_nq, c_nk, c_nv, T, qn_d, kn_d, cst, ident_d, QT_d, KT_d, VA_d, defer=False):
    NT = T // 128
    mk.phase_begin()
    ident, identf = load_consts_ident(mk, ident_d)
    gqk = mk.sb([128, 14, 64], F32, "gqk")
    mk.dma("sp", gqk.ap[:, 0:8, :], bc_mid(bcast_rows(qn_d.ap, 128), 8), reads=[qn_d], writes=[gqk])
    for ty in range(3):
        mk.dma("sp", gqk.ap[:, 8 + 2 * ty:10 + 2 * ty, :], bc_mid(bcast_rows(kn_d.ap[ty, :], 128), 2), reads=[kn_d], writes=[gqk])
    mk.op("dve", lambda en: en.tensor_scalar(gqk.ap[:, 0:8, :], gqk.ap[:, 0:8, :], 64 ** -0.5, None, ALU.mult), reads=[gqk], writes=[gqk])
    NB = 2 if defer else 3
    xin = [mk.sb([128, 14 * 64], F32, "xin") for _ in range(NB)]
    vin = [mk.sb([128, 256], F32, "vin") for _ in range(NB)]
    cs = [mk.sb([128, 64], F32, "cs") for _ in range(NB)]
    sq = [mk.sb([128, 14 * 64], F32, "sq") for _ in range(NB)]
    st = [mk.sb([128, 42], F32, "st") for _ in range(NB)]
    xn = [mk.sb([128, 14, 64], F32, "xn") for _ in range(NB)]
    r1 = [mk.sb([128, 14, 32], F32, "r1") for _ in range(NB)]
    r2 = [mk.sb([128, 14, 32], F32, "r2") for _ in range(NB)]
    xr = [mk.sb([128, 14, 64], BF16, "xr") for _ in range(NB)]
    xT = [mk.sb([64, 14, 128], BF16, "xT") for _ in range(NB)]
    va = [mk.sb([128, 4, 65], BF16, "va") for _ in range(NB)]
    NPS_ = 1 if defer else NB
    psA = [mk.ps([64, 7, 128], BF16, "psA") for _ in range(NPS_)]
    psB = [mk.ps([64, 7, 128], BF16, "psB") for _ in range(NPS_)]
    def n1body(tt):
        i = tt % NB
        t0 = tt * 128
        X = xin[i]
        mk.dma("sp", X.ap[:, 0:512], ztm_d.ap[t0:t0 + 128, c_nq:c_nq + 512], reads=[ztm_d], writes=[X])
        mk.dma("sp", X.ap[:, 512:896], ztm_d.ap[t0:t0 + 128, c_nk:c_nk + 384], reads=[ztm_d], writes=[X])
        mk.dma("sp", vin[i].ap, ztm_d.ap[t0:t0 + 128, c_nv:c_nv + 256], reads=[ztm_d], writes=[vin[i]])
        mk.dma("sp", cs[i].ap[:, 0:32], cst["cos"].ap[t0:t0 + 128, :], reads=[cst["cos"]], writes=[cs[i]])
        mk.dma("sp", cs[i].ap[:, 32:64], cst["sin"].ap[t0:t0 + 128, :], reads=[cst["sin"]], writes=[cs[i]])
        yield
        X3 = X.ap[:, :].rearrange("p (h d) -> p h d", h=14)
        S3 = sq[i].ap[:, :].rearrange("p (h d) -> p h d", h=14)
        mk.op("act", lambda en, i=i, X=X: en.activation(out=sq[i].ap, in_=X.ap, func=AF.Square), reads=[X], writes=[sq[i]])
        mk.op("dve", lambda en, i=i, S3=S3: en.tensor_reduce(st[i].ap[:, 0:14], S3, AX.X, ALU.add), reads=[sq[i]], writes=[st[i]])
        mk.op("act", lambda en, i=i: en.activation(out=st[i].ap[:, 14:28], in_=st[i].ap[:, 0:14], func=AF.Ln, scale=1.0 / 64, bias=EPS),
              reads=[st[i]], writes=[st[i]])
        mk.op("act", lambda en, i=i: en.activation(out=st[i].ap[:, 28:42], in_=st[i].ap[:, 14:28], func=AF.Exp, scale=-0.5), reads=[st[i]], writes=[st[i]])
        mk.op("dve", lambda en, i=i, X3=X3: en.tensor_tensor(xn[i].ap, X3, bc_last(st[i].ap[:, 28:42], 64), ALU.mult), reads=[X, st[i]], writes=[xn[i]])
        mk.op("pool", lambda en, i=i: en.tensor_tensor(xn[i].ap, xn[i].ap, gqk.ap, ALU.mult), reads=[xn[i], gqk], writes=[xn[i]])
        yield
        cb_ = bc_mid(cs[i].ap[:, 0:32], 14)
        sb_ = bc_mid(cs[i].ap[:, 32:64], 14)
        x1 = xn[i].ap[:, :, 0:32]
        x2 = xn[i].ap[:, :, 32:64]
        mk.op("dve", lambda en, i=i, x1=x1, cb_=cb_: en.tensor_tensor(r1[i].ap, x1, cb_, ALU.mult), reads=[xn[i], cs[i]], writes=[r1[i]])
        mk.op("pool", lambda en, i=i, x2=x2, sb_=sb_: en.tensor_tensor(r2[i].ap, x2, sb_, ALU.mult), reads=[xn[i], cs[i]], writes=[r2[i]])
        mk.op("dve", lambda en, i=i: en.tensor_tensor(xr[i].ap[:, :, 0:32], r1[i].ap, r2[i].ap, ALU.subtract), reads=[r1[i], r2[i]], writes=[xr[i]])
        mk.op("dve", lambda en, i=i, x2=x2, cb_=cb_: en.tensor_tensor(r1[i].ap, x2, cb_, ALU.mult), reads=[xn[i], cs[i]], writes=[r1[i]])
        mk.op("pool", lambda en, i=i, x1=x1, sb_=sb_: en.tensor_tensor(r2[i].ap, x1, sb_, ALU.mult), reads=[xn[i], cs[i]], writes=[r2[i]])
        mk.op("dve", lambda en, i=i: en.tensor_tensor(xr[i].ap[:, :, 32:64], r1[i].ap, r2[i].ap, ALU.add), reads=[r1[i], r2[i]], writes=[xr[i]])
        yield
        for half, psx in ((0, psA[i % NPS_]), (1, psB[i % NPS_])):
            for hh in range(7):
                h = half * 7 + hh
                mk.op("pe", lambda pe, h=h, hh=hh, i=i, psx=psx: pe.transpose(psx.ap[:, hh, :], xr[i].ap[:, h, :], ident.ap),
                      reads=[xr[i], ident], writes=[psx], inc=(hh == 6))
            if half == 0:
                mk.op("act", lambda en, i=i, psx=psx: en.copy(xT[i].ap[:, 0:7, :], psx.ap), reads=[psx], writes=[xT[i]])
            else:
                mk.op("dve", lambda en, i=i, psx=psx: en.tensor_copy(xT[i].ap[:, 7:14, :], psx.ap), reads=[psx], writes=[xT[i]])
        yield
        mk.dma("pool", QT_d.ap[:, :, t0:t0 + 128], xT[i].ap[:, 0:8, :], reads=[xT[i]], writes=[QT_d])
        mk.dma("pool", KT_d.ap[:, :, t0:t0 + 128], xT[i].ap[:, 8:14, :], reads=[xT[i]], writes=[KT_d])
        mk.op("pool", lambda en, i=i: en.memset(va[i].ap[:, :, 64:65], 1.0), writes=[va[i]])
        mk.op("act", lambda en, i=i: en.copy(va[i].ap[:, :, 0:64], vin[i].ap[:, :].rearrange("p (h d) -> p h d", h=4)), reads=[vin[i]], writes=[va[i]])
        mk.dma("pool", VA_d.ap[t0:t0 + 128, :, :], va[i].ap, reads=[va[i]], writes=[VA_d])
    if defer:
        return (n1body(tt) for tt in range(NT)), (lambda: mk.phase_end())
    run_pipelined((n1body(tt) for tt in range(NT)), NB, stagger=2)
    mk.phase_end()
    return None, None


def stage_nsa(mk, ztm_d, c_nq, c_nk, c_nv, c_ng, zfm_d, r_vc, y_d, T, qn_d, kn_d, posk_d, posv_d, w1k_d, w2k_d, w1v_d, w2v_d,
              cst, ident_d, QT_d, KT_d, VA_d, bg=None, skip_n1=False):
    NT = T // 128
    ncp = T // 16
    ncmp = (T - 32) // 16 + 1
    ns = T // 64
    mk.phase_begin()
    ident, identf = load_consts_ident(mk, ident_d)
    if not skip_n1:
        mk.mark("nsa_N1")
        nsa_n1(mk, ztm_d, c_nq, c_nk, c_nv, T, qn_d, kn_d, cst, ident_d, QT_d, KT_d, VA_d)
    mk.mark("nsa_res")
    KsT = mk.sb([128, 2, T], BF16, "KsT")
    KwT = mk.sb([128, 2, T], BF16, "KwT")
    mk.op("pool", lambda en: en.memset(KsT.ap[64:128, :, :], 0.0), writes=[KsT])
    mk.op("pool", lambda en: en.memset(KwT.ap[64:128, :, :], 0.0), writes=[KwT])
    mk.dma("sp", KsT.ap[0:64, :, :], KT_d.ap[:, 2:4, :], reads=[KT_d], writes=[KsT])
    mk.dma("sp", KwT.ap[0:64, :, :], KT_d.ap[:, 4:6, :], reads=[KT_d], writes=[KwT])
    VA = mk.sb([128, NT, 4, 65], BF16, "VA")
    mk.dma("sp", VA.ap, VA_d.ap.rearrange("(n p) f e -> p n f e", p=128), reads=[VA_d], writes=[VA])
    kcT = mk.sb([128, 2, ncp], BF16, "kcT")
    vcA = mk.sb([128, 2, ncp // 128, 65], BF16, "vcA")
    mk.op("pool", lambda en: en.memset(kcT.ap, 0.0), writes=[kcT])
    mk.op("pool", lambda en: en.memset(vcA.ap, 0.0), writes=[vcA])
    mk.op("pool", lambda en: en.memset(vcA.ap[:, :, :, 64:65], 1.0), writes=[vcA])
    mk.mark("nsa_N2")
    mk.phase_begin()
    pss = [mk.ps([128, 512], F32, "psm") for _ in range(4)]
    for which, (w1_d, w2_d, pos_d) in enumerate(((w1k_d, w2k_d, posk_d), (w1v_d, w2v_d, posv_d))):
        if which == 1:
            mk.phase_end()
        mk.phase_begin()
        w1f = mk.sb([64, 32, 128], F32, "w1f")
        w1 = mk.sb([64, 32, 128], BF16, "w1")
        mk.dma("sp", w1f.ap, w1_d.ap.rearrange("(l d) h -> d l h", d=64), reads=[w1_d], writes=[w1f])
        mk.op("pool", lambda en, w1=w1, w1f=w1f: en.tensor_copy(w1.ap, w1f.ap), reads=[w1f], writes=[w1])
        w2f = mk.sb([128, 64], F32, "w2f")
        w2 = mk.sb([128, 64], BF16, "w2")
        mk.dma("sp", w2f.ap, w2_d.ap, reads=[w2_d], writes=[w2f])
        mk.op("pool", lambda en, w2=w2, w2f=w2f: en.tensor_copy(w2.ap, w2f.ap), reads=[w2f], writes=[w2])
        posf = mk.sb([64, 32], F32, "posf")
        posb = mk.sb([64, 32], BF16, "posb")
        mk.dma("sp", posf.ap, pos_d.ap.rearrange("l d -> d l"), reads=[pos_d], writes=[posf], allow_slow_non_contiguous=True)
        mk.op("pool", lambda en, posb=posb, posf=posf: en.tensor_copy(posb.ap, posf.ap), reads=[posf], writes=[posb])
        bias = mk.sb([128, 1], F32, "bias")
        pbias = pss[0]
        mk.mm(pbias.ap[:, 0:1], [(w1.ap[:, l, :], posb.ap[:, l:l + 1]) for l in range(32)], reads=[w1, posb], tr=pbias)
        mk.op("act", lambda en, bias=bias, pbias=pbias: en.copy(bias.ap, pbias.ap[:, 0:1]), reads=[pbias], writes=[bias])
        for g in range(2):
            XT = mk.sb([64, T], BF16, "XT")
            if which == 0:
                mk.dma("sp", XT.ap, KT_d.ap[:, g, :], reads=[KT_d], writes=[XT])
            else:
                XTf = mk.sb([64, T], F32, "XTf")
                mk.dma("sp", XTf.ap, zfm_d.ap[r_vc + g * 64:r_vc + (g + 1) * 64, :], reads=[zfm_d], writes=[XTf])
                mk.op("pool", lambda en, XT=XT, XTf=XTf: en.tensor_copy(XT.ap, XTf.ap), reads=[XTf], writes=[XT])
            ph = pss[1 + g]
            mk.mm(ph.ap[:, 0:ncmp], [(w1.ap[:, l, :], XT.ap[:, l:l + 16 * (ncmp - 1) + 1:16]) for l in range(32)], reads=[w1, XT], tr=ph)
            xs = mk.sb([128, ncp], F32, "xs")
            x2 = mk.sb([128, ncp], F32, "x2")
            ge = mk.sb([128, ncp], BF16, "ge")
            mk.op("pool", lambda en, ge=ge: en.memset(ge.ap, 0.0), writes=[ge])
            mk.op("act", lambda en, xs=xs, ph=ph, bias=bias: en.activation(out=xs.ap[:, 0:ncmp], in_=ph.ap[:, 0:ncmp], func=AF.Identity, bias=bias.ap[:, 0:1]),
                  reads=[ph, bias], writes=[xs])
            mk.op("pool", lambda en, xs=xs, x2=x2: en.tensor_tensor(x2.ap[:, 0:ncmp], xs.ap[:, 0:ncmp], xs.ap[:, 0:ncmp], ALU.mult), reads=[xs], writes=[x2])
            mk.op("dve", lambda en, x2=x2: en.tensor_scalar(x2.ap[:, 0:ncmp], x2.ap[:, 0:ncmp], 0.044715, 1.0, ALU.mult, ALU.add), reads=[x2], writes=[x2])
            mk.op("dve", lambda en, xs=xs, x2=x2: en.tensor_tensor(x2.ap[:, 0:ncmp], x2.ap[:, 0:ncmp], xs.ap[:, 0:ncmp], ALU.mult), reads=[xs, x2], writes=[x2])
            mk.op("act", lambda en, x2=x2: en.activation(out=x2.ap[:, 0:ncmp], in_=x2.ap[:, 0:ncmp], func=AF.Sigmoid, scale=1.5957691216057308),
                  reads=[x2], writes=[x2])
            mk.op("dve", lambda en, xs=xs, x2=x2, ge=ge: en.tensor_tensor(ge.ap[:, 0:ncmp], x2.ap[:, 0:ncmp], xs.ap[:, 0:ncmp], ALU.mult), reads=[xs, x2], writes=[ge])
            po = pss[3]
            if which == 0:
                mk.mm(po.ap[0:64, 0:ncmp], [(w2.ap, ge.ap[:, 0:ncmp])], reads=[w2, ge], tr=po)
                mk.op("act", lambda en, g=g, po=po: en.copy(kcT.ap[0:64, g, 0:ncmp], po.ap[0:64, 0:ncmp]), reads=[po], writes=[kcT])
            else:
                for nt in range(ncp // 128):
                    mk.mm(po.ap[:, nt * 64:(nt + 1) * 64], [(ge.ap[:, nt * 128:(nt + 1) * 128], w2.ap)], reads=[w2, ge], tr=po)
                mk.op("act", lambda en, g=g, po=po: en.copy(vcA.ap[:, g, :, 0:64], po.ap[:, 0:(ncp // 128) * 64].rearrange("p (n d) -> p n d", d=64)),
                      reads=[po], writes=[vcA])
    mk.phase_end()
    mk.phase_end()
    mk.mark("nsa_N3")
    mk.phase_begin()
    causT = mk.sb([128, 128], F32, "causT")
    farT = mk.sb([128, 128], F32, "farT")
    mk.dma("sp", causT.ap, cst["causT"].ap, reads=[cst["causT"]], writes=[causT])
    mk.dma("sp", farT.ap, cst["farT"].ap, reads=[cst["farT"]], writes=[farT])
    esel = mk.sb([ns, T], BF16, "esel")
    mk.phase_begin()
    eself = mk.sb([ns, T], F32, "eself")
    mk.dma("sp", eself.ap, cst["esel"].ap, reads=[cst["esel"]], writes=[eself])
    mk.op("pool", lambda en: en.tensor_copy(esel.ap, eself.ap), reads=[eself], writes=[esel])
    mk.phase_end()
    NQ = ncp // 128
    NS = 2
    psS = [mk.ps([128, 512], F32, "psS") for _ in range(NS)]
    psO = [mk.ps([128, 512], F32, "psO") for _ in range(NS)]
    psT = [[mk.ps([128, 512], F32, "psST") for _ in range(1 if bg is not None else 2)] for _ in range(NS)]
    psM = psS
    mk.ndram = getattr(mk, "ndram", 0) + 1
    selTd = [mk.dram("selTd%d_%d" % (mk.ndram, j), [ns, 128], BF16) for j in range(NS)]
    def mkb(shape, name, dt=F32, n=NS):
        return [mk.sb(list(shape), dt, name) for _ in range(n)]
    QT = mkb([128, 4, 128], "QT", BF16)
    for q_ in QT:
        mk.op("pool", lambda en, q_=q_: en.memset(q_.ap[64:128, :, :], 0.0), writes=[q_])
    gl = mkb([128, 24], "gl")
    cm, cmT = mkb([128, ncp], "cm"), mkb([128, NQ, 128], "cmT")
    m1t, c2t = mkb([128, ns], "m1t"), mkb([128, ns], "c2t")
    sS = mkb([128, 4, ncp], "sS")
    sE = sS
    stt = mkb([128, 16], "stt")
    ph = mkb([128, ncp + 1], "ph")
    imp, wk = mkb([128, ns], "imp"), mkb([128, ns], "wk")
    m8 = mkb([128, 16], "m8")
    selb = mkb([128, ns], "selb")
    selT = mkb([ns, 128], "selT", BF16)
    eT = [mkb([128, 4, 128], "eT", BF16, 3) for _ in range(NS)]
    causB = mk.sb([128, 128], BF16, "causB")
    farB = mk.sb([128, 128], BF16, "farB")
    mk.op("dve", lambda en: en.tensor_copy(causB.ap, causT.ap), reads=[causT], writes=[causB])
    mk.op("dve", lambda en: en.tensor_copy(farB.ap, farT.ap), reads=[farT], writes=[farB])
    cmTb = mkb([128, NQ, 128], "cmTb", BF16)
    bm = [mkb([128, 128], "bm", BF16, 2) for _ in range(NS)]
    rr = mkb([128, 12], "rr")
    acc = mkb([128, 4, 64], "acc")
    tmpo = mkb([128, 4, 64], "tmpo")
    PTall = mkb([128, NT, 4, 128], "PTall", BF16)

    def attn_branch(i, tiles, Ops):
        PA = PTall[i]
        QTi = QT[i]
        n = len(tiles)
        for idx, (KT_ap, V_ap, mask_fn) in enumerate(tiles):
            pt = psT[i][idx % len(psT[i])]
            mk.mm(pt.ap, [(KT_ap, QTi.ap[:, :, :].rearrange("p h q -> p (h q)"))], reads=[QTi, KsT, KwT, kcT], tr=pt)
            e = eT[i][idx % 3]
            p4 = pt.ap[:, :].rearrange("p (h q) -> p h q", h=4)
            msk = mask_fn(idx)
            if msk is None:
                mk.op("act", lambda en, idx=idx, p4=p4: en.activation(out=PA.ap[:, idx, :, :], in_=p4, func=AF.Exp), reads=[pt], writes=[PA])
            else:
                mk.op("act", lambda en, e=e, p4=p4: en.activation(out=e.ap, in_=p4, func=AF.Exp), reads=[pt], writes=[e])
                mb, mreads = msk
                mk.op("dve", lambda en, idx=idx, e=e, mb=mb: en.tensor_tensor(PA.ap[:, idx, :, :], e.ap, bc_mid(mb, 4), ALU.mult), reads=[e] + mreads, writes=[PA])
            yield
        for h in range(4):
            for idx, (KT_ap, V_ap, mask_fn) in enumerate(tiles):
                mk.op("pe", lambda pe, h=h, idx=idx, V_ap=V_ap: pe.matmul(Ops.ap[:, h * 65:(h + 1) * 65], PA.ap[:, idx, h, :], V_ap, start=(idx == 0), stop=(idx == n - 1)),
                      reads=[PA, VA, vcA], writes=[Ops], inc=(idx == n - 1))
            yield

    def combine(i, g, x, first):
        Ops = psO[i]
        O3 = Ops.ap[:, 0:260].rearrange("p (h e) -> p h e", e=65)
        mk.op("dve", lambda en: en.tensor_scalar(rr[i].ap[:, 4 * x:4 * x + 4], O3[:, :, 64], 1e-30, None, ALU.max), reads=[Ops], writes=[rr[i]])
        mk.op("dve", lambda en: en.reciprocal(rr[i].ap[:, 4 * x:4 * x + 4], rr[i].ap[:, 4 * x:4 * x + 4]), reads=[rr[i]], writes=[rr[i]])
        gx = gl[i].ap[:, g * 12:(g + 1) * 12].rearrange("p (h x) -> p h x", x=3)[:, :, x]
        mk.op("dve", lambda en: en.tensor_tensor(rr[i].ap[:, 4 * x:4 * x + 4], rr[i].ap[:, 4 * x:4 * x + 4], gx, ALU.mult), reads=[rr[i], gl[i]], writes=[rr[i]])
        if first:
            mk.op("dve", lambda en: en.tensor_tensor(acc[i].ap, O3[:, :, 0:64], bc_last(rr[i].ap[:, 4 * x:4 * x + 4], 64), ALU.mult), reads=[Ops, rr[i]], writes=[acc[i]])
        else:
            mk.op("dve", lambda en: en.tensor_tensor(tmpo[i].ap, O3[:, :, 0:64], bc_last(rr[i].ap[:, 4 * x:4 * x + 4], 64), ALU.mult), reads=[Ops, rr[i]], writes=[tmpo[i]])
            mk.op("pool", lambda en: en.tensor_tensor(acc[i].ap, acc[i].ap, tmpo[i].ap, ALU.add), reads=[acc[i], tmpo[i]], writes=[acc[i]])

    def body(it, g, qt):
        i = it % NS
        t0 = qt * 128
        mk.dma("sp", QT[i].ap[0:64, :, :], QT_d.ap[:, 4 * g:4 * g + 4, t0:t0 + 128], reads=[QT_d], writes=[QT[i]])
        mk.dma("sp", gl[i].ap, ztm_d.ap[t0:t0 + 128, c_ng:c_ng + 24], reads=[ztm_d], writes=[gl[i]])
        mk.dma("sp", cm[i].ap, cst["cmpm"].ap[t0:t0 + 128, :], reads=[cst["cmpm"]], writes=[cm[i]])
        mk.dma("sp", cmT[i].ap, cst["cmpmT"].ap[:, t0:t0 + 128].rearrange("(n p) q -> p n q", p=128), reads=[cst["cmpmT"]], writes=[cmT[i]])
        mk.dma("sp", m1t[i].ap, cst["selm1"].ap[t0:t0 + 128, :], reads=[cst["selm1"]], writes=[m1t[i]])
        mk.dma("sp", c2t[i].ap, cst["selc2"].ap[t0:t0 + 128, :], reads=[cst["selc2"]], writes=[c2t[i]])
        exp_sigmoid(mk, gl[i], gl[i].ap, gl[i], gl[i].ap)
        mk.op("pool", lambda en: en.tensor_copy(cmTb[i].ap, cmT[i].ap), reads=[cmT[i]], writes=[cmTb[i]])
        yield
        tl = []
        for kt in range(max(0, qt - 4), qt + 1):
            if kt == qt:
                mf = lambda k: (causB.ap, [causB])
            elif kt == qt - 4:
                mf = lambda k: (farB.ap, [farB])
            else:
                mf = lambda k: None
            tl.append((KwT.ap[:, g, kt * 128:(kt + 1) * 128], VA.ap[:, kt, 2 + g, :], mf))
        yield from attn_branch(i, tl, psO[i])
        combine(i, g, 2, True)
        yield
        for hp in range(2):
            for hh in range(2):
                mk.mm(psS[i].ap[:, hh * ncp:(hh + 1) * ncp], [(QT[i].ap[:, hp * 2 + hh, :], kcT.ap[:, g, :])], reads=[QT[i], kcT], tr=psS[i])
            mk.op("act", lambda en, hp=hp: en.copy(sS[i].ap[:, hp * 2:hp * 2 + 2, :], psS[i].ap[:, 0:2 * ncp].rearrange("p (h n) -> p h n", h=2)),
                  reads=[psS[i]], writes=[sS[i]])
        yield
        mk.op("dve", lambda en: en.tensor_reduce(stt[i].ap[:, 0:4], sS[i].ap, AX.X, ALU.max), reads=[sS[i]], writes=[stt[i]])
        mk.op("dve", lambda en: en.tensor_tensor(sS[i].ap, sS[i].ap, bc_last(stt[i].ap[:, 0:4], ncp), ALU.subtract), reads=[sS[i], stt[i]], writes=[sS[i]])
        mk.op("act", lambda en: en.activation(out=sE[i].ap, in_=sS[i].ap, func=AF.Exp), reads=[sS[i]], writes=[sE[i]])
        mk.op("pool", lambda en: en.tensor_tensor(sE[i].ap, sE[i].ap, bc_mid(cm[i].ap, 4), ALU.mult), reads=[sE[i], cm[i]], writes=[sE[i]])
        yield
        mk.op("dve", lambda en: en.tensor_reduce(stt[i].ap[:, 4:8], sE[i].ap, AX.X, ALU.add), reads=[sE[i]], writes=[stt[i]])
        mk.op("dve", lambda en: en.tensor_scalar(stt[i].ap[:, 4:8], stt[i].ap[:, 4:8], 1e-30, None, ALU.max), reads=[stt[i]], writes=[stt[i]])
        mk.op("dve", lambda en: en.reciprocal(stt[i].ap[:, 8:12], stt[i].ap[:, 4:8]), reads=[stt[i]], writes=[stt[i]])
        mk.op("dve", lambda en: en.tensor_tensor(sE[i].ap, sE[i].ap, bc_last(stt[i].ap[:, 8:12], ncp), ALU.mult), reads=[sE[i], stt[i]], writes=[sE[i]])
        yield
        mk.op("pool", lambda en: en.memset(ph[i].ap[:, 0:1], 0.0), writes=[ph[i]])
        mk.op("dve", lambda en: en.tensor_reduce(ph[i].ap[:, 1:ncp + 1], sE[i].ap[:, :, :].rearrange("p h n -> p n h"), AX.X, ALU.add),
              reads=[sE[i]], writes=[ph[i]])
        mk.op("dve", lambda en: en.tensor_reduce(imp[i].ap, ph[i].ap[:, 1:ncp + 1].rearrange("p (j f) -> p j f", f=4), AX.X, ALU.add),
              reads=[ph[i]], writes=[imp[i]])
        mk.op("dve", lambda en: en.tensor_reduce(wk[i].ap, ph[i].ap[:, 0:ncp].rearrange("p (j f) -> p j f", f=4), AX.X, ALU.add),
              reads=[ph[i]], writes=[wk[i]])
        yield
        mk.op("dve", lambda en: en.tensor_tensor(imp[i].ap, imp[i].ap, wk[i].ap, ALU.add), reads=[imp[i], wk[i]], writes=[imp[i]])
        mk.op("dve", lambda en: en.scalar_tensor_tensor(imp[i].ap, imp[i].ap, 16.0, m1t[i].ap, ALU.mult, ALU.mult), reads=[imp[i], m1t[i]], writes=[imp[i]])
        mk.op("dve", lambda en: en.tensor_tensor(imp[i].ap, imp[i].ap, c2t[i].ap, ALU.add), reads=[imp[i], c2t[i]], writes=[imp[i]])
        yield
        mk.op("dve", lambda en: en.max(out=m8[i].ap[:, 0:8], in_=imp[i].ap), reads=[imp[i]], writes=[m8[i]])
        mk.op("dve", lambda en: en.match_replace(out=wk[i].ap, in_to_replace=m8[i].ap[:, 0:8], in_values=imp[i].ap, imm_value=-2.0),
              reads=[imp[i], m8[i]], writes=[wk[i]])
        mk.op("dve", lambda en: en.max(out=m8[i].ap[:, 8:16], in_=wk[i].ap), reads=[wk[i]], writes=[m8[i]])
        yield
        mk.op("dve", lambda en: en.tensor_reduce(stt[i].ap[:, 12:13], m8[i].ap[:, 8:16], AX.X, ALU.min), reads=[m8[i]], writes=[stt[i]])
        mk.op("dve", lambda en: en.tensor_scalar(selb[i].ap, imp[i].ap, stt[i].ap[:, 12:13], None, ALU.is_ge), reads=[imp[i], stt[i]], writes=[selb[i]])
        mk.op("pe", lambda pe: pe.transpose(psM[i].ap[0:ns, 0:128], selb[i].ap, identf.ap), reads=[selb[i], identf], writes=[psM[i]])
        mk.op("act", lambda en: en.copy(selT[i].ap, psM[i].ap[0:ns, 0:128]), reads=[psM[i]], writes=[selT[i]])
        yield
        yield from attn_branch(i, [(kcT.ap[:, g, nt * 128:(nt + 1) * 128], vcA.ap[:, g, nt, :], (lambda k, nt=nt: (cmTb[i].ap[:, nt, :], [cmTb[i]])))
                                  for nt in range(NQ)], psO[i])
        combine(i, g, 0, False)
        yield
        tl = []
        for kt in range(qt + 1):
            def mf(k, kt=kt):
                b = bm[i][k % 2]
                mk.mm(psM[i].ap[:, 0:128], [(esel.ap[:, kt * 128:(kt + 1) * 128], selT[i].ap)], reads=[esel, selT[i]], tr=psM[i])
                if kt == qt:
                    mk.op("dve", lambda en: en.tensor_tensor(b.ap, psM[i].ap[:, 0:128], causT.ap, ALU.mult), reads=[psM[i], causT], writes=[b])
                else:
                    mk.op("act", lambda en: en.copy(b.ap, psM[i].ap[:, 0:128]), reads=[psM[i]], writes=[b])
                return (b.ap, [b])
            tl.append((KsT.ap[:, g, kt * 128:(kt + 1) * 128], VA.ap[:, kt, g, :], mf))
        yield from attn_branch(i, tl, psO[i])
        combine(i, g, 1, False)
        mk.dma("pool", y_d.ap[t0:t0 + 128, g * 256:(g + 1) * 256], acc[i].ap[:, :, :].rearrange("p h d -> p (h d)"), reads=[acc[i]], writes=[y_d])

    order = [(g, qt) for qt in range(NT) for g in range(2)]
    if bg is not None:
        run_pipelined_multi([{"gens": (body(it, g, qt) for it, (g, qt) in enumerate(order)), "width": NS},
                             {"gens": bg, "width": 1, "period": 3}])
    else:
        run_pipelined((body(it, g, qt) for it, (g, qt) in enumerate(order)), NS, stagger=14)
    mk.phase_end()
    mk.phase_end()


T_SEQ = 4096
TAIL_T = 2176
NTM, NFM = 2208, 1408
C_NQ, C_NK, C_NV, C_NG, C_HF, C_HI, C_HG, C_CZ, C_CB, C_CA = 0, 512, 896, 1152, 1176, 1432, 1688, 1944, 2200, 2204
R_VC, R_HQ, R_HF, R_QKV = 0, 128, 384, 640
_r = lambda a, b: list(range(a, b))
W_IN_COLS = (_r(0, 512) + _r(512, 640) + _r(768, 896) + _r(1024, 1152) + _r(896, 1024) + _r(1152, 1280) + _r(1280, 1304)
             + _r(1560, 1816) + _r(1816, 2072) + _r(2072, 2328) + _r(3096, 3352) + _r(3352, 3356) + _r(3356, 3360)
             + _r(640, 768) + _r(1304, 1560) + _r(1560, 1816) + _r(2328, 3096))
assert len(W_IN_COLS) == NTM + NFM

LAYER_KEYS = ["norm_mix", "w_in_sel", "w_gate", "nsa_q_norm", "nsa_k_norm", "cmp_pos_k", "cmp_pos_v", "cmp_k_w1", "cmp_k_w2",
              "cmp_v_w1", "cmp_v_w2", "hgrn_out_norm", "gdn_conv_t", "gdn_a_log", "gdn_dt_bias", "gdn_out_norm",
              "w_branch_a", "w_branch_b", "w_branch_c", "w_mix_out", "norm_cross", "xattn_wq", "xattn_q_norm", "xattn_k_norm",
              "xattn_wo", "norm_ffn", "ffn_w_up", "ffn_conv_t", "ffn_w_down"]


def const_arrays(T):
    c = dict(gdn_consts())
    c.update(nsa_consts(T))
    c["ident"] = np.eye(128, dtype=np.float32)
    return c


def layer_arrays(inputs, l):
    f = lambda a: np.ascontiguousarray(np.asarray(a, dtype=np.float32))
    w_in = np.asarray(inputs["w_in"][l])
    d = {"w_in_sel": f(w_in[:, W_IN_COLS]), "w_gate": f(w_in[:, 3360:6432]),
         "gdn_conv_t": f(np.asarray(inputs["gdn_conv"][l]).T), "ffn_conv_t": f(np.asarray(inputs["ffn_conv"][l]).T)}
    for k in LAYER_KEYS:
        if k not in d:
            d[k] = f(inputs[k][l])
    return d


def build_program(T=T_SEQ, depth=2, shapes=None, stop_after=None):
    mk = MK()
    mk.live.append([])
    ext = lambda name, shape: mk.dram(name, shape, kind="ExternalInput")
    x_in = ext("x", [T, D])
    mem_in = ext("mem", [256, D])
    mem_norm = ext("mem_norm", [D])
    mem_w_kv = ext("mem_w_kv", [D, D])
    lbl = ext("hgrn_lb_logits", [2, 256])
    cst = {k: ext("c_" + k, list(v.shape)) for k, v in const_arrays(T).items()}
    L = []
    for l in range(depth):
        L.append({k: ext("L%d_%s" % (l, k), list(shapes[k])) for k in LAYER_KEYS})
    out = mk.dram("out", [TAIL_T, D], kind="ExternalOutput")
    hsel = ext("hsel", [1])
    xh = [mk.dram("xh%d" % j, [TAIL_T, D]) for j in range(3)]
    xs = [mk.dram("xA", [T, D]), mk.dram("xB", [T, D])]
    Ztm = mk.dram("Ztm", [T, NTM])
    Zfm = mk.dram("Zfm", [NFM, T])
    Y = mk.dram("Y", [T, D])
    MKV = mk.dram("MKV", [256, D])
    QT_d = mk.dram("QT", [64, 8, T], BF16)
    KT_d = mk.dram("KT", [64, 6, T], BF16)
    VA_d = mk.dram("VA", [T, 4, 65], BF16)
    ident = cst["ident"]
    stage_proj(mk, mem_in, mem_norm, mem_w_kv, MKV, None, 256, D, 0, ident)
    cur = x_in
    nstage = 0
    def nxt(last=False):
        return out if last else xs[nstage % 2]
    for l in range(depth):
        W = L[l]
        mk.mark("stage_proj")
        stage_proj(mk, cur, W["norm_mix"], W["w_in_sel"], Ztm, Zfm, T, NTM, NFM, ident)
        mk.mark("stage_gdn")
        n1_gens, n1_fin = nsa_n1(mk, Ztm, C_NQ, C_NK, C_NV, T, W["nsa_q_norm"], W["nsa_k_norm"], cst, ident, QT_d, KT_d, VA_d, defer=True)
        stage_gdn(mk, Ztm, C_CZ, C_CB, C_CA, Zfm, R_QKV, Y, 768, T, W["gdn_conv_t"], W["gdn_a_log"], W["gdn_dt_bias"], W["gdn_out_norm"], cst, bg=n1_gens)
        n1_fin()
        mk.mark("stage_nsa")
        hg_gens, hg_fin = stage_hgrn(mk, Ztm, C_HF, C_HI, C_HG, Zfm, R_HQ, R_HF, Y, 512, T, lbl, l, W["hgrn_out_norm"], cst["triI"], ident, defer=True)
        stage_nsa(mk, Ztm, C_NQ, C_NK, C_NV, C_NG, Zfm, R_VC, Y, T, W["nsa_q_norm"], W["nsa_k_norm"], W["cmp_pos_k"], W["cmp_pos_v"],
                  W["cmp_k_w1"], W["cmp_k_w2"], W["cmp_v_w1"], W["cmp_v_w2"], cst, ident, QT_d, KT_d, VA_d, bg=hg_gens, skip_n1=True)
        hg_fin()
        if stop_after == ("mix", l):
            mk.dma("sp", out.ap, Y.ap, reads=[Y], writes=[out])
            break
        last = (l == depth - 1)
        Tt = TAIL_T if last else T
        xin, yin = cur, Y
        if last:
            mk.mark("stage_select")
            xin, yin = xh[0], xh[1]
            stage_select(mk, cur, xin, Tt, T - Tt, hsel)
            stage_select(mk, Y, yin, Tt, T - Tt, hsel)
        x1 = xh[2] if last else nxt(); nstage += 1
        mk.mark("stage_merge")
        stage_merge(mk, xin, yin, x1, Tt, W["norm_mix"], W["w_gate"], W["w_branch_a"], W["w_branch_b"], W["w_branch_c"], W["w_mix_out"], ident)
        x2 = xh[0] if last else nxt(); nstage += 1
        mk.mark("stage_xattn")
        stage_xattn(mk, x1, x2, Tt, W["norm_cross"], MKV, W["xattn_wq"], W["xattn_q_norm"], W["xattn_k_norm"], W["xattn_wo"], ident)
        x3 = out if last else nxt(); nstage += 1
        mk.mark("stage_ffn")
        stage_ffn(mk, x2, x3, Tt, W["norm_ffn"], W["ffn_w_up"], W["ffn_conv_t"], W["ffn_w_down"], ident)
        cur = x3
    mk.mark("end")
    mk.finish()
    return mk


_PROG = {}


def kernel(**inputs):
    x = np.asarray(inputs["x"], dtype=np.float32)
    B, T, _ = x.shape
    depth = np.asarray(inputs["w_in"]).shape[0]
    layers = [layer_arrays(inputs, l) for l in range(depth)]
    shapes = {k: v.shape for k, v in layers[0].items()}
    key = (T, depth)
    if key not in _PROG:
        _PROG[key] = build_program(T, depth, shapes)
    mk = _PROG[key]
    f = lambda a: np.ascontiguousarray(np.asarray(a, dtype=np.float32))
    common = {"mem_norm": f(inputs["mem_norm"]), "mem_w_kv": f(inputs["mem_w_kv"]), "hgrn_lb_logits": f(inputs["hgrn_lb_logits"])}
    for k, v in const_arrays(T).items():
        common["c_" + k] = v
    for l in range(depth):
        for k, v in layers[l].items():
            common["L%d_%s" % (l, k)] = v
    n = 8
    in_maps = []
    for c in range(n):
        b = c % B
        m = dict(common)
        m["x"] = f(x[b])
        m["mem"] = f(np.asarray(inputs["mem"])[b])
        m["hsel"] = np.full((1,), float(c // B), np.float32)
        in_maps.append(m)
    res = run_bass_kernel_spmd(mk.nc, in_maps, core_ids=list(range(n)))
    Th = T // 2
    outs = []
    for b in range(B):
        lo = np.asarray(res.results[b]["out"], dtype=np.float32)[0:Th]
        hi = np.asarray(res.results[B + b]["out"], dtype=np.float32)[TAIL_T - Th:TAIL_T]
        outs.append(np.concatenate([lo, hi], axis=0))
    return np.stack(outs, axis=0)
```

```python
import numpy as np
import concourse.bass as bass
import concourse.mybir as mybir
from concourse.bass_utils import run_bass_kernel_spmd

F32 = mybir.dt.float32
BF16 = mybir.dt.bfloat16
AF = mybir.ActivationFunctionType
ALU = mybir.AluOpType
AX = mybir.AxisListType
F32R = mybir.dt.float32r
FP32R = False


class Buf:
    __slots__ = ("ap", "w", "r", "name")

    def __init__(self, ap, name):
        self.ap = ap
        self.name = name
        self.w = None
        self.r = {}

    def __getitem__(self, k):
        return self.ap[k]


class MK:
    NDMA = 6

    def __init__(self):
        self.nc = bass.Bass("TRN2", target_bir_lowering=False)
        nc = self.nc
        self.eng = {"pe": nc.tensor, "act": nc.scalar, "dve": nc.vector, "pool": nc.gpsimd, "sp": nc.sync}
        self.sem = {}
        self.cnt = {}
        for e in self.eng:
            self.sem[e] = nc.alloc_semaphore("s_" + e)
            self.cnt[e] = 0
        self.dq = {}
        for q in ("sp", "pool", "act"):
            ks = []
            for i in range(self.NDMA):
                k = "d_%s%d" % (q, i)
                self.sem[k] = nc.alloc_semaphore(k)
                self.cnt[k] = 0
                ks.append(k)
            self.dq[q] = [ks, 0]
        self.obs = {e: {} for e in self.eng}
        self.nbuf = 0
        self.live = []
        self.n_ins = 0
        self.marks = []
        self._pm = None

    def dram(self, name, shape, dt=F32, kind="Internal"):
        t = self.nc.dram_tensor(name, list(shape), dt, kind=kind)
        return Buf(t.ap(), name)

    def sb(self, shape, dt=F32, name=None):
        self.nbuf += 1
        name = "%s_%d" % (name or "sb", self.nbuf)
        g = self.nc.sbuf_tensor(name, list(shape), dt)
        h = g.__enter__()
        self.live[-1].append(g)
        return Buf(h.ap(), name)

    def ps(self, shape, dt=F32, name=None):
        self.nbuf += 1
        name = "%s_%d" % (name or "ps", self.nbuf)
        g = self.nc.psum_tensor(name, list(shape), dt)
        h = g.__enter__()
        self.live[-1].append(g)
        return Buf(h.ap(), name)

    def phase_begin(self):
        self.live.append([])

    def phase_end(self):
        self.barrier()
        for g in reversed(self.live.pop()):
            g.__exit__(None, None, None)

    def _wait(self, e, evs):
        eng = self.eng[e]
        ob = self.obs[e]
        for k, v in evs:
            if k == e and e == "pe":
                continue
            if ob.get(k, 0) < v:
                eng.wait_ge(self.sem[k], v)
                ob[k] = v

    def _deps(self, e, reads, writes):
        evs = []
        for t in reads:
            if t.w is not None:
                evs.append(t.w)
        for t in writes:
            if t.w is not None:
                evs.append(t.w)
            for k, v in t.r.items():
                if k != e:
                    evs.append((k, v))
        return evs

    def _mark(self, ev, reads, writes):
        k, v = ev
        for t in reads:
            if t.r.get(k, 0) < v:
                t.r[k] = v
        for t in writes:
            t.w = ev
            t.r = {}

    def op(self, e, fn, reads=(), writes=(), inc=True):
        self._wait(e, self._deps(e, reads, writes))
        ins = fn(self.eng[e])
        self.n_ins += 1
        self._domark(ins)
        if inc:
            self.cnt[e] += 1
            ins.then_inc(self.sem[e], 1)
            ev = (e, self.cnt[e])
        else:
            ev = (e, self.cnt[e] + 1)
        self._mark(ev, reads, writes)
        return ins

    def dma(self, q, out, in_, reads=(), writes=(), **kw):
        ks, i = self.dq[q]
        k = ks[i % len(ks)]
        self.dq[q][1] = i + 1
        evs = self._deps(q, reads, writes)
        if self.cnt[k] > 0:
            evs.append((k, self.cnt[k]))
        self._wait(q, evs)
        ins = self.eng[q].dma_start(out=out, in_=in_, **kw)
        self.n_ins += 1
        self._domark(ins)
        self.cnt[k] += 16
        ins.then_inc(self.sem[k], 16)
        self._mark((k, self.cnt[k]), reads, writes)
        return ins

    def mark(self, name):
        self._pm = name

    def _domark(self, ins):
        if self._pm is not None:
            try:
                self.marks.append((self._pm, ins.ins.name))
            except Exception as ex:
                self.marks.append((self._pm, repr(ex)))
            self._pm = None

    def barrier(self):
        allev = [(k, v) for k, v in self.cnt.items() if v > 0]
        for e in self.eng:
            self._wait(e, [(k, v) for k, v in allev if k != e])

    def finish(self):
        self.barrier()

    def mm(self, out, pairs, reads, tr=None):
        n = len(pairs)
        if FP32R:
            pairs = [((l.bitcast(F32R) if l.dtype == F32 else l), (r.bitcast(F32R) if r.dtype == F32 else r)) for l, r in pairs]
        for i, (l, r) in enumerate(pairs):
            self.op("pe", lambda pe, l=l, r=r, i=i: pe.matmul(out, l, r, start=(i == 0), stop=(i == n - 1)),
                    reads=reads, writes=[tr], inc=(i == n - 1))


def run_pipelined(gens, width, stagger=0):
    it = iter(gens)
    active = []
    done = set()
    exhausted = False
    head = stagger
    while True:
        while not exhausted and len(active) < (1 if head > 0 else width):
            g = next(it, None)
            if g is None:
                exhausted = True
                break
            active.append([g, None])
        if not active:
            break
        progressed = False
        for slot in list(active):
            g, blk = slot
            if blk is not None and blk not in done:
                continue
            slot[1] = None
            progressed = True
            try:
                r = next(g)
            except StopIteration:
                active.remove(slot)
                continue
            if isinstance(r, tuple):
                if r[0] == "wait":
                    slot[1] = r[1]
                elif r[0] == "done":
                    done.add(r[1])
        if head > 0:
            head -= 1
        assert progressed, "pipeline deadlock"


def run_pipelined_multi(streams):
    st = [{"it": iter(x["gens"]), "w": x["width"], "p": x.get("period", 1), "active": [], "ex": False} for x in streams]
    done = set()
    rnd = 0
    while True:
        alive = False
        for S_ in st:
            while not S_["ex"] and len(S_["active"]) < S_["w"]:
                g = next(S_["it"], None)
                if g is None:
                    S_["ex"] = True
                    break
                S_["active"].append([g, None])
            if S_["active"]:
                alive = True
        if not alive:
            break
        progressed = False
        only_bg = all((not S_["active"]) for S_ in st if S_["p"] == 1)
        for S_ in st:
            if S_["p"] > 1 and (rnd % S_["p"]) != 0 and not only_bg:
                continue
            for slot in list(S_["active"]):
                g, blk = slot
                if blk is not None and blk not in done:
                    continue
                slot[1] = None
                progressed = True
                try:
                    r = next(g)
                except StopIteration:
                    S_["active"].remove(slot)
                    continue
                if isinstance(r, tuple):
                    if r[0] == "wait":
                        slot[1] = r[1]
                    elif r[0] == "done":
                        done.add(r[1])
        rnd += 1
        assert progressed or any(S_["p"] > 1 for S_ in st), "pipeline deadlock"


D = 1024
EPS = 1e-6
TINY = 1e-20


def bcast_rows(ap1d, n):
    return ap1d.partition_broadcast(n)


def load_weight_bf16(mk, w_d, K, N, name, q="sp", chunk=2048):
    kc = K // 128
    wb = mk.sb([128, kc, N], BF16, name)
    mk.phase_begin()
    stg = [mk.sb([128, min(N, chunk)], F32, "wstg") for _ in range(4)]
    i = 0
    for k in range(kc):
        for c0 in range(0, N, chunk):
            c1 = min(N, c0 + chunk)
            s = stg[i % 4]
            mk.dma(("sp", "act")[i % 2], s.ap[:, 0:c1 - c0], w_d.ap[k * 128:(k + 1) * 128, c0:c1], reads=[w_d], writes=[s])
            e = "pool" if i % 2 == 0 else "dve"
            mk.op(e, lambda en, s=s, k=k, c0=c0, c1=c1: en.tensor_copy(wb.ap[:, k, c0:c1], s.ap[:, 0:c1 - c0]),
                  reads=[s], writes=[wb])
            i += 1
    mk.phase_end()
    return wb


def rmsnorm_to_fm(mk, xt, gB, ident, hT, col0, scr, ssb, hb, psT):
    mk.op("act", lambda en: en.activation(out=scr.ap, in_=xt.ap, func=AF.Square, accum_out=ssb.ap[:, 0:1]),
          reads=[xt], writes=[scr, ssb])
    mk.op("act", lambda en: en.activation(out=ssb.ap[:, 1:2], in_=ssb.ap[:, 0:1], func=AF.Sqrt, scale=1.0 / D, bias=EPS),
          reads=[ssb], writes=[ssb])
    mk.op("dve", lambda en: en.reciprocal(ssb.ap[:, 2:3], ssb.ap[:, 1:2]), reads=[ssb], writes=[ssb])
    mk.op("dve", lambda en: en.scalar_tensor_tensor(hb.ap, xt.ap, ssb.ap[:, 2:3], gB.ap, ALU.mult, ALU.mult),
          reads=[xt, ssb, gB], writes=[hb])
    tm_to_fm(mk, hb, ident, hT, col0, psT)


def tm_to_fm(mk, hb, ident, hT, col0, psT, nk=8):
    for k in range(nk):
        mk.op("pe", lambda pe, k=k: pe.transpose(psT.ap[:, k, :], hb.ap[:, k * 128:(k + 1) * 128], ident.ap),
              reads=[hb, ident], writes=[psT], inc=(k == nk - 1))
    mk.op("act", lambda en: en.copy(hT.ap[:, 0:nk, col0:col0 + 128], psT.ap[:, 0:nk, :]), reads=[psT], writes=[hT])


def norm_block(mk, jobs, gB, ident, slots):
    run_pipelined(norm_gens(mk, jobs, gB, ident, slots), len(slots), stagger=1)


def norm_gens(mk, jobs, gB, ident, slots):
    def gen(idx, xt, src, hT, col0, kind):
        sl = slots[idx % len(slots)]
        junk, ssb, hb, psT = sl["junk"], sl["ssb"], sl["hb"], sl["psT"]
        if src is not None:
            mk.dma("sp", xt.ap, src[1], reads=[src[0]], writes=[xt])
        if kind == "norm":
            mk.op("act", lambda en: en.activation(out=junk.ap, in_=xt.ap, func=AF.Square, accum_out=ssb.ap[:, 0:1]),
                  reads=[xt], writes=[junk, ssb])
            yield
            mk.op("act", lambda en: en.activation(out=ssb.ap[:, 1:2], in_=ssb.ap[:, 0:1], func=AF.Ln, scale=1.0 / D, bias=EPS),
                  reads=[ssb], writes=[ssb])
            mk.op("act", lambda en: en.activation(out=ssb.ap[:, 2:3], in_=ssb.ap[:, 1:2], func=AF.Exp, scale=-0.5), reads=[ssb], writes=[ssb])
            mk.op("dve", lambda en: en.scalar_tensor_tensor(hb.ap, xt.ap, ssb.ap[:, 2:3], gB.ap, ALU.mult, ALU.mult),
                  reads=[xt, ssb, gB], writes=[hb])
        else:
            mk.op("pool", lambda en: en.tensor_copy(hb.ap, xt.ap), reads=[xt], writes=[hb])
        yield
        for k in range(8):
            mk.op("pe", lambda pe, k=k: pe.transpose(psT.ap[:, k, :], hb.ap[:, k * 128:(k + 1) * 128], ident.ap),
                  reads=[hb, ident], writes=[psT], inc=(k == 7))
        yield
        mk.op("act", lambda en: en.copy(hT.ap[:, 0:8, col0:col0 + 128], psT.ap[:, 0:8, :]), reads=[psT], writes=[hT])
    return (gen(i, *j) for i, j in enumerate(jobs))


def norm_slots(mk, n, psTs):
    return [{"junk": mk.sb([128, D], BF16, "junk"), "ssb": mk.sb([128, 4], F32, "ssb"), "hb": mk.sb([128, D], BF16, "hb"), "psT": psTs[i]}
            for i in range(n)]


def stage_proj(mk, x_d, g_d, w_d, ztm_d, zfm_d, T, ntm, nfm, ident_d):
    mk.phase_begin()
    N = ntm + nfm
    wb = load_weight_bf16(mk, w_d, D, N, "w_in")
    gB = mk.sb([128, D], F32, "gB")
    mk.dma("sp", gB.ap, bcast_rows(g_d.ap, 128), reads=[g_d], writes=[gB])
    identf = mk.sb([128, 128], F32, "identf")
    ident = mk.sb([128, 128], BF16, "ident")
    mk.dma("sp", identf.ap, ident_d.ap, reads=[ident_d], writes=[identf])
    mk.op("dve", lambda en: en.tensor_copy(ident.ap, identf.ap), reads=[identf], writes=[ident])
    xts = [mk.sb([128, D], F32, "xt") for _ in range(2)]
    TB = min(512, T)
    NTB = TB // 128
    hTs = [mk.sb([128, 8, TB], BF16, "hT") for _ in range(2)]
    psTs = [mk.ps([128, 8, 128], BF16, "psT") for _ in range(2)]
    pss = [mk.ps([128, 512], F32, "psm") for _ in range(4)]
    nsl = norm_slots(mk, 2, psTs)
    ofm = [mk.sb([128, 512], F32, "ofm") for _ in range(3)]
    otm = [mk.sb([128, max(ntm, 1)], F32, "otm") for _ in range(2)]
    it = 0
    ip = 0
    io = 0
    nblk = T // TB

    def mk_jobs(tb):
        jobs = []
        for j in range(NTB):
            t0 = tb * TB + j * 128
            jobs.append((xts[(tb * NTB + j) % 2], (x_d, x_d.ap[t0:t0 + 128, :]), hTs[tb % 2], j * 128, "norm"))
        return jobs

    cnt = {"ip": 0, "io": 0}

    def mm_gen(tb):
        hT = hTs[tb % 2]
        for c in range(nfm // 128):
            ps = pss[cnt["ip"] % 4]
            cnt["ip"] += 1
            c0 = ntm + c * 128
            mk.mm(ps.ap[:, 0:TB], [(wb.ap[:, k, c0:c0 + 128], hT.ap[:, k, :]) for k in range(8)], reads=[wb, hT], tr=ps)
            o = ofm[cnt["io"] % 3]
            e = "act" if cnt["io"] % 2 == 0 else "dve"
            cnt["io"] += 1
            if e == "act":
                mk.op("act", lambda en, o=o, ps=ps: en.copy(o.ap[:, 0:TB], ps.ap[:, 0:TB]), reads=[ps], writes=[o])
            else:
                mk.op("dve", lambda en, o=o, ps=ps: en.tensor_copy(o.ap[:, 0:TB], ps.ap[:, 0:TB]), reads=[ps], writes=[o])
            mk.dma("pool", zfm_d.ap[c * 128:(c + 1) * 128, tb * TB:(tb + 1) * TB], o.ap[:, 0:TB], reads=[o], writes=[zfm_d])
            yield
        for j in range(NTB):
            o = otm[(tb * NTB + j) % 2]
            t0 = tb * TB + j * 128
            for c0 in range(0, ntm, 512):
                c1 = min(ntm, c0 + 512)
                ps = pss[cnt["ip"] % 4]
                cnt["ip"] += 1
                mk.mm(ps.ap[:, 0:c1 - c0], [(hT.ap[:, k, j * 128:(j + 1) * 128], wb.ap[:, k, c0:c1]) for k in range(8)],
                      reads=[wb, hT], tr=ps)
                e = "act" if cnt["io"] % 2 == 0 else "dve"
                cnt["io"] += 1
                if e == "act":
                    mk.op("act", lambda en, o=o, ps=ps, c0=c0, c1=c1: en.copy(o.ap[:, c0:c1], ps.ap[:, 0:c1 - c0]),
                          reads=[ps], writes=[o])
                else:
                    mk.op("dve", lambda en, o=o, ps=ps, c0=c0, c1=c1: en.tensor_copy(o.ap[:, c0:c1], ps.ap[:, 0:c1 - c0]),
                          reads=[ps], writes=[o])
                yield
            if ntm:
                mk.dma("pool", ztm_d.ap[t0:t0 + 128, :], o.ap, reads=[o], writes=[ztm_d])

    norm_block(mk, mk_jobs(0), gB, ident, nsl)
    for tb in range(nblk):
        streams = [{"gens": [mm_gen(tb)], "width": 1}]
        if tb + 1 < nblk:
            streams.append({"gens": norm_gens(mk, mk_jobs(tb + 1), gB, ident, nsl), "width": 2, "period": 2})
        run_pipelined_multi(streams)
    mk.phase_end()


def evac(mk, i, out_ap, in_ap, reads, writes):
    if i % 2 == 0:
        mk.op("act", lambda en: en.copy(out_ap, in_ap), reads=reads, writes=writes)
    else:
        mk.op("dve", lambda en: en.tensor_copy(out_ap, in_ap), reads=reads, writes=writes)


def load_consts_ident(mk, ident_d):
    identf = mk.sb([128, 128], F32, "identf")
    ident = mk.sb([128, 128], BF16, "ident")
    mk.dma("sp", identf.ap, ident_d.ap, reads=[ident_d], writes=[identf])
    mk.op("dve", lambda en: en.tensor_copy(ident.ap, identf.ap), reads=[identf], writes=[ident])
    return ident, identf


def tok_blocks(T, full=512):
    out = []
    rem = T % full
    pos = 0
    if rem:
        out.append((0, rem // 128))
        pos = rem
    while pos < T:
        out.append((pos, full // 128))
        pos += full
    return out


def stage_merge(mk, x_d, y_d, xo_d, T, gn_d, wg_d, wa_d, wb_d, wc_d, wmix_d, ident_d):
    mk.phase_begin()
    wg = load_weight_bf16(mk, wg_d, D, 3072, "wg")
    wa = load_weight_bf16(mk, wa_d, 512, D, "wa")
    wbb = load_weight_bf16(mk, wb_d, 256, D, "wb")
    wc = load_weight_bf16(mk, wc_d, 256, D, "wc")
    wmix = load_weight_bf16(mk, wmix_d, D, D, "wmix")
    gB = mk.sb([128, D], F32, "gB")
    mk.dma("sp", gB.ap, bcast_rows(gn_d.ap, 128), reads=[gn_d], writes=[gB])
    ident, _ = load_consts_ident(mk, ident_d)
    xts = [mk.sb([128, D], F32, "xt") for _ in range(8)]
    yts = [mk.sb([128, D], F32, "yt") for _ in range(2)]
    hTs = [mk.sb([128, 8, 512], BF16, "hT") for _ in range(2)]
    yTs = [mk.sb([128, 8, 512], BF16, "yT") for _ in range(2)]
    mT = mk.sb([128, 8, 512], BF16, "mT")
    sg = [mk.sb([128, 512], F32, "sg") for _ in range(3)]
    acc = [mk.sb([128, 512], F32, "acc") for _ in range(2)]
    tmp = [mk.sb([128, 512], F32, "tmp") for _ in range(2)]
    psTs = [mk.ps([128, 8, 128], BF16, "psT") for _ in range(2)]
    pss = [mk.ps([128, 512], F32, "psm") for _ in range(5)]
    nsl = norm_slots(mk, 2, psTs)
    blocks = tok_blocks(T)

    def mk_jobs(bi):
        tbase, nt = blocks[bi]
        jobs = []
        for j in range(nt):
            t0 = tbase + j * 128
            jobs.append((xts[(bi % 2) * 4 + j], (x_d, x_d.ap[t0:t0 + 128, :]), hTs[bi % 2], j * 128, "norm"))
            jobs.append((yts[j % 2], (y_d, y_d.ap[t0:t0 + 128, :]), yTs[bi % 2], j * 128, "cast"))
        return jobs

    cnt = {"ip": 0}

    def mm_gen(bi):
        tbase, nt = blocks[bi]
        Wd = nt * 128
        hT, yT = hTs[bi % 2], yTs[bi % 2]
        for c in range(8):
            a = acc[c % 2]
            for b, (wbr, koff, nk) in enumerate(((wa, 0, 4), (wbb, 4, 2), (wc, 6, 2))):
                psg = pss[cnt["ip"] % 5]
                cnt["ip"] += 1
                g0 = b * 1024 + c * 128
                mk.mm(psg.ap[:, 0:Wd], [(wg.ap[:, k, g0:g0 + 128], hT.ap[:, k, 0:Wd]) for k in range(8)], reads=[wg, hT], tr=psg)
                s = sg[b]
                mk.op("act", lambda en, s=s, psg=psg, Wd=Wd: en.activation(out=s.ap[:, 0:Wd], in_=psg.ap[:, 0:Wd], func=AF.Sigmoid),
                      reads=[psg], writes=[s])
                psb = pss[cnt["ip"] % 5]
                cnt["ip"] += 1
                mk.mm(psb.ap[:, 0:Wd], [(wbr.ap[:, k, c * 128:(c + 1) * 128], yT.ap[:, koff + k, 0:Wd]) for k in range(nk)],
                      reads=[wbr, yT], tr=psb)
                if b == 0:
                    mk.op("dve", lambda en, a=a, s=s, psb=psb, Wd=Wd: en.tensor_tensor(a.ap[:, 0:Wd], s.ap[:, 0:Wd], psb.ap[:, 0:Wd], ALU.mult),
                          reads=[s, psb], writes=[a])
                else:
                    t = tmp[b % 2]
                    mk.op("dve", lambda en, t=t, s=s, psb=psb, Wd=Wd: en.tensor_tensor(t.ap[:, 0:Wd], s.ap[:, 0:Wd], psb.ap[:, 0:Wd], ALU.mult),
                          reads=[s, psb], writes=[t])
                    if b == 1:
                        mk.op("pool", lambda en, a=a, t=t, Wd=Wd: en.tensor_tensor(a.ap[:, 0:Wd], a.ap[:, 0:Wd], t.ap[:, 0:Wd], ALU.add),
                              reads=[a, t], writes=[a])
                    else:
                        mk.op("pool", lambda en, a=a, t=t, c=c, Wd=Wd: en.tensor_tensor(mT.ap[:, c, 0:Wd], a.ap[:, 0:Wd], t.ap[:, 0:Wd], ALU.add),
                              reads=[a, t], writes=[mT])
                yield
        for j in range(nt):
            xt = xts[(bi % 2) * 4 + j]
            t0 = tbase + j * 128
            for c0 in (0, 512):
                ps = pss[cnt["ip"] % 5]
                cnt["ip"] += 1
                mk.mm(ps.ap, [(mT.ap[:, k, j * 128:(j + 1) * 128], wmix.ap[:, k, c0:c0 + 512]) for k in range(8)],
                      reads=[mT, wmix], tr=ps)
                mk.op("dve", lambda en, xt=xt, ps=ps, c0=c0: en.tensor_tensor(xt.ap[:, c0:c0 + 512], xt.ap[:, c0:c0 + 512], ps.ap, ALU.add),
                      reads=[xt, ps], writes=[xt])
                yield
            mk.dma("pool", xo_d.ap[t0:t0 + 128, :], xt.ap, reads=[xt], writes=[xo_d])

    norm_block(mk, mk_jobs(0), gB, ident, nsl)
    for bi in range(len(blocks)):
        streams = [{"gens": [mm_gen(bi)], "width": 1}]
        if bi + 1 < len(blocks):
            streams.append({"gens": norm_gens(mk, mk_jobs(bi + 1), gB, ident, nsl), "width": 2, "period": 2})
        run_pipelined_multi(streams)
    mk.phase_end()


def stage_select(mk, src_d, dst_d, Tt, off, hsel_d):
    mk.phase_begin()
    hB = mk.sb([128, 1], F32, "hB")
    mk.dma("sp", hB.ap, bcast_rows(hsel_d.ap, 128), reads=[hsel_d], writes=[hB])
    W = 3
    lo = [mk.sb([128, D], F32, "lo") for _ in range(W)]
    hi = [mk.sb([128, D], F32, "hi") for _ in range(W)]
    Th = off
    for tt in range(Tt // 128):
        k = tt % W
        t0 = tt * 128
        mk.dma("sp", lo[k].ap, src_d.ap[t0:t0 + 128, :], reads=[src_d], writes=[lo[k]])
        mk.dma("act", hi[k].ap, src_d.ap[Th + t0:Th + t0 + 128, :], reads=[src_d], writes=[hi[k]])
        eng = "dve" if tt % 2 == 0 else "pool"
        mk.op(eng, lambda en, k=k: en.tensor_tensor(hi[k].ap, hi[k].ap, lo[k].ap, ALU.subtract), reads=[hi[k], lo[k]], writes=[hi[k]])
        mk.op("dve", lambda en, k=k: en.scalar_tensor_tensor(lo[k].ap, hi[k].ap, hB.ap[:, 0:1], lo[k].ap, ALU.mult, ALU.add),
              reads=[hi[k], lo[k], hB], writes=[lo[k]])
        mk.dma("pool", dst_d.ap[t0:t0 + 128, :], lo[k].ap, reads=[lo[k]], writes=[dst_d])
    mk.phase_end()


def stage_ffn(mk, x_d, xo_d, T, gn_d, wup_d, cw_d, wd_d, ident_d):
    TB = 512
    NT = TB // 128
    mk.phase_begin()
    wup = load_weight_bf16(mk, wup_d, D, 5632, "wup")
    wd = load_weight_bf16(mk, wd_d, 2816, D, "wd")
    gB = mk.sb([128, D], F32, "gB")
    mk.dma("sp", gB.ap, bcast_rows(gn_d.ap, 128), reads=[gn_d], writes=[gB])
    cw = mk.sb([128, 44, 3], F32, "cw")
    mk.dma("sp", cw.ap, cw_d.ap.rearrange("(c p) k -> p c k", p=128), reads=[cw_d], writes=[cw])
    ident, _ = load_consts_ident(mk, ident_d)
    halos = [mk.sb([128, 2], F32, "halo") for _ in range(44)]
    for hb_ in halos:
        mk.op("pool", lambda en, hb_=hb_: en.memset(hb_.ap, 0.0), writes=[hb_])
    xts = [mk.sb([128, D], F32, "xt") for _ in range(2)]
    hTs = [mk.sb([128, 8, TB], BF16, "hT") for _ in range(2)]
    W = 2
    us = [[mk.sb([128, TB + 2], F32, "u") for _ in range(2)] for _ in range(W)]
    cas = [mk.sb([128, TB], F32, "ca") for _ in range(W)]
    cbs = [mk.sb([128, TB], F32, "cb") for _ in range(W)]
    gTs = [mk.sb([128, TB], BF16, "gT") for _ in range(22)]
    psTs = [mk.ps([128, 8, 128], BF16, "psT") for _ in range(1)]
    nsl = norm_slots(mk, 1, psTs)
    psu = [[mk.ps([128, 512], F32, "psu") for _ in range(2)] for _ in range(W)]
    psd = mk.ps([128, 512], F32, "psd")
    it = 0
    nit = [0]
    blocks = tok_blocks(T, TB)

    def mk_jobs(bi):
        tbase_, nt_ = blocks[bi]
        return [(xts[j % 2], (x_d, x_d.ap[tbase_ + j * 128:tbase_ + j * 128 + 128, :]), hTs[bi % 2], j * 128, "norm") for j in range(nt_)]

    norm_block(mk, mk_jobs(0), gB, ident, nsl)
    for bi, (tbase, nt) in enumerate(blocks):
        Wd = nt * 128
        hT = hTs[bi % 2]

        def cbody(c, k):
            cv = []
            for half, ct in enumerate((c, c + 22)):
                ps = psu[k][half]
                mk.mm(ps.ap[:, 0:Wd], [(wup.ap[:, kk, ct * 128:(ct + 1) * 128], hT.ap[:, kk, 0:Wd]) for kk in range(8)],
                      reads=[wup, hT], tr=ps)
                u = us[k][half]
                hl = halos[ct]
                mk.op("act", lambda en, u=u, hl=hl: en.copy(u.ap[:, 0:2], hl.ap), reads=[hl], writes=[u])
                mk.op("act", lambda en, u=u, ps=ps: en.copy(u.ap[:, 2:Wd + 2], ps.ap[:, 0:Wd]), reads=[ps], writes=[u])
                o = (cas if half == 0 else cbs)[k]
                mk.op("act", lambda en, o=o, ps=ps, ct=ct: en.activation(out=o.ap[:, 0:Wd], in_=ps.ap[:, 0:Wd], func=AF.Copy, scale=cw.ap[:, ct, 2:3]),
                      reads=[ps, cw], writes=[o])
                yield
                mk.op("act", lambda en, u=u, hl=hl: en.copy(hl.ap, u.ap[:, Wd:Wd + 2]), reads=[u], writes=[hl])
                mk.op("dve", lambda en, o=o, u=u, ct=ct: en.scalar_tensor_tensor(o.ap[:, 0:Wd], u.ap[:, 1:Wd + 1], cw.ap[:, ct, 1:2], o.ap[:, 0:Wd], ALU.mult, ALU.add),
                      reads=[u, cw, o], writes=[o])
                mk.op("dve", lambda en, o=o, u=u, ct=ct: en.scalar_tensor_tensor(o.ap[:, 0:Wd], u.ap[:, 0:Wd], cw.ap[:, ct, 0:1], o.ap[:, 0:Wd], ALU.mult, ALU.add),
                      reads=[u, cw, o], writes=[o])
                cv.append(o)
                yield
            a_, b_ = cv
            mk.op("act", lambda en, a_=a_: en.activation(out=a_.ap[:, 0:Wd], in_=a_.ap[:, 0:Wd], func=AF.Silu), reads=[a_], writes=[a_])
            yield
            mk.op("dve", lambda en, a_=a_, b_=b_, c=c: en.tensor_tensor(gTs[c].ap[:, 0:Wd], a_.ap[:, 0:Wd], b_.ap[:, 0:Wd], ALU.mult),
                  reads=[a_, b_], writes=[gTs[c]])

        def gens():
            for c in range(22):
                k = nit[0] % W
                nit[0] += 1
                yield cbody(c, k)
        streams = [{"gens": gens(), "width": W}]
        if bi + 1 < len(blocks):
            streams.append({"gens": norm_gens(mk, mk_jobs(bi + 1), gB, ident, nsl), "width": 1, "period": 3})
        run_pipelined_multi(streams)
        for j in range(nt):
            t0 = tbase + j * 128
            xt = xts[j % 2]
            mk.dma("sp", xt.ap, x_d.ap[t0:t0 + 128, :], reads=[x_d], writes=[xt])
            for c0 in (0, 512):
                ps = psd
                mk.mm(ps.ap, [(gTs[k].ap[:, j * 128:(j + 1) * 128], wd.ap[:, k, c0:c0 + 512]) for k in range(22)],
                      reads=gTs + [wd], tr=ps)
                mk.op("dve", lambda en, xt=xt, ps=ps, c0=c0: en.tensor_tensor(xt.ap[:, c0:c0 + 512], xt.ap[:, c0:c0 + 512], ps.ap, ALU.add),
                      reads=[xt, ps], writes=[xt])
            mk.dma("pool", xo_d.ap[t0:t0 + 128, :], xt.ap, reads=[xt], writes=[xo_d])
    mk.phase_end()

def bc_last(ap2, n):
    return ap2.unsqueeze(2).to_broadcast([ap2.shape[0], ap2.shape[1], n])


def bc_mid(ap2, h):
    return ap2.unsqueeze(1).to_broadcast([ap2.shape[0], h, ap2.shape[1]])


def exp_sigmoid(mk, out_b, out_ap, in_b, in_ap, neg=False, eng2="dve"):
    mk.op("act", lambda en: en.activation(out=out_ap, in_=in_ap, func=AF.Exp, scale=(1.0 if neg else -1.0)), reads=[in_b], writes=[out_b])
    mk.op("act", lambda en: en.activation(out=out_ap, in_=out_ap, func=AF.Ln, bias=1.0), reads=[out_b], writes=[out_b])
    mk.op("act", lambda en: en.activation(out=out_ap, in_=out_ap, func=AF.Exp, scale=-1.0), reads=[out_b], writes=[out_b])


def exp_silu(mk, out_b, out_ap, in_b, in_ap, tmp_b, tmp_ap):
    exp_sigmoid(mk, tmp_b, tmp_ap, in_b, in_ap)
    mk.op("dve", lambda en: en.tensor_tensor(out_ap, in_ap, tmp_ap, ALU.mult), reads=[in_b, tmp_b], writes=[out_b])


def head_rmsnorm(mk, src, H, dh, gB, out, scr, st, scale=1.0, np_=128):
    n = H * dh
    s3 = lambda b: b.ap[0:np_, 0:n].rearrange("p (h d) -> p h d", h=H)
    mk.op("pool", lambda en: en.tensor_tensor(scr.ap[0:np_, 0:n], src.ap[0:np_, 0:n], src.ap[0:np_, 0:n], ALU.mult),
          reads=[src], writes=[scr])
    mk.op("dve", lambda en: en.tensor_reduce(st.ap[0:np_, 0:H], s3(scr), AX.X, ALU.add), reads=[scr], writes=[st])
    mk.op("act", lambda en: en.activation(out=st.ap[0:np_, H:2 * H], in_=st.ap[0:np_, 0:H], func=AF.Ln, scale=1.0 / dh, bias=EPS),
          reads=[st], writes=[st])
    mk.op("act", lambda en: en.activation(out=st.ap[0:np_, 2 * H:3 * H], in_=st.ap[0:np_, H:2 * H], func=AF.Exp, scale=-0.5),
          reads=[st], writes=[st])
    mk.op("dve", lambda en: en.tensor_tensor(s3(scr), s3(src), bc_last(st.ap[0:np_, 2 * H:3 * H], dh), ALU.mult),
          reads=[src, st], writes=[scr])
    mk.op("dve", lambda en: en.scalar_tensor_tensor(s3(out), s3(scr), float(scale), bc_mid(gB.ap[0:np_, 0:dh], H), ALU.mult, ALU.mult),
          reads=[scr, gB], writes=[out])


def stage_xattn(mk, x_d, xo_d, T, gn_d, mkv_d, wq_d, qn_d, kn_d, wo_d, ident_d):
    mk.phase_begin()
    H, DH, M = 4, 128, 256
    wq = load_weight_bf16(mk, wq_d, D, 512, "wq")
    wo = load_weight_bf16(mk, wo_d, 512, D, "wo")
    gB = mk.sb([128, D], F32, "gB")
    mk.dma("sp", gB.ap, bcast_rows(gn_d.ap, 128), reads=[gn_d], writes=[gB])
    gq = mk.sb([128, DH], F32, "gq")
    mk.dma("sp", gq.ap, bcast_rows(qn_d.ap, 128), reads=[qn_d], writes=[gq])
    gk = mk.sb([128, DH], F32, "gk")
    mk.dma("sp", gk.ap, bcast_rows(kn_d.ap, 128), reads=[kn_d], writes=[gk])
    ident, _ = load_consts_ident(mk, ident_d)
    kT = mk.sb([128, H, M], BF16, "kT")
    vaug = mk.sb([128, 2, H, DH + 1], BF16, "vaug")
    scr = mk.sb([128, D], F32, "scr")
    st = mk.sb([128, 12], F32, "st")
    psTs = [mk.ps([128, 8, 128], BF16, "psT") for _ in range(2)]
    pss = [mk.ps([128, 512], F32, "psm") for _ in range(5)]
    hbs = [mk.sb([128, D], BF16, "hb") for _ in range(2)]
    mt_t = mk.sb([128, D], F32, "memt")
    mk.op("pool", lambda en: en.memset(vaug.ap, 1.0), writes=[vaug])
    for mt in range(2):
        mk.dma("sp", mt_t.ap, mkv_d.ap[mt * 128:(mt + 1) * 128, :], reads=[mkv_d], writes=[mt_t])
        hb = hbs[mt]
        head_rmsnorm(mk, mt_t, H, DH, gk, hb, scr, st)
        for h in range(H):
            mk.op("pe", lambda pe, h=h, hb=hb: pe.transpose(psTs[0].ap[:, h, :], hb.ap[:, h * DH:(h + 1) * DH], ident.ap),
                  reads=[hb, ident], writes=[psTs[0]], inc=(h == H - 1))
        mk.op("act", lambda en, mt=mt: en.copy(kT.ap[:, :, mt * 128:(mt + 1) * 128], psTs[0].ap[:, 0:H, :]),
              reads=[psTs[0]], writes=[kT])
        mk.op("dve", lambda en, mt=mt: en.tensor_copy(vaug.ap[:, mt, :, 0:DH], mt_t.ap[:, 512:1024].rearrange("p (h d) -> p h d", h=H)),
              reads=[mt_t], writes=[vaug])
    xts = [mk.sb([128, D], F32, "xt") for _ in range(8)]
    nsl = [{"junk": mk.sb([128, D], BF16, "junk"), "ssb": mk.sb([128, 4], F32, "ssb"), "hb": hbs[i_], "psT": psTs[i_]} for i_ in range(2)]
    hTs = [mk.sb([128, 8, 512], BF16, "hT") for _ in range(2)]
    W = 2
    qf = [mk.sb([128, 512], F32, "qf") for _ in range(W)]
    qb = [mk.sb([128, 512], BF16, "qb") for _ in range(W)]
    scrs = [mk.sb([128, 512], F32, "scrq") for _ in range(W)]
    sts = [mk.sb([128, 12], F32, "stq") for _ in range(W)]
    qT = mk.sb([128, H, 512], BF16, "qT")
    PT = mk.sb([128, H * 2, 512], BF16, "PT")
    of = [mk.sb([128, 512], F32, "of") for _ in range(W)]
    ob = [mk.sb([128, 512], BF16, "ob") for _ in range(W)]
    oT = [mk.sb([128, H, 128], BF16, "oT") for _ in range(W)]
    rd = [mk.sb([128, 4], F32, "rd") for _ in range(W)]
    pq = [pss[0], pss[1]]
    po = [[pss[2], pss[3]], [pss[4], pss[0]]]
    it = 0
    blocks = tok_blocks(T)

    def mk_jobs(bi_):
        tb_, nt_ = blocks[bi_]
        return [(xts[(bi_ % 2) * 4 + j], (x_d, x_d.ap[tb_ + j * 128:tb_ + j * 128 + 128, :]), hTs[bi_ % 2], j * 128, "norm") for j in range(nt_)]

    norm_block(mk, mk_jobs(0), gB, ident, nsl)
    for bi, (tbase, nt) in enumerate(blocks):
        Wd = nt * 128
        hT = hTs[bi % 2]

        def qbody(j):
            k = j % W
            ps = pq[k]
            mk.mm(ps.ap, [(hT.ap[:, kk, j * 128:(j + 1) * 128], wq.ap[:, kk, :]) for kk in range(8)], reads=[hT, wq], tr=ps)
            q = qf[k]
            mk.op("act", lambda en: en.copy(q.ap, ps.ap), reads=[ps], writes=[q])
            yield
            qq = qb[k]
            head_rmsnorm(mk, q, H, DH, gq, qq, scrs[k], sts[k], scale=DH ** -0.5)
            yield
            psT = psTs[k]
            for h in range(H):
                mk.op("pe", lambda pe, h=h: pe.transpose(psT.ap[:, h, :], qq.ap[:, h * DH:(h + 1) * DH], ident.ap),
                      reads=[qq, ident], writes=[psT], inc=(h == H - 1))
            mk.op("act", lambda en: en.copy(qT.ap[:, :, j * 128:(j + 1) * 128], psT.ap[:, 0:H, :]),
                  reads=[psT], writes=[qT])
        run_pipelined((qbody(j) for j in range(nt)), W, stagger=1)
        ip = 0
        for h in range(H):
            for mt in range(2):
                ps = pss[ip % 2]
                ip += 1
                mk.mm(ps.ap[:, 0:Wd], [(kT.ap[:, h, mt * 128:(mt + 1) * 128], qT.ap[:, h, 0:Wd])], reads=[kT, qT], tr=ps)
                mk.op("act", lambda en, ps=ps, h=h, mt=mt, Wd=Wd: en.activation(out=PT.ap[:, h * 2 + mt, 0:Wd], in_=ps.ap[:, 0:Wd], func=AF.Exp),
                      reads=[ps], writes=[PT])

        def obody(j):
            k = j % W
            xt = xts[(bi % 2) * 4 + j]
            t0 = tbase + j * 128
            o_f = of[k]
            r = rd[k]
            for hp in range(2):
                ps = po[k][hp]
                for hh in range(2):
                    h = hp * 2 + hh
                    mk.mm(ps.ap[:, hh * 129:(hh + 1) * 129],
                          [(PT.ap[:, h * 2 + mt, j * 128:(j + 1) * 128], vaug.ap[:, mt, h, :]) for mt in range(2)],
                          reads=[PT, vaug], tr=ps)
                for hh in range(2):
                    h = hp * 2 + hh
                    mk.op("dve", lambda en, ps=ps, hh=hh, h=h: en.reciprocal(r.ap[:, h:h + 1], ps.ap[:, hh * 129 + 128:hh * 129 + 129]),
                          reads=[ps], writes=[r])
                    mk.op("dve", lambda en, ps=ps, hh=hh, h=h: en.tensor_scalar(
                        o_f.ap[:, h * 128:(h + 1) * 128], ps.ap[:, hh * 129:hh * 129 + 128], r.ap[:, h:h + 1], None, ALU.mult),
                        reads=[ps, r], writes=[o_f])
                yield
            o_b = ob[k]
            mk.op("pool", lambda en: en.tensor_copy(o_b.ap, o_f.ap), reads=[o_f], writes=[o_b])
            psT = psTs[k]
            for h in range(H):
                mk.op("pe", lambda pe, h=h: pe.transpose(psT.ap[:, h, :], o_b.ap[:, h * DH:(h + 1) * DH], ident.ap),
                      reads=[o_b, ident], writes=[psT], inc=(h == H - 1))
            o_T = oT[k]
            mk.op("act", lambda en: en.copy(o_T.ap, psT.ap[:, 0:H, :]), reads=[psT], writes=[o_T])
            yield
            for ci, c0 in enumerate((0, 512)):
                ps = po[k][ci]
                mk.mm(ps.ap, [(o_T.ap[:, h, :], wo.ap[:, h, c0:c0 + 512]) for h in range(H)], reads=[o_T, wo], tr=ps)
                mk.op("dve", lambda en, ps=ps, c0=c0: en.tensor_tensor(xt.ap[:, c0:c0 + 512], xt.ap[:, c0:c0 + 512], ps.ap, ALU.add),
                      reads=[xt, ps], writes=[xt])
                yield
            mk.dma("pool", xo_d.ap[t0:t0 + 128, :], xt.ap, reads=[xt], writes=[xo_d])
        streams = [{"gens": (obody(j) for j in range(nt)), "width": W}]
        if bi + 1 < len(blocks):
            streams.append({"gens": norm_gens(mk, mk_jobs(bi + 1), gB, ident, nsl), "width": 2, "period": 2})
        run_pipelined_multi(streams)
    mk.phase_end()


def stage_hgrn(mk, ztm_d, c_hf, c_hi, c_hg, zfm_d, r_hq, r_hf, y_d, ycol, T, lbl_d, layer, onorm_d, tri_d, ident_d, defer=False):
    H, DK, C = 4, 64, 64
    mk.phase_begin()
    tri = mk.sb([64, 64], F32, "tri")
    mk.dma("sp", tri.ap, tri_d.ap, reads=[tri_d], writes=[tri])
    identf = mk.sb([128, 128], F32, "identf")
    mk.dma("sp", identf.ap, ident_d.ap, reads=[ident_d], writes=[identf])
    gO = mk.sb([64, 64], F32, "gO")
    mk.dma("sp", gO.ap, bcast_rows(onorm_d.ap, 64), reads=[onorm_d], writes=[gO])
    lbB = mk.sb([64, 256], F32, "lbB")
    omlbB = mk.sb([64, 256], F32, "omlbB")
    lbT = mk.sb([64, 4], F32, "lbT")
    omlbT = mk.sb([64, 4], F32, "omlbT")
    if layer == 0:
        mk.op("dve", lambda en: en.memset(lbB.ap, 0.0), writes=[lbB])
        mk.op("dve", lambda en: en.memset(lbT.ap, 0.0), writes=[lbT])
    else:
        l0 = mk.sb([64, 256], F32, "l0")
        l0T = mk.sb([64, 4], F32, "l0T")
        mk.dma("sp", l0.ap, bcast_rows(lbl_d.ap[0, :], 64), reads=[lbl_d], writes=[l0])
        mk.dma("sp", lbB.ap, bcast_rows(lbl_d.ap[1, :], 64), reads=[lbl_d], writes=[lbB])
        mk.dma("sp", l0T.ap, lbl_d.ap[0, :].rearrange("(h d) -> d h", h=4), reads=[lbl_d], writes=[l0T], allow_slow_non_contiguous=True)
        mk.dma("sp", lbT.ap, lbl_d.ap[1, :].rearrange("(h d) -> d h", h=4), reads=[lbl_d], writes=[lbT], allow_slow_non_contiguous=True)
        mk.op("dve", lambda en: en.tensor_tensor(lbB.ap, lbB.ap, l0.ap, ALU.subtract), reads=[lbB, l0], writes=[lbB])
        mk.op("act", lambda en: en.activation(out=lbB.ap, in_=lbB.ap, func=AF.Sigmoid), reads=[lbB], writes=[lbB])
        mk.op("dve", lambda en: en.tensor_tensor(lbT.ap, lbT.ap, l0T.ap, ALU.subtract), reads=[lbT, l0T], writes=[lbT])
        mk.op("act", lambda en: en.activation(out=lbT.ap, in_=lbT.ap, func=AF.Sigmoid), reads=[lbT], writes=[lbT])
    mk.op("dve", lambda en: en.tensor_scalar(omlbB.ap, lbB.ap, -1.0, 1.0, ALU.mult, ALU.add), reads=[lbB], writes=[omlbB])
    mk.op("dve", lambda en: en.tensor_scalar(omlbT.ap, lbT.ap, -1.0, 1.0, ALU.mult, ALU.add), reads=[lbT], writes=[omlbT])
    S = mk.sb([64, H, 64], F32, "S")
    mk.op("dve", lambda en: en.memset(S.ap, 0.0), writes=[S])
    NB = 2 if defer else 3
    def mkb(shape, name, n=NB, dt=F32):
        return [mk.sb(shape, dt, name) for _ in range(n)]
    qz, fz, fzt, vt, gz = mkb([64, H, 64], "qz"), mkb([64, H, 64], "fz"), mkb([64, 256], "fzt"), mkb([64, 256], "vt"), mkb([64, 256], "gz")
    qT, kT, lf = mkb([64, H, 64], "qT"), mkb([64, H, 64], "kT"), mkb([64, 256], "lf")
    bT, d1, e1, e2, e3 = mkb([64, H, 64], "bT"), mkb([64, H, 64], "d1"), mkb([64, H, 64], "e1"), mkb([64, H, 64], "e2"), mkb([64, H, 64], "e3")
    qtl, ktl, qe, kdT, kd = mkb([64, H, 64], "qtl"), mkb([64, H, 64], "ktl"), mkb([64, H, 64], "qe"), mkb([64, H, 64], "kdT"), mkb([64, H, 64], "kd")
    AT, ebl = mkb([64, H, 64], "AT"), mkb([64, 4], "ebl")
    of, scr, st, yo = mkb([64, 256], "of"), mkb([64, 256], "scr"), mkb([64, 12], "st"), mkb([64, 256], "yo")
    NPS = 2 if defer else 8
    PPS = 2 if defer else 4
    pss = [mk.ps([128, 512], F32, "psm") for _ in range(NPS)]
    ipc = [0, 0]
    def body(c):
        i = c % NB
        t0 = c * C
        pset = 0 if defer else c % 2
        def nps():
            p = pss[pset * PPS + ipc[pset] % PPS]
            ipc[pset] += 1
            return p
        mk.dma("sp", qz[i].ap, zfm_d.ap[r_hq:r_hq + 256, t0:t0 + C].rearrange("(h d) t -> d h t", h=H), reads=[zfm_d], writes=[qz[i]])
        mk.dma("sp", fz[i].ap, zfm_d.ap[r_hf:r_hf + 256, t0:t0 + C].rearrange("(h d) t -> d h t", h=H), reads=[zfm_d], writes=[fz[i]])
        mk.dma("sp", fzt[i].ap, ztm_d.ap[t0:t0 + C, c_hf:c_hf + 256], reads=[ztm_d], writes=[fzt[i]])
        mk.dma("sp", vt[i].ap, ztm_d.ap[t0:t0 + C, c_hi:c_hi + 256], reads=[ztm_d], writes=[vt[i]])
        mk.dma("sp", gz[i].ap, ztm_d.ap[t0:t0 + C, c_hg:c_hg + 256], reads=[ztm_d], writes=[gz[i]])
        yield
        exp_silu(mk, qT[i], qT[i].ap, qz[i], qz[i].ap, e1[i], e1[i].ap)
        exp_sigmoid(mk, kT[i], kT[i].ap, fz[i], fz[i].ap, neg=True, eng2="pool")
        mk.op("dve", lambda en, i=i: en.tensor_tensor(kT[i].ap, kT[i].ap, bc_last(omlbT.ap, 64), ALU.mult), reads=[kT[i], omlbT], writes=[kT[i]])
        yield
        exp_sigmoid(mk, lf[i], lf[i].ap, fzt[i], fzt[i].ap, eng2="pool")
        mk.op("dve", lambda en, i=i: en.tensor_tensor(lf[i].ap, lf[i].ap, omlbB.ap, ALU.mult), reads=[lf[i], omlbB], writes=[lf[i]])
        mk.op("dve", lambda en, i=i: en.tensor_tensor(lf[i].ap, lf[i].ap, lbB.ap, ALU.add), reads=[lf[i], lbB], writes=[lf[i]])
        mk.op("dve", lambda en, i=i: en.tensor_scalar(lf[i].ap, lf[i].ap, TINY, None, ALU.max), reads=[lf[i]], writes=[lf[i]])
        mk.op("act", lambda en, i=i: en.activation(out=lf[i].ap, in_=lf[i].ap, func=AF.Ln), reads=[lf[i]], writes=[lf[i]])
        yield
        psb = nps()
        for h in range(H):
            mk.mm(psb.ap[0:64, h * 64:(h + 1) * 64], [(lf[i].ap[:, h * 64:(h + 1) * 64], tri.ap)], reads=[lf[i], tri], tr=psb)
        mk.op("act", lambda en, i=i, psb=psb: en.copy(bT[i].ap, psb.ap[0:64, 0:256].rearrange("p (h t) -> p h t", h=H)), reads=[psb], writes=[bT[i]])
        mk.op("dve", lambda en, i=i: en.tensor_tensor(d1[i].ap, bT[i].ap, bc_last(bT[i].ap[:, :, 31], 64), ALU.subtract), reads=[bT[i]], writes=[d1[i]])
        mk.op("act", lambda en, i=i: en.activation(out=e1[i].ap, in_=d1[i].ap, func=AF.Exp), reads=[d1[i]], writes=[e1[i]])
        mk.op("act", lambda en, i=i: en.activation(out=e2[i].ap, in_=d1[i].ap, func=AF.Exp, scale=-1.0), reads=[d1[i]], writes=[e2[i]])
        mk.op("dve", lambda en, i=i: en.tensor_tensor(qtl[i].ap, qT[i].ap, e1[i].ap, ALU.mult), reads=[qT[i], e1[i]], writes=[qtl[i]])
        mk.op("dve", lambda en, i=i: en.tensor_tensor(ktl[i].ap, kT[i].ap, e2[i].ap, ALU.mult), reads=[kT[i], e2[i]], writes=[ktl[i]])
        mk.op("act", lambda en, i=i: en.activation(out=e1[i].ap, in_=bT[i].ap, func=AF.Exp), reads=[bT[i]], writes=[e1[i]])
        mk.op("dve", lambda en, i=i: en.tensor_tensor(qe[i].ap, qT[i].ap, e1[i].ap, ALU.mult), reads=[qT[i], e1[i]], writes=[qe[i]])
        mk.op("dve", lambda en, i=i: en.tensor_tensor(d1[i].ap, bT[i].ap, bc_last(bT[i].ap[:, :, 63], 64), ALU.subtract), reads=[bT[i]], writes=[d1[i]])
        mk.op("act", lambda en, i=i: en.activation(out=e3[i].ap, in_=d1[i].ap, func=AF.Exp, scale=-1.0), reads=[d1[i]], writes=[e3[i]])
        mk.op("dve", lambda en, i=i: en.tensor_tensor(kdT[i].ap, kT[i].ap, e3[i].ap, ALU.mult), reads=[kT[i], e3[i]], writes=[kdT[i]])
        mk.op("act", lambda en, i=i: en.activation(out=ebl[i].ap, in_=bT[i].ap[:, :, 63], func=AF.Exp), reads=[bT[i]], writes=[ebl[i]])
        yield
        pst = nps()
        for h in range(H):
            mk.op("pe", lambda pe, h=h, i=i, pst=pst: pe.transpose(pst.ap[0:64, h * 64:(h + 1) * 64], kdT[i].ap[:, h, :], identf.ap[0:64, 0:64]),
                  reads=[kdT[i], identf], writes=[pst], inc=(h == H - 1))
        mk.op("act", lambda en, i=i, pst=pst: en.copy(kd[i].ap, pst.ap[0:64, 0:256].rearrange("p (h t) -> p h t", h=H)), reads=[pst], writes=[kd[i]])
        yield
        psa = nps()
        for h in range(H):
            mk.mm(psa.ap[0:64, h * 64:(h + 1) * 64], [(ktl[i].ap[:, h, :], qtl[i].ap[:, h, :])], reads=[ktl[i], qtl[i]], tr=psa)
        mk.op("dve", lambda en, i=i, psa=psa: en.tensor_tensor(AT[i].ap, psa.ap[0:64, 0:256].rearrange("p (h t) -> p h t", h=H), bc_mid(tri.ap, H), ALU.mult),
              reads=[psa, tri], writes=[AT[i]])
        if c > 0:
            yield ("wait", ("S", c - 1))
        pso = nps()
        for h in range(H):
            mk.mm(pso.ap[0:64, h * 64:(h + 1) * 64], [(AT[i].ap[:, h, :], vt[i].ap[:, h * 64:(h + 1) * 64]), (qe[i].ap[:, h, :], S.ap[:, h, :])],
                  reads=[AT[i], vt[i], qe[i], S], tr=pso)
        pss_ = nps()
        for h in range(H):
            mk.mm(pss_.ap[0:64, h * 64:(h + 1) * 64], [(kd[i].ap[:, h, :], vt[i].ap[:, h * 64:(h + 1) * 64])], reads=[kd[i], vt[i]], tr=pss_)
        mk.op("dve", lambda en, i=i: en.tensor_tensor(S.ap, S.ap, bc_last(ebl[i].ap, 64), ALU.mult), reads=[S, ebl[i]], writes=[S])
        mk.op("dve", lambda en, pss_=pss_: en.tensor_tensor(S.ap, S.ap, pss_.ap[0:64, 0:256].rearrange("p (h t) -> p h t", h=H), ALU.add), reads=[S, pss_], writes=[S])
        yield ("done", ("S", c))
        mk.op("act", lambda en, i=i, pso=pso: en.copy(of[i].ap, pso.ap[0:64, 0:256]), reads=[pso], writes=[of[i]])
        head_rmsnorm(mk, of[i], H, 64, gO, yo[i], scr[i], st[i], np_=64)
        exp_sigmoid(mk, gz[i], gz[i].ap, gz[i], gz[i].ap, eng2="pool")
        mk.op("dve", lambda en, i=i: en.tensor_tensor(yo[i].ap, yo[i].ap, gz[i].ap, ALU.mult), reads=[yo[i], gz[i]], writes=[yo[i]])
        mk.dma("pool", y_d.ap[t0:t0 + C, ycol:ycol + 256], yo[i].ap, reads=[yo[i]], writes=[y_d])
    if defer:
        return (body(c) for c in range(T // C)), (lambda: mk.phase_end())
    run_pipelined((body(c) for c in range(T // C)), 2)
    mk.phase_end()


def stage_gdn(mk, ztm_d, c_cz, c_cb, c_ca, zfm_d, r_qkv, y_d, ycol, T, cw_d, alog_d, dtb_d, onorm_d, cst, bg=None):
    H, C = 4, 128
    mk.phase_begin()
    def cload(name):
        b = mk.sb([128, 128], F32, name)
        mk.dma("sp", b.ap, cst[name].ap, reads=[cst[name]], writes=[b])
        return b
    triI, lowS, negtriS, neglowS, id128, ones = (cload(n) for n in ("g_triI", "g_lowS", "g_negtriS", "g_neglowS", "g_id", "g_ones"))
    gO = mk.sb([128, 64], F32, "gO")
    mk.dma("sp", gO.ap, bcast_rows(onorm_d.ap, 128), reads=[onorm_d], writes=[gO])
    cw = mk.sb([64, 12, 4], F32, "cw")
    mk.dma("sp", cw.ap, cw_d.ap.rearrange("(b d) k -> d b k", d=64), reads=[cw_d], writes=[cw])
    negA = mk.sb([128, 4], F32, "negA")
    dtb = mk.sb([128, 4], F32, "dtb")
    mk.dma("sp", negA.ap, bcast_rows(alog_d.ap, 128), reads=[alog_d], writes=[negA])
    mk.dma("sp", dtb.ap, bcast_rows(dtb_d.ap, 128), reads=[dtb_d], writes=[dtb])
    mk.op("act", lambda en: en.activation(out=negA.ap, in_=negA.ap, func=AF.Exp), reads=[negA], writes=[negA])
    mk.op("dve", lambda en: en.tensor_scalar(negA.ap, negA.ap, -1.0, None, ALU.mult), reads=[negA], writes=[negA])
    S = mk.sb([64, H, 64], F32, "S")
    mk.op("dve", lambda en: en.memset(S.ap, 0.0), writes=[S])
    NB = 2
    def mkb(shape, name, n=NB, dt=F32):
        return [mk.sb(list(shape), dt, name) for _ in range(n)]
    FM = [64, H, C]
    TM = [128, H, 64]
    SQ = [128, H, C]
    u = mkb([64, 12, C + 3], "u")
    cz, sc = mkb([128, 256], "cz"), mkb([128, 8], "sc")
    tk = [mkb([64, 12, C], "tk%d" % k) for k in range(2)]
    qkv, sq, rinv = mkb([64, 12, C], "qkv"), mkb([64, 8, C], "sq"), mkb([64, 8, C], "rinv")
    kv_tm = mkb([128, 8, 64], "kvtm")
    beta, g, sp1, sp2 = mkb([128, 4], "beta"), mkb([128, 4], "g"), mkb([128, 4], "sp1"), mkb([128, 4], "sp2")
    rep = mkb([128, 8, 64], "rep")
    eb, bb = mkb(FM, "eb"), mkb(FM, "bb")
    kbT, kbeT, qeT = mkb(FM, "kbT"), mkb(FM, "kbeT"), mkb(FM, "qeT")
    Gt = mkb(SQ, "Gt")
    ET, E, ETi = mkb(SQ, "ET"), mkb(SQ, "E"), mkb(SQ, "ETi")
    NUs, NLs = [mkb(SQ, "NU%d" % j) for j in range(2)], [mkb(SQ, "NL%d" % j) for j in range(2)]
    P = mkb(SQ, "P")
    QKT = mkb(SQ, "QKT")
    bv, rhs, vn, kd, elb = mkb(TM, "bv"), mkb(TM, "rhs"), mkb(TM, "vn"), mkb(TM, "kd"), mkb([128, 4], "elb")
    of, scr, st, yo = mkb([128, 256], "of"), mkb([128, 256], "scr"), mkb([128, 12], "st"), mkb([128, 256], "yo")
    PPS = 3 if bg is not None else 4
    pss = [mk.ps([128, 512], F32, "psm") for _ in range(2 * PPS)]
    ipc = [0, 0]
    psq = lambda p: p.ap[:, 0:H * C].rearrange("p (h t) -> p h t", h=H)
    ptm = lambda p: p.ap[:, 0:H * 64].rearrange("p (h t) -> p h t", h=H)
    pfm = lambda p: p.ap[0:64, 0:H * C].rearrange("p (h t) -> p h t", h=H)

    def body(c):
        i = c % NB
        t0 = c * C
        U = u[i]
        NU = [NUs[0][i], NUs[1][i]]
        NL = [NLs[0][i], NLs[1][i]]
        pset = c % 2
        def nps():
            p = pss[pset * PPS + ipc[pset] % PPS]
            ipc[pset] += 1
            return p
        if c == 0:
            mk.op("pool", lambda en, U=U: en.memset(U.ap[:, :, 0:3], 0.0), writes=[U])
            mk.dma("sp", U.ap[:, :, 3:C + 3], zfm_d.ap[r_qkv:r_qkv + 768, 0:C].rearrange("(b d) t -> d b t", d=64), reads=[zfm_d], writes=[U])
        else:
            mk.dma("sp", U.ap, zfm_d.ap[r_qkv:r_qkv + 768, t0 - 3:t0 + C].rearrange("(b d) t -> d b t", d=64), reads=[zfm_d], writes=[U])
        mk.dma("sp", cz[i].ap, ztm_d.ap[t0:t0 + C, c_cz:c_cz + 256], reads=[ztm_d], writes=[cz[i]])
        mk.dma("sp", sc[i].ap[:, 0:4], ztm_d.ap[t0:t0 + C, c_cb:c_cb + 4], reads=[ztm_d], writes=[sc[i]])
        mk.dma("sp", sc[i].ap[:, 4:8], ztm_d.ap[t0:t0 + C, c_ca:c_ca + 4], reads=[ztm_d], writes=[sc[i]])
        yield
        a0, a1 = tk[0][i], tk[1][i]
        mk.op("dve", lambda en: en.tensor_tensor(a0.ap, U.ap[:, :, 0:C], bc_last(cw.ap[:, :, 0], C), ALU.mult), reads=[U, cw], writes=[a0])
        mk.op("pool", lambda en: en.tensor_tensor(a1.ap, U.ap[:, :, 1:C + 1], bc_last(cw.ap[:, :, 1], C), ALU.mult), reads=[U, cw], writes=[a1])
        mk.op("dve", lambda en: en.tensor_tensor(a0.ap, a0.ap, a1.ap, ALU.add), reads=[a0, a1], writes=[a0])
        mk.op("pool", lambda en: en.tensor_tensor(a1.ap, U.ap[:, :, 2:C + 2], bc_last(cw.ap[:, :, 2], C), ALU.mult), reads=[U, cw], writes=[a1])
        mk.op("dve", lambda en: en.tensor_tensor(a0.ap, a0.ap, a1.ap, ALU.add), reads=[a0, a1], writes=[a0])
        mk.op("pool", lambda en: en.tensor_tensor(a1.ap, U.ap[:, :, 3:C + 3], bc_last(cw.ap[:, :, 3], C), ALU.mult), reads=[U, cw], writes=[a1])
        mk.op("dve", lambda en: en.tensor_tensor(a0.ap, a0.ap, a1.ap, ALU.add), reads=[a0, a1], writes=[a0])
        mk.op("act", lambda en: en.activation(out=qkv[i].ap, in_=a0.ap, func=AF.Silu), reads=[a0], writes=[qkv[i]])
        yield
        mk.op("pool", lambda en: en.tensor_tensor(sq[i].ap, qkv[i].ap[:, 0:8, :], qkv[i].ap[:, 0:8, :], ALU.mult), reads=[qkv[i]], writes=[sq[i]])
        for half in range(2):
            pn = nps()
            mk.mm(pn.ap[0:64, 0:512], [(ones.ap[0:64, 0:64], sq[i].ap[:, half * 4:half * 4 + 4, :].rearrange("p b t -> p (b t)"))], reads=[ones, sq[i]], tr=pn)
            mk.op("act", lambda en, pn=pn, half=half: en.activation(out=rinv[i].ap[:, half * 4:half * 4 + 4, :], in_=pfm(pn), func=AF.Ln, bias=EPS), reads=[pn], writes=[rinv[i]])
        mk.op("act", lambda en: en.activation(out=rinv[i].ap, in_=rinv[i].ap, func=AF.Exp, scale=-0.5), reads=[rinv[i]], writes=[rinv[i]])
        mk.op("dve", lambda en: en.scalar_tensor_tensor(qkv[i].ap[:, 0:4, :], qkv[i].ap[:, 0:4, :], 64 ** -0.5, rinv[i].ap[:, 0:4, :], ALU.mult, ALU.mult),
              reads=[qkv[i], rinv[i]], writes=[qkv[i]])
        mk.op("dve", lambda en: en.tensor_tensor(qkv[i].ap[:, 4:8, :], qkv[i].ap[:, 4:8, :], rinv[i].ap[:, 4:8, :], ALU.mult),
              reads=[qkv[i], rinv[i]], writes=[qkv[i]])
        qT = lambda h: qkv[i].ap[:, h, :]
        kT = lambda h: qkv[i].ap[:, 4 + h, :]
        yield
        pt = nps()
        for b_ in range(8):
            mk.op("pe", lambda pe, b_=b_: pe.transpose(pt.ap[:, b_ * 64:(b_ + 1) * 64], qkv[i].ap[:, 4 + b_, :], id128.ap[0:64, 0:64]),
                  reads=[qkv[i], id128], writes=[pt], inc=(b_ == 7))
        mk.op("act", lambda en: en.copy(kv_tm[i].ap, pt.ap[:, :].rearrange("p (b d) -> p b d", b=8)), reads=[pt], writes=[kv_tm[i]])
        yield
        exp_sigmoid(mk, beta[i], beta[i].ap, sc[i], sc[i].ap[:, 0:4])
        mk.op("dve", lambda en: en.tensor_tensor(sp1[i].ap, sc[i].ap[:, 4:8], dtb.ap, ALU.add), reads=[sc[i], dtb], writes=[sp1[i]])
        mk.op("dve", lambda en: en.tensor_scalar(sp2[i].ap, sp1[i].ap, -1.0, None, ALU.mult), reads=[sp1[i]], writes=[sp2[i]])
        mk.op("dve", lambda en: en.tensor_tensor(sp2[i].ap, sp2[i].ap, sp1[i].ap, ALU.max), reads=[sp1[i], sp2[i]], writes=[sp2[i]])
        mk.op("act", lambda en: en.activation(out=sp2[i].ap, in_=sp2[i].ap, func=AF.Exp, scale=-1.0), reads=[sp2[i]], writes=[sp2[i]])
        mk.op("act", lambda en: en.activation(out=sp2[i].ap, in_=sp2[i].ap, func=AF.Ln, bias=1.0), reads=[sp2[i]], writes=[sp2[i]])
        mk.op("dve", lambda en: en.tensor_scalar(sp1[i].ap, sp1[i].ap, 0.0, None, ALU.max), reads=[sp1[i]], writes=[sp1[i]])
        mk.op("dve", lambda en: en.tensor_tensor(sp1[i].ap, sp1[i].ap, sp2[i].ap, ALU.add), reads=[sp1[i], sp2[i]], writes=[sp1[i]])
        mk.op("dve", lambda en: en.tensor_tensor(g[i].ap, sp1[i].ap, negA.ap, ALU.mult), reads=[sp1[i], negA], writes=[g[i]])
        yield
        mk.op("pool", lambda en: en.tensor_copy(rep[i].ap[:, 0:4, :], bc_last(beta[i].ap, 64)), reads=[beta[i]], writes=[rep[i]])
        mk.op("pool", lambda en: en.tensor_copy(rep[i].ap[:, 4:8, :], bc_last(g[i].ap, 64)), reads=[g[i]], writes=[rep[i]])
        pb1, pb2 = nps(), nps()
        for h in range(H):
            mk.mm(pb1.ap[0:64, h * C:(h + 1) * C], [(rep[i].ap[:, h, :], id128.ap)], reads=[rep[i], id128], tr=pb1)
        for h in range(H):
            mk.mm(pb2.ap[0:64, h * C:(h + 1) * C], [(rep[i].ap[:, 4 + h, :], triI.ap)], reads=[rep[i], triI], tr=pb2)
        mk.op("act", lambda en: en.copy(bb[i].ap, pfm(pb1)), reads=[pb1], writes=[bb[i]])
        mk.op("act", lambda en: en.activation(out=eb[i].ap, in_=pfm(pb2), func=AF.Exp), reads=[pb2], writes=[eb[i]])
        mk.op("dve", lambda en: en.tensor_tensor(kbT[i].ap, qkv[i].ap[:, 4:8, :], bb[i].ap, ALU.mult), reads=[qkv[i], bb[i]], writes=[kbT[i]])
        mk.op("dve", lambda en: en.tensor_tensor(kbeT[i].ap, kbT[i].ap, eb[i].ap, ALU.mult), reads=[kbT[i], eb[i]], writes=[kbeT[i]])
        mk.op("dve", lambda en: en.tensor_tensor(qeT[i].ap, qkv[i].ap[:, 0:4, :], eb[i].ap, ALU.mult), reads=[qkv[i], eb[i]], writes=[qeT[i]])
        yield
        mk.op("pool", lambda en: en.tensor_tensor(Gt[i].ap, bc_mid(triI.ap, H), bc_last(g[i].ap, C), ALU.mult), reads=[triI, g[i]], writes=[Gt[i]])
        pdT, pd = nps(), nps()
        for h in range(H):
            mk.mm(pdT.ap[:, h * C:(h + 1) * C], [(lowS.ap, Gt[i].ap[:, h, :])], reads=[lowS, Gt[i]], tr=pdT)
        for h in range(H):
            mk.mm(pd.ap[:, h * C:(h + 1) * C], [(Gt[i].ap[:, h, :], lowS.ap)], reads=[lowS, Gt[i]], tr=pd)
        mk.op("act", lambda en: en.activation(out=ET[i].ap, in_=psq(pdT), func=AF.Exp), reads=[pdT], writes=[ET[i]])
        mk.op("act", lambda en: en.activation(out=E[i].ap, in_=psq(pd), func=AF.Exp), reads=[pd], writes=[E[i]])
        mk.op("pool", lambda en: en.tensor_tensor(ETi[i].ap, ET[i].ap, bc_mid(triI.ap, H), ALU.mult), reads=[ET[i], triI], writes=[ETi[i]])
        mk.op("dve", lambda en: en.tensor_tensor(ET[i].ap, ET[i].ap, bc_mid(negtriS.ap, H), ALU.mult), reads=[ET[i], negtriS], writes=[ET[i]])
        mk.op("pool", lambda en: en.tensor_tensor(E[i].ap, E[i].ap, bc_mid(neglowS.ap, H), ALU.mult), reads=[E[i], neglowS], writes=[E[i]])
        yield
        pc = nps()
        mk.mm(pc.ap[:, 0:4], [(lowS.ap, g[i].ap)], reads=[lowS, g[i]], tr=pc)
        mk.op("act", lambda en: en.activation(out=elb[i].ap, in_=pc.ap[:, 0:4], func=AF.Exp), reads=[pc], writes=[elb[i]])
        mk.op("dve", lambda en: en.tensor_tensor(kd[i].ap, kv_tm[i].ap[:, 0:4, :], bc_last(elb[i].ap, 64), ALU.mult), reads=[kv_tm[i], elb[i]], writes=[kd[i]])
        mk.op("dve", lambda en: en.tensor_tensor(bv[i].ap, kv_tm[i].ap[:, 4:8, :], bc_last(beta[i].ap, 64), ALU.mult), reads=[kv_tm[i], beta[i]], writes=[bv[i]])
        yield
        pu, pl, pq = nps(), nps(), nps()
        for h in range(H):
            mk.mm(pu.ap[:, h * C:(h + 1) * C], [(kT(h), kbT[i].ap[:, h, :])], reads=[qkv[i], kbT[i]], tr=pu)
        for h in range(H):
            mk.mm(pl.ap[:, h * C:(h + 1) * C], [(kbT[i].ap[:, h, :], kT(h))], reads=[qkv[i], kbT[i]], tr=pl)
        for h in range(H):
            mk.mm(pq.ap[:, h * C:(h + 1) * C], [(kT(h), qT(h))], reads=[qkv[i]], tr=pq)
        mk.op("dve", lambda en: en.tensor_tensor(NU[0].ap, psq(pu), ET[i].ap, ALU.mult), reads=[pu, ET[i]], writes=[NU[0]])
        mk.op("dve", lambda en: en.tensor_tensor(NL[0].ap, psq(pl), E[i].ap, ALU.mult), reads=[pl, E[i]], writes=[NL[0]])
        mk.op("dve", lambda en: en.tensor_tensor(QKT[i].ap, psq(pq), ETi[i].ap, ALU.mult), reads=[pq, ETi[i]], writes=[QKT[i]])
        mk.op("pool", lambda en: en.tensor_tensor(P[i].ap, NU[0].ap, bc_mid(id128.ap, H), ALU.add), reads=[NU[0], id128], writes=[P[i]])
        yield
        cur = 0
        for j in range(1, 7):
            nxt = 1 - cur
            pu, pl = nps(), nps()
            for h in range(H):
                mk.mm(pu.ap[:, h * C:(h + 1) * C], [(NL[cur].ap[:, h, :], NU[cur].ap[:, h, :])], reads=[NL[cur], NU[cur]], tr=pu)
            for h in range(H):
                mk.mm(pl.ap[:, h * C:(h + 1) * C], [(NU[cur].ap[:, h, :], NL[cur].ap[:, h, :])], reads=[NL[cur], NU[cur]], tr=pl)
            mk.op("act", lambda en, nxt=nxt, pu=pu: en.copy(NU[nxt].ap, psq(pu)), reads=[pu], writes=[NU[nxt]])
            mk.op("dve", lambda en, nxt=nxt, pl=pl: en.tensor_copy(NL[nxt].ap, psq(pl)), reads=[pl], writes=[NL[nxt]])
            pp = nps()
            for h in range(H):
                mk.mm(pp.ap[:, h * C:(h + 1) * C], [(NL[nxt].ap[:, h, :], P[i].ap[:, h, :])], reads=[NL[nxt], P[i]], tr=pp)
            mk.op("dve", lambda en, pp=pp: en.tensor_tensor(P[i].ap, P[i].ap, psq(pp), ALU.add), reads=[P[i], pp], writes=[P[i]])
            cur = nxt
            yield
        if c > 0:
            yield ("wait", ("S", c - 1))
        p1 = nps()
        for h in range(H):
            mk.mm(p1.ap[:, h * 64:(h + 1) * 64], [(kbeT[i].ap[:, h, :], S.ap[:, h, :])], reads=[kbeT[i], S], tr=p1)
        mk.op("dve", lambda en: en.tensor_tensor(rhs[i].ap, bv[i].ap, ptm(p1), ALU.subtract), reads=[bv[i], p1], writes=[rhs[i]])
        p2 = nps()
        for h in range(H):
            mk.mm(p2.ap[:, h * 64:(h + 1) * 64], [(P[i].ap[:, h, :], rhs[i].ap[:, h, :])], reads=[P[i], rhs[i]], tr=p2)
        mk.op("act", lambda en: en.copy(vn[i].ap, ptm(p2)), reads=[p2], writes=[vn[i]])
        po, p4 = nps(), nps()
        for h in range(H):
            mk.mm(po.ap[:, h * 64:(h + 1) * 64], [(qeT[i].ap[:, h, :], S.ap[:, h, :]), (QKT[i].ap[:, h, :], vn[i].ap[:, h, :])],
                  reads=[qeT[i], S, QKT[i], vn[i]], tr=po)
        for h in range(H):
            mk.mm(p4.ap[0:64, h * 64:(h + 1) * 64], [(kd[i].ap[:, h, :], vn[i].ap[:, h, :])], reads=[kd[i], vn[i]], tr=p4)
        mk.op("dve", lambda en: en.tensor_tensor(S.ap, S.ap, bc_last(eb[i].ap[:, :, C - 1], 64), ALU.mult), reads=[S, eb[i]], writes=[S])
        mk.op("dve", lambda en: en.tensor_tensor(S.ap, S.ap, p4.ap[0:64, 0:256].rearrange("p (h t) -> p h t", h=H), ALU.add), reads=[S, p4], writes=[S])
        yield ("done", ("S", c))
        mk.op("act", lambda en: en.copy(of[i].ap, po.ap[:, 0:256]), reads=[po], writes=[of[i]])
        head_rmsnorm(mk, of[i], H, 64, gO, yo[i], scr[i], st[i])
        exp_silu(mk, cz[i], cz[i].ap, cz[i], cz[i].ap, scr[i], scr[i].ap)
        mk.op("dve", lambda en: en.tensor_tensor(yo[i].ap, yo[i].ap, cz[i].ap, ALU.mult), reads=[yo[i], cz[i]], writes=[yo[i]])
        mk.dma("pool", y_d.ap[t0:t0 + C, ycol:ycol + 256], yo[i].ap, reads=[yo[i]], writes=[y_d])
    if bg is not None:
        run_pipelined_multi([{"gens": (body(c) for c in range(T // C)), "width": 2}, {"gens": bg, "width": 1, "period": 2}])
    else:
        run_pipelined((body(c) for c in range(T // C)), 2)
    mk.phase_end()


def gdn_consts():
    a = np.arange(64)
    triI = (a[:, None] <= a[None, :]).astype(np.float32)
    b = np.arange(128)
    gI = (b[:, None] <= b[None, :]).astype(np.float32)
    gS = (b[:, None] < b[None, :]).astype(np.float32)
    gL = (b[:, None] > b[None, :]).astype(np.float32)
    return {"triI": triI, "g_triI": gI, "g_lowS": gL, "g_negtriS": -gS, "g_neglowS": -gL,
            "g_id": np.eye(128, dtype=np.float32), "g_ones": np.ones((128, 128), np.float32)}


def nsa_consts(T):
    t = np.arange(T)
    inv = 1.0 / (10000.0 ** (np.arange(0, 64, 2, dtype=np.float32) / 64))
    ang = t[:, None].astype(np.float32) * inv[None, :].astype(np.float32)
    ncp = T // 16
    n = np.arange(ncp)
    ncmp = (T - 32) // 16 + 1
    cm = ((16 * n[None, :] + 31 <= t[:, None]) & (n[None, :] < ncmp)).astype(np.float32)
    ns = T // 64
    j = np.arange(ns)[None, :]
    cur = (t // 64)[:, None]
    valid = j <= cur
    forced = (j == 0) | (j == cur) | (j == cur - 1)
    m1 = (valid & ~forced).astype(np.float32)
    c2 = (1e6 * (valid & forced) - 1.0 * (~valid)).astype(np.float32)
    a = np.arange(128)
    return {"cos": np.cos(ang).astype(np.float32), "sin": np.sin(ang).astype(np.float32),
            "cmpm": cm, "cmpmT": np.ascontiguousarray(cm.T), "selm1": m1, "selc2": c2,
            "causT": (a[:, None] <= a[None, :]).astype(np.float32), "farT": (a[:, None] > a[None, :]).astype(np.float32),
            "esel": (np.arange(ns)[:, None] == (t // 64)[None, :]).astype(np.float32)}


def nsa_n1(mk, ztm_d, c_nq, c_nk, c_nv, T, qn_d, kn_d, cst, ident_d, QT_d, KT_d, VA_d, defer=False):
    NT = T // 128
    mk.phase_begin()
    ident, identf = load_consts_ident(mk, ident_d)
    gqk = mk.sb([128, 14, 64], F32, "gqk")
    mk.dma("sp", gqk.ap[:, 0:8, :], bc_mid(bcast_rows(qn_d.ap, 128), 8), reads=[qn_d], writes=[gqk])
    for ty in range(3):
        mk.dma("sp", gqk.ap[:, 8 + 2 * ty:10 + 2 * ty, :], bc_mid(bcast_rows(kn_d.ap[ty, :], 128), 2), reads=[kn_d], writes=[gqk])
    mk.op("dve", lambda en: en.tensor_scalar(gqk.ap[:, 0:8, :], gqk.ap[:, 0:8, :], 64 ** -0.5, None, ALU.mult), reads=[gqk], writes=[gqk])
    NB = 2 if defer else 3
    xin = [mk.sb([128, 14 * 64], F32, "xin") for _ in range(NB)]
    vin = [mk.sb([128, 256], F32, "vin") for _ in range(NB)]
    cs = [mk.sb([128, 64], F32, "cs") for _ in range(NB)]
    sq = [mk.sb([128, 14 * 64], F32, "sq") for _ in range(NB)]
    st = [mk.sb([128, 42], F32, "st") for _ in range(NB)]
    xn = [mk.sb([128, 14, 64], F32, "xn") for _ in range(NB)]
    r1 = [mk.sb([128, 14, 32], F32, "r1") for _ in range(NB)]
    r2 = [mk.sb([128, 14, 32], F32, "r2") for _ in range(NB)]
    xr = [mk.sb([128, 14, 64], BF16, "xr") for _ in range(NB)]
    xT = [mk.sb([64, 14, 128], BF16, "xT") for _ in range(NB)]
    va = [mk.sb([128, 4, 65], BF16, "va") for _ in range(NB)]
    NPS_ = 1 if defer else NB
    psA = [mk.ps([64, 7, 128], BF16, "psA") for _ in range(NPS_)]
    psB = [mk.ps([64, 7, 128], BF16, "psB") for _ in range(NPS_)]
    def n1body(tt):
        i = tt % NB
        t0 = tt * 128
        X = xin[i]
        mk.dma("sp", X.ap[:, 0:512], ztm_d.ap[t0:t0 + 128, c_nq:c_nq + 512], reads=[ztm_d], writes=[X])
        mk.dma("sp", X.ap[:, 512:896], ztm_d.ap[t0:t0 + 128, c_nk:c_nk + 384], reads=[ztm_d], writes=[X])
        mk.dma("sp", vin[i].ap, ztm_d.ap[t0:t0 + 128, c_nv:c_nv + 256], reads=[ztm_d], writes=[vin[i]])
        mk.dma("sp", cs[i].ap[:, 0:32], cst["cos"].ap[t0:t0 + 128, :], reads=[cst["cos"]], writes=[cs[i]])
        mk.dma("sp", cs[i].ap[:, 32:64], cst["sin"].ap[t0:t0 + 128, :], reads=[cst["sin"]], writes=[cs[i]])
        yield
        X3 = X.ap[:, :].rearrange("p (h d) -> p h d", h=14)
        S3 = sq[i].ap[:, :].rearrange("p (h d) -> p h d", h=14)
        mk.op("act", lambda en, i=i, X=X: en.activation(out=sq[i].ap, in_=X.ap, func=AF.Square), reads=[X], writes=[sq[i]])
        mk.op("dve", lambda en, i=i, S3=S3: en.tensor_reduce(st[i].ap[:, 0:14], S3, AX.X, ALU.add), reads=[sq[i]], writes=[st[i]])
        mk.op("act", lambda en, i=i: en.activation(out=st[i].ap[:, 14:28], in_=st[i].ap[:, 0:14], func=AF.Ln, scale=1.0 / 64, bias=EPS),
              reads=[st[i]], writes=[st[i]])
        mk.op("act", lambda en, i=i: en.activation(out=st[i].ap[:, 28:42], in_=st[i].ap[:, 14:28], func=AF.Exp, scale=-0.5), reads=[st[i]], writes=[st[i]])
        mk.op("dve", lambda en, i=i, X3=X3: en.tensor_tensor(xn[i].ap, X3, bc_last(st[i].ap[:, 28:42], 64), ALU.mult), reads=[X, st[i]], writes=[xn[i]])
        mk.op("pool", lambda en, i=i: en.tensor_tensor(xn[i].ap, xn[i].ap, gqk.ap, ALU.mult), reads=[xn[i], gqk], writes=[xn[i]])
        yield
        cb_ = bc_mid(cs[i].ap[:, 0:32], 14)
        sb_ = bc_mid(cs[i].ap[:, 32:64], 14)
        x1 = xn[i].ap[:, :, 0:32]
        x2 = xn[i].ap[:, :, 32:64]
        mk.op("dve", lambda en, i=i, x1=x1, cb_=cb_: en.tensor_tensor(r1[i].ap, x1, cb_, ALU.mult), reads=[xn[i], cs[i]], writes=[r1[i]])
        mk.op("pool", lambda en, i=i, x2=x2, sb_=sb_: en.tensor_tensor(r2[i].ap, x2, sb_, ALU.mult), reads=[xn[i], cs[i]], writes=[r2[i]])
        mk.op("dve", lambda en, i=i: en.tensor_tensor(xr[i].ap[:, :, 0:32], r1[i].ap, r2[i].ap, ALU.subtract), reads=[r1[i], r2[i]], writes=[xr[i]])
        mk.op("dve", lambda en, i=i, x2=x2, cb_=cb_: en.tensor_tensor(r1[i].ap, x2, cb_, ALU.mult), reads=[xn[i], cs[i]], writes=[r1[i]])
        mk.op("pool", lambda en, i=i, x1=x1, sb_=sb_: en.tensor_tensor(r2[i].ap, x1, sb_, ALU.mult), reads=[xn[i], cs[i]], writes=[r2[i]])
        mk.op("dve", lambda en, i=i: en.tensor_tensor(xr[i].ap[:, :, 32:64], r1[i].ap, r2[i].ap, ALU.add), reads=[r1[i], r2[i]], writes=[xr[i]])
        yield
        for half, psx in ((0, psA[i % NPS_]), (1, psB[i % NPS_])):
            for hh in range(7):
                h = half * 7 + hh
                mk.op("pe", lambda pe, h=h, hh=hh, i=i, psx=psx: pe.transpose(psx.ap[:, hh, :], xr[i].ap[:, h, :], ident.ap),
                      reads=[xr[i], ident], writes=[psx], inc=(hh == 6))
            if half == 0:
                mk.op("act", lambda en, i=i, psx=psx: en.copy(xT[i].ap[:, 0:7, :], psx.ap), reads=[psx], writes=[xT[i]])
            else:
                mk.op("dve", lambda en, i=i, psx=psx: en.tensor_copy(xT[i].ap[:, 7:14, :], psx.ap), reads=[psx], writes=[xT[i]])
        yield
        mk.dma("pool", QT_d.ap[:, :, t0:t0 + 128], xT[i].ap[:, 0:8, :], reads=[xT[i]], writes=[QT_d])
        mk.dma("pool", KT_d.ap[:, :, t0:t0 + 128], xT[i].ap[:, 8:14, :], reads=[xT[i]], writes=[KT_d])
        mk.op("pool", lambda en, i=i: en.memset(va[i].ap[:, :, 64:65], 1.0), writes=[va[i]])
        mk.op("act", lambda en, i=i: en.copy(va[i].ap[:, :, 0:64], vin[i].ap[:, :].rearrange("p (h d) -> p h d", h=4)), reads=[vin[i]], writes=[va[i]])
        mk.dma("pool", VA_d.ap[t0:t0 + 128, :, :], va[i].ap, reads=[va[i]], writes=[VA_d])
    if defer:
        return (n1body(tt) for tt in range(NT)), (lambda: mk.phase_end())
    run_pipelined((n1body(tt) for tt in range(NT)), NB, stagger=2)
    mk.phase_end()
    return None, None


def stage_nsa(mk, ztm_d, c_nq, c_nk, c_nv, c_ng, zfm_d, r_vc, y_d, T, qn_d, kn_d, posk_d, posv_d, w1k_d, w2k_d, w1v_d, w2v_d,
              cst, ident_d, QT_d, KT_d, VA_d, bg=None, skip_n1=False):
    NT = T // 128
    ncp = T // 16
    ncmp = (T - 32) // 16 + 1
    ns = T // 64
    mk.phase_begin()
    ident, identf = load_consts_ident(mk, ident_d)
    if not skip_n1:
        mk.mark("nsa_N1")
        nsa_n1(mk, ztm_d, c_nq, c_nk, c_nv, T, qn_d, kn_d, cst, ident_d, QT_d, KT_d, VA_d)
    mk.mark("nsa_res")
    KsT = mk.sb([128, 2, T], BF16, "KsT")
    KwT = mk.sb([128, 2, T], BF16, "KwT")
    mk.op("pool", lambda en: en.memset(KsT.ap[64:128, :, :], 0.0), writes=[KsT])
    mk.op("pool", lambda en: en.memset(KwT.ap[64:128, :, :], 0.0), writes=[KwT])
    mk.dma("sp", KsT.ap[0:64, :, :], KT_d.ap[:, 2:4, :], reads=[KT_d], writes=[KsT])
    mk.dma("sp", KwT.ap[0:64, :, :], KT_d.ap[:, 4:6, :], reads=[KT_d], writes=[KwT])
    VA = mk.sb([128, NT, 4, 65], BF16, "VA")
    mk.dma("sp", VA.ap, VA_d.ap.rearrange("(n p) f e -> p n f e", p=128), reads=[VA_d], writes=[VA])
    kcT = mk.sb([128, 2, ncp], BF16, "kcT")
    vcA = mk.sb([128, 2, ncp // 128, 65], BF16, "vcA")
    mk.op("pool", lambda en: en.memset(kcT.ap, 0.0), writes=[kcT])
    mk.op("pool", lambda en: en.memset(vcA.ap, 0.0), writes=[vcA])
    mk.op("pool", lambda en: en.memset(vcA.ap[:, :, :, 64:65], 1.0), writes=[vcA])
    mk.mark("nsa_N2")
    mk.phase_begin()
    pss = [mk.ps([128, 512], F32, "psm") for _ in range(4)]
    for which, (w1_d, w2_d, pos_d) in enumerate(((w1k_d, w2k_d, posk_d), (w1v_d, w2v_d, posv_d))):
        if which == 1:
            mk.phase_end()
        mk.phase_begin()
        w1f = mk.sb([64, 32, 128], F32, "w1f")
        w1 = mk.sb([64, 32, 128], BF16, "w1")
        mk.dma("sp", w1f.ap, w1_d.ap.rearrange("(l d) h -> d l h", d=64), reads=[w1_d], writes=[w1f])
        mk.op("pool", lambda en, w1=w1, w1f=w1f: en.tensor_copy(w1.ap, w1f.ap), reads=[w1f], writes=[w1])
        w2f = mk.sb([128, 64], F32, "w2f")
        w2 = mk.sb([128, 64], BF16, "w2")
        mk.dma("sp", w2f.ap, w2_d.ap, reads=[w2_d], writes=[w2f])
        mk.op("pool", lambda en, w2=w2, w2f=w2f: en.tensor_copy(w2.ap, w2f.ap), reads=[w2f], writes=[w2])
        posf = mk.sb([64, 32], F32, "posf")
        posb = mk.sb([64, 32], BF16, "posb")
        mk.dma("sp", posf.ap, pos_d.ap.rearrange("l d -> d l"), reads=[pos_d], writes=[posf], allow_slow_non_contiguous=True)
        mk.op("pool", lambda en, posb=posb, posf=posf: en.tensor_copy(posb.ap, posf.ap), reads=[posf], writes=[posb])
        bias = mk.sb([128, 1], F32, "bias")
        pbias = pss[0]
        mk.mm(pbias.ap[:, 0:1], [(w1.ap[:, l, :], posb.ap[:, l:l + 1]) for l in range(32)], reads=[w1, posb], tr=pbias)
        mk.op("act", lambda en, bias=bias, pbias=pbias: en.copy(bias.ap, pbias.ap[:, 0:1]), reads=[pbias], writes=[bias])
        for g in range(2):
            XT = mk.sb([64, T], BF16, "XT")
            if which == 0:
                mk.dma("sp", XT.ap, KT_d.ap[:, g, :], reads=[KT_d], writes=[XT])
            else:
                XTf = mk.sb([64, T], F32, "XTf")
                mk.dma("sp", XTf.ap, zfm_d.ap[r_vc + g * 64:r_vc + (g + 1) * 64, :], reads=[zfm_d], writes=[XTf])
                mk.op("pool", lambda en, XT=XT, XTf=XTf: en.tensor_copy(XT.ap, XTf.ap), reads=[XTf], writes=[XT])
            ph = pss[1 + g]
            mk.mm(ph.ap[:, 0:ncmp], [(w1.ap[:, l, :], XT.ap[:, l:l + 16 * (ncmp - 1) + 1:16]) for l in range(32)], reads=[w1, XT], tr=ph)
            xs = mk.sb([128, ncp], F32, "xs")
            x2 = mk.sb([128, ncp], F32, "x2")
            ge = mk.sb([128, ncp], BF16, "ge")
            mk.op("pool", lambda en, ge=ge: en.memset(ge.ap, 0.0), writes=[ge])
            mk.op("act", lambda en, xs=xs, ph=ph, bias=bias: en.activation(out=xs.ap[:, 0:ncmp], in_=ph.ap[:, 0:ncmp], func=AF.Identity, bias=bias.ap[:, 0:1]),
                  reads=[ph, bias], writes=[xs])
            mk.op("pool", lambda en, xs=xs, x2=x2: en.tensor_tensor(x2.ap[:, 0:ncmp], xs.ap[:, 0:ncmp], xs.ap[:, 0:ncmp], ALU.mult), reads=[xs], writes=[x2])
            mk.op("dve", lambda en, x2=x2: en.tensor_scalar(x2.ap[:, 0:ncmp], x2.ap[:, 0:ncmp], 0.044715, 1.0, ALU.mult, ALU.add), reads=[x2], writes=[x2])
            mk.op("dve", lambda en, xs=xs, x2=x2: en.tensor_tensor(x2.ap[:, 0:ncmp], x2.ap[:, 0:ncmp], xs.ap[:, 0:ncmp], ALU.mult), reads=[xs, x2], writes=[x2])
            mk.op("act", lambda en, x2=x2: en.activation(out=x2.ap[:, 0:ncmp], in_=x2.ap[:, 0:ncmp], func=AF.Sigmoid, scale=1.5957691216057308),
                  reads=[x2], writes=[x2])
            mk.op("dve", lambda en, xs=xs, x2=x2, ge=ge: en.tensor_tensor(ge.ap[:, 0:ncmp], x2.ap[:, 0:ncmp], xs.ap[:, 0:ncmp], ALU.mult), reads=[xs, x2], writes=[ge])
            po = pss[3]
            if which == 0:
                mk.mm(po.ap[0:64, 0:ncmp], [(w2.ap, ge.ap[:, 0:ncmp])], reads=[w2, ge], tr=po)
                mk.op("act", lambda en, g=g, po=po: en.copy(kcT.ap[0:64, g, 0:ncmp], po.ap[0:64, 0:ncmp]), reads=[po], writes=[kcT])
            else:
                for nt in range(ncp // 128):
                    mk.mm(po.ap[:, nt * 64:(nt + 1) * 64], [(ge.ap[:, nt * 128:(nt + 1) * 128], w2.ap)], reads=[w2, ge], tr=po)
                mk.op("act", lambda en, g=g, po=po: en.copy(vcA.ap[:, g, :, 0:64], po.ap[:, 0:(ncp // 128) * 64].rearrange("p (n d) -> p n d", d=64)),
                      reads=[po], writes=[vcA])
    mk.phase_end()
    mk.phase_end()
    mk.mark("nsa_N3")
    mk.phase_begin()
    causT = mk.sb([128, 128], F32, "causT")
    farT = mk.sb([128, 128], F32, "farT")
    mk.dma("sp", causT.ap, cst["causT"].ap, reads=[cst["causT"]], writes=[causT])
    mk.dma("sp", farT.ap, cst["farT"].ap, reads=[cst["farT"]], writes=[farT])
    esel = mk.sb([ns, T], BF16, "esel")
    mk.phase_begin()
    eself = mk.sb([ns, T], F32, "eself")
    mk.dma("sp", eself.ap, cst["esel"].ap, reads=[cst["esel"]], writes=[eself])
    mk.op("pool", lambda en: en.tensor_copy(esel.ap, eself.ap), reads=[eself], writes=[esel])
    mk.phase_end()
    NQ = ncp // 128
    NS = 2
    psS = [mk.ps([128, 512], F32, "psS") for _ in range(NS)]
    psO = [mk.ps([128, 512], F32, "psO") for _ in range(NS)]
    psT = [[mk.ps([128, 512], F32, "psST") for _ in range(1 if bg is not None else 2)] for _ in range(NS)]
    psM = psS
    mk.ndram = getattr(mk, "ndram", 0) + 1
    selTd = [mk.dram("selTd%d_%d" % (mk.ndram, j), [ns, 128], BF16) for j in range(NS)]
    def mkb(shape, name, dt=F32, n=NS):
        return [mk.sb(list(shape), dt, name) for _ in range(n)]
    QT = mkb([128, 4, 128], "QT", BF16)
    for q_ in QT:
        mk.op("pool", lambda en, q_=q_: en.memset(q_.ap[64:128, :, :], 0.0), writes=[q_])
    gl = mkb([128, 24], "gl")
    cm, cmT = mkb([128, ncp], "cm"), mkb([128, NQ, 128], "cmT")
    m1t, c2t = mkb([128, ns], "m1t"), mkb([128, ns], "c2t")
    sS = mkb([128, 4, ncp], "sS")
    sE = sS
    stt = mkb([128, 16], "stt")
    ph = mkb([128, ncp + 1], "ph")
    imp, wk = mkb([128, ns], "imp"), mkb([128, ns], "wk")
    m8 = mkb([128, 16], "m8")
    selb = mkb([128, ns], "selb")
    selT = mkb([ns, 128], "selT", BF16)
    eT = [mkb([128, 4, 128], "eT", BF16, 3) for _ in range(NS)]
    causB = mk.sb([128, 128], BF16, "causB")
    farB = mk.sb([128, 128], BF16, "farB")
    mk.op("dve", lambda en: en.tensor_copy(causB.ap, causT.ap), reads=[causT], writes=[causB])
    mk.op("dve", lambda en: en.tensor_copy(farB.ap, farT.ap), reads=[farT], writes=[farB])
    cmTb = mkb([128, NQ, 128], "cmTb", BF16)
    bm = [mkb([128, 128], "bm", BF16, 2) for _ in range(NS)]
    rr = mkb([128, 12], "rr")
    acc = mkb([128, 4, 64], "acc")
    tmpo = mkb([128, 4, 64], "tmpo")
    PTall = mkb([128, NT, 4, 128], "PTall", BF16)

    def attn_branch(i, tiles, Ops):
        PA = PTall[i]
        QTi = QT[i]
        n = len(tiles)
        for idx, (KT_ap, V_ap, mask_fn) in enumerate(tiles):
            pt = psT[i][idx % len(psT[i])]
            mk.mm(pt.ap, [(KT_ap, QTi.ap[:, :, :].rearrange("p h q -> p (h q)"))], reads=[QTi, KsT, KwT, kcT], tr=pt)
            e = eT[i][idx % 3]
            p4 = pt.ap[:, :].rearrange("p (h q) -> p h q", h=4)
            msk = mask_fn(idx)
            if msk is None:
                mk.op("act", lambda en, idx=idx, p4=p4: en.activation(out=PA.ap[:, idx, :, :], in_=p4, func=AF.Exp), reads=[pt], writes=[PA])
            else:
                mk.op("act", lambda en, e=e, p4=p4: en.activation(out=e.ap, in_=p4, func=AF.Exp), reads=[pt], writes=[e])
                mb, mreads = msk
                mk.op("dve", lambda en, idx=idx, e=e, mb=mb: en.tensor_tensor(PA.ap[:, idx, :, :], e.ap, bc_mid(mb, 4), ALU.mult), reads=[e] + mreads, writes=[PA])
            yield
        for h in range(4):
            for idx, (KT_ap, V_ap, mask_fn) in enumerate(tiles):
                mk.op("pe", lambda pe, h=h, idx=idx, V_ap=V_ap: pe.matmul(Ops.ap[:, h * 65:(h + 1) * 65], PA.ap[:, idx, h, :], V_ap, start=(idx == 0), stop=(idx == n - 1)),
                      reads=[PA, VA, vcA], writes=[Ops], inc=(idx == n - 1))
            yield

    def combine(i, g, x, first):
        Ops = psO[i]
        O3 = Ops.ap[:, 0:260].rearrange("p (h e) -> p h e", e=65)
        mk.op("dve", lambda en: en.tensor_scalar(rr[i].ap[:, 4 * x:4 * x + 4], O3[:, :, 64], 1e-30, None, ALU.max), reads=[Ops], writes=[rr[i]])
        mk.op("dve", lambda en: en.reciprocal(rr[i].ap[:, 4 * x:4 * x + 4], rr[i].ap[:, 4 * x:4 * x + 4]), reads=[rr[i]], writes=[rr[i]])
        gx = gl[i].ap[:, g * 12:(g + 1) * 12].rearrange("p (h x) -> p h x", x=3)[:, :, x]
        mk.op("dve", lambda en: en.tensor_tensor(rr[i].ap[:, 4 * x:4 * x + 4], rr[i].ap[:, 4 * x:4 * x + 4], gx, ALU.mult), reads=[rr[i], gl[i]], writes=[rr[i]])
        if first:
            mk.op("dve", lambda en: en.tensor_tensor(acc[i].ap, O3[:, :, 0:64], bc_last(rr[i].ap[:, 4 * x:4 * x + 4], 64), ALU.mult), reads=[Ops, rr[i]], writes=[acc[i]])
        else:
            mk.op("dve", lambda en: en.tensor_tensor(tmpo[i].ap, O3[:, :, 0:64], bc_last(rr[i].ap[:, 4 * x:4 * x + 4], 64), ALU.mult), reads=[Ops, rr[i]], writes=[tmpo[i]])
            mk.op("pool", lambda en: en.tensor_tensor(acc[i].ap, acc[i].ap, tmpo[i].ap, ALU.add), reads=[acc[i], tmpo[i]], writes=[acc[i]])

    def body(it, g, qt):
        i = it % NS
        t0 = qt * 128
        mk.dma("sp", QT[i].ap[0:64, :, :], QT_d.ap[:, 4 * g:4 * g + 4, t0:t0 + 128], reads=[QT_d], writes=[QT[i]])
        mk.dma("sp", gl[i].ap, ztm_d.ap[t0:t0 + 128, c_ng:c_ng + 24], reads=[ztm_d], writes=[gl[i]])
        mk.dma("sp", cm[i].ap, cst["cmpm"].ap[t0:t0 + 128, :], reads=[cst["cmpm"]], writes=[cm[i]])
        mk.dma("sp", cmT[i].ap, cst["cmpmT"].ap[:, t0:t0 + 128].rearrange("(n p) q -> p n q", p=128), reads=[cst["cmpmT"]], writes=[cmT[i]])
        mk.dma("sp", m1t[i].ap, cst["selm1"].ap[t0:t0 + 128, :], reads=[cst["selm1"]], writes=[m1t[i]])
        mk.dma("sp", c2t[i].ap, cst["selc2"].ap[t0:t0 + 128, :], reads=[cst["selc2"]], writes=[c2t[i]])
        exp_sigmoid(mk, gl[i], gl[i].ap, gl[i], gl[i].ap)
        mk.op("pool", lambda en: en.tensor_copy(cmTb[i].ap, cmT[i].ap), reads=[cmT[i]], writes=[cmTb[i]])
        yield
        tl = []
        for kt in range(max(0, qt - 4), qt + 1):
            if kt == qt:
                mf = lambda k: (causB.ap, [causB])
            elif kt == qt - 4:
                mf = lambda k: (farB.ap, [farB])
            else:
                mf = lambda k: None
            tl.append((KwT.ap[:, g, kt * 128:(kt + 1) * 128], VA.ap[:, kt, 2 + g, :], mf))
        yield from attn_branch(i, tl, psO[i])
        combine(i, g, 2, True)
        yield
        for hp in range(2):
            for hh in range(2):
                mk.mm(psS[i].ap[:, hh * ncp:(hh + 1) * ncp], [(QT[i].ap[:, hp * 2 + hh, :], kcT.ap[:, g, :])], reads=[QT[i], kcT], tr=psS[i])
            mk.op("act", lambda en, hp=hp: en.copy(sS[i].ap[:, hp * 2:hp * 2 + 2, :], psS[i].ap[:, 0:2 * ncp].rearrange("p (h n) -> p h n", h=2)),
                  reads=[psS[i]], writes=[sS[i]])
        yield
        mk.op("dve", lambda en: en.tensor_reduce(stt[i].ap[:, 0:4], sS[i].ap, AX.X, ALU.max), reads=[sS[i]], writes=[stt[i]])
        mk.op("dve", lambda en: en.tensor_tensor(sS[i].ap, sS[i].ap, bc_last(stt[i].ap[:, 0:4], ncp), ALU.subtract), reads=[sS[i], stt[i]], writes=[sS[i]])
        mk.op("act", lambda en: en.activation(out=sE[i].ap, in_=sS[i].ap, func=AF.Exp), reads=[sS[i]], writes=[sE[i]])
        mk.op("pool", lambda en: en.tensor_tensor(sE[i].ap, sE[i].ap, bc_mid(cm[i].ap, 4), ALU.mult), reads=[sE[i], cm[i]], writes=[sE[i]])
        yield
        mk.op("dve", lambda en: en.tensor_reduce(stt[i].ap[:, 4:8], sE[i].ap, AX.X, ALU.add), reads=[sE[i]], writes=[stt[i]])
        mk.op("dve", lambda en: en.tensor_scalar(stt[i].ap[:, 4:8], stt[i].ap[:, 4:8], 1e-30, None, ALU.max), reads=[stt[i]], writes=[stt[i]])
        mk.op("dve", lambda en: en.reciprocal(stt[i].ap[:, 8:12], stt[i].ap[:, 4:8]), reads=[stt[i]], writes=[stt[i]])
        mk.op("dve", lambda en: en.tensor_tensor(sE[i].ap, sE[i].ap, bc_last(stt[i].ap[:, 8:12], ncp), ALU.mult), reads=[sE[i], stt[i]], writes=[sE[i]])
        yield
        mk.op("pool", lambda en: en.memset(ph[i].ap[:, 0:1], 0.0), writes=[ph[i]])
        mk.op("dve", lambda en: en.tensor_reduce(ph[i].ap[:, 1:ncp + 1], sE[i].ap[:, :, :].rearrange("p h n -> p n h"), AX.X, ALU.add),
              reads=[sE[i]], writes=[ph[i]])
        mk.op("dve", lambda en: en.tensor_reduce(imp[i].ap, ph[i].ap[:, 1:ncp + 1].rearrange("p (j f) -> p j f", f=4), AX.X, ALU.add),
              reads=[ph[i]], writes=[imp[i]])
        mk.op("dve", lambda en: en.tensor_reduce(wk[i].ap, ph[i].ap[:, 0:ncp].rearrange("p (j f) -> p j f", f=4), AX.X, ALU.add),
              reads=[ph[i]], writes=[wk[i]])
        yield
        mk.op("dve", lambda en: en.tensor_tensor(imp[i].ap, imp[i].ap, wk[i].ap, ALU.add), reads=[imp[i], wk[i]], writes=[imp[i]])
        mk.op("dve", lambda en: en.scalar_tensor_tensor(imp[i].ap, imp[i].ap, 16.0, m1t[i].ap, ALU.mult, ALU.mult), reads=[imp[i], m1t[i]], writes=[imp[i]])
        mk.op("dve", lambda en: en.tensor_tensor(imp[i].ap, imp[i].ap, c2t[i].ap, ALU.add), reads=[imp[i], c2t[i]], writes=[imp[i]])
        yield
        mk.op("dve", lambda en: en.max(out=m8[i].ap[:, 0:8], in_=imp[i].ap), reads=[imp[i]], writes=[m8[i]])
        mk.op("dve", lambda en: en.match_replace(out=wk[i].ap, in_to_replace=m8[i].ap[:, 0:8], in_values=imp[i].ap, imm_value=-2.0),
              reads=[imp[i], m8[i]], writes=[wk[i]])
        mk.op("dve", lambda en: en.max(out=m8[i].ap[:, 8:16], in_=wk[i].ap), reads=[wk[i]], writes=[m8[i]])
        yield
        mk.op("dve", lambda en: en.tensor_reduce(stt[i].ap[:, 12:13], m8[i].ap[:, 8:16], AX.X, ALU.min), reads=[m8[i]], writes=[stt[i]])
        mk.op("dve", lambda en: en.tensor_scalar(selb[i].ap, imp[i].ap, stt[i].ap[:, 12:13], None, ALU.is_ge), reads=[imp[i], stt[i]], writes=[selb[i]])
        mk.op("pe", lambda pe: pe.transpose(psM[i].ap[0:ns, 0:128], selb[i].ap, identf.ap), reads=[selb[i], identf], writes=[psM[i]])
        mk.op("act", lambda en: en.copy(selT[i].ap, psM[i].ap[0:ns, 0:128]), reads=[psM[i]], writes=[selT[i]])
        yield
        yield from attn_branch(i, [(kcT.ap[:, g, nt * 128:(nt + 1) * 128], vcA.ap[:, g, nt, :], (lambda k, nt=nt: (cmTb[i].ap[:, nt, :], [cmTb[i]])))
                                  for nt in range(NQ)], psO[i])
        combine(i, g, 0, False)
        yield
        tl = []
        for kt in range(qt + 1):
            def mf(k, kt=kt):
                b = bm[i][k % 2]
                mk.mm(psM[i].ap[:, 0:128], [(esel.ap[:, kt * 128:(kt + 1) * 128], selT[i].ap)], reads=[esel, selT[i]], tr=psM[i])
                if kt == qt:
                    mk.op("dve", lambda en: en.tensor_tensor(b.ap, psM[i].ap[:, 0:128], causT.ap, ALU.mult), reads=[psM[i], causT], writes=[b])
                else:
                    mk.op("act", lambda en: en.copy(b.ap, psM[i].ap[:, 0:128]), reads=[psM[i]], writes=[b])
                return (b.ap, [b])
            tl.append((KsT.ap[:, g, kt * 128:(kt + 1) * 128], VA.ap[:, kt, g, :], mf))
        yield from attn_branch(i, tl, psO[i])
        combine(i, g, 1, False)
        mk.dma("pool", y_d.ap[t0:t0 + 128, g * 256:(g + 1) * 256], acc[i].ap[:, :, :].rearrange("p h d -> p (h d)"), reads=[acc[i]], writes=[y_d])

    order = [(g, qt) for qt in range(NT) for g in range(2)]
    if bg is not None:
        run_pipelined_multi([{"gens": (body(it, g, qt) for it, (g, qt) in enumerate(order)), "width": NS},
                             {"gens": bg, "width": 1, "period": 3}])
    else:
        run_pipelined((body(it, g, qt) for it, (g, qt) in enumerate(order)), NS, stagger=14)
    mk.phase_end()
    mk.phase_end()


T_SEQ = 4096
TAIL_T = 2176
NTM, NFM = 2208, 1408
C_NQ, C_NK, C_NV, C_NG, C_HF, C_HI, C_HG, C_CZ, C_CB, C_CA = 0, 512, 896, 1152, 1176, 1432, 1688, 1944, 2200, 2204
R_VC, R_HQ, R_HF, R_QKV = 0, 128, 384, 640
_r = lambda a, b: list(range(a, b))
W_IN_COLS = (_r(0, 512) + _r(512, 640) + _r(768, 896) + _r(1024, 1152) + _r(896, 1024) + _r(1152, 1280) + _r(1280, 1304)
             + _r(1560, 1816) + _r(1816, 2072) + _r(2072, 2328) + _r(3096, 3352) + _r(3352, 3356) + _r(3356, 3360)
             + _r(640, 768) + _r(1304, 1560) + _r(1560, 1816) + _r(2328, 3096))
assert len(W_IN_COLS) == NTM + NFM

LAYER_KEYS = ["norm_mix", "w_in_sel", "w_gate", "nsa_q_norm", "nsa_k_norm", "cmp_pos_k", "cmp_pos_v", "cmp_k_w1", "cmp_k_w2",
              "cmp_v_w1", "cmp_v_w2", "hgrn_out_norm", "gdn_conv_t", "gdn_a_log", "gdn_dt_bias", "gdn_out_norm",
              "w_branch_a", "w_branch_b", "w_branch_c", "w_mix_out", "norm_cross", "xattn_wq", "xattn_q_norm", "xattn_k_norm",
              "xattn_wo", "norm_ffn", "ffn_w_up", "ffn_conv_t", "ffn_w_down"]


def const_arrays(T):
    c = dict(gdn_consts())
    c.update(nsa_consts(T))
    c["ident"] = np.eye(128, dtype=np.float32)
    return c


def layer_arrays(inputs, l):
    f = lambda a: np.ascontiguousarray(np.asarray(a, dtype=np.float32))
    w_in = np.asarray(inputs["w_in"][l])
    d = {"w_in_sel": f(w_in[:, W_IN_COLS]), "w_gate": f(w_in[:, 3360:6432]),
         "gdn_conv_t": f(np.asarray(inputs["gdn_conv"][l]).T), "ffn_conv_t": f(np.asarray(inputs["ffn_conv"][l]).T)}
    for k in LAYER_KEYS:
        if k not in d:
            d[k] = f(inputs[k][l])
    return d


def build_program(T=T_SEQ, depth=2, shapes=None, stop_after=None):
    mk = MK()
    mk.live.append([])
    ext = lambda name, shape: mk.dram(name, shape, kind="ExternalInput")
    x_in = ext("x", [T, D])
    mem_in = ext("mem", [256, D])
    mem_norm = ext("mem_norm", [D])
    mem_w_kv = ext("mem_w_kv", [D, D])
    lbl = ext("hgrn_lb_logits", [2, 256])
    cst = {k: ext("c_" + k, list(v.shape)) for k, v in const_arrays(T).items()}
    L = []
    for l in range(depth):
        L.append({k: ext("L%d_%s" % (l, k), list(shapes[k])) for k in LAYER_KEYS})
    out = mk.dram("out", [TAIL_T, D], kind="ExternalOutput")
    hsel = ext("hsel", [1])
    xh = [mk.dram("xh%d" % j, [TAIL_T, D]) for j in range(3)]
    xs = [mk.dram("xA", [T, D]), mk.dram("xB", [T, D])]
    Ztm = mk.dram("Ztm", [T, NTM])
    Zfm = mk.dram("Zfm", [NFM, T])
    Y = mk.dram("Y", [T, D])
    MKV = mk.dram("MKV", [256, D])
    QT_d = mk.dram("QT", [64, 8, T], BF16)
    KT_d = mk.dram("KT", [64, 6, T], BF16)
    VA_d = mk.dram("VA", [T, 4, 65], BF16)
    ident = cst["ident"]
    stage_proj(mk, mem_in, mem_norm, mem_w_kv, MKV, None, 256, D, 0, ident)
    cur = x_in
    nstage = 0
    def nxt(last=False):
        return out if last else xs[nstage % 2]
    for l in range(depth):
        W = L[l]
        mk.mark("stage_proj")
        stage_proj(mk, cur, W["norm_mix"], W["w_in_sel"], Ztm, Zfm, T, NTM, NFM, ident)
        mk.mark("stage_gdn")
        n1_gens, n1_fin = nsa_n1(mk, Ztm, C_NQ, C_NK, C_NV, T, W["nsa_q_norm"], W["nsa_k_norm"], cst, ident, QT_d, KT_d, VA_d, defer=True)
        stage_gdn(mk, Ztm, C_CZ, C_CB, C_CA, Zfm, R_QKV, Y, 768, T, W["gdn_conv_t"], W["gdn_a_log"], W["gdn_dt_bias"], W["gdn_out_norm"], cst, bg=n1_gens)
        n1_fin()
        mk.mark("stage_nsa")
        hg_gens, hg_fin = stage_hgrn(mk, Ztm, C_HF, C_HI, C_HG, Zfm, R_HQ, R_HF, Y, 512, T, lbl, l, W["hgrn_out_norm"], cst["triI"], ident, defer=True)
        stage_nsa(mk, Ztm, C_NQ, C_NK, C_NV, C_NG, Zfm, R_VC, Y, T, W["nsa_q_norm"], W["nsa_k_norm"], W["cmp_pos_k"], W["cmp_pos_v"],
                  W["cmp_k_w1"], W["cmp_k_w2"], W["cmp_v_w1"], W["cmp_v_w2"], cst, ident, QT_d, KT_d, VA_d, bg=hg_gens, skip_n1=True)
        hg_fin()
        if stop_after == ("mix", l):
            mk.dma("sp", out.ap, Y.ap, reads=[Y], writes=[out])
            break
        last = (l == depth - 1)
        Tt = TAIL_T if last else T
        xin, yin = cur, Y
        if last:
            mk.mark("stage_select")
            xin, yin = xh[0], xh[1]
            stage_select(mk, cur, xin, Tt, T - Tt, hsel)
            stage_select(mk, Y, yin, Tt, T - Tt, hsel)
        x1 = xh[2] if last else nxt(); nstage += 1
        mk.mark("stage_merge")
        stage_merge(mk, xin, yin, x1, Tt, W["norm_mix"], W["w_gate"], W["w_branch_a"], W["w_branch_b"], W["w_branch_c"], W["w_mix_out"], ident)
        x2 = xh[0] if last else nxt(); nstage += 1
        mk.mark("stage_xattn")
        stage_xattn(mk, x1, x2, Tt, W["norm_cross"], MKV, W["xattn_wq"], W["xattn_q_norm"], W["xattn_k_norm"], W["xattn_wo"], ident)
        x3 = out if last else nxt(); nstage += 1
        mk.mark("stage_ffn")
        stage_ffn(mk, x2, x3, Tt, W["norm_ffn"], W["ffn_w_up"], W["ffn_conv_t"], W["ffn_w_down"], ident)
        cur = x3
    mk.mark("end")
    mk.finish()
    return mk


_PROG = {}


def kernel(**inputs):
    x = np.asarray(inputs["x"], dtype=np.float32)
    B, T, _ = x.shape
    depth = np.asarray(inputs["w_in"]).shape[0]
    layers = [layer_arrays(inputs, l) for l in range(depth)]
    shapes = {k: v.shape for k, v in layers[0].items()}
    key = (T, depth)
    if key not in _PROG:
        _PROG[key] = build_program(T, depth, shapes)
    mk = _PROG[key]
    f = lambda a: np.ascontiguousarray(np.asarray(a, dtype=np.float32))
    common = {"mem_norm": f(inputs["mem_norm"]), "mem_w_kv": f(inputs["mem_w_kv"]), "hgrn_lb_logits": f(inputs["hgrn_lb_logits"])}
    for k, v in const_arrays(T).items():
        common["c_" + k] = v
    for l in range(depth):
        for k, v in layers[l].items():
            common["L%d_%s" % (l, k)] = v
    n = 8
    in_maps = []
    for c in range(n):
        b = c % B
        m = dict(common)
        m["x"] = f(x[b])
        m["mem"] = f(np.asarray(inputs["mem"])[b])
        m["hsel"] = np.full((1,), float(c // B), np.float32)
        in_maps.append(m)
    res = run_bass_kernel_spmd(mk.nc, in_maps, core_ids=list(range(n)))
    Th = T // 2
    outs = []
    for b in range(B):
        lo = np.asarray(res.results[b]["out"], dtype=np.float32)[0:Th]
        hi = np.asarray(res.results[B + b]["out"], dtype=np.float32)[TAIL_T - Th:TAIL_T]
        outs.append(np.concatenate([lo, hi], axis=0))
    return np.stack(outs, axis=0)
```
